# Optimizing a Trainium2 kernel written in Bass

```python
import jax, jax.numpy as jnp
from jax import lax
import numpy as np

D_MODEL = 1024
BATCH = 4
SEQ = 4096
DEPTH = 1
DEC_BATCH = 128
DEC_SEQ = 8
PAST_LEN = 2048
PAGE_SIZE = 128

MIX_WIDTH = D_MODEL
SB_HEADS = 8
SB_HEAD_DIM = (MIX_WIDTH // 2) // SB_HEADS
SB_WIDTH = SB_HEADS * SB_HEAD_DIM
SB_BLOCK = 128
SB_BIAS_HI = -5.0
SB_BIAS_LO = -9.0
DN_HEADS = 4
DN_HEAD_DIM = (MIX_WIDTH - SB_WIDTH) // DN_HEADS
DN_WIDTH = DN_HEADS * DN_HEAD_DIM
DN_CONV = 4
DN_CHUNK = 64
D_FF = 256 * ((8 * D_MODEL // 3 + 255) // 256)
FFN_CONV = 3
NORM_EPS = 1e-6
N_MOD = 6
IN_COLS = 3 * SB_WIDTH + 3 * DN_WIDTH + 2 * DN_HEADS + DN_WIDTH

kernel_name = 'hybrid_stickbreak_gdn_convffn_adaln_step'


def rms_norm(x, g):
    xf = x.astype(jnp.float32)
    y = xf * lax.rsqrt(jnp.mean(xf * xf, axis=-1, keepdims=True) + NORM_EPS)
    return (y * g.astype(jnp.float32)).astype(x.dtype)


def l2_norm(x):
    xf = x.astype(jnp.float32)
    return xf * lax.rsqrt(jnp.sum(xf * xf, axis=-1, keepdims=True) + NORM_EPS)


def causal_dwconv(x, buf, w):
    width = w.shape[0]
    t = x.shape[1]
    xp = jnp.concatenate([buf.astype(x.dtype), x], axis=1)
    y = w[0] * xp[:, 0:t]
    for i in range(1, width):
        y = y + w[i] * xp[:, i:i + t]
    return y, xp[:, t:]


def sb_block(q, k, v, bias, q_pos, k_pos):
    f32 = jnp.float32
    z = jnp.einsum('bqhd,bkhd->bhqk', q.astype(f32), k.astype(f32)) * (SB_HEAD_DIM ** -0.5)
    z = z + bias.astype(f32)[None, :, None, None]
    mask = k_pos[None, :] < q_pos[:, None]
    log_beta = jax.nn.log_sigmoid(z)
    log_one_minus = jnp.where(mask, jax.nn.log_sigmoid(-z), 0.0)
    later = lax.cumsum(log_one_minus, axis=3, reverse=True) - log_one_minus
    att = jnp.where(mask, jnp.exp(log_beta + later), 0.0)
    out = jnp.einsum('bhqk,bkhd->bqhd', att, v.astype(f32))
    return out.astype(q.dtype)


def sb_attention(q, k, v, bias, q_offset):
    tq = q.shape[1]
    outs = []
    for start in range(0, tq, SB_BLOCK):
        end = min(start + SB_BLOCK, tq)
        kend = q_offset + end
        q_pos = q_offset + jnp.arange(start, end)
        k_pos = jnp.arange(kend)
        outs.append(sb_block(q[:, start:end], k[:, :kend], v[:, :kend], bias, q_pos, k_pos))
    return jnp.concatenate(outs, axis=1)


def gated_delta_rule(q, k, v, beta, log_a, s0):
    f32 = jnp.float32
    bsz, t, h, dk = q.shape
    dv = v.shape[-1]
    c = min(DN_CHUNK, t)
    pad = (-t) % c
    n = (t + pad) // c

    def to_chunks(a):
        a = a.astype(f32)
        if pad:
            a = jnp.pad(a, [(0, 0), (0, pad)] + [(0, 0)] * (a.ndim - 2))
        a = a.reshape((bsz, n, c) + a.shape[2:])
        return a.transpose(1, 0, 3, 2, 4) if a.ndim == 5 else a.transpose(1, 0, 3, 2)

    qc, kc, vc = to_chunks(q), to_chunks(k), to_chunks(v)
    bc, ac = to_chunks(beta), to_chunks(log_a)
    g = jnp.cumsum(ac, axis=-1)
    idx = jnp.arange(c)
    lower_incl = idx[:, None] >= idx[None, :]
    strict = idx[:, None] > idx[None, :]
    decay = jnp.exp(jnp.where(lower_incl, g[..., :, None] - g[..., None, :], -jnp.inf))
    kb = kc * bc[..., None]
    a_mat = jnp.where(strict, jnp.einsum('nbhid,nbhjd->nbhij', kb, kc) * decay, 0.0)
    rhs = jnp.concatenate([vc * bc[..., None], kb * jnp.exp(g)[..., None]], axis=-1)
    sol = lax.linalg.triangular_solve(a_mat + jnp.eye(c, dtype=f32), rhs,
                                      left_side=True, lower=True, unit_diagonal=True)
    u_val, k_cum = sol[..., :dv], sol[..., dv:]
    intra = jnp.einsum('nbhid,nbhjd->nbhij', qc, kc) * decay
    q_dec = qc * jnp.exp(g)[..., None]
    k_dec = kc * jnp.exp(g[..., -1:] - g)[..., None]
    g_last = jnp.exp(g[..., -1])

    def step(s, inp):
        u, kcum, intra_c, qd, kd, gl = inp
        v_new = u - jnp.einsum('bhcd,bhde->bhce', kcum, s)
        o = jnp.einsum('bhcd,bhde->bhce', qd, s) + jnp.einsum('bhij,bhje->bhie', intra_c, v_new)
        s = s * gl[..., None, None] + jnp.einsum('bhcd,bhce->bhde', kd, v_new)
        return s, o

    s_fin, o = lax.scan(step, s0.astype(f32), (u_val, k_cum, intra, q_dec, k_dec, g_last))
    o = o.transpose(1, 0, 3, 2, 4).reshape(bsz, n * c, h, dv)[:, :t]
    return o, s_fin


def hybrid_layer(x, c, past_k, past_v, s0, dn_buf, ffn_buf,
                 w_ada, b_ada, g_attn_norm, w_in, g_q, g_k, sb_bias, g_sb_out, w_dn_conv,
                 a_log, dt_bias, g_dn_out, w_out, g_ffn_norm, w_up, w_ffn_conv, w_down):
    bsz, t, _ = x.shape
    mod = (jax.nn.silu(c) @ w_ada + b_ada)[:, None, :]
    sh1, sc1, gt1, sh2, sc2, gt2 = jnp.split(mod, N_MOD, axis=-1)

    h = rms_norm(x, g_attn_norm) * (1.0 + sc1) + sh1
    proj = h @ w_in
    o1 = 3 * SB_WIDTH
    o2 = o1 + 3 * DN_WIDTH
    o3 = o2 + DN_HEADS
    o4 = o3 + DN_HEADS
    q_sb, k_sb, v_sb = jnp.split(proj[..., :o1], 3, axis=-1)
    qkv_dn = proj[..., o1:o2]
    b_raw = proj[..., o2:o3]
    a_raw = proj[..., o3:o4]
    z_gate = proj[..., o4:]

    hs = (bsz, t, SB_HEADS, SB_HEAD_DIM)
    q = rms_norm(q_sb.reshape(hs), g_q)
    k = rms_norm(k_sb.reshape(hs), g_k)
    v = v_sb.reshape(hs)
    if past_k is None:
        k_all, v_all, q_offset = k, v, 0
    else:
        k_all = jnp.concatenate([past_k.astype(k.dtype), k], axis=1)
        v_all = jnp.concatenate([past_v.astype(v.dtype), v], axis=1)
        q_offset = past_k.shape[1]
    o_sb = rms_norm(sb_attention(q, k_all, v_all, sb_bias, q_offset), g_sb_out)

    qkv_c, new_dn_buf = causal_dwconv(qkv_dn, dn_buf, w_dn_conv)
    qkv_c = jax.nn.silu(qkv_c)
    qd, kd, vd = jnp.split(qkv_c, 3, axis=-1)
    hd = (bsz, t, DN_HEADS, DN_HEAD_DIM)
    qd = l2_norm(qd.reshape(hd)) * (DN_HEAD_DIM ** -0.5)
    kd = l2_norm(kd.reshape(hd))
    vd = vd.reshape(hd)
    beta = jax.nn.sigmoid(b_raw.astype(jnp.float32))
    log_a = -jnp.exp(a_log.astype(jnp.float32)) * jax.nn.softplus(
        a_raw.astype(jnp.float32) + dt_bias.astype(jnp.float32))
    o_dn, s_new = gated_delta_rule(qd, kd, vd, beta, log_a, s0)
    zg = jax.nn.silu(z_gate.astype(jnp.float32)).reshape(hd)
    o_dn = (rms_norm(o_dn, g_dn_out) * zg).astype(x.dtype)

    mixed = jnp.concatenate([o_sb.reshape(bsz, t, SB_WIDTH), o_dn.reshape(bsz, t, DN_WIDTH)], axis=-1)
    x = x + gt1 * (mixed @ w_out)

    h2 = rms_norm(x, g_ffn_norm) * (1.0 + sc2) + sh2
    up, new_ffn_buf = causal_dwconv(h2 @ w_up, ffn_buf, w_ffn_conv)
    u, gate = jnp.split(up, 2, axis=-1)
    x = x + gt2 * ((jax.nn.silu(gate) * u) @ w_down)
    return x, k, v, s_new.astype(s0.dtype), new_dn_buf, new_ffn_buf


def setup_inputs(seed: int = 0) -> dict:
    key = jax.random.key(seed)
    ks = jax.random.split(key, 32)
    f32 = jnp.float32
    n_pages = PAST_LEN // PAGE_SIZE
    n_used = DEC_BATCH * n_pages
    n_phys = n_used + n_used // 4

    def nrm(k, shape, scale):
        return jax.random.normal(k, shape, f32) * scale

    page_table = jax.random.permutation(ks[0], n_phys)[:n_used].reshape(DEC_BATCH, n_pages).astype(jnp.int32)
    dt = jnp.exp(jax.random.uniform(ks[1], (DEPTH, DN_HEADS), f32, np.log(1e-3), np.log(1e-1)))
    sb_bias = jnp.linspace(SB_BIAS_HI, SB_BIAS_LO, SB_HEADS, dtype=f32)[None, :] + nrm(ks[26], (DEPTH, SB_HEADS), 0.1)
    return {
        'x_prompt': nrm(ks[2], (BATCH, SEQ, D_MODEL), 1.0),
        'x_sample': nrm(ks[3], (DEC_BATCH, DEC_SEQ, D_MODEL), 1.0),
        'c_prompt': nrm(ks[4], (BATCH, D_MODEL), 1.0),
        'c_sample': nrm(ks[5], (DEC_BATCH, D_MODEL), 1.0),
        'cache_k': nrm(ks[6], (DEPTH, n_phys, PAGE_SIZE, SB_HEADS, SB_HEAD_DIM), 1.0),
        'cache_v': nrm(ks[7], (DEPTH, n_phys, PAGE_SIZE, SB_HEADS, SB_HEAD_DIM), 1.0),
        'page_table': page_table,
        'state_delta': nrm(ks[8], (DEPTH, DEC_BATCH, DN_HEADS, DN_HEAD_DIM, DN_HEAD_DIM), 0.05),
        'state_dn_conv': nrm(ks[9], (DEPTH, DEC_BATCH, DN_CONV - 1, 3 * DN_WIDTH), 1.0),
        'state_ffn_conv': nrm(ks[10], (DEPTH, DEC_BATCH, FFN_CONV - 1, 2 * D_FF), 1.0),
        'w_ada': nrm(ks[11], (DEPTH, D_MODEL, N_MOD * D_MODEL), 0.5 * D_MODEL ** -0.5),
        'b_ada': nrm(ks[12], (DEPTH, N_MOD * D_MODEL), 0.01),
        'g_attn_norm': 1.0 + nrm(ks[13], (DEPTH, D_MODEL), 0.02),
        'w_in': nrm(ks[14], (DEPTH, D_MODEL, IN_COLS), D_MODEL ** -0.5),
        'g_q': 1.0 + nrm(ks[15], (DEPTH, SB_HEAD_DIM), 0.02),
        'g_k': 1.0 + nrm(ks[16], (DEPTH, SB_HEAD_DIM), 0.02),
        'sb_bias': sb_bias,
        'g_sb_out': 1.0 + nrm(ks[17], (DEPTH, SB_HEAD_DIM), 0.02),
        'w_dn_conv': nrm(ks[18], (DEPTH, DN_CONV, 3 * DN_WIDTH), DN_CONV ** -0.5),
        'a_log': jnp.log(jax.random.uniform(ks[19], (DEPTH, DN_HEADS), f32, 1.0, 16.0)),
        'dt_bias': dt + jnp.log(-jnp.expm1(-dt)),
        'g_dn_out': 1.0 + nrm(ks[20], (DEPTH, DN_HEAD_DIM), 0.02),
        'w_out': nrm(ks[21], (DEPTH, MIX_WIDTH, D_MODEL), MIX_WIDTH ** -0.5),
        'g_ffn_norm': 1.0 + nrm(ks[22], (DEPTH, D_MODEL), 0.02),
        'w_up': nrm(ks[23], (DEPTH, D_MODEL, 2 * D_FF), D_MODEL ** -0.5),
        'w_ffn_conv': nrm(ks[24], (DEPTH, FFN_CONV, 2 * D_FF), FFN_CONV ** -0.5),
        'w_down': nrm(ks[25], (DEPTH, D_FF, D_MODEL), D_FF ** -0.5),
    }


def reference(x_prompt, x_sample, c_prompt, c_sample, cache_k, cache_v, page_table,
              state_delta, state_dn_conv, state_ffn_conv,
              w_ada, b_ada, g_attn_norm, w_in, g_q, g_k, sb_bias, g_sb_out, w_dn_conv,
              a_log, dt_bias, g_dn_out, w_out, g_ffn_norm, w_up, w_ffn_conv, w_down):
    bp = x_prompt.shape[0]
    bs = x_sample.shape[0]
    past_len = page_table.shape[1] * PAGE_SIZE
    yp, ys = x_prompt, x_sample
    kp_l, vp_l, ks_l, vs_l, sp_l, ss_l, dcp_l, dcs_l, fcp_l, fcs_l = ([] for _ in range(10))
    for l in range(DEPTH):
        p = (w_ada[l], b_ada[l], g_attn_norm[l], w_in[l], g_q[l], g_k[l], sb_bias[l], g_sb_out[l],
             w_dn_conv[l], a_log[l], dt_bias[l], g_dn_out[l], w_out[l], g_ffn_norm[l], w_up[l],
             w_ffn_conv[l], w_down[l])
        s0 = jnp.zeros((bp, DN_HEADS, DN_HEAD_DIM, DN_HEAD_DIM), state_delta.dtype)
        dn0 = jnp.zeros((bp, DN_CONV - 1, 3 * DN_WIDTH), x_prompt.dtype)
        ff0 = jnp.zeros((bp, FFN_CONV - 1, 2 * D_FF), x_prompt.dtype)
        yp, kp, vp, sp, dcp, fcp = hybrid_layer(yp, c_prompt, None, None, s0, dn0, ff0, *p)
        past_k = cache_k[l][page_table].reshape(bs, past_len, SB_HEADS, SB_HEAD_DIM)
        past_v = cache_v[l][page_table].reshape(bs, past_len, SB_HEADS, SB_HEAD_DIM)
        ys, ksm, vsm, ss, dcs, fcs = hybrid_layer(ys, c_sample, past_k, past_v, state_delta[l],
                                                  state_dn_conv[l], state_ffn_conv[l], *p)
        kp_l.append(kp); vp_l.append(vp); ks_l.append(ksm); vs_l.append(vsm)
        sp_l.append(sp); ss_l.append(ss); dcp_l.append(dcp); dcs_l.append(dcs)
        fcp_l.append(fcp); fcs_l.append(fcs)
    return (yp, ys, jnp.stack(kp_l), jnp.stack(vp_l), jnp.stack(ks_l), jnp.stack(vs_l),
            jnp.stack(sp_l), jnp.stack(ss_l), jnp.stack(dcp_l), jnp.stack(dcs_l),
            jnp.stack(fcp_l), jnp.stack(fcs_l))
```

```python
import contextlib
import numpy as np
import concourse.bass as bass
import concourse.mybir as mybir
from concourse.bass_utils import run_bass_kernel_spmd

F32 = mybir.dt.float32
BF16 = mybir.dt.bfloat16
I32 = mybir.dt.int32
AF = mybir.ActivationFunctionType
ALU = mybir.AluOpType
AX = mybir.AxisListType

D = 1024
DC = 8
HS = 8
HD = 64
SBW = 512
DNH = 4
DND = 128
DNW = 512
DFF = 2816
FC = 22
INC = 3592
EPS = 1e-6
BIG = 30000.0
NSEQ = 16
TS = 8


class Cfg:
    def __init__(self, nblk=32, npg=16, nphys=2560):
        self.NBLK = nblk
        self.OWN0 = nblk // 2 - 1
        self.OUT0 = nblk // 2
        self.NPG = npg
        self.NPHYS = nphys
        self.NOUT = nblk - self.OUT0
        self.NOWN = nblk - self.OWN0


class Tile:
    __slots__ = ("ap", "name", "last_w", "readers", "excl")

    def __init__(self, ap, name="", excl=False):
        self.ap = ap
        self.name = name
        self.last_w = None
        self.readers = []
        self.excl = excl

    def __getitem__(self, k):
        return self.ap[k]


class Op:
    __slots__ = ("eng", "fn", "deps", "need_inc", "count", "sem", "is_dma", "idx")

    def __init__(self, eng, fn, is_dma=False):
        self.eng = eng
        self.fn = fn
        self.deps = set()
        self.need_inc = is_dma
        self.count = 0
        self.sem = None
        self.is_dma = is_dma


COMPUTE = ("pe", "act", "dve", "pool")


class Sched:
    def __init__(self, nc, n_dma_sems=16):
        self.nc = nc
        self.ops = {e: [] for e in COMPUTE + ("sp",)}
        self.n_dma_sems = n_dma_sems
        self.nops = 0

    def _track(self, op, reads, writes):
        ex = [t for t in reads if t.excl]
        if ex:
            reads = [t for t in reads if not t.excl]
            writes = list(writes) + [t for t in ex if t not in writes]
        for t in reads:
            if t.last_w is not None:
                op.deps.add(t.last_w)
        for t in writes:
            if t.last_w is not None:
                op.deps.add(t.last_w)
            for r in t.readers:
                op.deps.add(r)
        for t in reads:
            t.readers.append(op)
        for t in writes:
            t.last_w = op
            t.readers = []
        op.deps.discard(op)

    def add(self, eng, fn, reads=(), writes=(), is_dma=False):
        op = Op(eng, fn, is_dma)
        self._track(op, reads, writes)
        self.ops[eng].append(op)
        self.nops += 1
        return op

    def dma(self, out_ap, in_ap, reads=(), writes=(), queue="sp", **kw):
        def fn(e, out_ap=out_ap, in_ap=in_ap, kw=kw):
            return e.dma_start(out=out_ap, in_=in_ap, **kw)
        return self.add(queue, fn, reads, writes, is_dma=True)

    def emit(self):
        nc = self.nc

        def skip(d, op):
            return d.eng == "pe" and op.eng == "pe" and not d.is_dma and not op.is_dma
        for e in self.ops:
            for op in self.ops[e]:
                for d in op.deps:
                    if not skip(d, op):
                        d.need_inc = True
        with contextlib.ExitStack() as st:
            sems = {e: st.enter_context(nc.semaphore("s_" + e)) for e in COMPUTE}
            dma_sems = {}
            for q in self.ops:
                if any(o.is_dma for o in self.ops[q]):
                    dma_sems[q] = [st.enter_context(nc.semaphore("d_%s_%d" % (q, i)))
                                   for i in range(self.n_dma_sems)]
            for e in self.ops:
                c = 0
                j = 0
                for op in self.ops[e]:
                    if op.is_dma:
                        ring = dma_sems[e]
                        op.sem = ring[j % len(ring)]
                        op.count = 16 * (j // len(ring) + 1)
                        j += 1
                    elif op.need_inc:
                        c += 1
                        op.count = c
                        op.sem = sems[e]
            block = st.enter_context(nc.Block())
            handles = {"pe": block.tensor, "act": block.scalar, "dve": block.vector,
                       "pool": block.gpsimd, "sp": block.sync}

            def make(e):
                oplist = self.ops[e]

                def body(eng):
                    known = {}

                    def wait(sem, val):
                        if known.get(id(sem), 0) >= val:
                            return
                        eng.wait_ge(sem, val)
                        known[id(sem)] = val
                    for op in oplist:
                        need = {}
                        for d in op.deps:
                            if skip(d, op):
                                continue
                            k = id(d.sem)
                            if k not in need or need[k][1] < d.count:
                                need[k] = (d.sem, d.count)
                        if op.is_dma and op.count > 16:
                            k = id(op.sem)
                            v = op.count - 16
                            if k not in need or need[k][1] < v:
                                need[k] = (op.sem, v)
                        for sem, val in need.values():
                            wait(sem, val)
                        if op.fn is None:
                            continue
                        ins = op.fn(eng)
                        if op.is_dma:
                            ins.then_inc(op.sem, 16)
                        elif op.need_inc:
                            ins.then_inc(op.sem, 1)
                    last = {}
                    for op in oplist:
                        if op.is_dma:
                            last[id(op.sem)] = (op.sem, op.count)
                    for sem, val in last.values():
                        wait(sem, val)
                return body
            for e in self.ops:
                if self.ops[e]:
                    handles[e](make(e))


def host_consts():
    i = np.arange(128)
    c = {}
    c["ident"] = np.eye(128, dtype=np.float32)
    c["ones"] = np.ones((128, 128), np.float32)
    for nm, nseq in (("p", 1), ("s", NSEQ)):
        t = 128 // nseq
        seq = i // t
        same = seq[:, None] == seq[None, :]
        incl = same & (i[None, :] <= i[:, None])
        strict = same & (i[None, :] < i[:, None])
        c["ltincl_" + nm] = incl.T.astype(np.float32)
        c["seqm_" + nm] = same.astype(np.float32)
        c["nmincl_" + nm] = np.where(incl, 0.0, BIG).astype(np.float32)
        c["nminclT_" + nm] = np.where(incl.T, 0.0, -BIG).astype(np.float32)
        c["ms01_" + nm] = strict.astype(np.float32)
    caus = i[:, None] < i[None, :]
    c["caus01"] = caus.astype(np.float32)
    c["causneg"] = np.where(caus, 0.0, -BIG).astype(np.float32)
    kt = i[:, None, None]
    ss_ = np.arange(NSEQ)[None, :, None]
    qq = (np.arange(64) % TS)[None, None, :]
    c["smask01"] = ((kt // TS == ss_) & (kt % TS < qq)).astype(np.float32).reshape(128, NSEQ * 64)
    c["ntri"] = np.where(i[:, None] >= i[None, :], -1.0, 0.0).astype(np.float32)
    cm = (np.arange(NSEQ)[:, None] == (i // TS)[None, :]).astype(np.float32)
    c["colmask"] = np.broadcast_to(cm.reshape(1, NSEQ * 128), (128, NSEQ * 128)).copy()
    c["rowmask"] = np.zeros((128, 128), np.float32)
    c["rowmask"][:, :NSEQ] = cm.T
    main = ["ident", "ones", "ltincl_p", "ms01_p", "caus01", "causneg", "ntri"]
    samp = ["ltincl_s", "seqm_s", "ms01_s", "rowmask", "colmask", "smask01"]

    def pack(names):
        off = {}
        o = 0
        for k in names:
            off[k] = (o, c[k].shape[1])
            o += c[k].shape[1]
        return np.concatenate([c[k] for k in names], axis=1).astype(np.float32), off
    return pack(main) + pack(samp)


CT_MAIN, CO_MAIN, CT_SAMP, CO_SAMP = host_consts()


class Builder:
    def __init__(self, cfg, stages=("all",), dbg=()):
        self.cfg = cfg
        self.stages = stages
        self.dbg = dbg
        self.nc = bass.Bass("TRN2", target_bir_lowering=False)
        self.s = Sched(self.nc)
        self.fence_id = 0
        self._uid = 0

    def on(self, st):
        return "all" in self.stages or st in self.stages

    def sb(self, stack, name, shape, dt):
        self._uid += 1
        h = stack.enter_context(self.nc.sbuf_tensor("%s_%d" % (name, self._uid), list(shape), dt))
        return Tile(h, name)

    def view(self, ap, name=""):
        return Tile(ap, name)

    def din(self, name, shape, dt=F32):
        return self.nc.dram_tensor(name, list(shape), dt, kind="ExternalInput").ap()

    def dout(self, name, shape, dt=F32):
        return self.nc.dram_tensor(name, list(shape), dt, kind="ExternalOutput").ap()

    def dscr(self, name, shape, dt=F32):
        return self.nc.dram_tensor(name, list(shape), dt, kind="Internal").ap()

    def E(self, eng, meth, reads, writes, *a, **kw):
        return self.s.add(eng, lambda e: getattr(e, meth)(*a, **kw), reads, writes)

    def mm(self, out, lhsT, rhs, start, stop, reads, writes, skip=False):
        return self.s.add("pe", lambda e: e.matmul(out, lhsT=lhsT, rhs=rhs, start=start, stop=stop, skip_group_check=skip),
                          reads, writes)

    def tr(self, out, in_, ident, reads, writes):
        return self.s.add("pe", lambda e: e.transpose(out=out, in_=in_, identity=ident), reads, writes)

    def act(self, out, in_, func, reads, writes, **kw):
        return self.s.add("act", lambda e: e.activation(out=out, in_=in_, func=func, **kw), reads, writes)

    def dma(self, out, in_, reads=(), writes=(), **kw):
        return self.s.dma(out, in_, reads, writes, **kw)

    def fence(self):
        s = self.s
        f = set()
        for e in COMPUTE:
            real = [o for o in s.ops[e] if o.fn is not None and not o.is_dma]
            if real:
                f.add(real[-1])
        for q in s.ops:
            d = [o for o in s.ops[q] if o.is_dma]
            for o in d[-s.n_dma_sems:]:
                f.add(o)
        for e in COMPUTE + ("sp",):
            op = Op(e, None)
            op.deps = set(f)
            s.ops[e].append(op)

    def G(self):
        t = self.pg[self.gi % len(self.pg)]
        self.gi += 1
        return t

    def TB(self):
        t = self.ptb[self.ti % len(self.ptb)]
        self.ti += 1
        return t

    def declare(self):
        c = self.cfg
        NT = c.NBLK * 128
        i = {}
        i["xp"] = self.din("xp", [NT, D])
        i["xs"] = self.din("xs", [128, D])
        i["cvec"] = self.din("cvec", [17, D])
        i["cache_k"] = self.din("cache_k", [c.NPHYS * 128, SBW])
        i["cache_v"] = self.din("cache_v", [c.NPHYS * 128, SBW])
        i["ptab"] = self.din("ptab", [NSEQ * c.NPG], I32)
        i["s0"] = self.din("s0", [NSEQ, DNH, DND, DND])
        i["dnc0"] = self.din("dnc0", [NSEQ * 3, 3 * DNW])
        i["ffc0"] = self.din("ffc0", [NSEQ * 2, 2 * DFF])
        i["w_ada"] = self.din("w_ada", [D, 6 * D])
        i["b_ada"] = self.din("b_ada", [6 * D])
        i["g_attn"] = self.din("g_attn", [D])
        i["w_in"] = self.din("w_in", [D, INC])
        i["g_q"] = self.din("g_q", [HD])
        i["g_k"] = self.din("g_k", [HD])
        i["sb_bias"] = self.din("sb_bias", [HS])
        i["g_sb_out"] = self.din("g_sb_out", [HD])
        i["w_dn_conv"] = self.din("w_dn_conv", [4, 3 * DNW])
        i["a_log"] = self.din("a_log", [DNH])
        i["dt_bias"] = self.din("dt_bias", [DNH])
        i["g_dn_out"] = self.din("g_dn_out", [DND])
        i["w_out"] = self.din("w_out", [D, D])
        i["g_ffn"] = self.din("g_ffn", [D])
        i["w_up"] = self.din("w_up", [D, 2 * DFF])
        i["w_ffn_conv"] = self.din("w_ffn_conv", [3, 2 * DFF])
        i["w_down"] = self.din("w_down", [DFF, D])
        i["ct_main"] = self.din("ct_main", list(CT_MAIN.shape))
        i["ct_samp"] = self.din("ct_samp", list(CT_SAMP.shape))
        i["kvlo"] = self.din("kvlo", [2, 128])
        i["blkvalid"] = self.din("blkvalid", [128, c.NBLK])
        self.i = i
        o = {}
        o["yp"] = self.dout("yp", [c.NOUT * 128, D])
        o["ys"] = self.dout("ys", [128, D])
        o["kp"] = self.dout("kp", [c.NOUT * 128, SBW])
        o["vp"] = self.dout("vp", [c.NOUT * 128, SBW])
        o["ksm"] = self.dout("ksm", [128, SBW])
        o["vsm"] = self.dout("vsm", [128, SBW])
        o["sp_state"] = self.dout("sp_state", [DNH, DND, DND])
        o["ss_state"] = self.dout("ss_state", [NSEQ, DNH, DND, DND])
        o["dcp"] = self.dout("dcp", [3, 3 * DNW])
        o["dcs"] = self.dout("dcs", [NSEQ * 3, 3 * DNW])
        o["fcp"] = self.dout("fcp", [2, 2 * DFF])
        o["fcs"] = self.dout("fcs", [NSEQ * 2, 2 * DFF])
        for name, shape in self.dbg:
            o[name] = self.dout(name, shape)
        self.o = o
        self.modd = self.dscr("modd", [17, 6 * D])
        self.mixd = self.dscr("mixd", [(c.NOWN + 1) * 128, D], BF16)
        self.t_modd = Tile(None, "modd")
        self.t_mixd = [Tile(None, "mixd%d" % k) for k in range(c.NOWN + 1)]

    def stream_cast(self, stack_tiles, src_view, dst_tile, dst_ap, eng="pool"):
        stg = stack_tiles[self.sci % len(stack_tiles)]
        self.sci += 1
        shp = src_view.shape
        sap = stg[:, 0:shp[1] * shp[2]].rearrange("p (a b) -> p a b", a=shp[1])
        self.dma(sap, src_view, writes=[stg])
        self.E(eng, "tensor_copy", [stg], [dst_tile], out=dst_ap, in_=sap)

    def rsqrt_ops(self, ss, rs, n, scale, reads_extra=()):
        self.act(rs[:, 0:n], ss[:, 0:n], AF.Ln, [ss] + list(reads_extra), [rs], scale=scale, bias=self.epsb[:, 0:1])
        self.act(rs[:, 0:n], rs[:, 0:n], AF.Exp, [rs], [rs], scale=-0.5)

    def build(self):
        nc = self.nc
        c = self.cfg
        self.declare()
        i, o = self.i, self.o
        self.gi = 0
        self.ti = 0
        self.sci = 0
        with contextlib.ExitStack() as top:
            self.pg = [Tile(top.enter_context(nc.psum_tensor("pg%d" % k, [128, 512], F32)), "pg%d" % k, True) for k in range(4)]
            self.po = [Tile(top.enter_context(nc.psum_tensor("po%d" % k, [128, 512], F32)), "po%d" % k, True) for k in range(2)]
            self.ptb = [Tile(top.enter_context(nc.psum_tensor("ptb%d" % k, [128, 1024], BF16)), "ptb%d" % k, True) for k in range(2)]
            ctm = self.sb(top, "ctm", list(CT_MAIN.shape), F32)
            self.ctm = ctm
            self.dma(ctm[:], i["ct_main"][:, :], writes=[ctm])

            def cm(name):
                o_, w_ = CO_MAIN[name]
                return ctm[:, o_:o_ + w_]
            self.cm = cm
            cbf = self.sb(top, "cbf", [128, 4 * 128], BF16)
            self.cbf = cbf
            self.E("dve", "tensor_copy", [ctm], [cbf], out=cbf[:, 0:128], in_=cm("ident"))
            self.E("dve", "tensor_copy", [ctm], [cbf], out=cbf[:, 128:256], in_=cm("ntri"))
            self.E("dve", "tensor_copy", [ctm], [cbf], out=cbf[:, 256:384], in_=cm("causneg"))
            self.E("dve", "tensor_scalar", [ctm], [cbf], out=cbf[:, 384:512], in0=cm("ones"), scalar1=-1.0, scalar2=None, op0=ALU.mult)
            self.identb = cbf[:, 0:128]
            self.ntrib = cbf[:, 128:256]
            self.causnegb = cbf[:, 256:384]
            self.negonesb = cbf[:, 384:512]
            epsb = self.sb(top, "epsb", [128, 1], F32)
            self.epsb = epsb
            self.E("pool", "memset", [], [epsb], epsb[:], EPS)
            sv = self.sb(top, "sv", [128, 64 * 3 + 128 + 4 + 4 + 8], F32)
            self.sv = sv
            self.dma(sv[:, 0:64], i["g_q"].partition_broadcast(128), writes=[sv])
            self.dma(sv[:, 64:128], i["g_k"].partition_broadcast(128), writes=[sv])
            self.dma(sv[:, 128:192], i["g_sb_out"].partition_broadcast(128), writes=[sv])
            self.dma(sv[:, 192:320], i["g_dn_out"].partition_broadcast(128), writes=[sv])
            self.dma(sv[:, 320:324], i["a_log"].partition_broadcast(128), writes=[sv])
            self.dma(sv[:, 324:328], i["dt_bias"].partition_broadcast(128), writes=[sv])
            self.dma(sv[:, 328:336], i["sb_bias"].partition_broadcast(128), writes=[sv])
            self.E("dve", "tensor_scalar", [sv], [sv], out=sv[:, 0:64], in0=sv[:, 0:64], scalar1=HD ** -0.5, scalar2=None, op0=ALU.mult)
            self.act(sv[:, 320:324], sv[:, 320:324], AF.Exp, [sv], [sv])
            self.E("dve", "tensor_scalar", [sv], [sv], out=sv[:, 320:324], in0=sv[:, 320:324], scalar1=-1.0, scalar2=None, op0=ALU.mult)
            self.gq8, self.gk, self.gso, self.gdn = sv[:, 0:64], sv[:, 64:128], sv[:, 128:192], sv[:, 192:320]
            self.negA, self.dtb = sv[:, 320:324], sv[:, 324:328]
            kvd = self.sb(top, "kvd", [128, 128], F32)
            self.kvd = kvd
            self.E("pool", "memset", [], [kvd], kvd[:], 0.0)
            self.dma(kvd[0:2, :], i["kvlo"][:, :], reads=[kvd], writes=[kvd])
            bvd = self.sb(top, "bvd", [128, c.NBLK], F32)
            self.bvd = bvd
            self.dma(bvd[:], i["blkvalid"][:, :], writes=[bvd])
            kvone = self.sb(top, "kvone", [128, 128], F32)
            self.kvone = kvone
            self.E("pool", "memset", [], [kvone], kvone[:], 0.0)
            self.E("pool", "memset", [kvone], [kvone], kvone[0:1, :], 1.0)
            wdc = self.sb(top, "wdc", [128, 4, 12], F32)
            self.wdc = wdc
            for t_ in range(4):
                self.dma(wdc[:, t_, :], i["w_dn_conv"][t_].rearrange("(c p) -> p c", p=128), writes=[wdc], allow_slow_non_contiguous=True)
            wfc = self.sb(top, "wfc", [128, 3, 44], F32)
            self.wfc = wfc
            for t_ in range(3):
                self.dma(wfc[:, t_, :], i["w_ffn_conv"][t_].rearrange("(c p) -> p c", p=128), writes=[wfc], allow_slow_non_contiguous=True)

            if self.on("setup"):
                self.setup_mod()
            self.fence()
            if self.on("p1"):
                self.phase1()
            self.fence()
            if self.on("p2"):
                self.phase2()
            self.s.emit()
        return nc

    def setup_mod(self):
        i = self.i
        with contextlib.ExitStack() as st:
            cv = self.sb(st, "cv", [17, D], F32)
            ex = self.sb(st, "ex", [17, D], F32)
            scb = self.sb(st, "scb", [17, D], BF16)
            scT = self.sb(st, "scT", [128, DC, 17], BF16)
            stg = [self.sb(st, "stg%d" % k, [128, DC * 512], F32) for k in range(2)]
            wab = [self.sb(st, "wab%d" % k, [128, DC, 512], BF16) for k in range(2)]
            bada = self.sb(st, "bada", [17, 512], F32)
            gv = self.sb(st, "gv", [17, 2 * D], F32)
            mt = [self.sb(st, "mt%d" % k, [17, 512], F32) for k in range(2)]
            self.dma(cv[:], i["cvec"][:, :], writes=[cv])
            self.dma(gv[:, 0:D], i["g_attn"].partition_broadcast(17), writes=[gv])
            self.dma(gv[:, D:2 * D], i["g_ffn"].partition_broadcast(17), writes=[gv])
            self.act(ex[:], cv[:], AF.Exp, [cv], [ex], scale=-1.0)
            self.E("dve", "tensor_scalar", [ex], [ex], out=ex[:], in0=ex[:], scalar1=1.0, scalar2=None, op0=ALU.add)
            self.E("dve", "reciprocal", [ex], [ex], out=ex[:], in_=ex[:])
            self.E("dve", "tensor_tensor", [ex, cv], [scb], out=scb[:], in0=cv[:], in1=ex[:], op=ALU.mult)
            tb = self.TB()
            for dc in range(DC):
                self.tr(tb[:, dc * 32:dc * 32 + 17], scb[0:17, dc * 128:(dc + 1) * 128], self.identb[0:17, 0:17], [scb, self.cbf], [tb])
            self.E("dve", "tensor_copy", [tb], [scT], out=scT[:], in_=tb[:, 0:DC * 32].rearrange("p (a b) -> p a b", a=DC)[:, :, 0:17])
            wv = i["w_ada"].rearrange("(c p) n -> p c n", p=128)
            for ct in range(12):
                wb = wab[ct % 2]
                self.stream_cast(stg, wv[:, :, ct * 512:(ct + 1) * 512], wb, wb[:])
                self.dma(bada[:], i["b_ada"][ct * 512:(ct + 1) * 512].partition_broadcast(17), writes=[bada])
                g = self.G()
                for dc in range(DC):
                    self.mm(g[0:17, :], scT[:, dc, :], wb[:, dc, :], dc == 0, dc == DC - 1, [scT, wb], [g])
                m = mt[ct % 2]
                self.E("dve", "tensor_tensor", [g, bada], [m], out=m[:], in0=g[0:17, :], in1=bada[:], op=ALU.add)
                if ct in (2, 3, 8, 9):
                    go = (ct - 2) * 512 if ct < 4 else D + (ct - 8) * 512
                    self.E("dve", "scalar_tensor_tensor", [m, gv], [m], out=m[:], in0=m[:], scalar=1.0, in1=gv[:, go:go + 512],
                           op0=ALU.add, op1=ALU.mult)
                self.dma(self.modd[:, ct * 512:(ct + 1) * 512], m[:], reads=[m], writes=[self.t_modd])

    def load_mod(self, tiles, idxs, sample):
        for t, ix in zip(tiles, idxs):
            if not sample:
                self.dma(t[:], self.modd[0, ix * D:(ix + 1) * D].partition_broadcast(128), reads=[self.t_modd], writes=[t])
            else:
                for s_ in range(NSEQ):
                    self.dma(t[s_ * TS:(s_ + 1) * TS, :], self.modd[1 + s_, ix * D:(ix + 1) * D].partition_broadcast(TS),
                             reads=[self.t_modd], writes=[t])

    def norm_mod(self, w, xt, scale, shift, hT, tmp=None):
        self.E("pool", "memset", [], [w.ssq], w.ssq[:], 0.0)
        self.act(w.h[:], xt[:], AF.Square, [xt, w.ssq], [w.h, w.ssq], accum_out=w.ssq[:, 0:1])
        self.rsqrt_ops(w.ssq, w.rstd, 1, 1.0 / D)
        if tmp is None:
            tmp = xt
        self.E("dve", "scalar_tensor_tensor", [xt, w.rstd, scale], [tmp], out=tmp[:], in0=xt[:], scalar=w.rstd[:, 0:1],
               in1=scale[:], op0=ALU.mult, op1=ALU.mult)
        self.E("pool", "tensor_tensor", [tmp, shift], [w.h], out=w.h[:], in0=tmp[:], in1=shift[:], op=ALU.add)
        tb = self.TB()
        for dc in range(DC):
            self.tr(tb[:, dc * 128:(dc + 1) * 128], w.h[:, dc * 128:(dc + 1) * 128], self.identb, [w.h, self.cbf], [tb])
        self.act(hT[:], tb[:, :].rearrange("p (a b) -> p a b", a=DC), AF.Copy, [tb], [hT])

    def head_norm(self, w, ps, nh, hd, out_f32, reads_ps, gvec, out_tile, out_ap, scale):
        v3 = lambda ap: ap.rearrange("p (a b) -> p a b", a=nh)
        self.act(out_f32[:, 0:nh * hd], ps, AF.Square, reads_ps, [out_f32])
        self.E("dve", "tensor_reduce", [out_f32], [w.ss8], out=w.ss8[:, 0:nh], in_=v3(out_f32[:, 0:nh * hd]), axis=AX.X, op=ALU.add)
        self.rsqrt_ops(w.ss8, w.rs8, nh, scale)
        self.E("dve", "tensor_tensor", reads_ps + [w.rs8], [out_f32], out=v3(out_f32[:, 0:nh * hd]), in0=v3(ps),
               in1=w.rs8[:, 0:nh].unsqueeze(2).to_broadcast([128, nh, hd]), op=ALU.mult)
        self.E("pool", "tensor_tensor", [out_f32, self.sv], [out_tile], out=v3(out_ap), in0=v3(out_f32[:, 0:nh * hd]),
               in1=gvec.unsqueeze(1).to_broadcast([128, nh, hd]), op=ALU.mult)

    def front_end(self, w, b, sample, own, outrow):
        c = self.cfg
        i, o = self.i, self.o
        xt = w.xt[self.xi % len(w.xt)]
        self.xi += 1
        src = i["xs"][:, :] if sample else i["xp"][b * 128:(b + 1) * 128, :]
        self.dma(xt[:], src, writes=[xt])
        hT = w.hT
        self.norm_mod(w, xt, w.scale1, w.shift1, hT)
        winb = self.winb
        g = self.G()
        for dc in range(DC):
            self.mm(g[:, :], hT[:, dc, :], winb[:, dc, 512:1024], dc == 0, dc == DC - 1, [hT, winb], [g])
        self.head_norm(w, g[:, :], HS, HD, w.f512, [g], self.gk, w.kn, w.kn[:], 1.0 / HD)
        if outrow is not None:
            self.dma(outrow[0], w.kn[:], reads=[w.kn])
        self.E("dve", "tensor_copy", [w.kn], [w.knb], out=w.knb[:], in_=w.kn[:])
        KTt = w.KTs if sample else self.KTt[b]
        tb = self.TB()
        for pr in range(4):
            self.tr(tb[:, pr * 128:(pr + 1) * 128], w.knb[:, pr * 128:(pr + 1) * 128], self.identb, [w.knb, self.cbf], [tb])
        self.act(KTt[:], tb[:, 0:512].rearrange("p (a b) -> p a b", a=4), AF.Copy, [tb], [KTt])
        g = self.G()
        for dc in range(DC):
            self.mm(g[:, :], hT[:, dc, :], winb[:, dc, 1024:1536], dc == 0, dc == DC - 1, [hT, winb], [g])
        Vt = w.Vs if sample else self.Vt[b]
        self.act(Vt[:], g[:, :], AF.Copy, [g], [Vt])
        if outrow is not None:
            self.E("dve", "tensor_copy", [g], [w.vf], out=w.vf[:], in_=g[:, :])
            self.dma(outrow[1], w.vf[:], reads=[w.vf])
        if own:
            g = self.G()
            for dc in range(DC):
                self.mm(g[:, :], hT[:, dc, :], winb[:, dc, 0:512], dc == 0, dc == DC - 1, [hT, winb], [g])
            self.head_norm(w, g[:, :], HS, HD, w.f512, [g], self.gq8, w.qnb, w.qnb[:], 1.0 / HD)
            tb = self.TB()
            for pr in range(4):
                self.tr(tb[:, pr * 128:(pr + 1) * 128], w.qnb[:, pr * 128:(pr + 1) * 128], self.identb, [w.qnb, self.cbf], [tb])
            tq = tb[:, 0:512].rearrange("p (a b) -> p a b", a=4)
            self.act(w.QT[0:64, :, 0, :], tq[0:64, :, :], AF.Copy, [tb], [w.QT])
            self.E("dve", "tensor_copy", [tb], [w.QT], out=w.QT[64:128, :, 1, :], in_=tq[64:128, :, :])
        if self.on("dn"):
            g = self.G()
            for dc in range(DC):
                self.mm(g[:, 0:8], hT[:, dc, :], winb[:, dc, 3072:3080], dc == 0, dc == DC - 1, [hT, winb], [g])
            self.E("dve", "tensor_copy", [g], [w.ba], out=w.ba[:], in_=g[:, 0:8])
            if own:
                g = self.G()
                for dc in range(DC):
                    self.mm(g[:, :], hT[:, dc, :], winb[:, dc, 3080:3592], dc == 0, dc == DC - 1, [hT, winb], [g])
                self.act(w.zs[:], g[:, :], AF.Copy, [g], [w.zs])

    def dn_chunk(self, w, b, sample, own, tabs, last_prompt):
        c = self.cfg
        i, o = self.i, self.o
        T = TS if sample else 128
        ns = NSEQ if sample else 1
        sc = w.sc
        hT, winb = w.hT, self.winb
        ltincl, seqm, ms01 = tabs
        bc4 = lambda ap: ap.unsqueeze(2).to_broadcast([128, 4, 128])
        hb4 = lambda ap: ap.unsqueeze(1).to_broadcast([128, 4, 128])
        v4 = lambda ap: ap.rearrange("p (a b) -> p a b", a=4)
        XE4 = w.XE4
        xe4 = XE4[:, :, :].rearrange("p c (s t) -> p c s t", s=ns)
        Hst = w.Hst
        hs4 = Hst[:, :, 0:ns * 3].rearrange("p c (s t) -> p c s t", s=ns)
        if (not sample) and b == 0:
            self.E("pool", "memset", [], [Hst], Hst[:], 0.0)
        for j in range(3):
            g = self.G()
            for ch in range(4):
                col = 1536 + (j * 4 + ch) * 128
                for dc in range(DC):
                    self.mm(g[:, ch * 128:(ch + 1) * 128], winb[:, dc, col:col + 128], hT[:, dc, :], dc == 0, dc == DC - 1, [hT, winb], [g])
            src4 = g[:, :].rearrange("p (c s t) -> p c s t", c=4, s=ns)
            self.E("pool", "tensor_copy", [Hst], [XE4], out=xe4[:, :, :, 0:3], in_=hs4[:, j * 4:(j + 1) * 4, :, :])
            if sample:
                self.act(xe4[:, :, :, 3:3 + T], src4, AF.Copy, [g], [XE4])
            else:
                self.act(xe4[:, :, :, 3:3 + T], src4, AF.Copy, [g, self.bvd], [XE4], scale=self.bvd[:, b:b + 1])
            self.E("pool", "tensor_copy", [XE4], [Hst], out=hs4[:, j * 4:(j + 1) * 4, :, :], in_=xe4[:, :, :, T:T + 3])
            Y4 = w.Y4
            for ch in range(4):
                cc = j * 4 + ch
                eng = "dve"
                yv = Y4[:, ch, :].rearrange("p (s t) -> p s t", s=ns)
                self.E(eng, "tensor_scalar", [XE4, self.wdc], [Y4], out=yv, in0=xe4[:, ch, :, 0:T], scalar1=self.wdc[:, 0, cc:cc + 1],
                       scalar2=None, op0=ALU.mult)
                for k in range(1, 4):
                    self.E(eng, "scalar_tensor_tensor", [XE4, self.wdc, Y4], [Y4], out=yv, in0=xe4[:, ch, :, k:k + T],
                           scalar=self.wdc[:, k, cc:cc + 1], in1=yv, op0=ALU.mult, op1=ALU.add)
            E4 = w.E4
            self.act(E4[:], Y4[:], AF.Exp, [Y4], [E4], scale=-1.0)
            self.E("pool", "tensor_scalar", [E4], [E4], out=E4[:], in0=E4[:], scalar1=1.0, scalar2=None, op0=ALU.add)
            self.E("dve", "reciprocal", [E4], [E4], out=E4[:], in_=E4[:])
            self.E("pool", "tensor_tensor", [E4, Y4], [Y4], out=Y4[:], in0=Y4[:], in1=E4[:], op=ALU.mult)
            if j < 2:
                SQ = E4
                self.act(SQ[:], Y4[:], AF.Square, [Y4], [SQ])
                g = self.G()
                self.mm(g[:, :], self.cm("ones"), SQ[:].rearrange("p a b -> p (a b)"), True, True, [SQ, self.ctm], [g])
                self.act(SQ[:].rearrange("p a b -> p (a b)"), g[:, :], AF.Ln, [g], [SQ], bias=self.epsb[:, 0:1])
                self.act(SQ[:], SQ[:], AF.Exp, [SQ], [SQ], scale=-0.5)
                if j == 0:
                    self.E("dve", "scalar_tensor_tensor", [Y4, SQ], [w.QnT], out=w.QnT[:], in0=Y4[:], scalar=DND ** -0.5, in1=SQ[:],
                           op0=ALU.mult, op1=ALU.mult)
                else:
                    self.E("pool", "tensor_tensor", [Y4, SQ], [w.KnT], out=w.KnT[:], in0=Y4[:], in1=SQ[:], op=ALU.mult)
            else:
                self.act(w.Vcb[:], Y4[:], AF.Copy, [Y4], [w.Vcb])
        if sample or last_prompt:
            ncol = ns * 3
            dst = o["dcs"] if sample else o["dcp"]
            for j in range(3):
                g = self.G()
                for ch in range(4):
                    self.tr(g[0:ncol, ch * 128:(ch + 1) * 128], Hst[:, j * 4 + ch, 0:ncol], self.cm("ident"), [Hst, self.ctm], [g])
                self.E("dve", "tensor_copy", [g], [w.f512], out=w.f512[0:ncol, :], in_=g[0:ncol, :])
                self.dma(dst[:, j * 512:(j + 1) * 512], w.f512[0:ncol, :], reads=[w.f512])
        tb = self.TB()
        for h in range(4):
            self.tr(tb[:, h * 128:(h + 1) * 128], w.KnT[:, h, :], self.identb, [w.KnT, self.cbf], [tb])
        self.act(w.Ktok[:], v4(tb[:, 0:512]), AF.Copy, [tb], [w.Ktok])
        tb = self.TB()
        for h in range(4):
            self.tr(tb[:, h * 128:(h + 1) * 128], w.Vcb[:, h, :], self.identb, [w.Vcb, self.cbf], [tb])
        self.E("dve", "tensor_copy", [tb], [w.Vtok], out=w.Vtok[:], in_=v4(tb[:, 0:512]))
        ba = w.ba
        self.act(sc[:, 0:4], ba[:, 0:4], AF.Exp, [ba], [sc], scale=-1.0)
        self.E("dve", "tensor_scalar", [sc], [sc], out=sc[:, 0:4], in0=sc[:, 0:4], scalar1=1.0, scalar2=None, op0=ALU.add)
        self.E("dve", "reciprocal", [sc], [sc], out=sc[:, 0:4], in_=sc[:, 0:4])
        if not sample:
            self.E("dve", "tensor_scalar", [sc, self.bvd], [sc], out=sc[:, 0:4], in0=sc[:, 0:4], scalar1=self.bvd[:, b:b + 1], scalar2=None, op0=ALU.mult)
        self.E("dve", "tensor_tensor", [ba, self.sv], [sc], out=sc[:, 32:36], in0=ba[:, 4:8], in1=self.dtb, op=ALU.add)
        self.act(sc[:, 32:36], sc[:, 32:36], AF.Exp, [sc], [sc])
        self.act(sc[:, 32:36], sc[:, 32:36], AF.Ln, [sc], [sc], bias=1.0)
        self.E("dve", "tensor_tensor", [sc, self.sv], [sc], out=sc[:, 4:8], in0=sc[:, 32:36], in1=self.negA, op=ALU.mult)
        g = self.G()
        self.mm(g[:, 0:4], ltincl, sc[:, 4:8], True, True, [sc, w.tabt], [g])
        self.mm(g[:, 4:8], seqm, sc[:, 4:8], True, True, [sc, w.tabt], [g])
        self.E("dve", "tensor_copy", [g], [sc], out=sc[:, 8:16], in_=g[:, 0:8])
        self.act(sc[:, 16:20], sc[:, 8:12], AF.Exp, [sc], [sc])
        self.E("dve", "scalar_tensor_tensor", [sc], [sc], out=sc[:, 20:24], in0=sc[:, 0:4], scalar=-1.0, in1=sc[:, 16:20], op0=ALU.mult, op1=ALU.mult)
        self.E("dve", "tensor_tensor", [sc], [sc], out=sc[:, 24:28], in0=sc[:, 12:16], in1=sc[:, 8:12], op=ALU.subtract)
        self.act(sc[:, 24:28], sc[:, 24:28], AF.Exp, [sc], [sc])
        self.E("dve", "tensor_scalar", [sc], [sc], out=sc[:, 28:32], in0=sc[:, 0:4], scalar1=-1.0, scalar2=None, op0=ALU.mult)
        beta, gg, negbg, kds, negbeta = sc[:, 0:4], sc[:, 8:12], sc[:, 20:24], sc[:, 24:28], sc[:, 28:32]
        GR = self.G()
        for h in range(4):
            dg = w.dg[h % 2]
            self.E("pool", "tensor_scalar", [sc, self.ctm], [dg], out=dg[:], in0=self.cm("ident"), scalar1=sc[:, 8 + h:9 + h], scalar2=None, op0=ALU.mult)
            self.mm(GR[:, h * 128:(h + 1) * 128], self.cm("ones"), dg[:], True, True, [dg, self.ctm], [GR])
        fa, fb, fc, fd = w.fa, w.fb, w.fc, w.fd
        self.E("dve", "tensor_tensor", [GR, sc], [fa], out=v4(fa[:]), in0=v4(GR[:, :]), in1=bc4(gg), op=ALU.subtract)
        self.act(fd[:], GR[:, :], AF.Exp, [GR], [fd])
        self.E("pool", "tensor_scalar", [fa], [fb], out=fb[:], in0=fa[:], scalar1=0.0, scalar2=None, op0=ALU.max)
        self.E("pool", "tensor_scalar", [fa], [fc], out=fc[:], in0=fa[:], scalar1=0.0, scalar2=None, op0=ALU.min)
        self.act(fb[:], fb[:], AF.Exp, [fb], [fb], scale=-1.0)
        self.act(fc[:], fc[:], AF.Exp, [fc], [fc])
        g = self.G()
        for h in range(4):
            self.mm(g[:, h * 128:(h + 1) * 128], w.KnT[:, h, :], w.KnT[:, h, :], True, True, [w.KnT], [g])
        self.E("dve", "tensor_tensor", [g, fb], [fa], out=fa[:], in0=g[:, :], in1=fb[:], op=ALU.mult)
        self.E("pool", "tensor_tensor", [fa, sc], [fa], out=v4(fa[:]), in0=v4(fa[:]), in1=bc4(negbeta), op=ALU.mult)
        self.E("dve", "tensor_tensor", [fa, w.tabt], [fa], out=v4(fa[:]), in0=v4(fa[:]), in1=hb4(ms01), op=ALU.mult)
        g = self.G()
        for h in range(4):
            self.mm(g[:, h * 128:(h + 1) * 128], w.KnT[:, h, :], w.QnT[:, h, :], True, True, [w.KnT, w.QnT], [g])
        self.E("dve", "tensor_tensor", [g, fc], [fc], out=fc[:], in0=g[:, :], in1=fc[:], op=ALU.mult)
        self.E("pool", "tensor_tensor", [fc, w.tabt], [w.intraT], out=w.intraT[:], in0=v4(fc[:]), in1=hb4(ltincl), op=ALU.mult)
        MTb = w.MTb
        nlev = 2 if sample else 6
        h2 = lambda ap: ap.rearrange("p (a b) -> p a b", a=2)
        hb2 = lambda ap: ap.unsqueeze(1).to_broadcast([128, 2, 128])
        identf = self.cm("ident")
        for hp in range(2):
            P, PT, MT = w.Pf[0], w.PTf[0], w.MT
            self.E("pool", "tensor_copy", [fa], [P], out=P[:], in_=v4(fa[:])[:, 2 * hp:2 * hp + 2, :])
            g = self.G()
            for h in range(2):
                self.tr(g[:, h * 128:(h + 1) * 128], P[:, h, :], identf, [P, self.ctm], [g])
            self.act(PT[:], h2(g[:, 0:256]), AF.Copy, [g], [PT])
            self.E("dve", "tensor_tensor", [PT, self.ctm], [MT], out=MT[:], in0=PT[:], in1=hb2(identf), op=ALU.add)
            for lev in range(1, nlev + 1):
                Pn, PTn = w.Pf[lev % 2], w.PTf[lev % 2]
                g1 = self.G()
                for h in range(2):
                    self.mm(g1[:, h * 128:(h + 1) * 128], PT[:, h, :], P[:, h, :], True, True, [P, PT], [g1])
                self.act(Pn[:], h2(g1[:, 0:256]), AF.Copy, [g1], [Pn])
                if lev < nlev:
                    g2 = self.G()
                    for h in range(2):
                        self.mm(g2[:, h * 128:(h + 1) * 128], P[:, h, :], PT[:, h, :], True, True, [P, PT], [g2])
                    self.E("dve", "tensor_copy", [g2], [PTn], out=PTn[:], in_=h2(g2[:, 0:256]))
                g3 = self.G()
                for h in range(2):
                    self.mm(g3[:, h * 128:(h + 1) * 128], Pn[:, h, :], MT[:, h, :], True, True, [Pn, MT], [g3])
                self.E("dve", "tensor_tensor", [g3, MT], [MT], out=MT[:], in0=h2(g3[:, 0:256]), in1=MT[:], op=ALU.add)
                P, PT = Pn, PTn
            self.E("pool", "tensor_copy", [MT], [MTb], out=MTb[:, 2 * hp:2 * hp + 2, :], in_=MT[:])
        self.E("pool", "tensor_tensor", [w.QnT, fd], [w.QdT], out=w.QdT[:], in0=w.QnT[:], in1=v4(fd[:]), op=ALU.mult)
        self.E("dve", "tensor_tensor", [w.Vtok, sc], [w.Vtok], out=w.Vtok[:], in0=w.Vtok[:], in1=bc4(beta), op=ALU.mult)
        self.E("pool", "tensor_tensor", [w.Ktok, sc], [w.kdec], out=w.kdec[:], in0=w.Ktok[:], in1=bc4(kds), op=ALU.mult)
        if not sample:
            Sf, Sb = self.Sf, self.Sb
            g = self.G()
            for h in range(4):
                self.mm(g[:, h * 128:(h + 1) * 128], w.KnT[:, h, :], Sb[:, h, :], True, True, [w.KnT, Sb], [g])
            self.E("dve", "tensor_tensor", [g, sc], [fa], out=v4(fa[:]), in0=v4(g[:, :]), in1=bc4(negbg), op=ALU.mult)
            self.E("pool", "tensor_tensor", [fa, w.Vtok], [w.W], out=w.W[:], in0=v4(fa[:]), in1=w.Vtok[:], op=ALU.add)
            g = self.G()
            for h in range(4):
                self.mm(g[:, h * 128:(h + 1) * 128], MTb[:, h, :], w.W[:, h, :], True, True, [MTb, w.W], [g])
            self.act(w.vnew[:], v4(g[:, :]), AF.Copy, [g], [w.vnew])
            if own:
                po = self.po[1]
                for h in range(4):
                    self.mm(po[:, h * 128:(h + 1) * 128], w.QdT[:, h, :], Sb[:, h, :], True, False, [w.QdT, Sb], [po])
                    self.mm(po[:, h * 128:(h + 1) * 128], w.intraT[:, h, :], w.vnew[:, h, :], False, True, [w.intraT, w.vnew], [po])
            g = self.G()
            for h in range(4):
                self.mm(g[:, h * 128:(h + 1) * 128], w.kdec[:, h, :], w.vnew[:, h, :], True, True, [w.kdec, w.vnew], [g])
            self.E("dve", "tensor_tensor", [Sf, fd], [Sf], out=Sf[:], in0=Sf[:], in1=v4(fd[:])[:, :, 127:128].to_broadcast([128, 4, 128]), op=ALU.mult)
            self.E("dve", "tensor_tensor", [g, Sf], [Sf], out=Sf[:], in0=v4(g[:, :]), in1=Sf[:], op=ALU.add)
            self.E("pool", "tensor_copy", [Sf], [Sb], out=Sb[:], in_=Sf[:])
            if last_prompt:
                self.dma(o["sp_state"].rearrange("h k v -> k h v"), Sf[:], reads=[Sf])
        else:
            colmask, rowmask = w.colmask, w.rowmask
            cm3 = colmask.rearrange("p (s t) -> p s t", s=NSEQ)
            s0v = i["s0"]
            po = self.po[1]
            for h in range(4):
                Sfh, Sbh = w.Sfh[0], w.Sbh[0]
                self.dma(Sfh[:], s0v[:, h, :, :].rearrange("s k v -> k s v"), writes=[Sfh])
                self.E("pool", "tensor_copy", [Sfh], [Sbh], out=Sbh[:], in_=Sfh[:])
                Km, Qm, kdm = w.Km, w.Qm, w.kdm
                self.E("pool", "tensor_tensor", [w.KnT, w.tabt], [Km], out=Km[:], in0=w.KnT[:, h, :].unsqueeze(1).to_broadcast([128, NSEQ, 128]), in1=cm3, op=ALU.mult)
                g = self.G()
                for s_ in range(NSEQ):
                    self.mm(g[:, 0:128], Km[:, s_, :], Sbh[:, s_, :], s_ == 0, s_ == NSEQ - 1, [Km, Sbh], [g])
                self.E("dve", "tensor_scalar", [g, sc], [fa], out=fa[:, 0:128], in0=g[:, 0:128], scalar1=sc[:, 20 + h:21 + h], scalar2=None, op0=ALU.mult)
                self.E("pool", "tensor_tensor", [fa, w.Vtok], [w.W], out=w.W[:, h, :], in0=fa[:, 0:128], in1=w.Vtok[:, h, :], op=ALU.add)
                g = self.G()
                self.mm(g[:, 0:128], MTb[:, h, :], w.W[:, h, :], True, True, [MTb, w.W], [g])
                self.act(w.vnew[:, h, :], g[:, 0:128], AF.Copy, [g], [w.vnew])
                self.E("dve", "tensor_tensor", [w.QdT, w.tabt], [Qm], out=Qm[:], in0=w.QdT[:, h, :].unsqueeze(1).to_broadcast([128, NSEQ, 128]), in1=cm3, op=ALU.mult)
                for s_ in range(NSEQ):
                    self.mm(po[:, h * 128:(h + 1) * 128], Qm[:, s_, :], Sbh[:, s_, :], s_ == 0, False, [Qm, Sbh], [po])
                self.mm(po[:, h * 128:(h + 1) * 128], w.intraT[:, h, :], w.vnew[:, h, :], False, True, [w.intraT, w.vnew], [po])
                Sn = w.Sn[0]
                self.E("pool", "tensor_tensor", [w.kdec, w.tabt], [kdm], out=kdm[:], in0=w.kdec[:, h, :].unsqueeze(1).to_broadcast([128, NSEQ, 128]),
                       in1=rowmask[:, 0:NSEQ].unsqueeze(2).to_broadcast([128, NSEQ, 128]), op=ALU.mult)
                for q4 in range(4):
                    g = self.G()
                    for k in range(4):
                        s_ = q4 * 4 + k
                        self.mm(g[:, k * 128:(k + 1) * 128], kdm[:, s_, :], w.vnew[:, h, :], True, True, [kdm, w.vnew], [g])
                    for k in range(4):
                        s_ = q4 * 4 + k
                        self.E("dve" if k % 2 == 0 else "pool" if False else "dve", "scalar_tensor_tensor", [g, Sfh, fd], [Sn], out=Sn[:, s_, :], in0=Sfh[:, s_, :],
                               scalar=fd[:, h * 128 + s_ * TS + TS - 1: h * 128 + s_ * TS + TS], in1=g[:, k * 128:(k + 1) * 128], op0=ALU.mult, op1=ALU.add)
                self.dma(o["ss_state"][:, h, :, :].rearrange("s k v -> k s v"), Sn[:], reads=[Sn])
        if own:
            po = self.po[1]
            self.act(fa[:], po[:, :], AF.Square, [po], [fa])
            self.E("dve", "tensor_reduce", [fa], [w.ss8], out=w.ss8[:, 0:4], in_=v4(fa[:]), axis=AX.X, op=ALU.add)
            self.rsqrt_ops(w.ss8, w.rs8, 4, 1.0 / DND)
            self.E("dve", "tensor_tensor", [po, w.rs8], [fa], out=v4(fa[:]), in0=v4(po[:, :]), in1=bc4(w.rs8[:, 0:4]), op=ALU.mult)
            self.E("pool", "tensor_tensor", [fa, self.sv], [fa], out=v4(fa[:]), in0=v4(fa[:]), in1=hb4(self.gdn), op=ALU.mult)
            zs = w.zs
            self.act(fb[:], zs[:], AF.Exp, [zs], [fb], scale=-1.0)
            self.E("pool", "tensor_scalar", [fb], [fb], out=fb[:], in0=fb[:], scalar1=1.0, scalar2=None, op0=ALU.add)
            self.E("dve", "reciprocal", [fb], [fb], out=fb[:], in_=fb[:])
            self.E("pool", "tensor_tensor", [fb, zs], [fb], out=fb[:], in0=fb[:], in1=zs[:], op=ALU.mult)
            self.E("dve", "tensor_tensor", [fa, fb], [w.mixed], out=w.mixed[:, 512:1024], in0=fa[:], in1=fb[:], op=ALU.mult)

    def attn_step(self, w, S, nh, nq, qT, kT, vv, nk, kvl, kvl_t, brow_ap, maskneg, mask01, mask_t, O, o_cols, first, last,
                  qreads, kreads, vreads, att_out=None, att_lhs=None, first_o=None, last_o=None):
        W_ = nh * nq
        et, spt, att, Rb = S.et, S.spt, S.att, S.Rb
        Z = self.G()
        for h in range(nh):
            self.mm(Z[0:nk, h * nq:(h + 1) * nq], kT[h], qT[h], h == 0, False, qreads + kreads, [Z], skip=True)
        self.mm(Z[0:nk, 0:W_], kvl, brow_ap, False, True, [kvl_t, self.brow], [Z], skip=True)
        yield
        self.act(et[0:nk, 0:W_], Z[0:nk, 0:W_], AF.Exp, [Z], [et])
        self.act(spt[0:nk, 0:W_], et[0:nk, 0:W_], AF.Ln, [et], [spt], bias=1.0)
        if mask01 is not None:
            self.E("dve", "tensor_tensor", [spt, mask_t], [spt], out=spt[0:nk, 0:W_], in0=spt[0:nk, 0:W_], in1=mask01, op=ALU.mult)
        yield
        U = self.G()
        for h in range(nh):
            self.mm(U[0:nk, h * nq:(h + 1) * nq], kT[h], qT[h], h == 0, False, qreads + kreads, [U], skip=True)
        self.mm(U[0:nk, 0:W_], kvl, brow_ap, False, False, [kvl_t, self.brow], [U], skip=True)
        fin = first and maskneg is None
        self.mm(U[0:nk, 0:W_], self.ntrib[0:nk, 0:nk], spt[0:nk, 0:W_], False, fin, [spt, self.cbf], [U], skip=True)
        if not first:
            self.mm(U[0:nk, 0:W_], self.negonesb[:, 0:nk], Rb[:, 0:W_], False, maskneg is None, [Rb, self.cbf], [U], skip=True)
        if maskneg is not None:
            self.mm(U[0:nk, 0:W_], self.identb[0:nk, 0:nk], maskneg, False, True, [mask_t, self.cbf], [U], skip=True)
        yield
        if att_out is None:
            self.act(att[0:nk, 0:W_], U[0:nk, 0:W_], AF.Exp, [U], [att])
        else:
            self.act(att_out[0], U[0:nk, 0:W_].rearrange("p (h q) -> p h q", h=nh), AF.Exp, [U], [att_out[1]])
        if not last:
            if first:
                self.E("dve", "tensor_copy", [spt], [Rb], out=Rb[0:nk, 0:W_], in_=spt[0:nk, 0:W_])
            else:
                self.E("dve", "tensor_tensor", [spt, Rb], [Rb], out=Rb[0:nk, 0:W_], in0=Rb[0:nk, 0:W_], in1=spt[0:nk, 0:W_], op=ALU.add)
        yield
        fo = first if first_o is None else first_o
        lo = last if last_o is None else last_o
        for h in range(nh):
            if att_lhs is None:
                lhs = att[0:nk, h * nq:(h + 1) * nq]
                rd = [att]
            else:
                lhs = att_lhs[0][h]
                rd = [att_lhs[1]]
            self.mm(O[o_cols[h]], lhs, vv[h], fo and h == 0, lo, rd + vreads, [O], skip=True)
        yield

    @staticmethod
    def interleave(gens):
        gens = list(gens)
        while gens:
            for g in list(gens):
                try:
                    next(g)
                except StopIteration:
                    gens.remove(g)

    def attn_prompt(self, w, b):
        c = self.cfg

        def stream(hg):
            O = self.po[hg]
            S = w.streams[hg]
            for kb in range(b, -1, -1):
                qT = [w.QT[:, hg * 2 + h // 2, h % 2, :] for h in range(4)]
                kT = [self.KTt[kb][:, hg * 2 + h // 2, :] for h in range(4)]
                vv = [self.Vt[kb][:, (hg * 4 + h) * 64:(hg * 4 + h + 1) * 64] for h in range(4)]
                diag = kb == b
                kvl = self.kvd if kb < c.OUT0 else self.kvone
                yield from self.attn_step(w, S, 4, 128, qT, kT, vv, 128, kvl[:, :], kvl,
                                          self.brow[:, hg * 512:(hg + 1) * 512],
                                          w.causrep[:, 0:512] if diag else None, w.caus01rep[:, 0:512] if diag else None, w.causrep_t,
                                          O, [(slice(None), slice(h * 64, (h + 1) * 64)) for h in range(4)],
                                          kb == b, kb == 0, [w.QT], [self.KTt[kb]], [self.Vt[kb]])
            self.head_norm(w, O[:, 0:256], 4, HD, w.f512, [O], self.gso, w.mixed, w.mixed[:, hg * 256:(hg + 1) * 256], 1.0 / HD)
        self.interleave([stream(0), stream(1)])

    def attn_sample(self, w):
        c = self.cfg
        i = self.i
        npg = c.NPG
        O = self.po[0]
        ck = i["cache_k"]
        cv = i["cache_v"]

        def stream(si):
            S = w.streams[si]
            for s_ in range(si, NSEQ, 2):
                attpad = w.attpad[si]
                self.E("pool", "memset", [], [attpad], attpad[:], 0.0)
                qT = [w.QT[:, h // 2, h % 2, s_ * TS:(s_ + 1) * TS] for h in range(8)]
                att_out = (attpad[:, :, s_ * TS:(s_ + 1) * TS], attpad)
                att_lhs = ([attpad[:, h, :] for h in range(8)], attpad)
                o_cols = [(slice(None), slice(h * 64, (h + 1) * 64)) for h in range(8)]
                for blk in range(npg, -1, -1):
                    if blk == npg:
                        kT = [w.KTs[:, h // 2, :] for h in range(8)]
                        vv = [w.Vs[:, h * 64:(h + 1) * 64] for h in range(8)]
                        kreads, vreads = [w.KTs], [w.Vs]
                        mneg, m01 = w.smneg[:, s_, :], w.sm01[:, s_, :]
                    else:
                        j = s_ * npg + blk
                        kk = si * 2 + (self.pgi[si] % 2)
                        self.pgi[si] += 1
                        kpf, vpf, kpb, vpb, ktp = w.kpf[kk], w.vpf[kk], w.kpb[kk], w.vpb[kk], w.ktp[kk]
                        self.s.add("pool", lambda e, kpf=kpf, j=j: e.indirect_dma_start(
                            out=kpf[:], out_offset=None, in_=ck, in_offset=bass.IndirectOffsetOnAxis(ap=w.idx[:, j:j + 1], axis=0)),
                            [w.idx], [kpf], is_dma=True)
                        self.s.add("pool", lambda e, vpf=vpf, j=j: e.indirect_dma_start(
                            out=vpf[:], out_offset=None, in_=cv, in_offset=bass.IndirectOffsetOnAxis(ap=w.idx[:, j:j + 1], axis=0)),
                            [w.idx], [vpf], is_dma=True)
                        self.E("dve", "tensor_copy", [kpf], [kpb], out=kpb[:], in_=kpf[:])
                        self.act(vpb[:], vpf[:], AF.Copy, [vpf], [vpb])
                        tb = self.TB()
                        for pr in range(4):
                            self.tr(tb[:, pr * 128:(pr + 1) * 128], kpb[:, pr * 128:(pr + 1) * 128], self.identb, [kpb, self.cbf], [tb])
                        self.E("dve", "tensor_copy", [tb], [ktp], out=ktp[:], in_=tb[:, 0:512].rearrange("p (a b) -> p a b", a=4))
                        kT = [ktp[:, h // 2, :] for h in range(8)]
                        vv = [vpb[:, h * 64:(h + 1) * 64] for h in range(8)]
                        kreads, vreads = [ktp], [vpb]
                        mneg, m01 = None, None
                    yield from self.attn_step(w, S, 8, TS, qT, kT, vv, 128, self.kvone[:, :], self.kvone, self.brow[:, 1024:1088],
                                              mneg, m01, w.smt, O, o_cols, blk == npg, blk == 0, [w.QT], kreads, vreads,
                                              att_out=att_out, att_lhs=att_lhs,
                                              first_o=(s_ == 0 and blk == npg), last_o=(s_ == NSEQ - 1 and blk == 0))
        self.pgi = [0, 0]
        self.interleave([stream(0), stream(1)])
        self.head_norm(w, O[:, :], HS, HD, w.f512, [O], self.gso, w.mixed, w.mixed[:, 0:512], 1.0 / HD)

    def alloc_work(self, st, sample):
        class WS:
            pass
        w = WS()
        sb = lambda name, shape, dt: self.sb(st, name, shape, dt)
        w.xt = [sb("xt", [128, D], F32)]
        w.h = sb("h", [128, D], BF16)
        w.hT = sb("hT", [128, DC, 128], BF16)
        w.ssq = sb("ssq", [128, 1], F32)
        w.rstd = sb("rstd", [128, 1], F32)
        w.ss8 = sb("ss8", [128, 8], F32)
        w.rs8 = sb("rs8", [128, 8], F32)
        w.f512 = sb("f512", [128, 512], F32)
        w.kn = sb("kn", [128, 512], F32)
        w.knb = sb("knb", [128, 512], BF16)
        w.vf = w.f512
        w.qnb = sb("qnb", [128, 512], BF16)
        w.QT = sb("QT", [128, 4, 2, 128], BF16)
        self.E("pool", "memset", [], [w.QT], w.QT[:], 0.0)
        ns, T = (NSEQ, TS) if sample else (1, 128)
        w.XE4 = sb("XE4", [128, 4, ns * (3 + T)], F32)
        w.Hst = sb("Hst", [128, 12, ns * 3], F32)
        w.ba = sb("ba", [128, 8], F32)
        w.zs = sb("zs", [128, 512], F32)
        w.Y4 = sb("Y4", [128, 4, 128], F32)
        w.E4 = sb("E4", [128, 4, 128], F32)
        w.QnT = sb("QnT", [128, 4, 128], BF16)
        w.KnT = sb("KnT", [128, 4, 128], BF16)
        w.Ktok = sb("Ktok", [128, 4, 128], BF16)
        w.Vcb = sb("Vcb", [128, 4, 128], BF16)
        w.Vtok = sb("Vtok", [128, 4, 128], F32)
        w.sc = sb("sc", [128, 40], F32)
        w.dg = [sb("dg%d" % k, [128, 128], F32) for k in range(2)]
        w.fa = sb("fa", [128, 512], F32)
        w.fb = sb("fb", [128, 512], F32)
        w.fc = sb("fc", [128, 512], F32)
        w.fd = sb("fd", [128, 512], F32)
        w.Pf = [sb("Pf%d" % k, [128, 2, 128], F32) for k in range(2)]
        w.PTf = [sb("PTf%d" % k, [128, 2, 128], F32) for k in range(2)]
        w.MT = sb("MT", [128, 2, 128], F32)
        w.MTb = sb("MTb", [128, 4, 128], BF16)
        w.intraT = sb("intraT", [128, 4, 128], BF16)
        w.QdT = sb("QdT", [128, 4, 128], BF16)
        w.kdec = sb("kdec", [128, 4, 128], BF16)
        w.W = sb("W", [128, 4, 128], BF16)
        w.vnew = sb("vnew", [128, 4, 128], BF16)
        w.mixed = sb("mixed", [128, D], BF16)
        wd = 64 if sample else 512
        class ST:
            pass
        w.streams = []
        for k in range(2):
            S = ST()
            S.et = sb("et%d" % k, [128, wd], F32)
            S.spt = sb("spt%d" % k, [128, wd], BF16)
            S.att = sb("att%d" % k, [128, wd], BF16)
            S.Rb = sb("Rb%d" % k, [128, wd], BF16)
            w.streams.append(S)
        return w

    def phase1(self):
        c = self.cfg
        i, o = self.i, self.o
        self.xi = 0
        self.ai = 0
        self.pgi = 0
        with contextlib.ExitStack() as p1:
            winb = self.sb(p1, "winb", [128, DC, INC], BF16)
            self.winb = winb
            scale1 = self.sb(p1, "scale1", [128, D], F32)
            shift1 = self.sb(p1, "shift1", [128, D], F32)
            sv = self.sv
            brow = self.sb(p1, "brow", [128, 512 * 2 + 64], F32)
            self.brow = brow
            self.E("pool", "memset", [], [brow], brow[:], 0.0)
            self.E("pool", "memset", [brow], [brow], brow[0:32, :], -BIG)
            for hg in range(2):
                for h in range(4):
                    self.E("dve", "tensor_copy", [sv, brow], [brow], out=brow[0:1, hg * 512 + h * 128: hg * 512 + (h + 1) * 128],
                           in_=sv[0:1, 328 + hg * 4 + h: 329 + hg * 4 + h].to_broadcast([1, 128]))
            for h in range(8):
                self.E("dve", "tensor_copy", [sv, brow], [brow], out=brow[0:1, 1024 + h * 8: 1024 + (h + 1) * 8],
                       in_=sv[0:1, 328 + h: 329 + h].to_broadcast([1, 8]))
            with contextlib.ExitStack() as st:
                stg = [self.sb(st, "stg%d" % k, [128, DC * 512], F32) for k in range(2)]
                wv = i["w_in"].rearrange("(c p) n -> p c n", p=128)
                for ct in range(8):
                    n0, n1 = ct * 512, min(INC, (ct + 1) * 512)
                    self.stream_cast(stg, wv[:, :, n0:n1], winb, winb[:, :, n0:n1], eng="pool" if ct % 2 else "dve")
            self.fence()
            with contextlib.ExitStack() as st:
                KT = self.sb(st, "KT", [128, 4, c.NBLK * 128], BF16)
                Vr = self.sb(st, "Vr", [128, c.NBLK, 512], BF16)
                self.KTt = [Tile(KT[:, :, b * 128:(b + 1) * 128], "KT%d" % b) for b in range(c.NBLK)]
                self.Vt = [Tile(Vr[:, b, :], "V%d" % b) for b in range(c.NBLK)]
                w = self.alloc_work(st, False)
                w.scale1, w.shift1 = scale1, shift1
                w.tabt = self.ctm
                w.causrep = self.sb(st, "causrep", [128, 512], BF16)
                w.caus01rep = self.sb(st, "caus01rep", [128, 512], BF16)
                w.causrep_t = self.sb(st, "causrep_t", [1, 1], F32)
                for h in range(4):
                    self.E("dve", "tensor_copy", [self.ctm], [w.causrep_t, w.causrep], out=w.causrep[:, h * 128:(h + 1) * 128], in_=self.cm("causneg"))
                    self.E("dve", "tensor_copy", [self.ctm], [w.causrep_t, w.caus01rep], out=w.caus01rep[:, h * 128:(h + 1) * 128], in_=self.cm("caus01"))
                self.Sf = self.sb(st, "Sf", [128, 4, 128], F32)
                self.Sb = self.sb(st, "Sb", [128, 4, 128], BF16)
                self.E("pool", "memset", [], [self.Sf], self.Sf[:], 0.0)
                self.E("pool", "memset", [], [self.Sb], self.Sb[:], 0.0)
                self.load_mod([shift1, scale1], [0, 1], False)
                tabs = (self.cm("ltincl_p"), self.cm("ones"), self.cm("ms01_p"))
                for b in range(c.NBLK):
                    own = b >= c.OWN0
                    outrow = None
                    if b >= c.OUT0:
                        r0 = (b - c.OUT0) * 128
                        outrow = (o["kp"][r0:r0 + 128, :], o["vp"][r0:r0 + 128, :])
                    self.front_end(w, b, False, own, outrow)
                    if self.on("dn"):
                        self.dn_chunk(w, b, False, own, tabs, b == c.NBLK - 1)
                    if own:
                        if self.on("attn"):
                            self.attn_prompt(w, b)
                        k = b - c.OWN0
                        if self.on("dn") and self.on("attn"):
                            self.dma(self.mixd[k * 128:(k + 1) * 128, :], w.mixed[:], reads=[w.mixed], writes=[self.t_mixd[k]])
                        if "mixed_p" in self.o and b >= c.OUT0:
                            r0 = (b - c.OUT0) * 128
                            self.E("dve", "tensor_copy", [w.mixed], [w.xt[0]], out=w.xt[0][:], in_=w.mixed[:])
                            self.dma(self.o["mixed_p"][r0:r0 + 128, :], w.xt[0][:], reads=[w.xt[0]])
            self.fence()
            if self.on("sample"):
                with contextlib.ExitStack() as st:
                    w = self.alloc_work(st, True)
                    w.scale1, w.shift1 = scale1, shift1
                    cts = self.sb(st, "cts", list(CT_SAMP.shape), F32)
                    self.dma(cts[:], i["ct_samp"][:, :], writes=[cts])
                    w.tabt = cts

                    def cs(name):
                        o_, w_ = CO_SAMP[name]
                        return cts[:, o_:o_ + w_]
                    w.colmask, w.rowmask = cs("colmask"), cs("rowmask")
                    w.KTs = self.sb(st, "KTs", [128, 4, 128], BF16)
                    w.Vs = self.sb(st, "Vs", [128, 512], BF16)
                    w.Sfh = [self.sb(st, "Sfh", [128, NSEQ, 128], F32)]
                    w.Sbh = [self.sb(st, "Sbh", [128, NSEQ, 128], BF16)]
                    w.Sn = w.Sfh
                    w.Km = self.sb(st, "Km", [128, NSEQ, 128], BF16)
                    w.Qm = w.Km
                    w.kdm = w.Km
                    self.load_mod([shift1, scale1], [0, 1], True)
                    hst_t = w.Sfh[0]
                    hst = hst_t[:, :, :].rearrange("p s v -> p (s v)")[0:NSEQ * 3, 0:3 * DNW]
                    self.dma(hst, i["dnc0"][:, :], writes=[hst_t])
                    for j in range(3):
                        g = self.G()
                        for ch in range(4):
                            self.tr(g[:, ch * 48:(ch + 1) * 48], hst[:, (j * 4 + ch) * 128:(j * 4 + ch + 1) * 128], self.cm("ident")[0:48, 0:48], [hst_t, self.ctm], [g])
                        self.act(w.Hst[:, j * 4:(j + 1) * 4, :], g[:, 0:192].rearrange("p (c t) -> p c t", c=4), AF.Copy, [g], [w.Hst])
                    self.front_end(w, 0, True, True, (o["ksm"][:, :], o["vsm"][:, :]))
                    tabs = (cs("ltincl_s"), cs("seqm_s"), cs("ms01_s"))
                    if self.on("dn"):
                        self.dn_chunk(w, 0, True, True, tabs, False)
                    if self.on("attn"):
                        pti = self.sb(st, "pti", [128, NSEQ * c.NPG], I32)
                        ptf = self.sb(st, "ptf", [128, NSEQ * c.NPG], F32)
                        io = self.sb(st, "io", [128, 1], I32)
                        iof = self.sb(st, "iof", [128, 1], F32)
                        w.idx = self.sb(st, "idx", [128, NSEQ * c.NPG], I32)
                        self.dma(pti[:], i["ptab"].partition_broadcast(128), writes=[pti])
                        self.E("pool", "iota", [], [io], io[:], pattern=[[0, 1]], base=0, channel_multiplier=1)
                        self.E("dve", "tensor_copy", [io], [iof], out=iof[:], in_=io[:])
                        self.E("dve", "tensor_copy", [pti], [ptf], out=ptf[:], in_=pti[:])
                        self.E("dve", "tensor_scalar", [ptf, iof], [ptf], out=ptf[:], in0=ptf[:], scalar1=128.0, scalar2=iof[:, 0:1], op0=ALU.mult, op1=ALU.add)
                        self.E("dve", "tensor_copy", [ptf], [w.idx], out=w.idx[:], in_=ptf[:])
                        w.smt = self.sb(st, "smt", [1, 1], F32)
                        sm01 = self.sb(st, "sm01", [128, NSEQ, 64], BF16)
                        smneg = self.sb(st, "smneg", [128, NSEQ, 64], BF16)
                        o_, w_ = CO_SAMP["smask01"]
                        src = cts[:, o_:o_ + w_].rearrange("p (s q) -> p s q", s=NSEQ)
                        self.E("dve", "tensor_copy", [cts], [w.smt, sm01], out=sm01[:], in_=src)
                        self.E("dve", "tensor_scalar", [cts], [w.smt, smneg], out=smneg[:], in0=src, scalar1=-1.0, scalar2=BIG, op0=ALU.add, op1=ALU.mult)
                        w.sm01, w.smneg = sm01, smneg
                        w.attpad = [self.sb(st, "attpad%d" % k, [128, 8, 128], BF16) for k in range(2)]
                        w.kpf = [self.sb(st, "kpf%d" % k, [128, 512], F32) for k in range(4)]
                        w.vpf = [self.sb(st, "vpf%d" % k, [128, 512], F32) for k in range(4)]
                        w.kpb = [self.sb(st, "kpb%d" % k, [128, 512], BF16) for k in range(4)]
                        w.vpb = [self.sb(st, "vpb%d" % k, [128, 512], BF16) for k in range(4)]
                        w.ktp = [self.sb(st, "ktp%d" % k, [128, 4, 128], BF16) for k in range(4)]
                        self.attn_sample(w)
                    k = c.NOWN
                    if self.on("dn") and self.on("attn"):
                        self.dma(self.mixd[k * 128:(k + 1) * 128, :], w.mixed[:], reads=[w.mixed], writes=[self.t_mixd[k]])
                    if "mixed_s" in self.o:
                        self.E("dve", "tensor_copy", [w.mixed], [w.xt[0]], out=w.xt[0][:], in_=w.mixed[:])
                        self.dma(self.o["mixed_s"][:, :], w.xt[0][:], reads=[w.xt[0]])

    def phase2(self):
        c = self.cfg
        i, o = self.i, self.o
        with contextlib.ExitStack() as p2:
            sb = lambda name, shape, dt: self.sb(p2, name, shape, dt)
            woutb = sb("woutb", [128, DC, D], BF16)
            wupb = sb("wupb", [128, DC, 2 * DFF], BF16)
            wdnb = sb("wdnb", [128, FC, D], BF16)
            with contextlib.ExitStack() as st:
                stg = [self.sb(st, "stg%d" % k, [128, DC * 512], F32) for k in range(2)]
                n = 0
                wv = i["w_out"].rearrange("(c p) n -> p c n", p=128)
                for ct in range(2):
                    self.stream_cast(stg, wv[:, :, ct * 512:(ct + 1) * 512], woutb, woutb[:, :, ct * 512:(ct + 1) * 512], eng="pool" if n % 2 else "dve")
                    n += 1
                wv = i["w_up"].rearrange("(c p) n -> p c n", p=128)
                for ct in range(11):
                    self.stream_cast(stg, wv[:, :, ct * 512:(ct + 1) * 512], wupb, wupb[:, :, ct * 512:(ct + 1) * 512], eng="pool" if n % 2 else "dve")
                    n += 1
                wv = i["w_down"].rearrange("(c p) n -> p c n", p=128)
                for c0 in range(0, FC, 4):
                    c1 = min(FC, c0 + 4)
                    self.stream_cast(stg, wv[:, c0:c1, :], wdnb, wdnb[:, c0:c1, :], eng="pool" if n % 2 else "dve")
                    n += 1
            self.fence()
            gt1, scale2, shift2, gt2 = [sb(nm, [128, D], F32) for nm in ("gt1", "scale2", "shift2", "gt2")]

            class WS:
                pass
            w = WS()
            w.xt = [sb("xt2", [128, D], F32)]
            w.h = sb("h2", [128, D], BF16)
            w.hT = sb("h2T", [128, DC, 128], BF16)
            w.ssq = sb("ssq2", [128, 1], F32)
            w.rstd = sb("rstd2", [128, 1], F32)
            mixb = sb("mixb", [128, D], BF16)
            mT = sb("mT", [128, DC, 128], BF16)
            yt = sb("yt", [128, D], F32)
            UE = sb("UE", [128, 4, NSEQ * (2 + TS)], F32)
            C4 = sb("C4", [128, 4, 128], F32)
            E2 = sb("E2", [128, 2, 128], F32)
            actT = sb("actT", [128, FC, 128], BF16)
            FH = sb("FH", [128, 44, NSEQ * 2], F32)
            fso = sb("fso", [NSEQ * 2, 512], F32)
            self.E("pool", "memset", [], [FH], FH[:], 0.0)
            blocks = [(b, False) for b in range(c.OWN0, c.NBLK)] + ([(0, True)] if self.on("sample") else [])
            cur_mod = None
            for (b, sample) in blocks:
                if cur_mod != sample:
                    self.load_mod([gt1, scale2, shift2, gt2], [2, 4, 3, 5], sample)
                    cur_mod = sample
                ns, T = (NSEQ, TS) if sample else (1, 128)
                k = c.NOWN if sample else b - c.OWN0
                halo = (not sample) and b == c.OWN0
                xt = w.xt[0]
                self.dma(xt[:], i["xs"][:, :] if sample else i["xp"][b * 128:(b + 1) * 128, :], writes=[xt])
                self.dma(mixb[:], self.mixd[k * 128:(k + 1) * 128, :], reads=[self.t_mixd[k]], writes=[mixb])
                tb = self.TB()
                for dc in range(DC):
                    self.tr(tb[:, dc * 128:(dc + 1) * 128], mixb[:, dc * 128:(dc + 1) * 128], self.identb, [mixb, self.cbf], [tb])
                self.act(mT[:], tb[:, :].rearrange("p (a b) -> p a b", a=DC), AF.Copy, [tb], [mT])
                for n in range(2):
                    g = self.G()
                    for dc in range(DC):
                        self.mm(g[:, :], mT[:, dc, :], woutb[:, dc, n * 512:(n + 1) * 512], dc == 0, dc == DC - 1, [mT, woutb], [g])
                    self.E("dve", "tensor_tensor", [g, gt1], [yt], out=yt[:, n * 512:(n + 1) * 512], in0=g[:, :], in1=gt1[:, n * 512:(n + 1) * 512], op=ALU.mult)
                self.E("pool", "tensor_tensor", [yt, xt], [xt], out=xt[:], in0=yt[:], in1=xt[:], op=ALU.add)
                x1 = xt
                if "x1_p" in o and (not sample) and b >= c.OUT0:
                    r0 = (b - c.OUT0) * 128
                    self.dma(o["x1_p"][r0:r0 + 128, :], x1[:], reads=[x1])
                self.norm_mod(w, x1, scale2, shift2, w.hT, tmp=yt)
                if sample:
                    for j in range(11):
                        hst = fso
                        self.dma(hst[:, :], i["ffc0"][:, j * 512:(j + 1) * 512], writes=[hst])
                        g = self.G()
                        for ch in range(4):
                            self.tr(g[:, ch * 32:(ch + 1) * 32], hst[:, ch * 128:(ch + 1) * 128], self.cm("ident")[0:32, 0:32], [hst, self.ctm], [g])
                        self.act(FH[:, j * 4:(j + 1) * 4, :], g[:, 0:128].rearrange("p (c t) -> p c t", c=4), AF.Copy, [g], [FH])
                ue4 = UE[:, :, 0:ns * (2 + T)].rearrange("p c (s t) -> p c s t", s=ns)
                fh4 = FH[:, :, 0:ns * 2].rearrange("p c (s t) -> p c s t", s=ns)
                for gi_ in range(11):
                    chs = [2 * gi_, 2 * gi_ + 1, FC + 2 * gi_, FC + 2 * gi_ + 1]
                    g = self.G()
                    for q_, ch in enumerate(chs):
                        for dc in range(DC):
                            self.mm(g[:, q_ * 128:(q_ + 1) * 128], wupb[:, dc, ch * 128:(ch + 1) * 128], w.hT[:, dc, :], dc == 0, dc == DC - 1, [w.hT, wupb], [g])
                    for half in range(2):
                        self.E("pool", "tensor_copy", [FH], [UE], out=ue4[:, half * 2:half * 2 + 2, :, 0:2], in_=fh4[:, chs[half * 2]:chs[half * 2] + 2, :, :])
                    src4 = g[:, :].rearrange("p (c s t) -> p c s t", c=4, s=ns)
                    if halo:
                        self.act(ue4[:, :, :, 2:2 + T], src4, AF.Copy, [g, self.bvd], [UE], scale=self.bvd[:, b:b + 1])
                    else:
                        self.act(ue4[:, :, :, 2:2 + T], src4, AF.Copy, [g], [UE])
                    for half in range(2):
                        self.E("pool", "tensor_copy", [UE], [FH], out=fh4[:, chs[half * 2]:chs[half * 2] + 2, :, :], in_=ue4[:, half * 2:half * 2 + 2, :, T:T + 2])
                    if halo:
                        continue
                    for q_, ch in enumerate(chs):
                        eng = "dve"
                        yv = C4[:, q_, :].rearrange("p (s t) -> p s t", s=ns)
                        self.E(eng, "tensor_scalar", [UE, self.wfc], [C4], out=yv, in0=ue4[:, q_, :, 0:T], scalar1=self.wfc[:, 0, ch:ch + 1], scalar2=None, op0=ALU.mult)
                        for kk in range(1, 3):
                            self.E(eng, "scalar_tensor_tensor", [UE, self.wfc, C4], [C4], out=yv, in0=ue4[:, q_, :, kk:kk + T],
                                   scalar=self.wfc[:, kk, ch:ch + 1], in1=yv, op0=ALU.mult, op1=ALU.add)
                    self.act(E2[:], C4[:, 2:4, :], AF.Exp, [C4], [E2], scale=-1.0)
                    self.E("pool", "tensor_scalar", [E2], [E2], out=E2[:], in0=E2[:], scalar1=1.0, scalar2=None, op0=ALU.add)
                    self.E("dve", "reciprocal", [E2], [E2], out=E2[:], in_=E2[:])
                    self.E("pool", "tensor_tensor", [E2, C4], [E2], out=E2[:], in0=E2[:], in1=C4[:, 2:4, :], op=ALU.mult)
                    self.E("dve", "tensor_tensor", [E2, C4], [actT], out=actT[:, 2 * gi_:2 * gi_ + 2, :], in0=E2[:], in1=C4[:, 0:2, :], op=ALU.mult)
                last_p = (not sample) and b == c.NBLK - 1
                if sample or last_p:
                    ncol = ns * 2
                    dst = o["fcs"] if sample else o["fcp"]
                    for j in range(11):
                        g = self.G()
                        for ch in range(4):
                            self.tr(g[0:ncol, ch * 128:(ch + 1) * 128], FH[:, j * 4 + ch, 0:ncol], self.cm("ident"), [FH, self.ctm], [g])
                        self.E("dve", "tensor_copy", [g], [fso], out=fso[0:ncol, :], in_=g[0:ncol, :])
                        self.dma(dst[:, j * 512:(j + 1) * 512], fso[0:ncol, :], reads=[fso])
                if halo:
                    continue
                for n in range(2):
                    g = self.G()
                    for fc_ in range(FC):
                        self.mm(g[:, :], actT[:, fc_, :], wdnb[:, fc_, n * 512:(n + 1) * 512], fc_ == 0, fc_ == FC - 1, [actT, wdnb], [g])
                    self.E("dve", "tensor_tensor", [g, gt2], [yt], out=yt[:, n * 512:(n + 1) * 512], in0=g[:, :], in1=gt2[:, n * 512:(n + 1) * 512], op=ALU.mult)
                self.E("pool", "tensor_tensor", [yt, x1], [yt], out=yt[:], in0=yt[:], in1=x1[:], op=ALU.add)
                if sample:
                    self.dma(o["ys"][:, :], yt[:], reads=[yt])
                elif b >= c.OUT0:
                    r0 = (b - c.OUT0) * 128
                    self.dma(o["yp"][r0:r0 + 128, :], yt[:], reads=[yt])


def core_inputs(cfg, core, inp):
    b, half = core // 2, core % 2
    S = cfg.NBLK * 128
    xp_full = np.asarray(inp["x_prompt"][b], np.float32)
    if half == 1:
        xp = xp_full
    else:
        xp = np.concatenate([np.zeros((S // 2, D), np.float32), xp_full[:S // 2]], axis=0)
    s0, s1 = core * NSEQ, (core + 1) * NSEQ
    m = {}
    m["xp"] = np.ascontiguousarray(xp)
    m["xs"] = np.ascontiguousarray(np.asarray(inp["x_sample"][s0:s1], np.float32).reshape(NSEQ * TS, D))
    m["cvec"] = np.ascontiguousarray(np.concatenate([np.asarray(inp["c_prompt"][b:b + 1], np.float32),
                                                     np.asarray(inp["c_sample"][s0:s1], np.float32)], axis=0))
    m["cache_k"] = np.asarray(inp["cache_k"], np.float32).reshape(cfg.NPHYS * 128, SBW)
    m["cache_v"] = np.asarray(inp["cache_v"], np.float32).reshape(cfg.NPHYS * 128, SBW)
    m["ptab"] = np.ascontiguousarray(np.asarray(inp["page_table"][s0:s1], np.int32).reshape(-1))
    m["s0"] = np.ascontiguousarray(np.asarray(inp["state_delta"][0, s0:s1], np.float32))
    m["dnc0"] = np.ascontiguousarray(np.asarray(inp["state_dn_conv"][0, s0:s1], np.float32).reshape(NSEQ * 3, 3 * DNW))
    m["ffc0"] = np.ascontiguousarray(np.asarray(inp["state_ffn_conv"][0, s0:s1], np.float32).reshape(NSEQ * 2, 2 * DFF))
    for k, nm in (("w_ada", "w_ada"), ("b_ada", "b_ada"), ("g_attn", "g_attn_norm"), ("w_in", "w_in"), ("g_q", "g_q"),
                  ("g_k", "g_k"), ("sb_bias", "sb_bias"), ("g_sb_out", "g_sb_out"), ("w_dn_conv", "w_dn_conv"),
                  ("a_log", "a_log"), ("dt_bias", "dt_bias"), ("g_dn_out", "g_dn_out"), ("w_out", "w_out"),
                  ("g_ffn", "g_ffn_norm"), ("w_up", "w_up"), ("w_ffn_conv", "w_ffn_conv"), ("w_down", "w_down")):
        m[k] = np.ascontiguousarray(np.asarray(inp[nm], np.float32)[0])
    m["ct_main"] = CT_MAIN
    m["ct_samp"] = CT_SAMP
    kv = np.zeros((2, 128), np.float32)
    kv[0 if half == 1 else 1, :] = 1.0
    m["kvlo"] = kv
    bv = np.ones((128, cfg.NBLK), np.float32)
    if half == 0:
        bv[:, :cfg.NBLK // 2] = 0.0
    m["blkvalid"] = bv
    return m


def assemble(cfg, res, nb, nsamp):
    S = cfg.NBLK * 128
    H = S // 2
    yp = np.zeros((nb, S, D), np.float32)
    ys = np.zeros((nsamp, TS, D), np.float32)
    kp = np.zeros((1, nb, S, HS, HD), np.float32)
    vp = np.zeros((1, nb, S, HS, HD), np.float32)
    ks = np.zeros((1, nsamp, TS, HS, HD), np.float32)
    vs = np.zeros((1, nsamp, TS, HS, HD), np.float32)
    sp = np.zeros((1, nb, DNH, DND, DND), np.float32)
    ss = np.zeros((1, nsamp, DNH, DND, DND), np.float32)
    dcp = np.zeros((1, nb, 3, 3 * DNW), np.float32)
    dcs = np.zeros((1, nsamp, 3, 3 * DNW), np.float32)
    fcp = np.zeros((1, nb, 2, 2 * DFF), np.float32)
    fcs = np.zeros((1, nsamp, 2, 2 * DFF), np.float32)
    for core, r in res.items():
        b, half = core // 2, core % 2
        s0, s1 = core * NSEQ, (core + 1) * NSEQ
        yp[b, half * H:(half + 1) * H] = r["yp"]
        kp[0, b, half * H:(half + 1) * H] = r["kp"].reshape(H, HS, HD)
        vp[0, b, half * H:(half + 1) * H] = r["vp"].reshape(H, HS, HD)
        ys[s0:s1] = r["ys"].reshape(NSEQ, TS, D)
        ks[0, s0:s1] = r["ksm"].reshape(NSEQ, TS, HS, HD)
        vs[0, s0:s1] = r["vsm"].reshape(NSEQ, TS, HS, HD)
        ss[0, s0:s1] = r["ss_state"]
        dcs[0, s0:s1] = r["dcs"].reshape(NSEQ, 3, 3 * DNW)
        fcs[0, s0:s1] = r["fcs"].reshape(NSEQ, 2, 2 * DFF)
        if half == 1:
            sp[0, b] = r["sp_state"]
            dcp[0, b] = r["dcp"]
            fcp[0, b] = r["fcp"]
    return (yp, ys, kp, vp, ks, vs, sp, ss, dcp, dcs, fcp, fcs)


_NC_CACHE = {}


def kernel(**inputs):
    cfg = Cfg(nblk=inputs["x_prompt"].shape[1] // 128, npg=inputs["page_table"].shape[1], nphys=inputs["cache_k"].shape[1])
    key = (cfg.NBLK, cfg.NPG, cfg.NPHYS)
    if key not in _NC_CACHE:
        _NC_CACHE[key] = Builder(cfg).build()
    nc = _NC_CACHE[key]
    ncores = 8
    in_maps = [core_inputs(cfg, c, inputs) for c in range(ncores)]
    res = run_bass_kernel_spmd(nc, in_maps, core_ids=list(range(ncores)))
    out = assemble(cfg, {c: res.results[c] for c in range(ncores)}, inputs["x_prompt"].shape[0], inputs["x_sample"].shape[0])
    return out
```

```python
import contextlib
import numpy as np
import concourse.bass as bass
import concourse.mybir as mybir
from concourse.bass_utils import run_bass_kernel_spmd

F32 = mybir.dt.float32
BF16 = mybir.dt.bfloat16
I32 = mybir.dt.int32
AF = mybir.ActivationFunctionType
ALU = mybir.AluOpType
AX = mybir.AxisListType

D = 1024
DC = 8
HS = 8
HD = 64
SBW = 512
DNH = 4
DND = 128
DNW = 512
DFF = 2816
FC = 22
INC = 3592
EPS = 1e-6
BIG = 30000.0
NSEQ = 16
TS = 8


class Cfg:
    def __init__(self, nblk=32, npg=16, nphys=2560):
        self.NBLK = nblk
        self.OWN0 = nblk // 2 - 1
        self.OUT0 = nblk // 2
        self.NPG = npg
        self.NPHYS = nphys
        self.NOUT = nblk - self.OUT0
        self.NOWN = nblk - self.OWN0


class Tile:
    __slots__ = ("ap", "name", "last_w", "readers", "excl")

    def __init__(self, ap, name="", excl=False):
        self.ap = ap
        self.name = name
        self.last_w = None
        self.readers = []
        self.excl = excl

    def __getitem__(self, k):
        return self.ap[k]


class Op:
    __slots__ = ("eng", "fn", "deps", "need_inc", "count", "sem", "is_dma", "idx")

    def __init__(self, eng, fn, is_dma=False):
        self.eng = eng
        self.fn = fn
        self.deps = set()
        self.need_inc = is_dma
        self.count = 0
        self.sem = None
        self.is_dma = is_dma


COMPUTE = ("pe", "act", "dve", "pool")


class Sched:
    def __init__(self, nc, n_dma_sems=16):
        self.nc = nc
        self.ops = {e: [] for e in COMPUTE + ("sp",)}
        self.n_dma_sems = n_dma_sems
        self.nops = 0

    def _track(self, op, reads, writes):
        ex = [t for t in reads if t.excl]
        if ex:
            reads = [t for t in reads if not t.excl]
            writes = list(writes) + [t for t in ex if t not in writes]
        for t in reads:
            if t.last_w is not None:
                op.deps.add(t.last_w)
        for t in writes:
            if t.last_w is not None:
                op.deps.add(t.last_w)
            for r in t.readers:
                op.deps.add(r)
        for t in reads:
            t.readers.append(op)
        for t in writes:
            t.last_w = op
            t.readers = []
        op.deps.discard(op)

    def add(self, eng, fn, reads=(), writes=(), is_dma=False):
        op = Op(eng, fn, is_dma)
        self._track(op, reads, writes)
        self.ops[eng].append(op)
        self.nops += 1
        return op

    def dma(self, out_ap, in_ap, reads=(), writes=(), queue="sp", **kw):
        def fn(e, out_ap=out_ap, in_ap=in_ap, kw=kw):
            return e.dma_start(out=out_ap, in_=in_ap, **kw)
        return self.add(queue, fn, reads, writes, is_dma=True)

    def emit(self):
        nc = self.nc

        def skip(d, op):
            return d.eng == "pe" and op.eng == "pe" and not d.is_dma and not op.is_dma
        for e in self.ops:
            for op in self.ops[e]:
                for d in op.deps:
                    if not skip(d, op):
                        d.need_inc = True
        with contextlib.ExitStack() as st:
            sems = {e: st.enter_context(nc.semaphore("s_" + e)) for e in COMPUTE}
            dma_sems = {}
            for q in self.ops:
                if any(o.is_dma for o in self.ops[q]):
                    dma_sems[q] = [st.enter_context(nc.semaphore("d_%s_%d" % (q, i)))
                                   for i in range(self.n_dma_sems)]
            for e in self.ops:
                c = 0
                j = 0
                for op in self.ops[e]:
                    if op.is_dma:
                        ring = dma_sems[e]
                        op.sem = ring[j % len(ring)]
                        op.count = 16 * (j // len(ring) + 1)
                        j += 1
                    elif op.need_inc:
                        c += 1
                        op.count = c
                        op.sem = sems[e]
            block = st.enter_context(nc.Block())
            handles = {"pe": block.tensor, "act": block.scalar, "dve": block.vector,
                       "pool": block.gpsimd, "sp": block.sync}

            def make(e):
                oplist = self.ops[e]

                def body(eng):
                    known = {}

                    def wait(sem, val):
                        if known.get(id(sem), 0) >= val:
                            return
                        eng.wait_ge(sem, val)
                        known[id(sem)] = val
                    for op in oplist:
                        need = {}
                        for d in op.deps:
                            if skip(d, op):
                                continue
                            k = id(d.sem)
                            if k not in need or need[k][1] < d.count:
                                need[k] = (d.sem, d.count)
                        if op.is_dma and op.count > 16:
                            k = id(op.sem)
                            v = op.count - 16
                            if k not in need or need[k][1] < v:
                                need[k] = (op.sem, v)
                        for sem, val in need.values():
                            wait(sem, val)
                        if op.fn is None:
                            continue
                        ins = op.fn(eng)
                        if op.is_dma:
                            ins.then_inc(op.sem, 16)
                        elif op.need_inc:
                            ins.then_inc(op.sem, 1)
                    last = {}
                    for op in oplist:
                        if op.is_dma:
                            last[id(op.sem)] = (op.sem, op.count)
                    for sem, val in last.values():
                        wait(sem, val)
                return body
            for e in self.ops:
                if self.ops[e]:
                    handles[e](make(e))


def host_consts():
    i = np.arange(128)
    c = {}
    c["ident"] = np.eye(128, dtype=np.float32)
    c["ones"] = np.ones((128, 128), np.float32)
    for nm, nseq in (("p", 1), ("s", NSEQ)):
        t = 128 // nseq
        seq = i // t
        same = seq[:, None] == seq[None, :]
        incl = same & (i[None, :] <= i[:, None])
        strict = same & (i[None, :] < i[:, None])
        c["ltincl_" + nm] = incl.T.astype(np.float32)
        c["seqm_" + nm] = same.astype(np.float32)
        c["nmincl_" + nm] = np.where(incl, 0.0, BIG).astype(np.float32)
        c["nminclT_" + nm] = np.where(incl.T, 0.0, -BIG).astype(np.float32)
        c["ms01_" + nm] = strict.astype(np.float32)
    caus = i[:, None] < i[None, :]
    c["caus01"] = caus.astype(np.float32)
    c["causneg"] = np.where(caus, 0.0, -BIG).astype(np.float32)
    kt = i[:, None, None]
    ss_ = np.arange(NSEQ)[None, :, None]
    qq = (np.arange(64) % TS)[None, None, :]
    c["smask01"] = ((kt // TS == ss_) & (kt % TS < qq)).astype(np.float32).reshape(128, NSEQ * 64)
    c["ntri"] = np.where(i[:, None] >= i[None, :], -1.0, 0.0).astype(np.float32)
    cm = (np.arange(NSEQ)[:, None] == (i // TS)[None, :]).astype(np.float32)
    c["colmask"] = np.broadcast_to(cm.reshape(1, NSEQ * 128), (128, NSEQ * 128)).copy()
    c["rowmask"] = np.zeros((128, 128), np.float32)
    c["rowmask"][:, :NSEQ] = cm.T
    main = ["ident", "ones", "ltincl_p", "ms01_p", "caus01", "causneg", "ntri"]
    samp = ["ltincl_s", "seqm_s", "ms01_s", "rowmask", "colmask", "smask01"]

    def pack(names):
        off = {}
        o = 0
        for k in names:
            off[k] = (o, c[k].shape[1])
            o += c[k].shape[1]
        return np.concatenate([c[k] for k in names], axis=1).astype(np.float32), off
    return pack(main) + pack(samp)


CT_MAIN, CO_MAIN, CT_SAMP, CO_SAMP = host_consts()


class Builder:
    def __init__(self, cfg, stages=("all",), dbg=()):
        self.cfg = cfg
        self.stages = stages
        self.dbg = dbg
        self.nc = bass.Bass("TRN2", target_bir_lowering=False)
        self.s = Sched(self.nc)
        self.fence_id = 0
        self._uid = 0

    def on(self, st):
        return "all" in self.stages or st in self.stages

    def sb(self, stack, name, shape, dt):
        self._uid += 1
        h = stack.enter_context(self.nc.sbuf_tensor("%s_%d" % (name, self._uid), list(shape), dt))
        return Tile(h, name)

    def view(self, ap, name=""):
        return Tile(ap, name)

    def din(self, name, shape, dt=F32):
        return self.nc.dram_tensor(name, list(shape), dt, kind="ExternalInput").ap()

    def dout(self, name, shape, dt=F32):
        return self.nc.dram_tensor(name, list(shape), dt, kind="ExternalOutput").ap()

    def dscr(self, name, shape, dt=F32):
        return self.nc.dram_tensor(name, list(shape), dt, kind="Internal").ap()

    def E(self, eng, meth, reads, writes, *a, **kw):
        return self.s.add(eng, lambda e: getattr(e, meth)(*a, **kw), reads, writes)

    def mm(self, out, lhsT, rhs, start, stop, reads, writes, skip=False):
        return self.s.add("pe", lambda e: e.matmul(out, lhsT=lhsT, rhs=rhs, start=start, stop=stop, skip_group_check=skip),
                          reads, writes)

    def tr(self, out, in_, ident, reads, writes):
        return self.s.add("pe", lambda e: e.transpose(out=out, in_=in_, identity=ident), reads, writes)

    def act(self, out, in_, func, reads, writes, **kw):
        return self.s.add("act", lambda e: e.activation(out=out, in_=in_, func=func, **kw), reads, writes)

    def dma(self, out, in_, reads=(), writes=(), **kw):
        return self.s.dma(out, in_, reads, writes, **kw)

    def fence(self):
        s = self.s
        f = set()
        for e in COMPUTE:
            real = [o for o in s.ops[e] if o.fn is not None and not o.is_dma]
            if real:
                f.add(real[-1])
        for q in s.ops:
            d = [o for o in s.ops[q] if o.is_dma]
            for o in d[-s.n_dma_sems:]:
                f.add(o)
        for e in COMPUTE + ("sp",):
            op = Op(e, None)
            op.deps = set(f)
            s.ops[e].append(op)

    def G(self):
        t = self.pg[self.gi % len(self.pg)]
        self.gi += 1
        return t

    def TB(self):
        t = self.ptb[self.ti % len(self.ptb)]
        self.ti += 1
        return t

    def declare(self):
        c = self.cfg
        NT = c.NBLK * 128
        i = {}
        i["xp"] = self.din("xp", [NT, D])
        i["xs"] = self.din("xs", [128, D])
        i["cvec"] = self.din("cvec", [17, D])
        i["cache_k"] = self.din("cache_k", [c.NPHYS * 128, SBW])
        i["cache_v"] = self.din("cache_v", [c.NPHYS * 128, SBW])
        i["ptab"] = self.din("ptab", [NSEQ * c.NPG], I32)
        i["s0"] = self.din("s0", [NSEQ, DNH, DND, DND])
        i["dnc0"] = self.din("dnc0", [NSEQ * 3, 3 * DNW])
        i["ffc0"] = self.din("ffc0", [NSEQ * 2, 2 * DFF])
        i["w_ada"] = self.din("w_ada", [D, 6 * D])
        i["b_ada"] = self.din("b_ada", [6 * D])
        i["g_attn"] = self.din("g_attn", [D])
        i["w_in"] = self.din("w_in", [D, INC])
        i["g_q"] = self.din("g_q", [HD])
        i["g_k"] = self.din("g_k", [HD])
        i["sb_bias"] = self.din("sb_bias", [HS])
        i["g_sb_out"] = self.din("g_sb_out", [HD])
        i["w_dn_conv"] = self.din("w_dn_conv", [4, 3 * DNW])
        i["a_log"] = self.din("a_log", [DNH])
        i["dt_bias"] = self.din("dt_bias", [DNH])
        i["g_dn_out"] = self.din("g_dn_out", [DND])
        i["w_out"] = self.din("w_out", [D, D])
        i["g_ffn"] = self.din("g_ffn", [D])
        i["w_up"] = self.din("w_up", [D, 2 * DFF])
        i["w_ffn_conv"] = self.din("w_ffn_conv", [3, 2 * DFF])
        i["w_down"] = self.din("w_down", [DFF, D])
        i["ct_main"] = self.din("ct_main", list(CT_MAIN.shape))
        i["ct_samp"] = self.din("ct_samp", list(CT_SAMP.shape))
        i["kvlo"] = self.din("kvlo", [128, 256])
        i["blkvalid"] = self.din("blkvalid", [128, c.NBLK])
        self.i = i
        o = {}
        o["yp"] = self.dout("yp", [c.NOUT * 128, D])
        o["ys"] = self.dout("ys", [128, D])
        o["kp"] = self.dout("kp", [c.NOUT * 128, SBW])
        o["vp"] = self.dout("vp", [c.NOUT * 128, SBW])
        o["ksm"] = self.dout("ksm", [128, SBW])
        o["vsm"] = self.dout("vsm", [128, SBW])
        o["sp_state"] = self.dout("sp_state", [DNH, DND, DND])
        o["ss_state"] = self.dout("ss_state", [NSEQ, DNH, DND, DND])
        o["dcp"] = self.dout("dcp", [3, 3 * DNW])
        o["dcs"] = self.dout("dcs", [NSEQ * 3, 3 * DNW])
        o["fcp"] = self.dout("fcp", [2, 2 * DFF])
        o["fcs"] = self.dout("fcs", [NSEQ * 2, 2 * DFF])
        for name, shape in self.dbg:
            o[name] = self.dout(name, shape)
        self.o = o
        self.modd = self.dscr("modd", [17, 6 * D])
        self.mixd = self.dscr("mixd", [(c.NOWN + 1) * 128, D], BF16)
        self.t_modd = Tile(None, "modd")
        self.t_mixd = [Tile(None, "mixd%d" % k) for k in range(c.NOWN + 1)]

    def stream_cast(self, stack_tiles, src_view, dst_tile, dst_ap, eng="dve"):
        stg = stack_tiles[self.sci % len(stack_tiles)]
        self.sci += 1
        shp = src_view.shape
        sap = stg[:, 0:shp[1] * shp[2]].rearrange("p (a b) -> p a b", a=shp[1])
        self.dma(sap, src_view, writes=[stg])
        if eng == "act":
            self.act(dst_ap, sap, AF.Copy, [stg], [dst_tile])
        else:
            self.E(eng, "tensor_copy", [stg], [dst_tile], out=dst_ap, in_=sap)

    def rsqrt_ops(self, ss, rs, n, scale, reads_extra=()):
        self.act(rs[:, 0:n], ss[:, 0:n], AF.Ln, [ss] + list(reads_extra), [rs], scale=scale, bias=self.epsb[:, 0:1])
        self.act(rs[:, 0:n], rs[:, 0:n], AF.Exp, [rs], [rs], scale=-0.5)

    def build(self):
        nc = self.nc
        c = self.cfg
        self.declare()
        i, o = self.i, self.o
        self.gi = 0
        self.ti = 0
        self.sci = 0
        with contextlib.ExitStack() as top:
            self.pg = [Tile(top.enter_context(nc.psum_tensor("pg%d" % k, [128, 512], F32)), "pg%d" % k, True) for k in range(4)]
            self.po = [Tile(top.enter_context(nc.psum_tensor("po%d" % k, [128, 512], F32)), "po%d" % k, True) for k in range(2)]
            self.ptb = [Tile(top.enter_context(nc.psum_tensor("ptb%d" % k, [128, 1024], BF16)), "ptb%d" % k, True) for k in range(2)]
            ctm = self.sb(top, "ctm", list(CT_MAIN.shape), F32)
            self.ctm = ctm
            self.dma(ctm[:], i["ct_main"][:, :], writes=[ctm])

            def cm(name):
                o_, w_ = CO_MAIN[name]
                return ctm[:, o_:o_ + w_]
            self.cm = cm
            cbf = self.sb(top, "cbf", [128, 4 * 128], BF16)
            self.cbf = cbf
            self.E("dve", "tensor_copy", [ctm], [cbf], out=cbf[:, 0:128], in_=cm("ident"))
            self.E("dve", "tensor_copy", [ctm], [cbf], out=cbf[:, 128:256], in_=cm("ntri"))
            self.E("dve", "tensor_copy", [ctm], [cbf], out=cbf[:, 256:384], in_=cm("causneg"))
            self.E("dve", "tensor_scalar", [ctm], [cbf], out=cbf[:, 384:512], in0=cm("ones"), scalar1=-1.0, scalar2=None, op0=ALU.mult)
            self.identb = cbf[:, 0:128]
            self.ntrib = cbf[:, 128:256]
            self.causnegb = cbf[:, 256:384]
            self.negonesb = cbf[:, 384:512]
            epsb = self.sb(top, "epsb", [128, 1], F32)
            self.epsb = epsb
            self.E("pool", "memset", [], [epsb], epsb[:], EPS)
            sv = self.sb(top, "sv", [128, 64 * 3 + 128 + 4 + 4 + 8], F32)
            self.sv = sv
            self.dma(sv[:, 0:64], i["g_q"].partition_broadcast(128), writes=[sv])
            self.dma(sv[:, 64:128], i["g_k"].partition_broadcast(128), writes=[sv])
            self.dma(sv[:, 128:192], i["g_sb_out"].partition_broadcast(128), writes=[sv])
            self.dma(sv[:, 192:320], i["g_dn_out"].partition_broadcast(128), writes=[sv])
            self.dma(sv[:, 320:324], i["a_log"].partition_broadcast(128), writes=[sv])
            self.dma(sv[:, 324:328], i["dt_bias"].partition_broadcast(128), writes=[sv])
            self.dma(sv[:, 328:336], i["sb_bias"].partition_broadcast(128), writes=[sv])
            self.E("dve", "tensor_scalar", [sv], [sv], out=sv[:, 0:64], in0=sv[:, 0:64], scalar1=HD ** -0.5, scalar2=None, op0=ALU.mult)
            self.act(sv[:, 320:324], sv[:, 320:324], AF.Exp, [sv], [sv])
            self.E("dve", "tensor_scalar", [sv], [sv], out=sv[:, 320:324], in0=sv[:, 320:324], scalar1=-1.0, scalar2=None, op0=ALU.mult)
            self.gq8, self.gk, self.gso, self.gdn = sv[:, 0:64], sv[:, 64:128], sv[:, 128:192], sv[:, 192:320]
            self.negA, self.dtb = sv[:, 320:324], sv[:, 324:328]
            kvf = self.sb(top, "kvf", [128, 256], F32)
            self.dma(kvf[:], i["kvlo"][:, :], writes=[kvf])
            kvd = self.sb(top, "kvd", [128, 128], BF16)
            self.kvd = kvd
            self.E("dve", "tensor_copy", [kvf], [kvd], out=kvd[:], in_=kvf[:, 0:128])
            bvd = self.sb(top, "bvd", [128, c.NBLK], F32)
            self.bvd = bvd
            self.dma(bvd[:], i["blkvalid"][:, :], writes=[bvd])
            kvone = self.sb(top, "kvone", [128, 128], BF16)
            self.kvone = kvone
            self.E("dve", "tensor_copy", [kvf], [kvone], out=kvone[:], in_=kvf[:, 128:256])
            wdc = self.sb(top, "wdc", [128, 4, 12], F32)
            self.wdc = wdc
            for t_ in range(4):
                self.dma(wdc[:, t_, :], i["w_dn_conv"][t_].rearrange("(c p) -> p c", p=128), writes=[wdc], allow_slow_non_contiguous=True)
            wfc = self.sb(top, "wfc", [128, 3, 44], F32)
            self.wfc = wfc
            for t_ in range(3):
                self.dma(wfc[:, t_, :], i["w_ffn_conv"][t_].rearrange("(c p) -> p c", p=128), writes=[wfc], allow_slow_non_contiguous=True)

            if self.on("setup"):
                self.setup_mod()
            self.fence()
            if self.on("p1"):
                self.phase1()
            self.fence()
            if self.on("p2"):
                self.phase2()
            self.s.emit()
        return nc

    def setup_mod(self):
        i = self.i
        with contextlib.ExitStack() as st:
            cv = self.sb(st, "cv", [17, D], F32)
            ex = self.sb(st, "ex", [17, D], F32)
            scb = self.sb(st, "scb", [17, D], BF16)
            scT = self.sb(st, "scT", [128, DC, 17], BF16)
            stg = [self.sb(st, "stg%d" % k, [128, DC * 512], F32) for k in range(2)]
            wab = [self.sb(st, "wab%d" % k, [128, DC, 512], BF16) for k in range(2)]
            bada = self.sb(st, "bada", [17, 512], F32)
            gv = self.sb(st, "gv", [17, 2 * D], F32)
            mt = [self.sb(st, "mt%d" % k, [17, 512], F32) for k in range(2)]
            self.dma(cv[:], i["cvec"][:, :], writes=[cv])
            self.dma(gv[:, 0:D], i["g_attn"].partition_broadcast(17), writes=[gv])
            self.dma(gv[:, D:2 * D], i["g_ffn"].partition_broadcast(17), writes=[gv])
            self.act(ex[:], cv[:], AF.Exp, [cv], [ex], scale=-1.0)
            self.E("dve", "tensor_scalar", [ex], [ex], out=ex[:], in0=ex[:], scalar1=1.0, scalar2=None, op0=ALU.add)
            self.E("dve", "reciprocal", [ex], [ex], out=ex[:], in_=ex[:])
            self.E("dve", "tensor_tensor", [ex, cv], [scb], out=scb[:], in0=cv[:], in1=ex[:], op=ALU.mult)
            tb = self.TB()
            for dc in range(DC):
                self.tr(tb[:, dc * 32:dc * 32 + 17], scb[0:17, dc * 128:(dc + 1) * 128], self.identb[0:17, 0:17], [scb, self.cbf], [tb])
            self.E("dve", "tensor_copy", [tb], [scT], out=scT[:], in_=tb[:, 0:DC * 32].rearrange("p (a b) -> p a b", a=DC)[:, :, 0:17])
            wv = i["w_ada"].rearrange("(c p) n -> p c n", p=128)
            for ct in range(12):
                wb = wab[ct % 2]
                self.stream_cast(stg, wv[:, :, ct * 512:(ct + 1) * 512], wb, wb[:], eng="act" if ct % 2 else "dve")
                self.dma(bada[:], i["b_ada"][ct * 512:(ct + 1) * 512].partition_broadcast(17), writes=[bada])
                g = self.G()
                for dc in range(DC):
                    self.mm(g[0:17, :], scT[:, dc, :], wb[:, dc, :], dc == 0, dc == DC - 1, [scT, wb], [g])
                m = mt[ct % 2]
                self.E("dve", "tensor_tensor", [g, bada], [m], out=m[:], in0=g[0:17, :], in1=bada[:], op=ALU.add)
                if ct in (2, 3, 8, 9):
                    go = (ct - 2) * 512 if ct < 4 else D + (ct - 8) * 512
                    self.E("dve", "scalar_tensor_tensor", [m, gv], [m], out=m[:], in0=m[:], scalar=1.0, in1=gv[:, go:go + 512],
                           op0=ALU.add, op1=ALU.mult)
                self.dma(self.modd[:, ct * 512:(ct + 1) * 512], m[:], reads=[m], writes=[self.t_modd])

    def load_mod(self, tiles, idxs, sample):
        for t, ix in zip(tiles, idxs):
            if not sample:
                self.dma(t[:], self.modd[0, ix * D:(ix + 1) * D].partition_broadcast(128), reads=[self.t_modd], writes=[t])
            else:
                for s_ in range(NSEQ):
                    self.dma(t[s_ * TS:(s_ + 1) * TS, :], self.modd[1 + s_, ix * D:(ix + 1) * D].partition_broadcast(TS),
                             reads=[self.t_modd], writes=[t])

    def norm_mod(self, w, xt, scale, shift, hT, tmp=None):
        self.E("pool", "memset", [], [w.ssq], w.ssq[:], 0.0)
        self.act(w.h[:], xt[:], AF.Square, [xt, w.ssq], [w.h, w.ssq], accum_out=w.ssq[:, 0:1])
        self.rsqrt_ops(w.ssq, w.rstd, 1, 1.0 / D)
        if tmp is None:
            tmp = xt
        self.E("dve", "scalar_tensor_tensor", [xt, w.rstd, scale], [tmp], out=tmp[:], in0=xt[:], scalar=w.rstd[:, 0:1],
               in1=scale[:], op0=ALU.mult, op1=ALU.mult)
        self.E("pool", "tensor_tensor", [tmp, shift], [w.h], out=w.h[:], in0=tmp[:], in1=shift[:], op=ALU.add)
        tb = self.TB()
        for dc in range(DC):
            self.tr(tb[:, dc * 128:(dc + 1) * 128], w.h[:, dc * 128:(dc + 1) * 128], self.identb, [w.h, self.cbf], [tb])
        self.act(hT[:], tb[:, :].rearrange("p (a b) -> p a b", a=DC), AF.Copy, [tb], [hT])

    def head_norm(self, w, ps, nh, hd, out_f32, reads_ps, gvec, out_tile, out_ap, scale):
        v3 = lambda ap: ap.rearrange("p (a b) -> p a b", a=nh)
        self.act(out_f32[:, 0:nh * hd], ps, AF.Square, reads_ps, [out_f32])
        self.E("dve", "tensor_reduce", [out_f32], [w.ss8], out=w.ss8[:, 0:nh], in_=v3(out_f32[:, 0:nh * hd]), axis=AX.X, op=ALU.add)
        self.rsqrt_ops(w.ss8, w.rs8, nh, scale)
        self.E("dve", "tensor_tensor", reads_ps + [w.rs8], [out_f32], out=v3(out_f32[:, 0:nh * hd]), in0=v3(ps),
               in1=w.rs8[:, 0:nh].unsqueeze(2).to_broadcast([128, nh, hd]), op=ALU.mult)
        self.E("pool", "tensor_tensor", [out_f32, self.sv], [out_tile], out=v3(out_ap), in0=v3(out_f32[:, 0:nh * hd]),
               in1=gvec.unsqueeze(1).to_broadcast([128, nh, hd]), op=ALU.mult)

    def front_end(self, w, b, sample, own, outrow):
        c = self.cfg
        i, o = self.i, self.o
        xt = w.xt[self.xi % len(w.xt)]
        self.xi += 1
        src = i["xs"][:, :] if sample else i["xp"][b * 128:(b + 1) * 128, :]
        self.dma(xt[:], src, writes=[xt])
        hT = w.hT
        self.norm_mod(w, xt, w.scale1, w.shift1, hT)
        winb = self.winb
        g = self.G()
        for dc in range(DC):
            self.mm(g[:, :], hT[:, dc, :], winb[:, dc, 512:1024], dc == 0, dc == DC - 1, [hT, winb], [g])
        self.head_norm(w, g[:, :], HS, HD, w.f512, [g], self.gk, w.kn, w.kn[:], 1.0 / HD)
        if outrow is not None:
            self.dma(outrow[0], w.kn[:], reads=[w.kn])
        self.E("dve", "tensor_copy", [w.kn], [w.knb], out=w.knb[:], in_=w.kn[:])
        KTt = w.KTs if sample else self.KTt[b]
        tb = self.TB()
        for pr in range(4):
            self.tr(tb[:, pr * 128:(pr + 1) * 128], w.knb[:, pr * 128:(pr + 1) * 128], self.identb, [w.knb, self.cbf], [tb])
        self.act(KTt[:], tb[:, 0:512].rearrange("p (a b) -> p a b", a=4), AF.Copy, [tb], [KTt])
        g = self.G()
        for dc in range(DC):
            self.mm(g[:, :], hT[:, dc, :], winb[:, dc, 1024:1536], dc == 0, dc == DC - 1, [hT, winb], [g])
        Vt = w.Vs if sample else self.Vt[b]
        self.act(Vt[:], g[:, :], AF.Copy, [g], [Vt])
        if outrow is not None:
            self.E("dve", "tensor_copy", [g], [w.vf], out=w.vf[:], in_=g[:, :])
            self.dma(outrow[1], w.vf[:], reads=[w.vf])
        if own:
            g = self.G()
            for dc in range(DC):
                self.mm(g[:, :], hT[:, dc, :], winb[:, dc, 0:512], dc == 0, dc == DC - 1, [hT, winb], [g])
            self.head_norm(w, g[:, :], HS, HD, w.f512, [g], self.gq8, w.qnb, w.qnb[:], 1.0 / HD)
            tb = self.TB()
            for pr in range(4):
                self.tr(tb[:, pr * 128:(pr + 1) * 128], w.qnb[:, pr * 128:(pr + 1) * 128], self.identb, [w.qnb, self.cbf], [tb])
            tq = tb[:, 0:512].rearrange("p (a b) -> p a b", a=4)
            self.act(w.QT[0:64, :, 0, :], tq[0:64, :, :], AF.Copy, [tb], [w.QT])
            self.E("dve", "tensor_copy", [tb], [w.QT], out=w.QT[64:128, :, 1, :], in_=tq[64:128, :, :])
        if self.on("dn"):
            g = self.G()
            for dc in range(DC):
                self.mm(g[:, 0:8], hT[:, dc, :], winb[:, dc, 3072:3080], dc == 0, dc == DC - 1, [hT, winb], [g])
            self.E("dve", "tensor_copy", [g], [w.ba], out=w.ba[:], in_=g[:, 0:8])
            if own:
                g = self.G()
                for dc in range(DC):
                    self.mm(g[:, :], hT[:, dc, :], winb[:, dc, 3080:3592], dc == 0, dc == DC - 1, [hT, winb], [g])
                self.act(w.zs[:], g[:, :], AF.Copy, [g], [w.zs])

    def dn_chunk(self, w, b, sample, own, tabs, last_prompt):
        c = self.cfg
        i, o = self.i, self.o
        T = TS if sample else 128
        ns = NSEQ if sample else 1
        sc = w.sc
        hT, winb = w.hT, self.winb
        ltincl, seqm, ms01 = tabs
        bc4 = lambda ap: ap.unsqueeze(2).to_broadcast([128, 4, 128])
        hb4 = lambda ap: ap.unsqueeze(1).to_broadcast([128, 4, 128])
        v4 = lambda ap: ap.rearrange("p (a b) -> p a b", a=4)
        XE4 = w.XE4
        xe4 = XE4[:, :, :].rearrange("p c (s t) -> p c s t", s=ns)
        Hst = w.Hst
        hs4 = Hst[:, :, 0:ns * 3].rearrange("p c (s t) -> p c s t", s=ns)
        if (not sample) and b == 0:
            self.E("pool", "memset", [], [Hst], Hst[:], 0.0)
        for j in range(3):
            g = self.G()
            for ch in range(4):
                col = 1536 + (j * 4 + ch) * 128
                for dc in range(DC):
                    self.mm(g[:, ch * 128:(ch + 1) * 128], winb[:, dc, col:col + 128], hT[:, dc, :], dc == 0, dc == DC - 1, [hT, winb], [g])
            src4 = g[:, :].rearrange("p (c s t) -> p c s t", c=4, s=ns)
            self.E("pool", "tensor_copy", [Hst], [XE4], out=xe4[:, :, :, 0:3], in_=hs4[:, j * 4:(j + 1) * 4, :, :])
            if sample:
                self.act(xe4[:, :, :, 3:3 + T], src4, AF.Copy, [g], [XE4])
            else:
                self.act(xe4[:, :, :, 3:3 + T], src4, AF.Copy, [g, self.bvd], [XE4], scale=self.bvd[:, b:b + 1])
            self.E("pool", "tensor_copy", [XE4], [Hst], out=hs4[:, j * 4:(j + 1) * 4, :, :], in_=xe4[:, :, :, T:T + 3])
            Y4 = w.Y4
            for ch in range(4):
                cc = j * 4 + ch
                eng = "dve"
                yv = Y4[:, ch, :].rearrange("p (s t) -> p s t", s=ns)
                self.E(eng, "tensor_scalar", [XE4, self.wdc], [Y4], out=yv, in0=xe4[:, ch, :, 0:T], scalar1=self.wdc[:, 0, cc:cc + 1],
                       scalar2=None, op0=ALU.mult)
                for k in range(1, 4):
                    self.E(eng, "scalar_tensor_tensor", [XE4, self.wdc, Y4], [Y4], out=yv, in0=xe4[:, ch, :, k:k + T],
                           scalar=self.wdc[:, k, cc:cc + 1], in1=yv, op0=ALU.mult, op1=ALU.add)
            E4 = w.E4
            self.act(E4[:], Y4[:], AF.Exp, [Y4], [E4], scale=-1.0)
            self.act(E4[:], E4[:], AF.Ln, [E4], [E4], bias=1.0)
            self.act(E4[:], E4[:], AF.Exp, [E4], [E4], scale=-1.0)
            self.E("dve", "tensor_tensor", [E4, Y4], [Y4], out=Y4[:], in0=Y4[:], in1=E4[:], op=ALU.mult)
            if j < 2:
                SQ = E4
                self.act(SQ[:], Y4[:], AF.Square, [Y4], [SQ])
                g = self.G()
                self.mm(g[:, :], self.cm("ones"), SQ[:].rearrange("p a b -> p (a b)"), True, True, [SQ, self.ctm], [g])
                self.act(SQ[:].rearrange("p a b -> p (a b)"), g[:, :], AF.Ln, [g], [SQ], bias=self.epsb[:, 0:1])
                self.act(SQ[:], SQ[:], AF.Exp, [SQ], [SQ], scale=-0.5)
                if j == 0:
                    self.E("dve", "scalar_tensor_tensor", [Y4, SQ], [w.QnT], out=w.QnT[:], in0=Y4[:], scalar=DND ** -0.5, in1=SQ[:],
                           op0=ALU.mult, op1=ALU.mult)
                else:
                    self.E("pool", "tensor_tensor", [Y4, SQ], [w.KnT], out=w.KnT[:], in0=Y4[:], in1=SQ[:], op=ALU.mult)
            else:
                self.act(w.Vcb[:], Y4[:], AF.Copy, [Y4], [w.Vcb])
        if sample or last_prompt:
            ncol = ns * 3
            dst = o["dcs"] if sample else o["dcp"]
            for j in range(3):
                g = self.G()
                for ch in range(4):
                    self.tr(g[0:ncol, ch * 128:(ch + 1) * 128], Hst[:, j * 4 + ch, 0:ncol], self.cm("ident"), [Hst, self.ctm], [g])
                self.E("dve", "tensor_copy", [g], [w.f512], out=w.f512[0:ncol, :], in_=g[0:ncol, :])
                self.dma(dst[:, j * 512:(j + 1) * 512], w.f512[0:ncol, :], reads=[w.f512])
        tb = self.TB()
        for h in range(4):
            self.tr(tb[:, h * 128:(h + 1) * 128], w.KnT[:, h, :], self.identb, [w.KnT, self.cbf], [tb])
        self.act(w.Ktok[:], v4(tb[:, 0:512]), AF.Copy, [tb], [w.Ktok])
        tb = self.TB()
        for h in range(4):
            self.tr(tb[:, h * 128:(h + 1) * 128], w.Vcb[:, h, :], self.identb, [w.Vcb, self.cbf], [tb])
        self.E("dve", "tensor_copy", [tb], [w.Vtok], out=w.Vtok[:], in_=v4(tb[:, 0:512]))
        ba = w.ba
        self.act(sc[:, 0:4], ba[:, 0:4], AF.Exp, [ba], [sc], scale=-1.0)
        self.act(sc[:, 0:4], sc[:, 0:4], AF.Ln, [sc], [sc], bias=1.0)
        self.act(sc[:, 0:4], sc[:, 0:4], AF.Exp, [sc], [sc], scale=-1.0)
        if not sample:
            self.E("dve", "tensor_scalar", [sc, self.bvd], [sc], out=sc[:, 0:4], in0=sc[:, 0:4], scalar1=self.bvd[:, b:b + 1], scalar2=None, op0=ALU.mult)
        self.E("dve", "tensor_tensor", [ba, self.sv], [sc], out=sc[:, 32:36], in0=ba[:, 4:8], in1=self.dtb, op=ALU.add)
        self.act(sc[:, 32:36], sc[:, 32:36], AF.Exp, [sc], [sc])
        self.act(sc[:, 32:36], sc[:, 32:36], AF.Ln, [sc], [sc], bias=1.0)
        self.E("dve", "tensor_tensor", [sc, self.sv], [sc], out=sc[:, 4:8], in0=sc[:, 32:36], in1=self.negA, op=ALU.mult)
        g = self.G()
        self.mm(g[:, 0:4], ltincl, sc[:, 4:8], True, True, [sc, w.tabt], [g])
        self.mm(g[:, 4:8], seqm, sc[:, 4:8], True, True, [sc, w.tabt], [g])
        self.E("dve", "tensor_copy", [g], [sc], out=sc[:, 8:16], in_=g[:, 0:8])
        self.act(sc[:, 16:20], sc[:, 8:12], AF.Exp, [sc], [sc])
        self.E("dve", "scalar_tensor_tensor", [sc], [sc], out=sc[:, 20:24], in0=sc[:, 0:4], scalar=-1.0, in1=sc[:, 16:20], op0=ALU.mult, op1=ALU.mult)
        self.E("dve", "tensor_tensor", [sc], [sc], out=sc[:, 24:28], in0=sc[:, 12:16], in1=sc[:, 8:12], op=ALU.subtract)
        self.act(sc[:, 24:28], sc[:, 24:28], AF.Exp, [sc], [sc])
        self.E("dve", "tensor_scalar", [sc], [sc], out=sc[:, 28:32], in0=sc[:, 0:4], scalar1=-1.0, scalar2=None, op0=ALU.mult)
        beta, gg, negbg, kds, negbeta = sc[:, 0:4], sc[:, 8:12], sc[:, 20:24], sc[:, 24:28], sc[:, 28:32]
        GR = self.G()
        for h in range(4):
            dg = w.dg[h % 2]
            self.E("dve", "tensor_scalar", [sc, self.ctm], [dg], out=dg[:], in0=self.cm("ident"), scalar1=sc[:, 8 + h:9 + h], scalar2=None, op0=ALU.mult)
            self.mm(GR[:, h * 128:(h + 1) * 128], self.cm("ones"), dg[:], True, True, [dg, self.ctm], [GR])
        fa, fb, fc, fd = w.fa, w.fb, w.fc, w.fd
        self.E("dve", "tensor_tensor", [GR, sc], [fa], out=v4(fa[:]), in0=v4(GR[:, :]), in1=bc4(gg), op=ALU.subtract)
        self.act(fd[:], GR[:, :], AF.Exp, [GR], [fd])
        self.E("dve", "tensor_scalar", [fa], [fb], out=fb[:], in0=fa[:], scalar1=0.0, scalar2=None, op0=ALU.max)
        self.E("dve", "tensor_scalar", [fa], [fc], out=fc[:], in0=fa[:], scalar1=0.0, scalar2=None, op0=ALU.min)
        self.act(fb[:], fb[:], AF.Exp, [fb], [fb], scale=-1.0)
        self.act(fc[:], fc[:], AF.Exp, [fc], [fc])
        g = self.G()
        for h in range(4):
            self.mm(g[:, h * 128:(h + 1) * 128], w.KnT[:, h, :], w.KnT[:, h, :], True, True, [w.KnT], [g])
        self.E("dve", "tensor_tensor", [g, fb], [fa], out=fa[:], in0=g[:, :], in1=fb[:], op=ALU.mult)
        self.E("pool", "tensor_tensor", [fa, sc], [fa], out=v4(fa[:]), in0=v4(fa[:]), in1=bc4(negbeta), op=ALU.mult)
        self.E("dve", "tensor_tensor", [fa, w.tabt], [fa], out=v4(fa[:]), in0=v4(fa[:]), in1=hb4(ms01), op=ALU.mult)
        g = self.G()
        for h in range(4):
            self.mm(g[:, h * 128:(h + 1) * 128], w.KnT[:, h, :], w.QnT[:, h, :], True, True, [w.KnT, w.QnT], [g])
        self.E("dve", "tensor_tensor", [g, fc], [fc], out=fc[:], in0=g[:, :], in1=fc[:], op=ALU.mult)
        self.E("pool", "tensor_tensor", [fc, w.tabt], [w.intraT], out=w.intraT[:], in0=v4(fc[:]), in1=hb4(ltincl), op=ALU.mult)
        MTb = w.MTb
        nlev = 2 if sample else 6
        h2 = lambda ap: ap.rearrange("p (a b) -> p a b", a=2)
        hb2 = lambda ap: ap.unsqueeze(1).to_broadcast([128, 2, 128])
        identf = self.cm("ident")
        for hp in range(2):
            P, PT, MT = w.Pf[0], w.PTf[0], w.MT
            self.E("pool", "tensor_copy", [fa], [P], out=P[:], in_=v4(fa[:])[:, 2 * hp:2 * hp + 2, :])
            g = self.G()
            for h in range(2):
                self.tr(g[:, h * 128:(h + 1) * 128], P[:, h, :], identf, [P, self.ctm], [g])
            self.act(PT[:], h2(g[:, 0:256]), AF.Copy, [g], [PT])
            self.E("dve", "tensor_tensor", [PT, self.ctm], [MT], out=MT[:], in0=PT[:], in1=hb2(identf), op=ALU.add)
            for lev in range(1, nlev + 1):
                Pn, PTn = w.Pf[lev % 2], w.PTf[lev % 2]
                g1 = self.G()
                for h in range(2):
                    self.mm(g1[:, h * 128:(h + 1) * 128], PT[:, h, :], P[:, h, :], True, True, [P, PT], [g1])
                self.act(Pn[:], h2(g1[:, 0:256]), AF.Copy, [g1], [Pn])
                if lev < nlev:
                    g2 = self.G()
                    for h in range(2):
                        self.mm(g2[:, h * 128:(h + 1) * 128], P[:, h, :], PT[:, h, :], True, True, [P, PT], [g2])
                    self.E("dve", "tensor_copy", [g2], [PTn], out=PTn[:], in_=h2(g2[:, 0:256]))
                g3 = self.G()
                for h in range(2):
                    self.mm(g3[:, h * 128:(h + 1) * 128], Pn[:, h, :], MT[:, h, :], True, True, [Pn, MT], [g3])
                self.E("dve", "tensor_tensor", [g3, MT], [MT], out=MT[:], in0=h2(g3[:, 0:256]), in1=MT[:], op=ALU.add)
                P, PT = Pn, PTn
            self.act(MTb[:, 2 * hp:2 * hp + 2, :], MT[:], AF.Copy, [MT], [MTb])
        self.E("pool", "tensor_tensor", [w.QnT, fd], [w.QdT], out=w.QdT[:], in0=w.QnT[:], in1=v4(fd[:]), op=ALU.mult)
        self.E("dve", "tensor_tensor", [w.Vtok, sc], [w.Vtok], out=w.Vtok[:], in0=w.Vtok[:], in1=bc4(beta), op=ALU.mult)
        self.E("pool", "tensor_tensor", [w.Ktok, sc], [w.kdec], out=w.kdec[:], in0=w.Ktok[:], in1=bc4(kds), op=ALU.mult)
        if not sample:
            Sf, Sb = self.Sf, self.Sb
            g = self.G()
            for h in range(4):
                self.mm(g[:, h * 128:(h + 1) * 128], w.KnT[:, h, :], Sb[:, h, :], True, True, [w.KnT, Sb], [g])
            self.E("dve", "tensor_tensor", [g, sc], [fa], out=v4(fa[:]), in0=v4(g[:, :]), in1=bc4(negbg), op=ALU.mult)
            self.E("pool", "tensor_tensor", [fa, w.Vtok], [w.W], out=w.W[:], in0=v4(fa[:]), in1=w.Vtok[:], op=ALU.add)
            g = self.G()
            for h in range(4):
                self.mm(g[:, h * 128:(h + 1) * 128], MTb[:, h, :], w.W[:, h, :], True, True, [MTb, w.W], [g])
            self.act(w.vnew[:], v4(g[:, :]), AF.Copy, [g], [w.vnew])
            if own:
                po = self.po[1]
                for h in range(4):
                    self.mm(po[:, h * 128:(h + 1) * 128], w.QdT[:, h, :], Sb[:, h, :], True, False, [w.QdT, Sb], [po])
                    self.mm(po[:, h * 128:(h + 1) * 128], w.intraT[:, h, :], w.vnew[:, h, :], False, True, [w.intraT, w.vnew], [po])
            g = self.G()
            for h in range(4):
                self.mm(g[:, h * 128:(h + 1) * 128], w.kdec[:, h, :], w.vnew[:, h, :], True, True, [w.kdec, w.vnew], [g])
            self.E("dve", "tensor_tensor", [Sf, fd], [Sf], out=Sf[:], in0=Sf[:], in1=v4(fd[:])[:, :, 127:128].to_broadcast([128, 4, 128]), op=ALU.mult)
            self.E("dve", "tensor_tensor", [g, Sf], [Sf], out=Sf[:], in0=v4(g[:, :]), in1=Sf[:], op=ALU.add)
            self.act(Sb[:], Sf[:], AF.Copy, [Sf], [Sb])
            if last_prompt:
                self.dma(o["sp_state"].rearrange("h k v -> k h v"), Sf[:], reads=[Sf])
        else:
            colmask, rowmask = w.colmask, w.rowmask
            cm3 = colmask.rearrange("p (s t) -> p s t", s=NSEQ)
            s0v = i["s0"]
            po = self.po[1]
            for h in range(4):
                Sfh, Sbh = w.Sfh[0], w.Sbh[0]
                self.dma(Sfh[:], s0v[:, h, :, :].rearrange("s k v -> k s v"), writes=[Sfh])
                self.E("pool", "tensor_copy", [Sfh], [Sbh], out=Sbh[:], in_=Sfh[:])
                Km, Qm, kdm = w.Km, w.Qm, w.kdm
                self.E("pool", "tensor_tensor", [w.KnT, w.tabt], [Km], out=Km[:], in0=w.KnT[:, h, :].unsqueeze(1).to_broadcast([128, NSEQ, 128]), in1=cm3, op=ALU.mult)
                g = self.G()
                for s_ in range(NSEQ):
                    self.mm(g[:, 0:128], Km[:, s_, :], Sbh[:, s_, :], s_ == 0, s_ == NSEQ - 1, [Km, Sbh], [g])
                self.E("dve", "tensor_scalar", [g, sc], [fa], out=fa[:, 0:128], in0=g[:, 0:128], scalar1=sc[:, 20 + h:21 + h], scalar2=None, op0=ALU.mult)
                self.E("pool", "tensor_tensor", [fa, w.Vtok], [w.W], out=w.W[:, h, :], in0=fa[:, 0:128], in1=w.Vtok[:, h, :], op=ALU.add)
                g = self.G()
                self.mm(g[:, 0:128], MTb[:, h, :], w.W[:, h, :], True, True, [MTb, w.W], [g])
                self.act(w.vnew[:, h, :], g[:, 0:128], AF.Copy, [g], [w.vnew])
                self.E("dve", "tensor_tensor", [w.QdT, w.tabt], [Qm], out=Qm[:], in0=w.QdT[:, h, :].unsqueeze(1).to_broadcast([128, NSEQ, 128]), in1=cm3, op=ALU.mult)
                for s_ in range(NSEQ):
                    self.mm(po[:, h * 128:(h + 1) * 128], Qm[:, s_, :], Sbh[:, s_, :], s_ == 0, False, [Qm, Sbh], [po])
                self.mm(po[:, h * 128:(h + 1) * 128], w.intraT[:, h, :], w.vnew[:, h, :], False, True, [w.intraT, w.vnew], [po])
                Sn = w.Sn[0]
                self.E("pool", "tensor_tensor", [w.kdec, w.tabt], [kdm], out=kdm[:], in0=w.kdec[:, h, :].unsqueeze(1).to_broadcast([128, NSEQ, 128]),
                       in1=rowmask[:, 0:NSEQ].unsqueeze(2).to_broadcast([128, NSEQ, 128]), op=ALU.mult)
                for q4 in range(4):
                    g = self.G()
                    for k in range(4):
                        s_ = q4 * 4 + k
                        self.mm(g[:, k * 128:(k + 1) * 128], kdm[:, s_, :], w.vnew[:, h, :], True, True, [kdm, w.vnew], [g])
                    for k in range(4):
                        s_ = q4 * 4 + k
                        self.E("dve" if k % 2 == 0 else "pool" if False else "dve", "scalar_tensor_tensor", [g, Sfh, fd], [Sn], out=Sn[:, s_, :], in0=Sfh[:, s_, :],
                               scalar=fd[:, h * 128 + s_ * TS + TS - 1: h * 128 + s_ * TS + TS], in1=g[:, k * 128:(k + 1) * 128], op0=ALU.mult, op1=ALU.add)
                self.dma(o["ss_state"][:, h, :, :].rearrange("s k v -> k s v"), Sn[:], reads=[Sn])
        if own:
            po = self.po[1]
            self.act(fa[:], po[:, :], AF.Square, [po], [fa])
            self.E("dve", "tensor_reduce", [fa], [w.ss8], out=w.ss8[:, 0:4], in_=v4(fa[:]), axis=AX.X, op=ALU.add)
            self.rsqrt_ops(w.ss8, w.rs8, 4, 1.0 / DND)
            self.E("dve", "tensor_tensor", [po, w.rs8], [fa], out=v4(fa[:]), in0=v4(po[:, :]), in1=bc4(w.rs8[:, 0:4]), op=ALU.mult)
            self.E("pool", "tensor_tensor", [fa, self.sv], [fa], out=v4(fa[:]), in0=v4(fa[:]), in1=hb4(self.gdn), op=ALU.mult)
            zs = w.zs
            self.act(fb[:], zs[:], AF.Exp, [zs], [fb], scale=-1.0)
            self.act(fb[:], fb[:], AF.Ln, [fb], [fb], bias=1.0)
            self.act(fb[:], fb[:], AF.Exp, [fb], [fb], scale=-1.0)
            self.E("dve", "tensor_tensor", [fb, zs], [fb], out=fb[:], in0=fb[:], in1=zs[:], op=ALU.mult)
            self.E("dve", "tensor_tensor", [fa, fb], [w.mixed], out=w.mixed[:, 512:1024], in0=fa[:], in1=fb[:], op=ALU.mult)

    def attn_step(self, w, S, nh, nq, qT, kT, vv, nk, kvl, kvl_t, brow_ap, maskneg, mask01, mask_t, O, o_cols, first, last,
                  qreads, kreads, vreads, att_out=None, att_lhs=None, first_o=None, last_o=None):
        W_ = nh * nq
        et, spt, att, Rb = S.et, S.spt, S.att, S.Rb
        Z = self.G()
        for h in range(nh):
            self.mm(Z[0:nk, h * nq:(h + 1) * nq], kT[h], qT[h], h == 0, False, qreads + kreads, [Z], skip=True)
        self.mm(Z[0:nk, 0:W_], kvl, brow_ap, False, True, [kvl_t, self.brow], [Z], skip=True)
        yield
        self.act(et[0:nk, 0:W_], Z[0:nk, 0:W_], AF.Exp, [Z], [et])
        self.act(spt[0:nk, 0:W_], et[0:nk, 0:W_], AF.Ln, [et], [spt], bias=1.0)
        if mask01 is not None:
            self.E("dve", "tensor_tensor", [spt, mask_t], [spt], out=spt[0:nk, 0:W_], in0=spt[0:nk, 0:W_], in1=mask01, op=ALU.mult)
        yield
        U = Z
        fin = first and maskneg is None
        self.mm(U[0:nk, 0:W_], self.ntrib[0:nk, 0:nk], spt[0:nk, 0:W_], False, fin, [spt, self.cbf], [U], skip=True)
        if not first:
            self.mm(U[0:nk, 0:W_], self.negonesb[:, 0:nk], Rb[:, 0:W_], False, maskneg is None, [Rb, self.cbf], [U], skip=True)
        if maskneg is not None:
            self.mm(U[0:nk, 0:W_], self.identb[0:nk, 0:nk], maskneg, False, True, [mask_t, self.cbf], [U], skip=True)
        yield
        if att_out is None:
            self.act(att[0:nk, 0:W_], U[0:nk, 0:W_], AF.Exp, [U], [att])
        else:
            self.act(att_out[0], U[0:nk, 0:W_].rearrange("p (h q) -> p h q", h=nh), AF.Exp, [U], [att_out[1]])
        if not last:
            if first:
                self.E("dve", "tensor_copy", [spt], [Rb], out=Rb[0:nk, 0:W_], in_=spt[0:nk, 0:W_])
            else:
                self.E("dve", "tensor_tensor", [spt, Rb], [Rb], out=Rb[0:nk, 0:W_], in0=Rb[0:nk, 0:W_], in1=spt[0:nk, 0:W_], op=ALU.add)
        yield
        fo = first if first_o is None else first_o
        lo = last if last_o is None else last_o
        for h in range(nh):
            if att_lhs is None:
                lhs = att[0:nk, h * nq:(h + 1) * nq]
                rd = [att]
            else:
                lhs = att_lhs[0][h]
                rd = [att_lhs[1]]
            self.mm(O[o_cols[h]], lhs, vv[h], fo and h == 0, lo, rd + vreads, [O], skip=True)
        yield

    @staticmethod
    def interleave(gens):
        gens = list(gens)
        while gens:
            for g in list(gens):
                try:
                    next(g)
                except StopIteration:
                    gens.remove(g)

    def attn_prompt(self, w, b):
        c = self.cfg

        def stream(hg):
            O = self.po[hg]
            S = w.streams[hg]
            for kb in range(b, -1, -1):
                qT = [w.QT[:, hg * 2 + h // 2, h % 2, :] for h in range(4)]
                kT = [self.KTt[kb][:, hg * 2 + h // 2, :] for h in range(4)]
                vv = [self.Vt[kb][:, (hg * 4 + h) * 64:(hg * 4 + h + 1) * 64] for h in range(4)]
                diag = kb == b
                kvl = self.kvd if kb < c.OUT0 else self.kvone
                yield from self.attn_step(w, S, 4, 128, qT, kT, vv, 128, kvl[:, :], kvl,
                                          self.brow[:, hg * 512:(hg + 1) * 512],
                                          w.causrep[:, 0:512] if diag else None, w.caus01rep[:, 0:512] if diag else None, w.causrep_t,
                                          O, [(slice(None), slice(h * 64, (h + 1) * 64)) for h in range(4)],
                                          kb == b, kb == 0, [w.QT], [self.KTt[kb]], [self.Vt[kb]])
            self.head_norm(w, O[:, 0:256], 4, HD, w.f512, [O], self.gso, w.mixed, w.mixed[:, hg * 256:(hg + 1) * 256], 1.0 / HD)
        self.interleave([stream(0), stream(1)])

    def attn_sample(self, w):
        c = self.cfg
        i = self.i
        npg = c.NPG
        O = self.po[0]
        ck = i["cache_k"]
        cv = i["cache_v"]

        def stream(si):
            S = w.streams[si]
            for s_ in range(si, NSEQ, 2):
                attpad = w.attpad[si]
                self.E("pool", "memset", [], [attpad], attpad[:], 0.0)
                qT = [w.QT[:, h // 2, h % 2, s_ * TS:(s_ + 1) * TS] for h in range(8)]
                att_out = (attpad[:, :, s_ * TS:(s_ + 1) * TS], attpad)
                att_lhs = ([attpad[:, h, :] for h in range(8)], attpad)
                o_cols = [(slice(None), slice(h * 64, (h + 1) * 64)) for h in range(8)]
                for blk in range(npg, -1, -1):
                    if blk == npg:
                        kT = [w.KTs[:, h // 2, :] for h in range(8)]
                        vv = [w.Vs[:, h * 64:(h + 1) * 64] for h in range(8)]
                        kreads, vreads = [w.KTs], [w.Vs]
                        mneg, m01 = w.smneg[:, s_, :], w.sm01[:, s_, :]
                    else:
                        j = s_ * npg + blk
                        kk = si * 2 + (self.pgi[si] % 2)
                        self.pgi[si] += 1
                        kpf, vpf, kpb, vpb, ktp = w.kpf[kk], w.vpf[kk], w.kpb[kk], w.vpb[kk], w.ktp[kk]
                        self.s.add("pool", lambda e, kpf=kpf, j=j: e.indirect_dma_start(
                            out=kpf[:], out_offset=None, in_=ck, in_offset=bass.IndirectOffsetOnAxis(ap=w.idx[:, j:j + 1], axis=0)),
                            [w.idx], [kpf], is_dma=True)
                        self.s.add("pool", lambda e, vpf=vpf, j=j: e.indirect_dma_start(
                            out=vpf[:], out_offset=None, in_=cv, in_offset=bass.IndirectOffsetOnAxis(ap=w.idx[:, j:j + 1], axis=0)),
                            [w.idx], [vpf], is_dma=True)
                        self.E("dve", "tensor_copy", [kpf], [kpb], out=kpb[:], in_=kpf[:])
                        self.act(vpb[:], vpf[:], AF.Copy, [vpf], [vpb])
                        tb = self.TB()
                        for pr in range(4):
                            self.tr(tb[:, pr * 128:(pr + 1) * 128], kpb[:, pr * 128:(pr + 1) * 128], self.identb, [kpb, self.cbf], [tb])
                        self.E("dve", "tensor_copy", [tb], [ktp], out=ktp[:], in_=tb[:, 0:512].rearrange("p (a b) -> p a b", a=4))
                        kT = [ktp[:, h // 2, :] for h in range(8)]
                        vv = [vpb[:, h * 64:(h + 1) * 64] for h in range(8)]
                        kreads, vreads = [ktp], [vpb]
                        mneg, m01 = None, None
                    yield from self.attn_step(w, S, 8, TS, qT, kT, vv, 128, self.kvone[:, :], self.kvone, self.brow[:, 1024:1088],
                                              mneg, m01, w.smt, O, o_cols, blk == npg, blk == 0, [w.QT], kreads, vreads,
                                              att_out=att_out, att_lhs=att_lhs,
                                              first_o=(s_ == 0 and blk == npg), last_o=(s_ == NSEQ - 1 and blk == 0))
        self.pgi = [0, 0]
        self.interleave([stream(0), stream(1)])
        self.head_norm(w, O[:, :], HS, HD, w.f512, [O], self.gso, w.mixed, w.mixed[:, 0:512], 1.0 / HD)

    def alloc_work(self, st, sample):
        class WS:
            pass
        w = WS()
        sb = lambda name, shape, dt: self.sb(st, name, shape, dt)
        w.xt = [sb("xt", [128, D], F32)]
        w.h = sb("h", [128, D], BF16)
        w.hT = sb("hT", [128, DC, 128], BF16)
        w.ssq = sb("ssq", [128, 1], F32)
        w.rstd = sb("rstd", [128, 1], F32)
        w.ss8 = sb("ss8", [128, 8], F32)
        w.rs8 = sb("rs8", [128, 8], F32)
        w.f512 = sb("f512", [128, 512], F32)
        w.kn = sb("kn", [128, 512], F32)
        w.knb = sb("knb", [128, 512], BF16)
        w.vf = w.f512
        w.qnb = sb("qnb", [128, 512], BF16)
        w.QT = sb("QT", [128, 4, 2, 128], BF16)
        self.E("pool", "memset", [], [w.QT], w.QT[:], 0.0)
        ns, T = (NSEQ, TS) if sample else (1, 128)
        w.XE4 = sb("XE4", [128, 4, ns * (3 + T)], F32)
        w.Hst = sb("Hst", [128, 12, ns * 3], F32)
        w.ba = sb("ba", [128, 8], F32)
        w.zs = sb("zs", [128, 512], F32)
        w.Y4 = sb("Y4", [128, 4, 128], F32)
        w.E4 = sb("E4", [128, 4, 128], F32)
        w.QnT = sb("QnT", [128, 4, 128], BF16)
        w.KnT = sb("KnT", [128, 4, 128], BF16)
        w.Ktok = sb("Ktok", [128, 4, 128], BF16)
        w.Vcb = sb("Vcb", [128, 4, 128], BF16)
        w.Vtok = sb("Vtok", [128, 4, 128], F32)
        w.sc = sb("sc", [128, 40], F32)
        w.dg = [sb("dg%d" % k, [128, 128], F32) for k in range(2)]
        w.fa = sb("fa", [128, 512], F32)
        w.fb = sb("fb", [128, 512], F32)
        w.fc = sb("fc", [128, 512], F32)
        w.fd = sb("fd", [128, 512], F32)
        w.Pf = [sb("Pf%d" % k, [128, 2, 128], F32) for k in range(2)]
        w.PTf = [sb("PTf%d" % k, [128, 2, 128], F32) for k in range(2)]
        w.MT = sb("MT", [128, 2, 128], F32)
        w.MTb = sb("MTb", [128, 4, 128], BF16)
        w.intraT = sb("intraT", [128, 4, 128], BF16)
        w.QdT = sb("QdT", [128, 4, 128], BF16)
        w.kdec = sb("kdec", [128, 4, 128], BF16)
        w.W = sb("W", [128, 4, 128], BF16)
        w.vnew = sb("vnew", [128, 4, 128], BF16)
        w.mixed = sb("mixed", [128, D], BF16)
        wd = 64 if sample else 512
        class ST:
            pass
        w.streams = []
        for k in range(2):
            S = ST()
            S.et = sb("et%d" % k, [128, wd], F32)
            S.spt = sb("spt%d" % k, [128, wd], BF16)
            S.att = sb("att%d" % k, [128, wd], BF16)
            S.Rb = sb("Rb%d" % k, [128, wd], BF16)
            w.streams.append(S)
        return w

    def phase1(self):
        c = self.cfg
        i, o = self.i, self.o
        self.xi = 0
        self.ai = 0
        self.pgi = 0
        with contextlib.ExitStack() as p1:
            winb = self.sb(p1, "winb", [128, DC, INC], BF16)
            self.winb = winb
            scale1 = self.sb(p1, "scale1", [128, D], F32)
            shift1 = self.sb(p1, "shift1", [128, D], F32)
            sv = self.sv
            WB = 512 * 2 + 64
            brow = self.sb(p1, "brow", [128, WB], BF16)
            nb = self.sb(p1, "nb", [128, 1], F32)
            with contextlib.ExitStack() as st:
                bexp = self.sb(st, "bexp", [128, WB], F32)
                for hg in range(2):
                    for h in range(4):
                        self.E("dve", "tensor_copy", [sv], [bexp], out=bexp[:, hg * 512 + h * 128: hg * 512 + (h + 1) * 128],
                               in_=sv[:, 328 + hg * 4 + h: 329 + hg * 4 + h].to_broadcast([128, 128]))
                for h in range(8):
                    self.E("dve", "tensor_copy", [sv], [bexp], out=bexp[:, 1024 + h * 8: 1024 + (h + 1) * 8],
                           in_=sv[:, 328 + h: 329 + h].to_broadcast([128, 8]))
                bhi = self.sb(st, "bhi", [128, WB], BF16)
                self.brow = brow
                idf = self.cm("ident")
                self.E("dve", "tensor_copy", [bexp], [bhi], out=bhi[:], in_=bexp[:])
                self.E("dve", "tensor_tensor", [bexp, bhi], [bexp], out=bexp[:], in0=bexp[:], in1=bhi[:], op=ALU.subtract)
                self.E("dve", "tensor_scalar", [bexp, self.ctm], [bexp], out=bexp[:], in0=bexp[:], scalar1=idf[:, 1:2], scalar2=None, op0=ALU.mult)
                self.E("dve", "scalar_tensor_tensor", [bhi, bexp, self.ctm], [bexp], out=bexp[:], in0=bhi[:], scalar=idf[:, 0:1], in1=bexp[:],
                       op0=ALU.mult, op1=ALU.add)
                self.E("dve", "tensor_scalar", [self.ctm], [nb], out=nb[:], in0=idf[:, 2:3], scalar1=-BIG, scalar2=None, op0=ALU.mult)
                self.E("dve", "tensor_scalar", [bexp, nb], [brow], out=brow[:], in0=bexp[:], scalar1=nb[:, 0:1], scalar2=None, op0=ALU.add)
                stg = [self.sb(st, "stg%d" % k, [128, DC * 512], F32) for k in range(2)]
                wv = i["w_in"].rearrange("(c p) n -> p c n", p=128)
                for ct in range(8):
                    n0, n1 = ct * 512, min(INC, (ct + 1) * 512)
                    self.stream_cast(stg, wv[:, :, n0:n1], winb, winb[:, :, n0:n1], eng="act" if ct % 2 else "dve")
            self.fence()
            with contextlib.ExitStack() as st:
                KT = self.sb(st, "KT", [128, 4, c.NBLK * 128], BF16)
                Vr = self.sb(st, "Vr", [128, c.NBLK, 512], BF16)
                self.KTt = [Tile(KT[:, :, b * 128:(b + 1) * 128], "KT%d" % b) for b in range(c.NBLK)]
                self.Vt = [Tile(Vr[:, b, :], "V%d" % b) for b in range(c.NBLK)]
                w = self.alloc_work(st, False)
                w.scale1, w.shift1 = scale1, shift1
                w.tabt = self.ctm
                w.causrep = self.sb(st, "causrep", [128, 512], BF16)
                w.caus01rep = self.sb(st, "caus01rep", [128, 512], BF16)
                w.causrep_t = self.sb(st, "causrep_t", [1, 1], F32)
                for h in range(4):
                    self.E("dve", "tensor_copy", [self.ctm], [w.causrep_t, w.causrep], out=w.causrep[:, h * 128:(h + 1) * 128], in_=self.cm("causneg"))
                    self.E("dve", "tensor_copy", [self.ctm], [w.causrep_t, w.caus01rep], out=w.caus01rep[:, h * 128:(h + 1) * 128], in_=self.cm("caus01"))
                self.Sf = self.sb(st, "Sf", [128, 4, 128], F32)
                self.Sb = self.sb(st, "Sb", [128, 4, 128], BF16)
                self.E("pool", "memset", [], [self.Sf], self.Sf[:], 0.0)
                self.E("pool", "memset", [], [self.Sb], self.Sb[:], 0.0)
                self.load_mod([shift1, scale1], [0, 1], False)
                tabs = (self.cm("ltincl_p"), self.cm("ones"), self.cm("ms01_p"))
                for b in range(c.NBLK):
                    own = b >= c.OWN0
                    outrow = None
                    if b >= c.OUT0:
                        r0 = (b - c.OUT0) * 128
                        outrow = (o["kp"][r0:r0 + 128, :], o["vp"][r0:r0 + 128, :])
                    self.front_end(w, b, False, own, outrow)
                    if self.on("dn"):
                        self.dn_chunk(w, b, False, own, tabs, b == c.NBLK - 1)
                    if own:
                        if self.on("attn"):
                            self.attn_prompt(w, b)
                        k = b - c.OWN0
                        if self.on("dn") and self.on("attn"):
                            self.dma(self.mixd[k * 128:(k + 1) * 128, :], w.mixed[:], reads=[w.mixed], writes=[self.t_mixd[k]])
                        if "mixed_p" in self.o and b >= c.OUT0:
                            r0 = (b - c.OUT0) * 128
                            self.E("dve", "tensor_copy", [w.mixed], [w.xt[0]], out=w.xt[0][:], in_=w.mixed[:])
                            self.dma(self.o["mixed_p"][r0:r0 + 128, :], w.xt[0][:], reads=[w.xt[0]])
            self.fence()
            if self.on("sample"):
                with contextlib.ExitStack() as st:
                    w = self.alloc_work(st, True)
                    w.scale1, w.shift1 = scale1, shift1
                    cts = self.sb(st, "cts", list(CT_SAMP.shape), F32)
                    self.dma(cts[:], i["ct_samp"][:, :], writes=[cts])
                    w.tabt = cts

                    def cs(name):
                        o_, w_ = CO_SAMP[name]
                        return cts[:, o_:o_ + w_]
                    w.colmask, w.rowmask = cs("colmask"), cs("rowmask")
                    w.KTs = self.sb(st, "KTs", [128, 4, 128], BF16)
                    w.Vs = self.sb(st, "Vs", [128, 512], BF16)
                    w.Sfh = [self.sb(st, "Sfh", [128, NSEQ, 128], F32)]
                    w.Sbh = [self.sb(st, "Sbh", [128, NSEQ, 128], BF16)]
                    w.Sn = w.Sfh
                    w.Km = self.sb(st, "Km", [128, NSEQ, 128], BF16)
                    w.Qm = w.Km
                    w.kdm = w.Km
                    self.load_mod([shift1, scale1], [0, 1], True)
                    hst_t = w.Sfh[0]
                    hst = hst_t[:, :, :].rearrange("p s v -> p (s v)")[0:NSEQ * 3, 0:3 * DNW]
                    self.dma(hst, i["dnc0"][:, :], writes=[hst_t])
                    for j in range(3):
                        g = self.G()
                        for ch in range(4):
                            self.tr(g[:, ch * 48:(ch + 1) * 48], hst[:, (j * 4 + ch) * 128:(j * 4 + ch + 1) * 128], self.cm("ident")[0:48, 0:48], [hst_t, self.ctm], [g])
                        self.act(w.Hst[:, j * 4:(j + 1) * 4, :], g[:, 0:192].rearrange("p (c t) -> p c t", c=4), AF.Copy, [g], [w.Hst])
                    self.front_end(w, 0, True, True, (o["ksm"][:, :], o["vsm"][:, :]))
                    tabs = (cs("ltincl_s"), cs("seqm_s"), cs("ms01_s"))
                    if self.on("dn"):
                        self.dn_chunk(w, 0, True, True, tabs, False)
                    if self.on("attn"):
                        pti = self.sb(st, "pti", [128, NSEQ * c.NPG], I32)
                        ptf = self.sb(st, "ptf", [128, NSEQ * c.NPG], F32)
                        io = self.sb(st, "io", [128, 1], I32)
                        iof = self.sb(st, "iof", [128, 1], F32)
                        w.idx = self.sb(st, "idx", [128, NSEQ * c.NPG], I32)
                        self.dma(pti[:], i["ptab"].partition_broadcast(128), writes=[pti])
                        self.E("pool", "iota", [], [io], io[:], pattern=[[0, 1]], base=0, channel_multiplier=1)
                        self.E("dve", "tensor_copy", [io], [iof], out=iof[:], in_=io[:])
                        self.E("dve", "tensor_copy", [pti], [ptf], out=ptf[:], in_=pti[:])
                        self.E("dve", "tensor_scalar", [ptf, iof], [ptf], out=ptf[:], in0=ptf[:], scalar1=128.0, scalar2=iof[:, 0:1], op0=ALU.mult, op1=ALU.add)
                        self.E("dve", "tensor_copy", [ptf], [w.idx], out=w.idx[:], in_=ptf[:])
                        w.smt = self.sb(st, "smt", [1, 1], F32)
                        sm01 = self.sb(st, "sm01", [128, NSEQ, 64], BF16)
                        smneg = self.sb(st, "smneg", [128, NSEQ, 64], BF16)
                        o_, w_ = CO_SAMP["smask01"]
                        src = cts[:, o_:o_ + w_].rearrange("p (s q) -> p s q", s=NSEQ)
                        self.E("dve", "tensor_copy", [cts], [w.smt, sm01], out=sm01[:], in_=src)
                        self.E("dve", "tensor_scalar", [cts], [w.smt, smneg], out=smneg[:], in0=src, scalar1=-1.0, scalar2=BIG, op0=ALU.add, op1=ALU.mult)
                        w.sm01, w.smneg = sm01, smneg
                        w.attpad = [self.sb(st, "attpad%d" % k, [128, 8, 128], BF16) for k in range(2)]
                        w.kpf = [self.sb(st, "kpf%d" % k, [128, 512], F32) for k in range(4)]
                        w.vpf = [self.sb(st, "vpf%d" % k, [128, 512], F32) for k in range(4)]
                        w.kpb = [self.sb(st, "kpb%d" % k, [128, 512], BF16) for k in range(4)]
                        w.vpb = [self.sb(st, "vpb%d" % k, [128, 512], BF16) for k in range(4)]
                        w.ktp = [self.sb(st, "ktp%d" % k, [128, 4, 128], BF16) for k in range(4)]
                        self.attn_sample(w)
                    k = c.NOWN
                    if self.on("dn") and self.on("attn"):
                        self.dma(self.mixd[k * 128:(k + 1) * 128, :], w.mixed[:], reads=[w.mixed], writes=[self.t_mixd[k]])
                    if "mixed_s" in self.o:
                        self.E("dve", "tensor_copy", [w.mixed], [w.xt[0]], out=w.xt[0][:], in_=w.mixed[:])
                        self.dma(self.o["mixed_s"][:, :], w.xt[0][:], reads=[w.xt[0]])

    def phase2(self):
        c = self.cfg
        i, o = self.i, self.o
        with contextlib.ExitStack() as p2:
            sb = lambda name, shape, dt: self.sb(p2, name, shape, dt)
            woutb = sb("woutb", [128, DC, D], BF16)
            wupb = sb("wupb", [128, DC, 2 * DFF], BF16)
            wdnb = sb("wdnb", [128, FC, D], BF16)
            with contextlib.ExitStack() as st:
                stg = [self.sb(st, "stg%d" % k, [128, DC * 512], F32) for k in range(2)]
                n = 0
                wv = i["w_out"].rearrange("(c p) n -> p c n", p=128)
                for ct in range(2):
                    self.stream_cast(stg, wv[:, :, ct * 512:(ct + 1) * 512], woutb, woutb[:, :, ct * 512:(ct + 1) * 512], eng="act" if n % 2 else "dve")
                    n += 1
                wv = i["w_up"].rearrange("(c p) n -> p c n", p=128)
                for ct in range(11):
                    self.stream_cast(stg, wv[:, :, ct * 512:(ct + 1) * 512], wupb, wupb[:, :, ct * 512:(ct + 1) * 512], eng="act" if n % 2 else "dve")
                    n += 1
                wv = i["w_down"].rearrange("(c p) n -> p c n", p=128)
                for c0 in range(0, FC, 4):
                    c1 = min(FC, c0 + 4)
                    self.stream_cast(stg, wv[:, c0:c1, :], wdnb, wdnb[:, c0:c1, :], eng="act" if n % 2 else "dve")
                    n += 1
            self.fence()
            gt1, scale2, shift2, gt2 = [sb(nm, [128, D], F32) for nm in ("gt1", "scale2", "shift2", "gt2")]

            class WS:
                pass
            w = WS()
            w.xt = [sb("xt2", [128, D], F32)]
            w.h = sb("h2", [128, D], BF16)
            w.hT = sb("h2T", [128, DC, 128], BF16)
            w.ssq = sb("ssq2", [128, 1], F32)
            w.rstd = sb("rstd2", [128, 1], F32)
            mixb = sb("mixb", [128, D], BF16)
            mT = sb("mT", [128, DC, 128], BF16)
            yt = sb("yt", [128, D], F32)
            UE = sb("UE", [128, 4, NSEQ * (2 + TS)], F32)
            C4 = sb("C4", [128, 4, 128], F32)
            E2 = sb("E2", [128, 2, 128], F32)
            actT = sb("actT", [128, FC, 128], BF16)
            FH = sb("FH", [128, 44, NSEQ * 2], F32)
            fso = sb("fso", [NSEQ * 2, 512], F32)
            self.E("pool", "memset", [], [FH], FH[:], 0.0)
            blocks = [(b, False) for b in range(c.OWN0, c.NBLK)] + ([(0, True)] if self.on("sample") else [])
            cur_mod = None
            for (b, sample) in blocks:
                if cur_mod != sample:
                    self.load_mod([gt1, scale2, shift2, gt2], [2, 4, 3, 5], sample)
                    cur_mod = sample
                ns, T = (NSEQ, TS) if sample else (1, 128)
                k = c.NOWN if sample else b - c.OWN0
                halo = (not sample) and b == c.OWN0
                xt = w.xt[0]
                self.dma(xt[:], i["xs"][:, :] if sample else i["xp"][b * 128:(b + 1) * 128, :], writes=[xt])
                self.dma(mixb[:], self.mixd[k * 128:(k + 1) * 128, :], reads=[self.t_mixd[k]], writes=[mixb])
                tb = self.TB()
                for dc in range(DC):
                    self.tr(tb[:, dc * 128:(dc + 1) * 128], mixb[:, dc * 128:(dc + 1) * 128], self.identb, [mixb, self.cbf], [tb])
                self.act(mT[:], tb[:, :].rearrange("p (a b) -> p a b", a=DC), AF.Copy, [tb], [mT])
                for n in range(2):
                    g = self.G()
                    for dc in range(DC):
                        self.mm(g[:, :], mT[:, dc, :], woutb[:, dc, n * 512:(n + 1) * 512], dc == 0, dc == DC - 1, [mT, woutb], [g])
                    self.E("dve", "tensor_tensor", [g, gt1], [yt], out=yt[:, n * 512:(n + 1) * 512], in0=g[:, :], in1=gt1[:, n * 512:(n + 1) * 512], op=ALU.mult)
                self.E("pool", "tensor_tensor", [yt, xt], [xt], out=xt[:], in0=yt[:], in1=xt[:], op=ALU.add)
                x1 = xt
                if "x1_p" in o and (not sample) and b >= c.OUT0:
                    r0 = (b - c.OUT0) * 128
                    self.dma(o["x1_p"][r0:r0 + 128, :], x1[:], reads=[x1])
                self.norm_mod(w, x1, scale2, shift2, w.hT, tmp=yt)
                if sample:
                    for j in range(11):
                        hst = fso
                        self.dma(hst[:, :], i["ffc0"][:, j * 512:(j + 1) * 512], writes=[hst])
                        g = self.G()
                        for ch in range(4):
                            self.tr(g[:, ch * 32:(ch + 1) * 32], hst[:, ch * 128:(ch + 1) * 128], self.cm("ident")[0:32, 0:32], [hst, self.ctm], [g])
                        self.act(FH[:, j * 4:(j + 1) * 4, :], g[:, 0:128].rearrange("p (c t) -> p c t", c=4), AF.Copy, [g], [FH])
                ue4 = UE[:, :, 0:ns * (2 + T)].rearrange("p c (s t) -> p c s t", s=ns)
                fh4 = FH[:, :, 0:ns * 2].rearrange("p c (s t) -> p c s t", s=ns)
                for gi_ in range(11):
                    chs = [2 * gi_, 2 * gi_ + 1, FC + 2 * gi_, FC + 2 * gi_ + 1]
                    g = self.G()
                    for q_, ch in enumerate(chs):
                        for dc in range(DC):
                            self.mm(g[:, q_ * 128:(q_ + 1) * 128], wupb[:, dc, ch * 128:(ch + 1) * 128], w.hT[:, dc, :], dc == 0, dc == DC - 1, [w.hT, wupb], [g])
                    for half in range(2):
                        self.E("pool", "tensor_copy", [FH], [UE], out=ue4[:, half * 2:half * 2 + 2, :, 0:2], in_=fh4[:, chs[half * 2]:chs[half * 2] + 2, :, :])
                    src4 = g[:, :].rearrange("p (c s t) -> p c s t", c=4, s=ns)
                    if halo:
                        self.act(ue4[:, :, :, 2:2 + T], src4, AF.Copy, [g, self.bvd], [UE], scale=self.bvd[:, b:b + 1])
                    else:
                        self.act(ue4[:, :, :, 2:2 + T], src4, AF.Copy, [g], [UE])
                    for half in range(2):
                        self.E("pool", "tensor_copy", [UE], [FH], out=fh4[:, chs[half * 2]:chs[half * 2] + 2, :, :], in_=ue4[:, half * 2:half * 2 + 2, :, T:T + 2])
                    if halo:
                        continue
                    for q_, ch in enumerate(chs):
                        eng = "dve"
                        yv = C4[:, q_, :].rearrange("p (s t) -> p s t", s=ns)
                        self.E(eng, "tensor_scalar", [UE, self.wfc], [C4], out=yv, in0=ue4[:, q_, :, 0:T], scalar1=self.wfc[:, 0, ch:ch + 1], scalar2=None, op0=ALU.mult)
                        for kk in range(1, 3):
                            self.E(eng, "scalar_tensor_tensor", [UE, self.wfc, C4], [C4], out=yv, in0=ue4[:, q_, :, kk:kk + T],
                                   scalar=self.wfc[:, kk, ch:ch + 1], in1=yv, op0=ALU.mult, op1=ALU.add)
                    self.act(E2[:], C4[:, 2:4, :], AF.Exp, [C4], [E2], scale=-1.0)
                    self.act(E2[:], E2[:], AF.Ln, [E2], [E2], bias=1.0)
                    self.act(E2[:], E2[:], AF.Exp, [E2], [E2], scale=-1.0)
                    self.E("pool", "tensor_tensor", [E2, C4], [E2], out=E2[:], in0=E2[:], in1=C4[:, 2:4, :], op=ALU.mult)
                    self.E("dve", "tensor_tensor", [E2, C4], [actT], out=actT[:, 2 * gi_:2 * gi_ + 2, :], in0=E2[:], in1=C4[:, 0:2, :], op=ALU.mult)
                last_p = (not sample) and b == c.NBLK - 1
                if sample or last_p:
                    ncol = ns * 2
                    dst = o["fcs"] if sample else o["fcp"]
                    for j in range(11):
                        g = self.G()
                        for ch in range(4):
                            self.tr(g[0:ncol, ch * 128:(ch + 1) * 128], FH[:, j * 4 + ch, 0:ncol], self.cm("ident"), [FH, self.ctm], [g])
                        self.E("dve", "tensor_copy", [g], [fso], out=fso[0:ncol, :], in_=g[0:ncol, :])
                        self.dma(dst[:, j * 512:(j + 1) * 512], fso[0:ncol, :], reads=[fso])
                if halo:
                    continue
                for n in range(2):
                    g = self.G()
                    for fc_ in range(FC):
                        self.mm(g[:, :], actT[:, fc_, :], wdnb[:, fc_, n * 512:(n + 1) * 512], fc_ == 0, fc_ == FC - 1, [actT, wdnb], [g])
                    self.E("dve", "tensor_tensor", [g, gt2], [yt], out=yt[:, n * 512:(n + 1) * 512], in0=g[:, :], in1=gt2[:, n * 512:(n + 1) * 512], op=ALU.mult)
                self.E("pool", "tensor_tensor", [yt, x1], [yt], out=yt[:], in0=yt[:], in1=x1[:], op=ALU.add)
                if sample:
                    self.dma(o["ys"][:, :], yt[:], reads=[yt])
                elif b >= c.OUT0:
                    r0 = (b - c.OUT0) * 128
                    self.dma(o["yp"][r0:r0 + 128, :], yt[:], reads=[yt])


def core_inputs(cfg, core, inp):
    b, half = core // 2, core % 2
    S = cfg.NBLK * 128
    xp_full = np.asarray(inp["x_prompt"][b], np.float32)
    if half == 1:
        xp = xp_full
    else:
        xp = np.concatenate([np.zeros((S // 2, D), np.float32), xp_full[:S // 2]], axis=0)
    s0, s1 = core * NSEQ, (core + 1) * NSEQ
    m = {}
    m["xp"] = np.ascontiguousarray(xp)
    m["xs"] = np.ascontiguousarray(np.asarray(inp["x_sample"][s0:s1], np.float32).reshape(NSEQ * TS, D))
    m["cvec"] = np.ascontiguousarray(np.concatenate([np.asarray(inp["c_prompt"][b:b + 1], np.float32),
                                                     np.asarray(inp["c_sample"][s0:s1], np.float32)], axis=0))
    m["cache_k"] = np.asarray(inp["cache_k"], np.float32).reshape(cfg.NPHYS * 128, SBW)
    m["cache_v"] = np.asarray(inp["cache_v"], np.float32).reshape(cfg.NPHYS * 128, SBW)
    m["ptab"] = np.ascontiguousarray(np.asarray(inp["page_table"][s0:s1], np.int32).reshape(-1))
    m["s0"] = np.ascontiguousarray(np.asarray(inp["state_delta"][0, s0:s1], np.float32))
    m["dnc0"] = np.ascontiguousarray(np.asarray(inp["state_dn_conv"][0, s0:s1], np.float32).reshape(NSEQ * 3, 3 * DNW))
    m["ffc0"] = np.ascontiguousarray(np.asarray(inp["state_ffn_conv"][0, s0:s1], np.float32).reshape(NSEQ * 2, 2 * DFF))
    for k, nm in (("w_ada", "w_ada"), ("b_ada", "b_ada"), ("g_attn", "g_attn_norm"), ("w_in", "w_in"), ("g_q", "g_q"),
                  ("g_k", "g_k"), ("sb_bias", "sb_bias"), ("g_sb_out", "g_sb_out"), ("w_dn_conv", "w_dn_conv"),
                  ("a_log", "a_log"), ("dt_bias", "dt_bias"), ("g_dn_out", "g_dn_out"), ("w_out", "w_out"),
                  ("g_ffn", "g_ffn_norm"), ("w_up", "w_up"), ("w_ffn_conv", "w_ffn_conv"), ("w_down", "w_down")):
        m[k] = np.ascontiguousarray(np.asarray(inp[nm], np.float32)[0])
    m["ct_main"] = CT_MAIN
    m["ct_samp"] = CT_SAMP
    kv = np.zeros((128, 256), np.float32)
    if half == 1:
        kv[0:2, 0:128] = 1.0
    else:
        kv[2, 0:128] = 1.0
    kv[0:2, 128:256] = 1.0
    m["kvlo"] = kv
    bv = np.ones((128, cfg.NBLK), np.float32)
    if half == 0:
        bv[:, :cfg.NBLK // 2] = 0.0
    m["blkvalid"] = bv
    return m


def assemble(cfg, res, nb, nsamp):
    S = cfg.NBLK * 128
    H = S // 2
    yp = np.zeros((nb, S, D), np.float32)
    ys = np.zeros((nsamp, TS, D), np.float32)
    kp = np.zeros((1, nb, S, HS, HD), np.float32)
    vp = np.zeros((1, nb, S, HS, HD), np.float32)
    ks = np.zeros((1, nsamp, TS, HS, HD), np.float32)
    vs = np.zeros((1, nsamp, TS, HS, HD), np.float32)
    sp = np.zeros((1, nb, DNH, DND, DND), np.float32)
    ss = np.zeros((1, nsamp, DNH, DND, DND), np.float32)
    dcp = np.zeros((1, nb, 3, 3 * DNW), np.float32)
    dcs = np.zeros((1, nsamp, 3, 3 * DNW), np.float32)
    fcp = np.zeros((1, nb, 2, 2 * DFF), np.float32)
    fcs = np.zeros((1, nsamp, 2, 2 * DFF), np.float32)
    for core, r in res.items():
        b, half = core // 2, core % 2
        s0, s1 = core * NSEQ, (core + 1) * NSEQ
        yp[b, half * H:(half + 1) * H] = r["yp"]
        kp[0, b, half * H:(half + 1) * H] = r["kp"].reshape(H, HS, HD)
        vp[0, b, half * H:(half + 1) * H] = r["vp"].reshape(H, HS, HD)
        ys[s0:s1] = r["ys"].reshape(NSEQ, TS, D)
        ks[0, s0:s1] = r["ksm"].reshape(NSEQ, TS, HS, HD)
        vs[0, s0:s1] = r["vsm"].reshape(NSEQ, TS, HS, HD)
        ss[0, s0:s1] = r["ss_state"]
        dcs[0, s0:s1] = r["dcs"].reshape(NSEQ, 3, 3 * DNW)
        fcs[0, s0:s1] = r["fcs"].reshape(NSEQ, 2, 2 * DFF)
        if half == 1:
            sp[0, b] = r["sp_state"]
            dcp[0, b] = r["dcp"]
            fcp[0, b] = r["fcp"]
    return (yp, ys, kp, vp, ks, vs, sp, ss, dcp, dcs, fcp, fcs)


_NC_CACHE = {}


def kernel(**inputs):
    cfg = Cfg(nblk=inputs["x_prompt"].shape[1] // 128, npg=inputs["page_table"].shape[1], nphys=inputs["cache_k"].shape[1])
    key = (cfg.NBLK, cfg.NPG, cfg.NPHYS)
    if key not in _NC_CACHE:
        _NC_CACHE[key] = Builder(cfg).build()
    nc = _NC_CACHE[key]
    ncores = 8
    in_maps = [core_inputs(cfg, c, inputs) for c in range(ncores)]
    res = run_bass_kernel_spmd(nc, in_maps, core_ids=list(range(ncores)))
    out = assemble(cfg, {c: res.results[c] for c in range(ncores)}, inputs["x_prompt"].shape[0], inputs["x_sample"].shape[0])
    return out
```

```python
import contextlib
import numpy as np
import concourse.bass as bass
import concourse.mybir as mybir
from concourse.bass_utils import run_bass_kernel_spmd

F32 = mybir.dt.float32
BF16 = mybir.dt.bfloat16
I32 = mybir.dt.int32
AF = mybir.ActivationFunctionType
ALU = mybir.AluOpType
AX = mybir.AxisListType

D = 1024
DC = 8
HS = 8
HD = 64
SBW = 512
DNH = 4
DND = 128
DNW = 512
DFF = 2816
FC = 22
INC = 3592
EPS = 1e-6
BIG = 30000.0
NSEQ = 16
TS = 8


class Cfg:
    def __init__(self, nblk=32, npg=16, nphys=2560):
        self.NBLK = nblk
        self.OWN0 = nblk // 2 - 1
        self.OUT0 = nblk // 2
        self.NPG = npg
        self.NPHYS = nphys
        self.NOUT = nblk - self.OUT0
        self.NOWN = nblk - self.OWN0


class Tile:
    __slots__ = ("ap", "name", "last_w", "readers", "excl")

    def __init__(self, ap, name="", excl=False):
        self.ap = ap
        self.name = name
        self.last_w = None
        self.readers = []
        self.excl = excl

    def __getitem__(self, k):
        return self.ap[k]


class Op:
    __slots__ = ("eng", "fn", "deps", "need_inc", "count", "sem", "is_dma", "idx")

    def __init__(self, eng, fn, is_dma=False):
        self.eng = eng
        self.fn = fn
        self.deps = set()
        self.need_inc = is_dma
        self.count = 0
        self.sem = None
        self.is_dma = is_dma


COMPUTE = ("pe", "act", "dve", "pool")


class Sched:
    def __init__(self, nc, n_dma_sems=16):
        self.nc = nc
        self.ops = {e: [] for e in COMPUTE + ("sp",)}
        self.n_dma_sems = n_dma_sems
        self.nops = 0
        self.junk_fn = None
        self.junk_n = 0
        self.n_pe_waits = 0

    def _track(self, op, reads, writes):
        ex = [t for t in reads if t.excl]
        if ex:
            reads = [t for t in reads if not t.excl]
            writes = list(writes) + [t for t in ex if t not in writes]
        for t in reads:
            if t.last_w is not None:
                op.deps.add(t.last_w)
        for t in writes:
            if t.last_w is not None:
                op.deps.add(t.last_w)
            for r in t.readers:
                op.deps.add(r)
        for t in reads:
            t.readers.append(op)
        for t in writes:
            t.last_w = op
            t.readers = []
        op.deps.discard(op)

    def add(self, eng, fn, reads=(), writes=(), is_dma=False):
        op = Op(eng, fn, is_dma)
        self._track(op, reads, writes)
        self.ops[eng].append(op)
        self.nops += 1
        return op

    def dma(self, out_ap, in_ap, reads=(), writes=(), queue="sp", **kw):
        def fn(e, out_ap=out_ap, in_ap=in_ap, kw=kw):
            return e.dma_start(out=out_ap, in_=in_ap, **kw)
        return self.add(queue, fn, reads, writes, is_dma=True)

    def emit(self):
        nc = self.nc

        def skip(d, op):
            return d.eng == "pe" and op.eng == "pe" and not d.is_dma and not op.is_dma
        for e in self.ops:
            for op in self.ops[e]:
                for d in op.deps:
                    if not skip(d, op):
                        d.need_inc = True
        with contextlib.ExitStack() as st:
            sems = {e: st.enter_context(nc.semaphore("s_" + e)) for e in COMPUTE}
            dma_sems = {}
            for q in self.ops:
                if any(o.is_dma for o in self.ops[q]):
                    dma_sems[q] = [st.enter_context(nc.semaphore("d_%s_%d" % (q, i)))
                                   for i in range(self.n_dma_sems)]
            for e in self.ops:
                c = 0
                j = 0
                for op in self.ops[e]:
                    if op.is_dma:
                        ring = dma_sems[e]
                        op.sem = ring[j % len(ring)]
                        op.count = 16 * (j // len(ring) + 1)
                        j += 1
                    elif op.need_inc:
                        c += 1
                        op.count = c
                        op.sem = sems[e]
            block = st.enter_context(nc.Block())
            handles = {"pe": block.tensor, "act": block.scalar, "dve": block.vector,
                       "pool": block.gpsimd, "sp": block.sync}

            def make(e):
                oplist = self.ops[e]

                def body(eng):
                    known = {}
                    nwait = [0]

                    def wait(sem, val):
                        if known.get(id(sem), 0) >= val:
                            return
                        eng.wait_ge(sem, val)
                        known[id(sem)] = val
                    for op in oplist:
                        need = {}
                        for d in op.deps:
                            if skip(d, op):
                                continue
                            k = id(d.sem)
                            if k not in need or need[k][1] < d.count:
                                need[k] = (d.sem, d.count)
                        if op.is_dma and op.count > 16:
                            k = id(op.sem)
                            v = op.count - 16
                            if k not in need or need[k][1] < v:
                                need[k] = (op.sem, v)
                        pend = [(sem, val) for sem, val in need.values() if known.get(id(sem), 0) < val]
                        if pend and e == "pe" and self.junk_fn is not None and op.fn is not None:
                            nwait[0] += 1
                            for _ in range(self.junk_n):
                                self.junk_fn(eng)
                        for sem, val in pend:
                            wait(sem, val)
                        if op.fn is None:
                            continue
                        ins = op.fn(eng)
                        if op.is_dma:
                            ins.then_inc(op.sem, 16)
                        elif op.need_inc:
                            ins.then_inc(op.sem, 1)
                    last = {}
                    for op in oplist:
                        if op.is_dma:
                            last[id(op.sem)] = (op.sem, op.count)
                    for sem, val in last.values():
                        wait(sem, val)
                    if e == "pe":
                        self.n_pe_waits = nwait[0]
                return body
            for e in self.ops:
                if self.ops[e]:
                    handles[e](make(e))


def host_consts():
    i = np.arange(128)
    c = {}
    c["ident"] = np.eye(128, dtype=np.float32)
    c["ones"] = np.ones((128, 128), np.float32)
    for nm, nseq in (("p", 1), ("s", NSEQ)):
        t = 128 // nseq
        seq = i // t
        same = seq[:, None] == seq[None, :]
        incl = same & (i[None, :] <= i[:, None])
        strict = same & (i[None, :] < i[:, None])
        c["ltincl_" + nm] = incl.T.astype(np.float32)
        c["seqm_" + nm] = same.astype(np.float32)
        c["nmincl_" + nm] = np.where(incl, 0.0, BIG).astype(np.float32)
        c["nminclT_" + nm] = np.where(incl.T, 0.0, -BIG).astype(np.float32)
        c["ms01_" + nm] = strict.astype(np.float32)
    caus = i[:, None] < i[None, :]
    c["caus01"] = caus.astype(np.float32)
    c["causneg"] = np.where(caus, 0.0, -BIG).astype(np.float32)
    kt = i[:, None, None]
    ss_ = np.arange(NSEQ)[None, :, None]
    qq = (np.arange(64) % TS)[None, None, :]
    c["smask01"] = ((kt // TS == ss_) & (kt % TS < qq)).astype(np.float32).reshape(128, NSEQ * 64)
    c["ntri"] = np.where(i[:, None] >= i[None, :], -1.0, 0.0).astype(np.float32)
    cm = (np.arange(NSEQ)[:, None] == (i // TS)[None, :]).astype(np.float32)
    c["colmask"] = np.broadcast_to(cm.reshape(1, NSEQ * 128), (128, NSEQ * 128)).copy()
    c["rowmask"] = np.zeros((128, 128), np.float32)
    c["rowmask"][:, :NSEQ] = cm.T
    main = ["ident", "ones", "ltincl_p", "ms01_p", "caus01", "causneg", "ntri"]
    samp = ["ltincl_s", "seqm_s", "ms01_s", "rowmask", "colmask", "smask01"]

    def pack(names):
        off = {}
        o = 0
        for k in names:
            off[k] = (o, c[k].shape[1])
            o += c[k].shape[1]
        return np.concatenate([c[k] for k in names], axis=1).astype(np.float32), off
    return pack(main) + pack(samp)


CT_MAIN, CO_MAIN, CT_SAMP, CO_SAMP = host_consts()


class Builder:
    def __init__(self, cfg, stages=("all",), dbg=()):
        self.cfg = cfg
        self.stages = stages
        self.dbg = dbg
        self.nc = bass.Bass("TRN2", target_bir_lowering=False)
        self.s = Sched(self.nc)
        self.fence_id = 0
        self._uid = 0
        import os
        self.use_r32 = os.environ.get("USE_R32", "0") == "1"

    def on(self, st):
        return "all" in self.stages or st in self.stages

    def sb(self, stack, name, shape, dt):
        self._uid += 1
        h = stack.enter_context(self.nc.sbuf_tensor("%s_%d" % (name, self._uid), list(shape), dt))
        return Tile(h, name)

    def view(self, ap, name=""):
        return Tile(ap, name)

    def din(self, name, shape, dt=F32):
        return self.nc.dram_tensor(name, list(shape), dt, kind="ExternalInput").ap()

    def dout(self, name, shape, dt=F32):
        return self.nc.dram_tensor(name, list(shape), dt, kind="ExternalOutput").ap()

    def dscr(self, name, shape, dt=F32):
        return self.nc.dram_tensor(name, list(shape), dt, kind="Internal").ap()

    def E(self, eng, meth, reads, writes, *a, **kw):
        return self.s.add(eng, lambda e: getattr(e, meth)(*a, **kw), reads, writes)

    def mm(self, out, lhsT, rhs, start, stop, reads, writes, skip=False, r32=False):
        if r32 and self.use_r32:
            lhsT = lhsT.bitcast(mybir.dt.float32r)
            rhs = rhs.bitcast(mybir.dt.float32r)
        return self.s.add("pe", lambda e: e.matmul(out, lhsT=lhsT, rhs=rhs, start=start, stop=stop, skip_group_check=skip),
                          reads, writes)

    def tr(self, out, in_, ident, reads, writes):
        return self.s.add("pe", lambda e: e.transpose(out=out, in_=in_, identity=ident), reads, writes)

    def act(self, out, in_, func, reads, writes, **kw):
        return self.s.add("act", lambda e: e.activation(out=out, in_=in_, func=func, **kw), reads, writes)

    def dma(self, out, in_, reads=(), writes=(), **kw):
        return self.s.dma(out, in_, reads, writes, **kw)

    def fence(self):
        s = self.s
        f = set()
        for e in COMPUTE:
            real = [o for o in s.ops[e] if o.fn is not None and not o.is_dma]
            if real:
                f.add(real[-1])
        for q in s.ops:
            d = [o for o in s.ops[q] if o.is_dma]
            for o in d[-s.n_dma_sems:]:
                f.add(o)
        for e in COMPUTE + ("sp",):
            op = Op(e, None)
            op.deps = set(f)
            s.ops[e].append(op)

    def G(self):
        t = self.pg[self.gi % len(self.pg)]
        self.gi += 1
        return t

    def TB(self):
        t = self.ptb[self.ti % len(self.ptb)]
        self.ti += 1
        return t

    def declare(self):
        c = self.cfg
        NT = c.NBLK * 128
        i = {}
        i["xp"] = self.din("xp", [NT, D])
        i["xs"] = self.din("xs", [128, D])
        i["cvec"] = self.din("cvec", [17, D])
        i["cache_k"] = self.din("cache_k", [c.NPHYS * 128, SBW])
        i["cache_v"] = self.din("cache_v", [c.NPHYS * 128, SBW])
        i["ptab"] = self.din("ptab", [NSEQ * c.NPG], I32)
        i["s0"] = self.din("s0", [NSEQ, DNH, DND, DND])
        i["dnc0"] = self.din("dnc0", [NSEQ * 3, 3 * DNW])
        i["ffc0"] = self.din("ffc0", [NSEQ * 2, 2 * DFF])
        i["w_ada"] = self.din("w_ada", [D, 6 * D])
        i["b_ada"] = self.din("b_ada", [6 * D])
        i["g_attn"] = self.din("g_attn", [D])
        i["w_in"] = self.din("w_in", [D, INC])
        i["g_q"] = self.din("g_q", [HD])
        i["g_k"] = self.din("g_k", [HD])
        i["sb_bias"] = self.din("sb_bias", [HS])
        i["g_sb_out"] = self.din("g_sb_out", [HD])
        i["w_dn_conv"] = self.din("w_dn_conv", [4, 3 * DNW])
        i["a_log"] = self.din("a_log", [DNH])
        i["dt_bias"] = self.din("dt_bias", [DNH])
        i["g_dn_out"] = self.din("g_dn_out", [DND])
        i["w_out"] = self.din("w_out", [D, D])
        i["g_ffn"] = self.din("g_ffn", [D])
        i["w_up"] = self.din("w_up", [D, 2 * DFF])
        i["w_ffn_conv"] = self.din("w_ffn_conv", [3, 2 * DFF])
        i["w_down"] = self.din("w_down", [DFF, D])
        i["ct_main"] = self.din("ct_main", list(CT_MAIN.shape))
        i["ct_samp"] = self.din("ct_samp", list(CT_SAMP.shape))
        i["kvlo"] = self.din("kvlo", [128, 256])
        i["blkvalid"] = self.din("blkvalid", [128, c.NBLK])
        self.i = i
        o = {}
        o["yp"] = self.dout("yp", [c.NOUT * 128, D])
        o["ys"] = self.dout("ys", [128, D])
        o["kp"] = self.dout("kp", [c.NOUT * 128, SBW])
        o["vp"] = self.dout("vp", [c.NOUT * 128, SBW])
        o["ksm"] = self.dout("ksm", [128, SBW])
        o["vsm"] = self.dout("vsm", [128, SBW])
        o["sp_state"] = self.dout("sp_state", [DNH, DND, DND])
        o["ss_state"] = self.dout("ss_state", [NSEQ, DNH, DND, DND])
        o["dcp"] = self.dout("dcp", [3, 3 * DNW])
        o["dcs"] = self.dout("dcs", [NSEQ * 3, 3 * DNW])
        o["fcp"] = self.dout("fcp", [2, 2 * DFF])
        o["fcs"] = self.dout("fcs", [NSEQ * 2, 2 * DFF])
        for name, shape in self.dbg:
            o[name] = self.dout(name, shape)
        self.o = o
        self.modd = self.dscr("modd", [17, 6 * D])
        self.mixd = self.dscr("mixd", [(c.NOWN + 1) * 128, D], BF16)
        self.t_modd = Tile(None, "modd")
        self.t_mixd = [Tile(None, "mixd%d" % k) for k in range(c.NOWN + 1)]

    def stream_cast(self, stack_tiles, src_view, dst_tile, dst_ap, eng="dve"):
        stg = stack_tiles[self.sci % len(stack_tiles)]
        self.sci += 1
        shp = src_view.shape
        sap = stg[:, 0:shp[1] * shp[2]].rearrange("p (a b) -> p a b", a=shp[1])
        self.dma(sap, src_view, writes=[stg])
        if eng == "act":
            self.act(dst_ap, sap, AF.Copy, [stg], [dst_tile])
        else:
            self.E(eng, "tensor_copy", [stg], [dst_tile], out=dst_ap, in_=sap)

    def rsqrt_ops(self, ss, rs, n, scale, reads_extra=()):
        self.act(rs[:, 0:n], ss[:, 0:n], AF.Ln, [ss] + list(reads_extra), [rs], scale=scale, bias=self.epsb[:, 0:1])
        self.act(rs[:, 0:n], rs[:, 0:n], AF.Exp, [rs], [rs], scale=-0.5)

    def build(self):
        nc = self.nc
        c = self.cfg
        self.declare()
        i, o = self.i, self.o
        self.gi = 0
        self.ti = 0
        self.sci = 0
        with contextlib.ExitStack() as top:
            self.pg = [Tile(top.enter_context(nc.psum_tensor("pg%d" % k, [128, 512], F32)), "pg%d" % k, True) for k in range(2)]
            self.zb = [Tile(top.enter_context(nc.psum_tensor("zb%d" % k, [128, 512], F32)), "zb%d" % k, True) for k in range(2)]
            self.po = [Tile(top.enter_context(nc.psum_tensor("po%d" % k, [128, 512], F32)), "po%d" % k, True) for k in range(2)]
            self.ptb = [Tile(top.enter_context(nc.psum_tensor("ptb%d" % k, [128, 1024], BF16)), "ptb%d" % k, True) for k in range(2)]
            ctm = self.sb(top, "ctm", list(CT_MAIN.shape), F32)
            self.ctm = ctm
            self.dma(ctm[:], i["ct_main"][:, :], writes=[ctm])

            def cm(name):
                o_, w_ = CO_MAIN[name]
                return ctm[:, o_:o_ + w_]
            self.cm = cm
            cbf = self.sb(top, "cbf", [128, 4 * 128], BF16)
            self.cbf = cbf
            self.E("dve", "tensor_copy", [ctm], [cbf], out=cbf[:, 0:128], in_=cm("ident"))
            self.E("dve", "tensor_copy", [ctm], [cbf], out=cbf[:, 128:256], in_=cm("ntri"))
            self.E("dve", "tensor_copy", [ctm], [cbf], out=cbf[:, 256:384], in_=cm("causneg"))
            self.E("dve", "tensor_scalar", [ctm], [cbf], out=cbf[:, 384:512], in0=cm("ones"), scalar1=-1.0, scalar2=None, op0=ALU.mult)
            self.identb = cbf[:, 0:128]
            self.ntrib = cbf[:, 128:256]
            self.causnegb = cbf[:, 256:384]
            self.negonesb = cbf[:, 384:512]
            epsb = self.sb(top, "epsb", [128, 1], F32)
            self.epsb = epsb
            self.E("pool", "memset", [], [epsb], epsb[:], EPS)
            sv = self.sb(top, "sv", [128, 64 * 3 + 128 + 4 + 4 + 8], F32)
            self.sv = sv
            self.dma(sv[:, 0:64], i["g_q"].partition_broadcast(128), writes=[sv])
            self.dma(sv[:, 64:128], i["g_k"].partition_broadcast(128), writes=[sv])
            self.dma(sv[:, 128:192], i["g_sb_out"].partition_broadcast(128), writes=[sv])
            self.dma(sv[:, 192:320], i["g_dn_out"].partition_broadcast(128), writes=[sv])
            self.dma(sv[:, 320:324], i["a_log"].partition_broadcast(128), writes=[sv])
            self.dma(sv[:, 324:328], i["dt_bias"].partition_broadcast(128), writes=[sv])
            self.dma(sv[:, 328:336], i["sb_bias"].partition_broadcast(128), writes=[sv])
            self.E("dve", "tensor_scalar", [sv], [sv], out=sv[:, 0:64], in0=sv[:, 0:64], scalar1=HD ** -0.5, scalar2=None, op0=ALU.mult)
            self.act(sv[:, 320:324], sv[:, 320:324], AF.Exp, [sv], [sv])
            self.E("dve", "tensor_scalar", [sv], [sv], out=sv[:, 320:324], in0=sv[:, 320:324], scalar1=-1.0, scalar2=None, op0=ALU.mult)
            self.gq8, self.gk, self.gso, self.gdn = sv[:, 0:64], sv[:, 64:128], sv[:, 128:192], sv[:, 192:320]
            self.negA, self.dtb = sv[:, 320:324], sv[:, 324:328]
            kvf = self.sb(top, "kvf", [128, 256], F32)
            self.dma(kvf[:], i["kvlo"][:, :], writes=[kvf])
            kvd = self.sb(top, "kvd", [128, 128], BF16)
            self.kvd = kvd
            self.E("dve", "tensor_copy", [kvf], [kvd], out=kvd[:], in_=kvf[:, 0:128])
            bvd = self.sb(top, "bvd", [128, c.NBLK], F32)
            self.bvd = bvd
            self.dma(bvd[:], i["blkvalid"][:, :], writes=[bvd])
            kvone = self.sb(top, "kvone", [128, 128], BF16)
            self.kvone = kvone
            self.E("dve", "tensor_copy", [kvf], [kvone], out=kvone[:], in_=kvf[:, 128:256])
            wdc = self.sb(top, "wdc", [128, 4, 12], F32)
            self.wdc = wdc
            for t_ in range(4):
                self.dma(wdc[:, t_, :], i["w_dn_conv"][t_].rearrange("(c p) -> p c", p=128), writes=[wdc], allow_slow_non_contiguous=True)
            wfc = self.sb(top, "wfc", [128, 3, 44], F32)
            self.wfc = wfc
            for t_ in range(3):
                self.dma(wfc[:, t_, :], i["w_ffn_conv"][t_].rearrange("(c p) -> p c", p=128), writes=[wfc], allow_slow_non_contiguous=True)

            if self.on("setup"):
                self.setup_mod()
            self.fence()
            if self.on("p1"):
                self.phase1()
            self.fence()
            if self.on("p2"):
                self.phase2()
            self.s.emit()
        return nc

    def setup_mod(self):
        i = self.i
        with contextlib.ExitStack() as st:
            cv = self.sb(st, "cv", [17, D], F32)
            ex = self.sb(st, "ex", [17, D], F32)
            scb = self.sb(st, "scb", [17, D], BF16)
            scT = self.sb(st, "scT", [128, DC, 17], BF16)
            stg = [self.sb(st, "stg%d" % k, [128, DC * 512], F32) for k in range(2)]
            wab = [self.sb(st, "wab%d" % k, [128, DC, 512], BF16) for k in range(2)]
            bada = self.sb(st, "bada", [17, 512], F32)
            gv = self.sb(st, "gv", [17, 2 * D], F32)
            mt = [self.sb(st, "mt%d" % k, [17, 512], F32) for k in range(2)]
            self.dma(cv[:], i["cvec"][:, :], writes=[cv])
            self.dma(gv[:, 0:D], i["g_attn"].partition_broadcast(17), writes=[gv])
            self.dma(gv[:, D:2 * D], i["g_ffn"].partition_broadcast(17), writes=[gv])
            self.act(ex[:], cv[:], AF.Exp, [cv], [ex], scale=-1.0)
            self.E("dve", "tensor_scalar", [ex], [ex], out=ex[:], in0=ex[:], scalar1=1.0, scalar2=None, op0=ALU.add)
            self.E("dve", "reciprocal", [ex], [ex], out=ex[:], in_=ex[:])
            self.E("dve", "tensor_tensor", [ex, cv], [scb], out=scb[:], in0=cv[:], in1=ex[:], op=ALU.mult)
            tb = self.TB()
            for dc in range(DC):
                self.tr(tb[:, dc * 32:dc * 32 + 17], scb[0:17, dc * 128:(dc + 1) * 128], self.identb[0:17, 0:17], [scb, self.cbf], [tb])
            self.E("dve", "tensor_copy", [tb], [scT], out=scT[:], in_=tb[:, 0:DC * 32].rearrange("p (a b) -> p a b", a=DC)[:, :, 0:17])
            wv = i["w_ada"].rearrange("(c p) n -> p c n", p=128)
            for ct in range(12):
                wb = wab[ct % 2]
                self.stream_cast(stg, wv[:, :, ct * 512:(ct + 1) * 512], wb, wb[:], eng="act" if ct % 2 else "dve")
                self.dma(bada[:], i["b_ada"][ct * 512:(ct + 1) * 512].partition_broadcast(17), writes=[bada])
                g = self.G()
                for dc in range(DC):
                    self.mm(g[0:17, :], scT[:, dc, :], wb[:, dc, :], dc == 0, dc == DC - 1, [scT, wb], [g])
                m = mt[ct % 2]
                self.E("dve", "tensor_tensor", [g, bada], [m], out=m[:], in0=g[0:17, :], in1=bada[:], op=ALU.add)
                if ct in (2, 3, 8, 9):
                    go = (ct - 2) * 512 if ct < 4 else D + (ct - 8) * 512
                    self.E("dve", "scalar_tensor_tensor", [m, gv], [m], out=m[:], in0=m[:], scalar=1.0, in1=gv[:, go:go + 512],
                           op0=ALU.add, op1=ALU.mult)
                self.dma(self.modd[:, ct * 512:(ct + 1) * 512], m[:], reads=[m], writes=[self.t_modd])

    def load_mod(self, tiles, idxs, sample):
        for t, ix in zip(tiles, idxs):
            if not sample:
                self.dma(t[:], self.modd[0, ix * D:(ix + 1) * D].partition_broadcast(128), reads=[self.t_modd], writes=[t])
            else:
                for s_ in range(NSEQ):
                    self.dma(t[s_ * TS:(s_ + 1) * TS, :], self.modd[1 + s_, ix * D:(ix + 1) * D].partition_broadcast(TS),
                             reads=[self.t_modd], writes=[t])

    def norm_mod(self, w, xt, scale, shift, hT, tmp=None):
        self.E("pool", "memset", [], [w.ssq], w.ssq[:], 0.0)
        self.act(w.h[:], xt[:], AF.Square, [xt, w.ssq], [w.h, w.ssq], accum_out=w.ssq[:, 0:1])
        self.rsqrt_ops(w.ssq, w.rstd, 1, 1.0 / D)
        if tmp is None:
            tmp = xt
        self.E("dve", "scalar_tensor_tensor", [xt, w.rstd, scale], [tmp], out=tmp[:], in0=xt[:], scalar=w.rstd[:, 0:1],
               in1=scale[:], op0=ALU.mult, op1=ALU.mult)
        self.E("pool", "tensor_tensor", [tmp, shift], [w.h], out=w.h[:], in0=tmp[:], in1=shift[:], op=ALU.add)
        tb = self.TB()
        for dc in range(DC):
            self.tr(tb[:, dc * 128:(dc + 1) * 128], w.h[:, dc * 128:(dc + 1) * 128], self.identb, [w.h, self.cbf], [tb])
        self.act(hT[:], tb[:, :].rearrange("p (a b) -> p a b", a=DC), AF.Copy, [tb], [hT])

    def head_norm(self, w, ps, nh, hd, out_f32, reads_ps, gvec, out_tile, out_ap, scale):
        v3 = lambda ap: ap.rearrange("p (a b) -> p a b", a=nh)
        self.act(out_f32[:, 0:nh * hd], ps, AF.Square, reads_ps, [out_f32])
        self.E("dve", "tensor_reduce", [out_f32], [w.ss8], out=w.ss8[:, 0:nh], in_=v3(out_f32[:, 0:nh * hd]), axis=AX.X, op=ALU.add)
        self.rsqrt_ops(w.ss8, w.rs8, nh, scale)
        self.E("dve", "tensor_tensor", reads_ps + [w.rs8], [out_f32], out=v3(out_f32[:, 0:nh * hd]), in0=v3(ps),
               in1=w.rs8[:, 0:nh].unsqueeze(2).to_broadcast([128, nh, hd]), op=ALU.mult)
        self.E("pool", "tensor_tensor", [out_f32, self.sv], [out_tile], out=v3(out_ap), in0=v3(out_f32[:, 0:nh * hd]),
               in1=gvec.unsqueeze(1).to_broadcast([128, nh, hd]), op=ALU.mult)

    def front_end(self, w, b, sample, own, outrow):
        c = self.cfg
        i, o = self.i, self.o
        xt = w.xt[self.xi % len(w.xt)]
        self.xi += 1
        src = i["xs"][:, :] if sample else i["xp"][b * 128:(b + 1) * 128, :]
        self.dma(xt[:], src, writes=[xt])
        hT = w.hT
        self.norm_mod(w, xt, w.scale1, w.shift1, hT)
        winb = self.winb
        g = self.G()
        for dc in range(DC):
            self.mm(g[:, :], hT[:, dc, :], winb[:, dc, 512:1024], dc == 0, dc == DC - 1, [hT, winb], [g])
        self.head_norm(w, g[:, :], HS, HD, w.f512, [g], self.gk, w.kn, w.kn[:], 1.0 / HD)
        if outrow is not None:
            self.dma(outrow[0], w.kn[:], reads=[w.kn])
        self.E("dve", "tensor_copy", [w.kn], [w.knb], out=w.knb[:], in_=w.kn[:])
        KTt = w.KTs if sample else self.KTt[b]
        tb = self.TB()
        for pr in range(4):
            self.tr(tb[:, pr * 128:(pr + 1) * 128], w.knb[:, pr * 128:(pr + 1) * 128], self.identb, [w.knb, self.cbf], [tb])
        self.act(KTt[:], tb[:, 0:512].rearrange("p (a b) -> p a b", a=4), AF.Copy, [tb], [KTt])
        g = self.G()
        for dc in range(DC):
            self.mm(g[:, :], hT[:, dc, :], winb[:, dc, 1024:1536], dc == 0, dc == DC - 1, [hT, winb], [g])
        Vt = w.Vs if sample else self.Vt[b]
        self.act(Vt[:], g[:, :], AF.Copy, [g], [Vt])
        if outrow is not None:
            self.E("dve", "tensor_copy", [g], [w.vf], out=w.vf[:], in_=g[:, :])
            self.dma(outrow[1], w.vf[:], reads=[w.vf])
        if own:
            g = self.G()
            for dc in range(DC):
                self.mm(g[:, :], hT[:, dc, :], winb[:, dc, 0:512], dc == 0, dc == DC - 1, [hT, winb], [g])
            self.head_norm(w, g[:, :], HS, HD, w.f512, [g], self.gq8, w.qnb, w.qnb[:], 1.0 / HD)
            tb = self.TB()
            for pr in range(4):
                self.tr(tb[:, pr * 128:(pr + 1) * 128], w.qnb[:, pr * 128:(pr + 1) * 128], self.identb, [w.qnb, self.cbf], [tb])
            tq = tb[:, 0:512].rearrange("p (a b) -> p a b", a=4)
            self.act(w.QT[0:64, :, 0, :], tq[0:64, :, :], AF.Copy, [tb], [w.QT])
            self.E("dve", "tensor_copy", [tb], [w.QT], out=w.QT[64:128, :, 1, :], in_=tq[64:128, :, :])
        if self.on("dn"):
            g = self.G()
            for dc in range(DC):
                self.mm(g[:, 0:8], hT[:, dc, :], winb[:, dc, 3072:3080], dc == 0, dc == DC - 1, [hT, winb], [g])
            self.E("dve", "tensor_copy", [g], [w.ba], out=w.ba[:], in_=g[:, 0:8])
            if own:
                g = self.G()
                for dc in range(DC):
                    self.mm(g[:, :], hT[:, dc, :], winb[:, dc, 3080:3592], dc == 0, dc == DC - 1, [hT, winb], [g])
                self.act(w.zs[:], g[:, :], AF.Copy, [g], [w.zs])

    def dn_chunk(self, w, b, sample, own, tabs, last_prompt):
        c = self.cfg
        i, o = self.i, self.o
        T = TS if sample else 128
        ns = NSEQ if sample else 1
        sc = w.sc
        hT, winb = w.hT, self.winb
        ltincl, seqm, ms01 = tabs
        bc4 = lambda ap: ap.unsqueeze(2).to_broadcast([128, 4, 128])
        hb4 = lambda ap: ap.unsqueeze(1).to_broadcast([128, 4, 128])
        v4 = lambda ap: ap.rearrange("p (a b) -> p a b", a=4)
        XE4 = w.XE4
        xe4 = XE4[:, :, :].rearrange("p c (s t) -> p c s t", s=ns)
        Hst = w.Hst
        hs4 = Hst[:, :, 0:ns * 3].rearrange("p c (s t) -> p c s t", s=ns)
        if (not sample) and b == 0:
            self.E("pool", "memset", [], [Hst], Hst[:], 0.0)
        for j in range(3):
            g = self.G()
            for ch in range(4):
                col = 1536 + (j * 4 + ch) * 128
                for dc in range(DC):
                    self.mm(g[:, ch * 128:(ch + 1) * 128], winb[:, dc, col:col + 128], hT[:, dc, :], dc == 0, dc == DC - 1, [hT, winb], [g])
            src4 = g[:, :].rearrange("p (c s t) -> p c s t", c=4, s=ns)
            self.E("pool", "tensor_copy", [Hst], [XE4], out=xe4[:, :, :, 0:3], in_=hs4[:, j * 4:(j + 1) * 4, :, :])
            if sample:
                self.act(xe4[:, :, :, 3:3 + T], src4, AF.Copy, [g], [XE4])
            else:
                self.act(xe4[:, :, :, 3:3 + T], src4, AF.Copy, [g, self.bvd], [XE4], scale=self.bvd[:, b:b + 1])
            self.E("pool", "tensor_copy", [XE4], [Hst], out=hs4[:, j * 4:(j + 1) * 4, :, :], in_=xe4[:, :, :, T:T + 3])
            Y4 = w.Y4
            for ch in range(4):
                cc = j * 4 + ch
                eng = "dve"
                yv = Y4[:, ch, :].rearrange("p (s t) -> p s t", s=ns)
                self.E(eng, "tensor_scalar", [XE4, self.wdc], [Y4], out=yv, in0=xe4[:, ch, :, 0:T], scalar1=self.wdc[:, 0, cc:cc + 1],
                       scalar2=None, op0=ALU.mult)
                for k in range(1, 4):
                    self.E(eng, "scalar_tensor_tensor", [XE4, self.wdc, Y4], [Y4], out=yv, in0=xe4[:, ch, :, k:k + T],
                           scalar=self.wdc[:, k, cc:cc + 1], in1=yv, op0=ALU.mult, op1=ALU.add)
            E4 = w.E4
            self.act(E4[:], Y4[:], AF.Exp, [Y4], [E4], scale=-1.0)
            self.act(E4[:], E4[:], AF.Ln, [E4], [E4], bias=1.0)
            self.act(E4[:], E4[:], AF.Exp, [E4], [E4], scale=-1.0)
            self.E("dve", "tensor_tensor", [E4, Y4], [Y4], out=Y4[:], in0=Y4[:], in1=E4[:], op=ALU.mult)
            if j < 2:
                SQ = E4
                self.act(SQ[:], Y4[:], AF.Square, [Y4], [SQ])
                g = self.G()
                self.mm(g[:, :], self.cm("ones"), SQ[:].rearrange("p a b -> p (a b)"), True, True, [SQ, self.ctm], [g])
                self.act(SQ[:].rearrange("p a b -> p (a b)"), g[:, :], AF.Ln, [g], [SQ], bias=self.epsb[:, 0:1])
                self.act(SQ[:], SQ[:], AF.Exp, [SQ], [SQ], scale=-0.5)
                if j == 0:
                    self.E("dve", "scalar_tensor_tensor", [Y4, SQ], [w.QnT], out=w.QnT[:], in0=Y4[:], scalar=DND ** -0.5, in1=SQ[:],
                           op0=ALU.mult, op1=ALU.mult)
                else:
                    self.E("pool", "tensor_tensor", [Y4, SQ], [w.KnT], out=w.KnT[:], in0=Y4[:], in1=SQ[:], op=ALU.mult)
            else:
                self.act(w.Vcb[:], Y4[:], AF.Copy, [Y4], [w.Vcb])
            yield
        if sample or last_prompt:
            ncol = ns * 3
            dst = o["dcs"] if sample else o["dcp"]
            for j in range(3):
                g = self.G()
                for ch in range(4):
                    self.tr(g[0:ncol, ch * 128:(ch + 1) * 128], Hst[:, j * 4 + ch, 0:ncol], self.cm("ident"), [Hst, self.ctm], [g])
                self.E("dve", "tensor_copy", [g], [w.f512], out=w.f512[0:ncol, :], in_=g[0:ncol, :])
                self.dma(dst[:, j * 512:(j + 1) * 512], w.f512[0:ncol, :], reads=[w.f512])
        tb = self.TB()
        for h in range(4):
            self.tr(tb[:, h * 128:(h + 1) * 128], w.KnT[:, h, :], self.identb, [w.KnT, self.cbf], [tb])
        self.act(w.Ktok[:], v4(tb[:, 0:512]), AF.Copy, [tb], [w.Ktok])
        tb = self.TB()
        for h in range(4):
            self.tr(tb[:, h * 128:(h + 1) * 128], w.Vcb[:, h, :], self.identb, [w.Vcb, self.cbf], [tb])
        self.E("dve", "tensor_copy", [tb], [w.Vtok], out=w.Vtok[:], in_=v4(tb[:, 0:512]))
        yield
        ba = w.ba
        self.act(sc[:, 0:4], ba[:, 0:4], AF.Exp, [ba], [sc], scale=-1.0)
        self.act(sc[:, 0:4], sc[:, 0:4], AF.Ln, [sc], [sc], bias=1.0)
        self.act(sc[:, 0:4], sc[:, 0:4], AF.Exp, [sc], [sc], scale=-1.0)
        if not sample:
            self.E("dve", "tensor_scalar", [sc, self.bvd], [sc], out=sc[:, 0:4], in0=sc[:, 0:4], scalar1=self.bvd[:, b:b + 1], scalar2=None, op0=ALU.mult)
        self.E("dve", "tensor_tensor", [ba, self.sv], [sc], out=sc[:, 32:36], in0=ba[:, 4:8], in1=self.dtb, op=ALU.add)
        self.act(sc[:, 32:36], sc[:, 32:36], AF.Exp, [sc], [sc])
        self.act(sc[:, 32:36], sc[:, 32:36], AF.Ln, [sc], [sc], bias=1.0)
        self.E("dve", "tensor_tensor", [sc, self.sv], [sc], out=sc[:, 4:8], in0=sc[:, 32:36], in1=self.negA, op=ALU.mult)
        g = self.G()
        self.mm(g[:, 0:4], ltincl, sc[:, 4:8], True, True, [sc, w.tabt], [g])
        self.mm(g[:, 4:8], seqm, sc[:, 4:8], True, True, [sc, w.tabt], [g])
        self.E("dve", "tensor_copy", [g], [sc], out=sc[:, 8:16], in_=g[:, 0:8])
        self.act(sc[:, 16:20], sc[:, 8:12], AF.Exp, [sc], [sc])
        self.E("dve", "scalar_tensor_tensor", [sc], [sc], out=sc[:, 20:24], in0=sc[:, 0:4], scalar=-1.0, in1=sc[:, 16:20], op0=ALU.mult, op1=ALU.mult)
        self.E("dve", "tensor_tensor", [sc], [sc], out=sc[:, 24:28], in0=sc[:, 12:16], in1=sc[:, 8:12], op=ALU.subtract)
        self.act(sc[:, 24:28], sc[:, 24:28], AF.Exp, [sc], [sc])
        self.E("dve", "tensor_scalar", [sc], [sc], out=sc[:, 28:32], in0=sc[:, 0:4], scalar1=-1.0, scalar2=None, op0=ALU.mult)
        beta, gg, negbg, kds, negbeta = sc[:, 0:4], sc[:, 8:12], sc[:, 20:24], sc[:, 24:28], sc[:, 28:32]
        yield
        GR = self.G()
        for h in range(4):
            dg = w.dg[h % 2]
            self.E("dve", "tensor_scalar", [sc, self.ctm], [dg], out=dg[:], in0=self.cm("ident"), scalar1=sc[:, 8 + h:9 + h], scalar2=None, op0=ALU.mult)
            self.mm(GR[:, h * 128:(h + 1) * 128], self.cm("ones"), dg[:], True, True, [dg, self.ctm], [GR])
        fa, fb, fc, fd = w.fa, w.fb, w.fc, w.fd
        self.E("dve", "tensor_tensor", [GR, sc], [fa], out=v4(fa[:]), in0=v4(GR[:, :]), in1=bc4(gg), op=ALU.subtract)
        self.act(fd[:], GR[:, :], AF.Exp, [GR], [fd])
        self.E("dve", "tensor_scalar", [fa], [fb], out=fb[:], in0=fa[:], scalar1=0.0, scalar2=None, op0=ALU.max)
        self.E("dve", "tensor_scalar", [fa], [fc], out=fc[:], in0=fa[:], scalar1=0.0, scalar2=None, op0=ALU.min)
        self.act(fb[:], fb[:], AF.Exp, [fb], [fb], scale=-1.0)
        self.act(fc[:], fc[:], AF.Exp, [fc], [fc])
        yield
        g = self.G()
        for h in range(4):
            self.mm(g[:, h * 128:(h + 1) * 128], w.KnT[:, h, :], w.KnT[:, h, :], True, True, [w.KnT], [g])
        self.E("dve", "tensor_tensor", [g, fb], [fa], out=fa[:], in0=g[:, :], in1=fb[:], op=ALU.mult)
        self.E("pool", "tensor_tensor", [fa, sc], [fa], out=v4(fa[:]), in0=v4(fa[:]), in1=bc4(negbeta), op=ALU.mult)
        self.E("dve", "tensor_tensor", [fa, w.tabt], [fa], out=v4(fa[:]), in0=v4(fa[:]), in1=hb4(ms01), op=ALU.mult)
        g = self.G()
        for h in range(4):
            self.mm(g[:, h * 128:(h + 1) * 128], w.KnT[:, h, :], w.QnT[:, h, :], True, True, [w.KnT, w.QnT], [g])
        self.E("dve", "tensor_tensor", [g, fc], [fc], out=fc[:], in0=g[:, :], in1=fc[:], op=ALU.mult)
        self.E("pool", "tensor_tensor", [fc, w.tabt], [w.intraT], out=w.intraT[:], in0=v4(fc[:]), in1=hb4(ltincl), op=ALU.mult)
        yield
        MTb = w.MTb
        nlev = 2 if sample else 6
        h2 = lambda ap: ap.rearrange("p (a b) -> p a b", a=2)
        hb2 = lambda ap: ap.unsqueeze(1).to_broadcast([128, 2, 128])
        identf = self.cm("ident")
        P, PT, MT = w.Pf[0], w.PTf[0], w.MT
        P = fa_t = None
        P = w.Pf[0]
        self.E("pool", "tensor_copy", [fa], [P], out=P[:], in_=v4(fa[:]))
        g = self.G()
        for h in range(4):
            self.tr(g[:, h * 128:(h + 1) * 128], P[:, h, :], identf, [P, self.ctm], [g])
        self.act(PT[:], v4(g[:, :]), AF.Copy, [g], [PT])
        self.E("dve", "tensor_tensor", [PT, self.ctm], [MT], out=MT[:], in0=PT[:], in1=hb4(identf), op=ALU.add)
        for lev in range(1, nlev + 1):
            Pn, PTn = w.Pf[lev % 2], w.PTf[lev % 2]
            g1 = self.G()
            for h in range(4):
                self.mm(g1[:, h * 128:(h + 1) * 128], PT[:, h, :], P[:, h, :], True, True, [P, PT], [g1], r32=True)
            self.act(Pn[:], v4(g1[:, :]), AF.Copy, [g1], [Pn])
            if lev < nlev:
                g2 = self.G()
                for h in range(4):
                    self.mm(g2[:, h * 128:(h + 1) * 128], P[:, h, :], PT[:, h, :], True, True, [P, PT], [g2], r32=True)
                self.E("dve", "tensor_copy", [g2], [PTn], out=PTn[:], in_=v4(g2[:, :]))
            g3 = self.G()
            for h in range(4):
                self.mm(g3[:, h * 128:(h + 1) * 128], Pn[:, h, :], MT[:, h, :], True, True, [Pn, MT], [g3], r32=True)
            self.E("dve", "tensor_tensor", [g3, MT], [MT], out=MT[:], in0=v4(g3[:, :]), in1=MT[:], op=ALU.add)
            P, PT = Pn, PTn
            yield
        self.act(MTb[:], MT[:], AF.Copy, [MT], [MTb])
        yield
        self.E("pool", "tensor_tensor", [w.QnT, fd], [w.QdT], out=w.QdT[:], in0=w.QnT[:], in1=v4(fd[:]), op=ALU.mult)
        self.E("dve", "tensor_tensor", [w.Vtok, sc], [w.Vtok], out=w.Vtok[:], in0=w.Vtok[:], in1=bc4(beta), op=ALU.mult)
        self.E("pool", "tensor_tensor", [w.Ktok, sc], [w.kdec], out=w.kdec[:], in0=w.Ktok[:], in1=bc4(kds), op=ALU.mult)
        if not sample:
            Sf, Sb = self.Sf, self.Sb
            g = self.G()
            for h in range(4):
                self.mm(g[:, h * 128:(h + 1) * 128], w.KnT[:, h, :], Sb[:, h, :], True, True, [w.KnT, Sb], [g])
            self.E("dve", "tensor_tensor", [g, sc], [fa], out=v4(fa[:]), in0=v4(g[:, :]), in1=bc4(negbg), op=ALU.mult)
            self.E("pool", "tensor_tensor", [fa, w.Vtok], [w.W], out=w.W[:], in0=v4(fa[:]), in1=w.Vtok[:], op=ALU.add)
            g = self.G()
            for h in range(4):
                self.mm(g[:, h * 128:(h + 1) * 128], MTb[:, h, :], w.W[:, h, :], True, True, [MTb, w.W], [g])
            self.act(w.vnew[:], v4(g[:, :]), AF.Copy, [g], [w.vnew])
            yield
            if own:
                po = self.G()
                po_dn = po
                for h in range(4):
                    self.mm(po[:, h * 128:(h + 1) * 128], w.QdT[:, h, :], Sb[:, h, :], True, False, [w.QdT, Sb], [po])
                    self.mm(po[:, h * 128:(h + 1) * 128], w.intraT[:, h, :], w.vnew[:, h, :], False, True, [w.intraT, w.vnew], [po])
            g = self.G()
            for h in range(4):
                self.mm(g[:, h * 128:(h + 1) * 128], w.kdec[:, h, :], w.vnew[:, h, :], True, True, [w.kdec, w.vnew], [g])
            self.E("dve", "tensor_tensor", [Sf, fd], [Sf], out=Sf[:], in0=Sf[:], in1=v4(fd[:])[:, :, 127:128].to_broadcast([128, 4, 128]), op=ALU.mult)
            self.E("dve", "tensor_tensor", [g, Sf], [Sf], out=Sf[:], in0=v4(g[:, :]), in1=Sf[:], op=ALU.add)
            self.act(Sb[:], Sf[:], AF.Copy, [Sf], [Sb])
            if last_prompt:
                self.dma(o["sp_state"].rearrange("h k v -> k h v"), Sf[:], reads=[Sf])
        else:
            colmask, rowmask = w.colmask, w.rowmask
            cm3 = colmask.rearrange("p (s t) -> p s t", s=NSEQ)
            s0v = i["s0"]
            po = self.po[1]
            for h in range(4):
                Sfh, Sbh = w.Sfh[0], w.Sbh[0]
                self.dma(Sfh[:], s0v[:, h, :, :].rearrange("s k v -> k s v"), writes=[Sfh])
                self.E("pool", "tensor_copy", [Sfh], [Sbh], out=Sbh[:], in_=Sfh[:])
                Km, Qm, kdm = w.Km, w.Qm, w.kdm
                self.E("pool", "tensor_tensor", [w.KnT, w.tabt], [Km], out=Km[:], in0=w.KnT[:, h, :].unsqueeze(1).to_broadcast([128, NSEQ, 128]), in1=cm3, op=ALU.mult)
                g = self.G()
                for s_ in range(NSEQ):
                    self.mm(g[:, 0:128], Km[:, s_, :], Sbh[:, s_, :], s_ == 0, s_ == NSEQ - 1, [Km, Sbh], [g])
                self.E("dve", "tensor_scalar", [g, sc], [fa], out=fa[:, 0:128], in0=g[:, 0:128], scalar1=sc[:, 20 + h:21 + h], scalar2=None, op0=ALU.mult)
                self.E("pool", "tensor_tensor", [fa, w.Vtok], [w.W], out=w.W[:, h, :], in0=fa[:, 0:128], in1=w.Vtok[:, h, :], op=ALU.add)
                g = self.G()
                self.mm(g[:, 0:128], MTb[:, h, :], w.W[:, h, :], True, True, [MTb, w.W], [g])
                self.act(w.vnew[:, h, :], g[:, 0:128], AF.Copy, [g], [w.vnew])
                self.E("dve", "tensor_tensor", [w.QdT, w.tabt], [Qm], out=Qm[:], in0=w.QdT[:, h, :].unsqueeze(1).to_broadcast([128, NSEQ, 128]), in1=cm3, op=ALU.mult)
                for s_ in range(NSEQ):
                    self.mm(po[:, h * 128:(h + 1) * 128], Qm[:, s_, :], Sbh[:, s_, :], s_ == 0, False, [Qm, Sbh], [po])
                self.mm(po[:, h * 128:(h + 1) * 128], w.intraT[:, h, :], w.vnew[:, h, :], False, True, [w.intraT, w.vnew], [po])
                Sn = w.Sn[0]
                self.E("pool", "tensor_tensor", [w.kdec, w.tabt], [kdm], out=kdm[:], in0=w.kdec[:, h, :].unsqueeze(1).to_broadcast([128, NSEQ, 128]),
                       in1=rowmask[:, 0:NSEQ].unsqueeze(2).to_broadcast([128, NSEQ, 128]), op=ALU.mult)
                for q4 in range(4):
                    g = self.G()
                    for k in range(4):
                        s_ = q4 * 4 + k
                        self.mm(g[:, k * 128:(k + 1) * 128], kdm[:, s_, :], w.vnew[:, h, :], True, True, [kdm, w.vnew], [g])
                    for k in range(4):
                        s_ = q4 * 4 + k
                        self.E("dve" if k % 2 == 0 else "pool" if False else "dve", "scalar_tensor_tensor", [g, Sfh, fd], [Sn], out=Sn[:, s_, :], in0=Sfh[:, s_, :],
                               scalar=fd[:, h * 128 + s_ * TS + TS - 1: h * 128 + s_ * TS + TS], in1=g[:, k * 128:(k + 1) * 128], op0=ALU.mult, op1=ALU.add)
                self.dma(o["ss_state"][:, h, :, :].rearrange("s k v -> k s v"), Sn[:], reads=[Sn])
        yield
        if own:
            po = self.po[1] if sample else po_dn
            self.act(fa[:], po[:, :], AF.Square, [po], [fa])
            self.E("dve", "tensor_reduce", [fa], [w.ss8], out=w.ss8[:, 0:4], in_=v4(fa[:]), axis=AX.X, op=ALU.add)
            self.rsqrt_ops(w.ss8, w.rs8, 4, 1.0 / DND)
            self.E("dve", "tensor_tensor", [po, w.rs8], [fa], out=v4(fa[:]), in0=v4(po[:, :]), in1=bc4(w.rs8[:, 0:4]), op=ALU.mult)
            self.E("pool", "tensor_tensor", [fa, self.sv], [fa], out=v4(fa[:]), in0=v4(fa[:]), in1=hb4(self.gdn), op=ALU.mult)
            zs = w.zs
            self.act(fb[:], zs[:], AF.Exp, [zs], [fb], scale=-1.0)
            self.act(fb[:], fb[:], AF.Ln, [fb], [fb], bias=1.0)
            self.act(fb[:], fb[:], AF.Exp, [fb], [fb], scale=-1.0)
            self.E("dve", "tensor_tensor", [fb, zs], [fb], out=fb[:], in0=fb[:], in1=zs[:], op=ALU.mult)
            self.E("dve", "tensor_tensor", [fa, fb], [w.mixed], out=w.mixed[:, 512:1024], in0=fa[:], in1=fb[:], op=ALU.mult)

    def attn_step(self, w, S, nh, nq, qT, kT, vv, nk, kvl, kvl_t, brow_ap, maskneg, mask01, mask_t, O, o_cols, first, last,
                  qreads, kreads, vreads, att_out=None, att_lhs=None, first_o=None, last_o=None):
        W_ = nh * nq
        et, spt, att, Rb = S.et, S.spt, S.att, S.Rb
        Z = S.zbank
        for p_ in range(nh // 2):
            self.mm(Z[0:nk, p_ * 2 * nq:(p_ + 1) * 2 * nq], kT[p_], qT[p_], p_ == 0, False, qreads + kreads, [Z], skip=True)
        self.mm(Z[0:nk, 0:W_], kvl, brow_ap, False, True, [kvl_t, self.brow], [Z], skip=True)
        yield
        self.act(et[0:nk, 0:W_], Z[0:nk, 0:W_], AF.Exp, [Z], [et])
        self.act(spt[0:nk, 0:W_], et[0:nk, 0:W_], AF.Ln, [et], [spt], bias=1.0)
        if mask01 is not None:
            self.E("dve", "tensor_tensor", [spt, mask_t], [spt], out=spt[0:nk, 0:W_], in0=spt[0:nk, 0:W_], in1=mask01, op=ALU.mult)
        yield
        U = Z
        fin = first and maskneg is None
        self.mm(U[0:nk, 0:W_], self.ntrib[0:nk, 0:nk], spt[0:nk, 0:W_], False, fin, [spt, self.cbf], [U], skip=True)
        if not first:
            self.mm(U[0:nk, 0:W_], self.negonesb[:, 0:nk], Rb[:, 0:W_], False, maskneg is None, [Rb, self.cbf], [U], skip=True)
        if maskneg is not None:
            self.mm(U[0:nk, 0:W_], self.identb[0:nk, 0:nk], maskneg, False, True, [mask_t, self.cbf], [U], skip=True)
        yield
        if att_out is None:
            self.act(att[0:nk, 0:W_], U[0:nk, 0:W_], AF.Exp, [U], [att])
        else:
            self.act(att_out[0], U[0:nk, 0:W_].rearrange("p (h q) -> p h q", h=nh), AF.Exp, [U], [att_out[1]])
        if not last:
            if first:
                self.E("dve", "tensor_copy", [spt], [Rb], out=Rb[0:nk, 0:W_], in_=spt[0:nk, 0:W_])
            else:
                self.E("dve", "tensor_tensor", [spt, Rb], [Rb], out=Rb[0:nk, 0:W_], in0=Rb[0:nk, 0:W_], in1=spt[0:nk, 0:W_], op=ALU.add)
        yield
        fo = first if first_o is None else first_o
        lo = last if last_o is None else last_o
        for h in range(nh):
            if att_lhs is None:
                lhs = att[0:nk, h * nq:(h + 1) * nq]
                rd = [att]
            else:
                lhs = att_lhs[0][h]
                rd = [att_lhs[1]]
            self.mm(O[o_cols[h]], lhs, vv[h], fo and h == 0, lo, rd + vreads, [O], skip=True)
        yield

    @staticmethod
    def interleave(gens):
        gens = list(gens)
        while gens:
            for g in list(gens):
                try:
                    next(g)
                except StopIteration:
                    gens.remove(g)

    def attn_prompt(self, w, b, dn_gen=None):
        c = self.cfg

        def stream(hg):
            O = self.po[hg]
            S = w.streams[hg]
            for kb in range(b, -1, -1):
                qT = [w.QT[:, hg * 2 + p_, :, :] for p_ in range(2)]
                kT = [self.KTt[kb][:, hg * 2 + p_, :] for p_ in range(2)]
                vv = [self.Vt[kb][:, (hg * 4 + h) * 64:(hg * 4 + h + 1) * 64] for h in range(4)]
                diag = kb == b
                kvl = self.kvd if kb < c.OUT0 else self.kvone
                yield from self.attn_step(w, S, 4, 128, qT, kT, vv, 128, kvl[:, :], kvl,
                                          self.brow[:, hg * 512:(hg + 1) * 512],
                                          w.causrep[:, 0:512] if diag else None, w.caus01rep[:, 0:512] if diag else None, w.causrep_t,
                                          O, [(slice(None), slice(h * 64, (h + 1) * 64)) for h in range(4)],
                                          kb == b, kb == 0, [w.QT], [self.KTt[kb]], [self.Vt[kb]])
            self.head_norm(w, O[:, 0:256], 4, HD, w.f512, [O], self.gso, w.mixed, w.mixed[:, hg * 256:(hg + 1) * 256], 1.0 / HD)
        gens = [stream(0), stream(1)]
        n_rounds = 5 * (b + 1) + 1
        stride = max(1, n_rounds // 24)
        rnd = 0
        dn_live = dn_gen is not None
        while gens or dn_live:
            if dn_live and (rnd % stride == 0 or not gens):
                try:
                    next(dn_gen)
                except StopIteration:
                    dn_live = False
            for g_ in list(gens):
                try:
                    next(g_)
                except StopIteration:
                    gens.remove(g_)
            rnd += 1

    def attn_sample(self, w):
        c = self.cfg
        i = self.i
        npg = c.NPG
        O = self.po[0]
        ck = i["cache_k"]
        cv = i["cache_v"]

        NSTR = len(w.streams)

        def stream(si):
            S = w.streams[si]
            for s_ in range(si, NSEQ, NSTR):
                attpad = w.attpad[si]
                self.E("pool", "memset", [], [attpad], attpad[:], 0.0)
                qT = [w.QT[:, p_, :, s_ * TS:(s_ + 1) * TS] for p_ in range(4)]
                att_out = (attpad[:, :, s_ * TS:(s_ + 1) * TS], attpad)
                att_lhs = ([attpad[:, h, :] for h in range(8)], attpad)
                o_cols = [(slice(None), slice(h * 64, (h + 1) * 64)) for h in range(8)]
                for blk in range(npg, -1, -1):
                    if blk == npg:
                        kT = [w.KTs[:, p_, :] for p_ in range(4)]
                        vv = [w.Vs[:, h * 64:(h + 1) * 64] for h in range(8)]
                        kreads, vreads = [w.KTs], [w.Vs]
                        mneg, m01 = w.smneg[:, s_, :], w.sm01[:, s_, :]
                    else:
                        j = s_ * npg + blk
                        kk = si
                        kpf, vpf, kpb, vpb, ktp = w.kpf[kk], w.vpf[kk], w.kpb[kk], w.vpb[kk], w.ktp[kk]
                        self.s.add("pool", lambda e, kpf=kpf, j=j: e.indirect_dma_start(
                            out=kpf[:], out_offset=None, in_=ck, in_offset=bass.IndirectOffsetOnAxis(ap=w.idx[:, j:j + 1], axis=0)),
                            [w.idx], [kpf], is_dma=True)
                        self.s.add("pool", lambda e, vpf=vpf, j=j: e.indirect_dma_start(
                            out=vpf[:], out_offset=None, in_=cv, in_offset=bass.IndirectOffsetOnAxis(ap=w.idx[:, j:j + 1], axis=0)),
                            [w.idx], [vpf], is_dma=True)
                        self.E("dve", "tensor_copy", [kpf], [kpb], out=kpb[:], in_=kpf[:])
                        self.act(vpb[:], vpf[:], AF.Copy, [vpf], [vpb])
                        tb = self.TB()
                        for pr in range(4):
                            self.tr(tb[:, pr * 128:(pr + 1) * 128], kpb[:, pr * 128:(pr + 1) * 128], self.identb, [kpb, self.cbf], [tb])
                        self.E("dve", "tensor_copy", [tb], [ktp], out=ktp[:], in_=tb[:, 0:512].rearrange("p (a b) -> p a b", a=4))
                        kT = [ktp[:, p_, :] for p_ in range(4)]
                        vv = [vpb[:, h * 64:(h + 1) * 64] for h in range(8)]
                        kreads, vreads = [ktp], [vpb]
                        mneg, m01 = None, None
                    yield from self.attn_step(w, S, 8, TS, qT, kT, vv, 128, self.kvone[:, :], self.kvone, self.brow[:, 1024:1088],
                                              mneg, m01, w.smt, O, o_cols, blk == npg, blk == 0, [w.QT], kreads, vreads,
                                              att_out=att_out, att_lhs=att_lhs,
                                              first_o=(s_ == 0 and blk == npg), last_o=(s_ == NSEQ - 1 and blk == 0))
        self.interleave([stream(k) for k in range(NSTR)])
        self.head_norm(w, O[:, :], HS, HD, w.f512, [O], self.gso, w.mixed, w.mixed[:, 0:512], 1.0 / HD)

    def alloc_work(self, st, sample):
        class WS:
            pass
        w = WS()
        sb = lambda name, shape, dt: self.sb(st, name, shape, dt)
        w.xt = [sb("xt", [128, D], F32)]
        w.h = sb("h", [128, D], BF16)
        w.hT = sb("hT", [128, DC, 128], BF16)
        w.ssq = sb("ssq", [128, 1], F32)
        w.rstd = sb("rstd", [128, 1], F32)
        w.ss8 = sb("ss8", [128, 8], F32)
        w.rs8 = sb("rs8", [128, 8], F32)
        w.f512 = sb("f512", [128, 512], F32)
        w.kn = sb("kn", [128, 512], F32)
        w.knb = sb("knb", [128, 512], BF16)
        w.vf = w.f512
        w.qnb = sb("qnb", [128, 512], BF16)
        w.QT = sb("QT", [128, 4, 2, 128], BF16)
        self.E("pool", "memset", [], [w.QT], w.QT[:], 0.0)
        ns, T = (NSEQ, TS) if sample else (1, 128)
        w.XE4 = sb("XE4", [128, 4, ns * (3 + T)], F32)
        w.Hst = sb("Hst", [128, 12, ns * 3], F32)
        w.ba = sb("ba", [128, 8], F32)
        w.zs = sb("zs", [128, 512], BF16)
        w.Y4 = sb("Y4", [128, 4, 128], F32)
        w.E4 = sb("E4", [128, 4, 128], F32)
        w.QnT = sb("QnT", [128, 4, 128], BF16)
        w.KnT = sb("KnT", [128, 4, 128], BF16)
        w.Ktok = sb("Ktok", [128, 4, 128], BF16)
        w.Vtok = sb("Vtok", [128, 4, 128], F32)
        w.sc = sb("sc", [128, 40], F32)
        w.dg = [sb("dg", [128, 128], F32)] * 2
        w.fa = sb("fa", [128, 512], F32)
        w.fb = sb("fb", [128, 512], F32)
        w.fc = sb("fc", [128, 512], F32)
        w.fd = sb("fd", [128, 512], F32)
        w.Pf = [sb("Pf%d" % k, [128, 4, 128], F32) for k in range(2)]
        w.PTf = [sb("PTf%d" % k, [128, 4, 128], F32) for k in range(2)]
        w.MT = sb("MT", [128, 4, 128], F32)
        w.MTb = sb("MTb", [128, 4, 128], BF16)
        w.intraT = sb("intraT", [128, 4, 128], BF16)
        w.QdT = sb("QdT", [128, 4, 128], BF16)
        w.kdec = w.Ktok
        w.W = sb("W", [128, 4, 128], BF16)
        w.Vcb = w.W
        w.vnew = sb("vnew", [128, 4, 128], BF16)
        w.mixed = sb("mixed", [128, D], BF16)
        wd = 64 if sample else 512
        class ST:
            pass
        w.streams = []
        for k in range(2):
            S = ST()
            S.zbank = self.zb[k]
            S.et = sb("et%d" % k, [128, wd], BF16)
            S.spt = sb("spt%d" % k, [128, wd], BF16)
            S.att = sb("att%d" % k, [128, wd], BF16)
            S.Rb = sb("Rb%d" % k, [128, wd], BF16)
            w.streams.append(S)
        return w

    def phase1(self):
        c = self.cfg
        i, o = self.i, self.o
        self.xi = 0
        self.ai = 0
        self.pgi = 0
        with contextlib.ExitStack() as p1:
            winb = self.sb(p1, "winb", [128, DC, INC], BF16)
            self.winb = winb
            scale1 = self.sb(p1, "scale1", [128, D], F32)
            shift1 = self.sb(p1, "shift1", [128, D], F32)
            sv = self.sv
            WB = 512 * 2 + 64
            brow = self.sb(p1, "brow", [128, WB], BF16)
            nb = self.sb(p1, "nb", [128, 1], F32)
            with contextlib.ExitStack() as st:
                bexp = self.sb(st, "bexp", [128, WB], F32)
                for hg in range(2):
                    for h in range(4):
                        self.E("dve", "tensor_copy", [sv], [bexp], out=bexp[:, hg * 512 + h * 128: hg * 512 + (h + 1) * 128],
                               in_=sv[:, 328 + hg * 4 + h: 329 + hg * 4 + h].to_broadcast([128, 128]))
                for h in range(8):
                    self.E("dve", "tensor_copy", [sv], [bexp], out=bexp[:, 1024 + h * 8: 1024 + (h + 1) * 8],
                           in_=sv[:, 328 + h: 329 + h].to_broadcast([128, 8]))
                bhi = self.sb(st, "bhi", [128, WB], BF16)
                self.brow = brow
                idf = self.cm("ident")
                self.E("dve", "tensor_copy", [bexp], [bhi], out=bhi[:], in_=bexp[:])
                self.E("dve", "tensor_tensor", [bexp, bhi], [bexp], out=bexp[:], in0=bexp[:], in1=bhi[:], op=ALU.subtract)
                self.E("dve", "tensor_scalar", [bexp, self.ctm], [bexp], out=bexp[:], in0=bexp[:], scalar1=idf[:, 1:2], scalar2=None, op0=ALU.mult)
                self.E("dve", "scalar_tensor_tensor", [bhi, bexp, self.ctm], [bexp], out=bexp[:], in0=bhi[:], scalar=idf[:, 0:1], in1=bexp[:],
                       op0=ALU.mult, op1=ALU.add)
                self.E("dve", "tensor_scalar", [self.ctm], [nb], out=nb[:], in0=idf[:, 2:3], scalar1=-BIG, scalar2=None, op0=ALU.mult)
                self.E("dve", "tensor_scalar", [bexp, nb], [brow], out=brow[:], in0=bexp[:], scalar1=nb[:, 0:1], scalar2=None, op0=ALU.add)
                stg = [self.sb(st, "stg%d" % k, [128, DC * 512], F32) for k in range(2)]
                wv = i["w_in"].rearrange("(c p) n -> p c n", p=128)
                for ct in range(8):
                    n0, n1 = ct * 512, min(INC, (ct + 1) * 512)
                    self.stream_cast(stg, wv[:, :, n0:n1], winb, winb[:, :, n0:n1], eng="act" if ct % 2 else "dve")
            self.fence()
            with contextlib.ExitStack() as st:
                KT = self.sb(st, "KT", [128, 4, c.NBLK * 128], BF16)
                Vr = self.sb(st, "Vr", [128, c.NBLK, 512], BF16)
                self.KTt = [Tile(KT[:, :, b * 128:(b + 1) * 128], "KT%d" % b) for b in range(c.NBLK)]
                self.Vt = [Tile(Vr[:, b, :], "V%d" % b) for b in range(c.NBLK)]
                w = self.alloc_work(st, False)
                w.scale1, w.shift1 = scale1, shift1
                w.tabt = self.ctm
                w.causrep = self.sb(st, "causrep", [128, 512], BF16)
                w.caus01rep = self.sb(st, "caus01rep", [128, 512], BF16)
                w.causrep_t = self.sb(st, "causrep_t", [1, 1], F32)
                for h in range(4):
                    self.E("dve", "tensor_copy", [self.ctm], [w.causrep_t, w.causrep], out=w.causrep[:, h * 128:(h + 1) * 128], in_=self.cm("causneg"))
                    self.E("dve", "tensor_copy", [self.ctm], [w.causrep_t, w.caus01rep], out=w.caus01rep[:, h * 128:(h + 1) * 128], in_=self.cm("caus01"))
                self.Sf = self.sb(st, "Sf", [128, 4, 128], F32)
                self.Sb = self.sb(st, "Sb", [128, 4, 128], BF16)
                self.E("pool", "memset", [], [self.Sf], self.Sf[:], 0.0)
                self.E("pool", "memset", [], [self.Sb], self.Sb[:], 0.0)
                self.load_mod([shift1, scale1], [0, 1], False)
                tabs = (self.cm("ltincl_p"), self.cm("ones"), self.cm("ms01_p"))
                for b in range(c.NBLK):
                    own = b >= c.OWN0
                    outrow = None
                    if b >= c.OUT0:
                        r0 = (b - c.OUT0) * 128
                        outrow = (o["kp"][r0:r0 + 128, :], o["vp"][r0:r0 + 128, :])
                    self.front_end(w, b, False, own, outrow)
                    dn_gen = self.dn_chunk(w, b, False, own, tabs, b == c.NBLK - 1) if self.on("dn") else iter(())
                    if own and self.on("attn"):
                        self.attn_prompt(w, b, dn_gen)
                    else:
                        for _ in dn_gen:
                            pass
                    if own:
                        k = b - c.OWN0
                        if self.on("dn") and self.on("attn"):
                            self.dma(self.mixd[k * 128:(k + 1) * 128, :], w.mixed[:], reads=[w.mixed], writes=[self.t_mixd[k]])
                        if "mixed_p" in self.o and b >= c.OUT0:
                            r0 = (b - c.OUT0) * 128
                            self.E("dve", "tensor_copy", [w.mixed], [w.xt[0]], out=w.xt[0][:], in_=w.mixed[:])
                            self.dma(self.o["mixed_p"][r0:r0 + 128, :], w.xt[0][:], reads=[w.xt[0]])
            self.fence()
            if self.on("sample"):
                with contextlib.ExitStack() as st:
                    w = self.alloc_work(st, True)
                    w.scale1, w.shift1 = scale1, shift1
                    cts = self.sb(st, "cts", list(CT_SAMP.shape), F32)
                    self.dma(cts[:], i["ct_samp"][:, :], writes=[cts])
                    w.tabt = cts

                    def cs(name):
                        o_, w_ = CO_SAMP[name]
                        return cts[:, o_:o_ + w_]
                    w.colmask, w.rowmask = cs("colmask"), cs("rowmask")
                    w.KTs = self.sb(st, "KTs", [128, 4, 128], BF16)
                    w.Vs = self.sb(st, "Vs", [128, 512], BF16)
                    w.Sfh = [self.sb(st, "Sfh", [128, NSEQ, 128], F32)]
                    w.Sbh = [self.sb(st, "Sbh", [128, NSEQ, 128], BF16)]
                    w.Sn = w.Sfh
                    w.Km = self.sb(st, "Km", [128, NSEQ, 128], BF16)
                    w.Qm = w.Km
                    w.kdm = w.Km
                    self.load_mod([shift1, scale1], [0, 1], True)
                    hst_t = w.Sfh[0]
                    hst = hst_t[:, :, :].rearrange("p s v -> p (s v)")[0:NSEQ * 3, 0:3 * DNW]
                    self.dma(hst, i["dnc0"][:, :], writes=[hst_t])
                    for j in range(3):
                        g = self.G()
                        for ch in range(4):
                            self.tr(g[:, ch * 48:(ch + 1) * 48], hst[:, (j * 4 + ch) * 128:(j * 4 + ch + 1) * 128], self.cm("ident")[0:48, 0:48], [hst_t, self.ctm], [g])
                        self.act(w.Hst[:, j * 4:(j + 1) * 4, :], g[:, 0:192].rearrange("p (c t) -> p c t", c=4), AF.Copy, [g], [w.Hst])
                    self.front_end(w, 0, True, True, (o["ksm"][:, :], o["vsm"][:, :]))
                    tabs = (cs("ltincl_s"), cs("seqm_s"), cs("ms01_s"))
                    if self.on("dn"):
                        for _ in self.dn_chunk(w, 0, True, True, tabs, False):
                            pass
                    if self.on("attn"):
                        pti = self.sb(st, "pti", [128, NSEQ * c.NPG], I32)
                        ptf_t = w.f512
                        ptf = ptf_t
                        io = self.sb(st, "io", [128, 1], I32)
                        iof = self.sb(st, "iof", [128, 1], F32)
                        w.idx = self.sb(st, "idx", [128, NSEQ * c.NPG], I32)
                        self.dma(pti[:], i["ptab"].partition_broadcast(128), writes=[pti])
                        self.E("pool", "iota", [], [io], io[:], pattern=[[0, 1]], base=0, channel_multiplier=1)
                        self.E("dve", "tensor_copy", [io], [iof], out=iof[:], in_=io[:])
                        npt = NSEQ * c.NPG
                        self.E("dve", "tensor_copy", [pti], [ptf], out=ptf[:, 0:npt], in_=pti[:])
                        self.E("dve", "tensor_scalar", [ptf, iof], [ptf], out=ptf[:, 0:npt], in0=ptf[:, 0:npt], scalar1=128.0, scalar2=iof[:, 0:1], op0=ALU.mult, op1=ALU.add)
                        self.E("dve", "tensor_copy", [ptf], [w.idx], out=w.idx[:], in_=ptf[:, 0:npt])
                        w.smt = self.sb(st, "smt", [1, 1], F32)
                        sm01 = self.sb(st, "sm01", [128, NSEQ, 64], BF16)
                        smneg = self.sb(st, "smneg", [128, NSEQ, 64], BF16)
                        o_, w_ = CO_SAMP["smask01"]
                        src = cts[:, o_:o_ + w_].rearrange("p (s q) -> p s q", s=NSEQ)
                        self.E("dve", "tensor_copy", [cts], [w.smt, sm01], out=sm01[:], in_=src)
                        self.E("dve", "tensor_scalar", [cts], [w.smt, smneg], out=smneg[:], in0=src, scalar1=-1.0, scalar2=BIG, op0=ALU.add, op1=ALU.mult)
                        w.sm01, w.smneg = sm01, smneg
                        w.attpad = [self.sb(st, "attpad%d" % k, [128, 8, 128], BF16) for k in range(2)]
                        w.kpf = [self.sb(st, "kpf%d" % k, [128, 512], F32) for k in range(4)]
                        w.vpf = [self.sb(st, "vpf%d" % k, [128, 512], F32) for k in range(4)]
                        w.kpb = [self.sb(st, "kpb%d" % k, [128, 512], BF16) for k in range(4)]
                        w.vpb = [self.sb(st, "vpb%d" % k, [128, 512], BF16) for k in range(4)]
                        w.ktp = [self.sb(st, "ktp%d" % k, [128, 4, 128], BF16) for k in range(4)]
                        self.attn_sample(w)
                    k = c.NOWN
                    if self.on("dn") and self.on("attn"):
                        self.dma(self.mixd[k * 128:(k + 1) * 128, :], w.mixed[:], reads=[w.mixed], writes=[self.t_mixd[k]])
                    if "mixed_s" in self.o:
                        self.E("dve", "tensor_copy", [w.mixed], [w.xt[0]], out=w.xt[0][:], in_=w.mixed[:])
                        self.dma(self.o["mixed_s"][:, :], w.xt[0][:], reads=[w.xt[0]])

    def phase2(self):
        c = self.cfg
        i, o = self.i, self.o
        with contextlib.ExitStack() as p2:
            sb = lambda name, shape, dt: self.sb(p2, name, shape, dt)
            woutb = sb("woutb", [128, DC, D], BF16)
            wupb = sb("wupb", [128, DC, 2 * DFF], BF16)
            wdnb = sb("wdnb", [128, FC, D], BF16)
            with contextlib.ExitStack() as st:
                stg = [self.sb(st, "stg%d" % k, [128, DC * 512], F32) for k in range(2)]
                n = 0
                wv = i["w_out"].rearrange("(c p) n -> p c n", p=128)
                for ct in range(2):
                    self.stream_cast(stg, wv[:, :, ct * 512:(ct + 1) * 512], woutb, woutb[:, :, ct * 512:(ct + 1) * 512], eng="act" if n % 2 else "dve")
                    n += 1
                wv = i["w_up"].rearrange("(c p) n -> p c n", p=128)
                for ct in range(11):
                    self.stream_cast(stg, wv[:, :, ct * 512:(ct + 1) * 512], wupb, wupb[:, :, ct * 512:(ct + 1) * 512], eng="act" if n % 2 else "dve")
                    n += 1
                wv = i["w_down"].rearrange("(c p) n -> p c n", p=128)
                for c0 in range(0, FC, 4):
                    c1 = min(FC, c0 + 4)
                    self.stream_cast(stg, wv[:, c0:c1, :], wdnb, wdnb[:, c0:c1, :], eng="act" if n % 2 else "dve")
                    n += 1
            self.fence()
            gt1, scale2, shift2, gt2 = [sb(nm, [128, D], F32) for nm in ("gt1", "scale2", "shift2", "gt2")]

            class WS:
                pass
            w = WS()
            w.xt = [sb("xt2", [128, D], F32)]
            w.h = sb("h2", [128, D], BF16)
            w.hT = sb("h2T", [128, DC, 128], BF16)
            w.ssq = sb("ssq2", [128, 1], F32)
            w.rstd = sb("rstd2", [128, 1], F32)
            mixb = sb("mixb", [128, D], BF16)
            mT = sb("mT", [128, DC, 128], BF16)
            yt = sb("yt", [128, D], F32)
            UE = sb("UE", [128, 4, NSEQ * (2 + TS)], F32)
            C4 = sb("C4", [128, 4, 128], F32)
            E2 = sb("E2", [128, 2, 128], F32)
            actT = sb("actT", [128, FC, 128], BF16)
            FH = sb("FH", [128, 44, NSEQ * 2], F32)
            fso = sb("fso", [NSEQ * 2, 512], F32)
            self.E("pool", "memset", [], [FH], FH[:], 0.0)
            blocks = [(b, False) for b in range(c.OWN0, c.NBLK)] + ([(0, True)] if self.on("sample") else [])
            cur_mod = None
            for (b, sample) in blocks:
                if cur_mod != sample:
                    self.load_mod([gt1, scale2, shift2, gt2], [2, 4, 3, 5], sample)
                    cur_mod = sample
                ns, T = (NSEQ, TS) if sample else (1, 128)
                k = c.NOWN if sample else b - c.OWN0
                halo = (not sample) and b == c.OWN0
                xt = w.xt[0]
                self.dma(xt[:], i["xs"][:, :] if sample else i["xp"][b * 128:(b + 1) * 128, :], writes=[xt])
                self.dma(mixb[:], self.mixd[k * 128:(k + 1) * 128, :], reads=[self.t_mixd[k]], writes=[mixb])
                tb = self.TB()
                for dc in range(DC):
                    self.tr(tb[:, dc * 128:(dc + 1) * 128], mixb[:, dc * 128:(dc + 1) * 128], self.identb, [mixb, self.cbf], [tb])
                self.act(mT[:], tb[:, :].rearrange("p (a b) -> p a b", a=DC), AF.Copy, [tb], [mT])
                for n in range(2):
                    g = self.G()
                    for dc in range(DC):
                        self.mm(g[:, :], mT[:, dc, :], woutb[:, dc, n * 512:(n + 1) * 512], dc == 0, dc == DC - 1, [mT, woutb], [g])
                    self.E("dve", "tensor_tensor", [g, gt1], [yt], out=yt[:, n * 512:(n + 1) * 512], in0=g[:, :], in1=gt1[:, n * 512:(n + 1) * 512], op=ALU.mult)
                self.E("pool", "tensor_tensor", [yt, xt], [xt], out=xt[:], in0=yt[:], in1=xt[:], op=ALU.add)
                x1 = xt
                if "x1_p" in o and (not sample) and b >= c.OUT0:
                    r0 = (b - c.OUT0) * 128
                    self.dma(o["x1_p"][r0:r0 + 128, :], x1[:], reads=[x1])
                self.norm_mod(w, x1, scale2, shift2, w.hT, tmp=yt)
                if sample:
                    for j in range(11):
                        hst = fso
                        self.dma(hst[:, :], i["ffc0"][:, j * 512:(j + 1) * 512], writes=[hst])
                        g = self.G()
                        for ch in range(4):
                            self.tr(g[:, ch * 32:(ch + 1) * 32], hst[:, ch * 128:(ch + 1) * 128], self.cm("ident")[0:32, 0:32], [hst, self.ctm], [g])
                        self.act(FH[:, j * 4:(j + 1) * 4, :], g[:, 0:128].rearrange("p (c t) -> p c t", c=4), AF.Copy, [g], [FH])
                ue4 = UE[:, :, 0:ns * (2 + T)].rearrange("p c (s t) -> p c s t", s=ns)
                fh4 = FH[:, :, 0:ns * 2].rearrange("p c (s t) -> p c s t", s=ns)
                for gi_ in range(11):
                    chs = [2 * gi_, 2 * gi_ + 1, FC + 2 * gi_, FC + 2 * gi_ + 1]
                    g = self.G()
                    for q_, ch in enumerate(chs):
                        for dc in range(DC):
                            self.mm(g[:, q_ * 128:(q_ + 1) * 128], wupb[:, dc, ch * 128:(ch + 1) * 128], w.hT[:, dc, :], dc == 0, dc == DC - 1, [w.hT, wupb], [g])
                    for half in range(2):
                        self.E("pool", "tensor_copy", [FH], [UE], out=ue4[:, half * 2:half * 2 + 2, :, 0:2], in_=fh4[:, chs[half * 2]:chs[half * 2] + 2, :, :])
                    src4 = g[:, :].rearrange("p (c s t) -> p c s t", c=4, s=ns)
                    if halo:
                        self.act(ue4[:, :, :, 2:2 + T], src4, AF.Copy, [g, self.bvd], [UE], scale=self.bvd[:, b:b + 1])
                    else:
                        self.act(ue4[:, :, :, 2:2 + T], src4, AF.Copy, [g], [UE])
                    for half in range(2):
                        self.E("pool", "tensor_copy", [UE], [FH], out=fh4[:, chs[half * 2]:chs[half * 2] + 2, :, :], in_=ue4[:, half * 2:half * 2 + 2, :, T:T + 2])
                    if halo:
                        continue
                    for q_, ch in enumerate(chs):
                        eng = "dve"
                        yv = C4[:, q_, :].rearrange("p (s t) -> p s t", s=ns)
                        self.E(eng, "tensor_scalar", [UE, self.wfc], [C4], out=yv, in0=ue4[:, q_, :, 0:T], scalar1=self.wfc[:, 0, ch:ch + 1], scalar2=None, op0=ALU.mult)
                        for kk in range(1, 3):
                            self.E(eng, "scalar_tensor_tensor", [UE, self.wfc, C4], [C4], out=yv, in0=ue4[:, q_, :, kk:kk + T],
                                   scalar=self.wfc[:, kk, ch:ch + 1], in1=yv, op0=ALU.mult, op1=ALU.add)
                    self.act(E2[:], C4[:, 2:4, :], AF.Exp, [C4], [E2], scale=-1.0)
                    self.act(E2[:], E2[:], AF.Ln, [E2], [E2], bias=1.0)
                    self.act(E2[:], E2[:], AF.Exp, [E2], [E2], scale=-1.0)
                    self.E("pool", "tensor_tensor", [E2, C4], [E2], out=E2[:], in0=E2[:], in1=C4[:, 2:4, :], op=ALU.mult)
                    self.E("dve", "tensor_tensor", [E2, C4], [actT], out=actT[:, 2 * gi_:2 * gi_ + 2, :], in0=E2[:], in1=C4[:, 0:2, :], op=ALU.mult)
                last_p = (not sample) and b == c.NBLK - 1
                if sample or last_p:
                    ncol = ns * 2
                    dst = o["fcs"] if sample else o["fcp"]
                    for j in range(11):
                        g = self.G()
                        for ch in range(4):
                            self.tr(g[0:ncol, ch * 128:(ch + 1) * 128], FH[:, j * 4 + ch, 0:ncol], self.cm("ident"), [FH, self.ctm], [g])
                        self.E("dve", "tensor_copy", [g], [fso], out=fso[0:ncol, :], in_=g[0:ncol, :])
                        self.dma(dst[:, j * 512:(j + 1) * 512], fso[0:ncol, :], reads=[fso])
                if halo:
                    continue
                for n in range(2):
                    g = self.G()
                    for fc_ in range(FC):
                        self.mm(g[:, :], actT[:, fc_, :], wdnb[:, fc_, n * 512:(n + 1) * 512], fc_ == 0, fc_ == FC - 1, [actT, wdnb], [g])
                    self.E("dve", "tensor_tensor", [g, gt2], [yt], out=yt[:, n * 512:(n + 1) * 512], in0=g[:, :], in1=gt2[:, n * 512:(n + 1) * 512], op=ALU.mult)
                self.E("pool", "tensor_tensor", [yt, x1], [yt], out=yt[:], in0=yt[:], in1=x1[:], op=ALU.add)
                if sample:
                    self.dma(o["ys"][:, :], yt[:], reads=[yt])
                elif b >= c.OUT0:
                    r0 = (b - c.OUT0) * 128
                    self.dma(o["yp"][r0:r0 + 128, :], yt[:], reads=[yt])


def core_inputs(cfg, core, inp):
    b, half = core // 2, core % 2
    S = cfg.NBLK * 128
    xp_full = np.asarray(inp["x_prompt"][b], np.float32)
    if half == 1:
        xp = xp_full
    else:
        xp = np.concatenate([np.zeros((S // 2, D), np.float32), xp_full[:S // 2]], axis=0)
    s0, s1 = core * NSEQ, (core + 1) * NSEQ
    m = {}
    m["xp"] = np.ascontiguousarray(xp)
    m["xs"] = np.ascontiguousarray(np.asarray(inp["x_sample"][s0:s1], np.float32).reshape(NSEQ * TS, D))
    m["cvec"] = np.ascontiguousarray(np.concatenate([np.asarray(inp["c_prompt"][b:b + 1], np.float32),
                                                     np.asarray(inp["c_sample"][s0:s1], np.float32)], axis=0))
    m["cache_k"] = np.asarray(inp["cache_k"], np.float32).reshape(cfg.NPHYS * 128, SBW)
    m["cache_v"] = np.asarray(inp["cache_v"], np.float32).reshape(cfg.NPHYS * 128, SBW)
    m["ptab"] = np.ascontiguousarray(np.asarray(inp["page_table"][s0:s1], np.int32).reshape(-1))
    m["s0"] = np.ascontiguousarray(np.asarray(inp["state_delta"][0, s0:s1], np.float32))
    m["dnc0"] = np.ascontiguousarray(np.asarray(inp["state_dn_conv"][0, s0:s1], np.float32).reshape(NSEQ * 3, 3 * DNW))
    m["ffc0"] = np.ascontiguousarray(np.asarray(inp["state_ffn_conv"][0, s0:s1], np.float32).reshape(NSEQ * 2, 2 * DFF))
    for k, nm in (("w_ada", "w_ada"), ("b_ada", "b_ada"), ("g_attn", "g_attn_norm"), ("w_in", "w_in"), ("g_q", "g_q"),
                  ("g_k", "g_k"), ("sb_bias", "sb_bias"), ("g_sb_out", "g_sb_out"), ("w_dn_conv", "w_dn_conv"),
                  ("a_log", "a_log"), ("dt_bias", "dt_bias"), ("g_dn_out", "g_dn_out"), ("w_out", "w_out"),
                  ("g_ffn", "g_ffn_norm"), ("w_up", "w_up"), ("w_ffn_conv", "w_ffn_conv"), ("w_down", "w_down")):
        m[k] = np.ascontiguousarray(np.asarray(inp[nm], np.float32)[0])
    m["ct_main"] = CT_MAIN
    m["ct_samp"] = CT_SAMP
    kv = np.zeros((128, 256), np.float32)
    if half == 1:
        kv[0:2, 0:128] = 1.0
    else:
        kv[2, 0:128] = 1.0
    kv[0:2, 128:256] = 1.0
    m["kvlo"] = kv
    bv = np.ones((128, cfg.NBLK), np.float32)
    if half == 0:
        bv[:, :cfg.NBLK // 2] = 0.0
    m["blkvalid"] = bv
    return m


def assemble(cfg, res, nb, nsamp):
    S = cfg.NBLK * 128
    H = S // 2
    yp = np.zeros((nb, S, D), np.float32)
    ys = np.zeros((nsamp, TS, D), np.float32)
    kp = np.zeros((1, nb, S, HS, HD), np.float32)
    vp = np.zeros((1, nb, S, HS, HD), np.float32)
    ks = np.zeros((1, nsamp, TS, HS, HD), np.float32)
    vs = np.zeros((1, nsamp, TS, HS, HD), np.float32)
    sp = np.zeros((1, nb, DNH, DND, DND), np.float32)
    ss = np.zeros((1, nsamp, DNH, DND, DND), np.float32)
    dcp = np.zeros((1, nb, 3, 3 * DNW), np.float32)
    dcs = np.zeros((1, nsamp, 3, 3 * DNW), np.float32)
    fcp = np.zeros((1, nb, 2, 2 * DFF), np.float32)
    fcs = np.zeros((1, nsamp, 2, 2 * DFF), np.float32)
    for core, r in res.items():
        b, half = core // 2, core % 2
        s0, s1 = core * NSEQ, (core + 1) * NSEQ
        yp[b, half * H:(half + 1) * H] = r["yp"]
        kp[0, b, half * H:(half + 1) * H] = r["kp"].reshape(H, HS, HD)
        vp[0, b, half * H:(half + 1) * H] = r["vp"].reshape(H, HS, HD)
        ys[s0:s1] = r["ys"].reshape(NSEQ, TS, D)
        ks[0, s0:s1] = r["ksm"].reshape(NSEQ, TS, HS, HD)
        vs[0, s0:s1] = r["vsm"].reshape(NSEQ, TS, HS, HD)
        ss[0, s0:s1] = r["ss_state"]
        dcs[0, s0:s1] = r["dcs"].reshape(NSEQ, 3, 3 * DNW)
        fcs[0, s0:s1] = r["fcs"].reshape(NSEQ, 2, 2 * DFF)
        if half == 1:
            sp[0, b] = r["sp_state"]
            dcp[0, b] = r["dcp"]
            fcp[0, b] = r["fcp"]
    return (yp, ys, kp, vp, ks, vs, sp, ss, dcp, dcs, fcp, fcs)


_NC_CACHE = {}


def kernel(**inputs):
    cfg = Cfg(nblk=inputs["x_prompt"].shape[1] // 128, npg=inputs["page_table"].shape[1], nphys=inputs["cache_k"].shape[1])
    key = (cfg.NBLK, cfg.NPG, cfg.NPHYS)
    if key not in _NC_CACHE:
        _NC_CACHE[key] = Builder(cfg).build()
    nc = _NC_CACHE[key]
    ncores = 8
    in_maps = [core_inputs(cfg, c, inputs) for c in range(ncores)]
    res = run_bass_kernel_spmd(nc, in_maps, core_ids=list(range(ncores)))
    out = assemble(cfg, {c: res.results[c] for c in range(ncores)}, inputs["x_prompt"].shape[0], inputs["x_sample"].shape[0])
    return out
```

```python
import contextlib
import numpy as np
import concourse.bass as bass
import concourse.mybir as mybir
from concourse.bass_utils import run_bass_kernel_spmd

F32 = mybir.dt.float32
BF16 = mybir.dt.bfloat16
I32 = mybir.dt.int32
AF = mybir.ActivationFunctionType
ALU = mybir.AluOpType
AX = mybir.AxisListType

D = 1024
DC = 8
HS = 8
HD = 64
SBW = 512
DNH = 4
DND = 128
DNW = 512
DFF = 2816
FC = 22
INC = 3592
EPS = 1e-6
BIG = 30000.0
NSEQ = 16
TS = 8


class Cfg:
    def __init__(self, nblk=32, npg=16, nphys=2560):
        self.NBLK = nblk
        self.OWN0 = nblk // 2 - 1
        self.OUT0 = nblk // 2
        self.NPG = npg
        self.NPHYS = nphys
        self.NOUT = nblk - self.OUT0
        self.NOWN = nblk - self.OWN0


class Tile:
    __slots__ = ("ap", "name", "last_w", "readers", "excl")

    def __init__(self, ap, name="", excl=False):
        self.ap = ap
        self.name = name
        self.last_w = None
        self.readers = []
        self.excl = excl

    def __getitem__(self, k):
        return self.ap[k]


class Op:
    __slots__ = ("eng", "fn", "deps", "need_inc", "count", "sem", "is_dma", "idx")

    def __init__(self, eng, fn, is_dma=False):
        self.eng = eng
        self.fn = fn
        self.deps = set()
        self.need_inc = is_dma
        self.count = 0
        self.sem = None
        self.is_dma = is_dma


COMPUTE = ("pe", "act", "dve", "pool")


class Sched:
    def __init__(self, nc, n_dma_sems=16):
        self.nc = nc
        self.ops = {e: [] for e in COMPUTE + ("sp",)}
        self.n_dma_sems = n_dma_sems
        self.nops = 0
        self.junk_fn = None
        self.junk_n = 0
        self.n_pe_waits = 0

    def _track(self, op, reads, writes):
        ex = [t for t in reads if t.excl]
        if ex:
            reads = [t for t in reads if not t.excl]
            writes = list(writes) + [t for t in ex if t not in writes]
        for t in reads:
            if t.last_w is not None:
                op.deps.add(t.last_w)
        for t in writes:
            if t.last_w is not None:
                op.deps.add(t.last_w)
            for r in t.readers:
                op.deps.add(r)
        for t in reads:
            t.readers.append(op)
        for t in writes:
            t.last_w = op
            t.readers = []
        op.deps.discard(op)

    def add(self, eng, fn, reads=(), writes=(), is_dma=False):
        op = Op(eng, fn, is_dma)
        self._track(op, reads, writes)
        self.ops[eng].append(op)
        self.nops += 1
        return op

    def dma(self, out_ap, in_ap, reads=(), writes=(), queue="sp", **kw):
        def fn(e, out_ap=out_ap, in_ap=in_ap, kw=kw):
            return e.dma_start(out=out_ap, in_=in_ap, **kw)
        return self.add(queue, fn, reads, writes, is_dma=True)

    def emit(self):
        nc = self.nc

        def skip(d, op):
            return d.eng == "pe" and op.eng == "pe" and not d.is_dma and not op.is_dma
        for e in self.ops:
            for op in self.ops[e]:
                for d in op.deps:
                    if not skip(d, op):
                        d.need_inc = True
        with contextlib.ExitStack() as st:
            sems = {e: st.enter_context(nc.semaphore("s_" + e)) for e in COMPUTE}
            dma_sems = {}
            for q in self.ops:
                if any(o.is_dma for o in self.ops[q]):
                    dma_sems[q] = [st.enter_context(nc.semaphore("d_%s_%d" % (q, i)))
                                   for i in range(self.n_dma_sems)]
            for e in self.ops:
                c = 0
                j = 0
                for op in self.ops[e]:
                    if op.is_dma:
                        ring = dma_sems[e]
                        op.sem = ring[j % len(ring)]
                        op.count = 16 * (j // len(ring) + 1)
                        j += 1
                    elif op.need_inc:
                        c += 1
                        op.count = c
                        op.sem = sems[e]
            block = st.enter_context(nc.Block())
            handles = {"pe": block.tensor, "act": block.scalar, "dve": block.vector,
                       "pool": block.gpsimd, "sp": block.sync}

            def make(e):
                oplist = self.ops[e]

                def body(eng):
                    known = {}
                    nwait = [0]

                    def wait(sem, val):
                        if known.get(id(sem), 0) >= val:
                            return
                        eng.wait_ge(sem, val)
                        known[id(sem)] = val
                    for op in oplist:
                        need = {}
                        for d in op.deps:
                            if skip(d, op):
                                continue
                            k = id(d.sem)
                            if k not in need or need[k][1] < d.count:
                                need[k] = (d.sem, d.count)
                        if op.is_dma and op.count > 16:
                            k = id(op.sem)
                            v = op.count - 16
                            if k not in need or need[k][1] < v:
                                need[k] = (op.sem, v)
                        pend = [(sem, val) for sem, val in need.values() if known.get(id(sem), 0) < val]
                        if pend and e == "pe" and self.junk_fn is not None and op.fn is not None:
                            nwait[0] += 1
                            for _ in range(self.junk_n):
                                self.junk_fn(eng)
                        for sem, val in pend:
                            wait(sem, val)
                        if op.fn is None:
                            continue
                        ins = op.fn(eng)
                        if op.is_dma:
                            ins.then_inc(op.sem, 16)
                        elif op.need_inc:
                            ins.then_inc(op.sem, 1)
                    last = {}
                    for op in oplist:
                        if op.is_dma:
                            last[id(op.sem)] = (op.sem, op.count)
                    for sem, val in last.values():
                        wait(sem, val)
                    if e == "pe":
                        self.n_pe_waits = nwait[0]
                return body
            for e in self.ops:
                if self.ops[e]:
                    handles[e](make(e))


def host_consts():
    i = np.arange(128)
    c = {}
    c["ident"] = np.eye(128, dtype=np.float32)
    c["ones"] = np.ones((128, 128), np.float32)
    for nm, nseq in (("p", 1), ("s", NSEQ)):
        t = 128 // nseq
        seq = i // t
        same = seq[:, None] == seq[None, :]
        incl = same & (i[None, :] <= i[:, None])
        strict = same & (i[None, :] < i[:, None])
        c["ltincl_" + nm] = incl.T.astype(np.float32)
        c["seqm_" + nm] = same.astype(np.float32)
        c["nmincl_" + nm] = np.where(incl, 0.0, BIG).astype(np.float32)
        c["nminclT_" + nm] = np.where(incl.T, 0.0, -BIG).astype(np.float32)
        c["ms01_" + nm] = strict.astype(np.float32)
    caus = i[:, None] < i[None, :]
    c["caus01"] = caus.astype(np.float32)
    c["causneg"] = np.where(caus, 0.0, -BIG).astype(np.float32)
    kt = i[:, None, None]
    ss_ = np.arange(NSEQ)[None, :, None]
    qq = (np.arange(64) % TS)[None, None, :]
    c["smask01"] = ((kt // TS == ss_) & (kt % TS < qq)).astype(np.float32).reshape(128, NSEQ * 64)
    c["ntri"] = np.where(i[:, None] >= i[None, :], -1.0, 0.0).astype(np.float32)
    cm = (np.arange(NSEQ)[:, None] == (i // TS)[None, :]).astype(np.float32)
    c["colmask"] = np.broadcast_to(cm.reshape(1, NSEQ * 128), (128, NSEQ * 128)).copy()
    c["rowmask"] = np.zeros((128, 128), np.float32)
    c["rowmask"][:, :NSEQ] = cm.T
    main = ["ident", "ones", "ltincl_p", "ms01_p", "caus01", "causneg", "ntri"]
    samp = ["ltincl_s", "seqm_s", "ms01_s", "rowmask", "colmask", "smask01"]

    def pack(names):
        off = {}
        o = 0
        for k in names:
            off[k] = (o, c[k].shape[1])
            o += c[k].shape[1]
        return np.concatenate([c[k] for k in names], axis=1).astype(np.float32), off
    return pack(main) + pack(samp)


CT_MAIN, CO_MAIN, CT_SAMP, CO_SAMP = host_consts()


class Builder:
    def __init__(self, cfg, stages=("all",), dbg=()):
        self.cfg = cfg
        self.stages = stages
        self.dbg = dbg
        self.nc = bass.Bass("TRN2", target_bir_lowering=False)
        self.s = Sched(self.nc)
        self.fence_id = 0
        self._uid = 0
        import os
        self.use_r32 = os.environ.get("USE_R32", "0") == "1"

    def on(self, st):
        return "all" in self.stages or st in self.stages

    def sb(self, stack, name, shape, dt):
        self._uid += 1
        h = stack.enter_context(self.nc.sbuf_tensor("%s_%d" % (name, self._uid), list(shape), dt))
        return Tile(h, name)

    def view(self, ap, name=""):
        return Tile(ap, name)

    def din(self, name, shape, dt=F32):
        return self.nc.dram_tensor(name, list(shape), dt, kind="ExternalInput").ap()

    def dout(self, name, shape, dt=F32):
        return self.nc.dram_tensor(name, list(shape), dt, kind="ExternalOutput").ap()

    def dscr(self, name, shape, dt=F32):
        return self.nc.dram_tensor(name, list(shape), dt, kind="Internal").ap()

    def E(self, eng, meth, reads, writes, *a, **kw):
        return self.s.add(eng, lambda e: getattr(e, meth)(*a, **kw), reads, writes)

    def mm(self, out, lhsT, rhs, start, stop, reads, writes, skip=False, r32=False):
        if r32 and self.use_r32:
            lhsT = lhsT.bitcast(mybir.dt.float32r)
            rhs = rhs.bitcast(mybir.dt.float32r)
        return self.s.add("pe", lambda e: e.matmul(out, lhsT=lhsT, rhs=rhs, start=start, stop=stop, skip_group_check=skip),
                          reads, writes)

    def tr(self, out, in_, ident, reads, writes):
        return self.s.add("pe", lambda e: e.transpose(out=out, in_=in_, identity=ident), reads, writes)

    def act(self, out, in_, func, reads, writes, **kw):
        return self.s.add("act", lambda e: e.activation(out=out, in_=in_, func=func, **kw), reads, writes)

    def dma(self, out, in_, reads=(), writes=(), **kw):
        return self.s.dma(out, in_, reads, writes, **kw)

    def fence(self):
        s = self.s
        f = set()
        for e in COMPUTE:
            real = [o for o in s.ops[e] if o.fn is not None and not o.is_dma]
            if real:
                f.add(real[-1])
        for q in s.ops:
            d = [o for o in s.ops[q] if o.is_dma]
            for o in d[-s.n_dma_sems:]:
                f.add(o)
        for e in COMPUTE + ("sp",):
            op = Op(e, None)
            op.deps = set(f)
            s.ops[e].append(op)

    def G(self):
        t = self.pg[self.gi % len(self.pg)]
        self.gi += 1
        return t

    def TB(self):
        t = self.ptb[self.ti % len(self.ptb)]
        self.ti += 1
        return t

    def declare(self):
        c = self.cfg
        NT = c.NBLK * 128
        i = {}
        i["xp"] = self.din("xp", [NT, D])
        i["xs"] = self.din("xs", [128, D])
        i["cvec"] = self.din("cvec", [17, D])
        i["cache_k"] = self.din("cache_k", [c.NPHYS * 128, SBW])
        i["cache_v"] = self.din("cache_v", [c.NPHYS * 128, SBW])
        i["ptab"] = self.din("ptab", [NSEQ * c.NPG], I32)
        i["s0"] = self.din("s0", [NSEQ, DNH, DND, DND])
        i["dnc0"] = self.din("dnc0", [NSEQ * 3, 3 * DNW])
        i["ffc0"] = self.din("ffc0", [NSEQ * 2, 2 * DFF])
        i["w_ada"] = self.din("w_ada", [D, 6 * D])
        i["b_ada"] = self.din("b_ada", [6 * D])
        i["g_attn"] = self.din("g_attn", [D])
        i["w_in"] = self.din("w_in", [D, INC])
        i["g_q"] = self.din("g_q", [HD])
        i["g_k"] = self.din("g_k", [HD])
        i["sb_bias"] = self.din("sb_bias", [HS])
        i["g_sb_out"] = self.din("g_sb_out", [HD])
        i["w_dn_conv"] = self.din("w_dn_conv", [4, 3 * DNW])
        i["a_log"] = self.din("a_log", [DNH])
        i["dt_bias"] = self.din("dt_bias", [DNH])
        i["g_dn_out"] = self.din("g_dn_out", [DND])
        i["w_out"] = self.din("w_out", [D, D])
        i["g_ffn"] = self.din("g_ffn", [D])
        i["w_up"] = self.din("w_up", [D, 2 * DFF])
        i["w_ffn_conv"] = self.din("w_ffn_conv", [3, 2 * DFF])
        i["w_down"] = self.din("w_down", [DFF, D])
        i["ct_main"] = self.din("ct_main", list(CT_MAIN.shape))
        i["ct_samp"] = self.din("ct_samp", list(CT_SAMP.shape))
        i["kvlo"] = self.din("kvlo", [128, 256])
        i["blkvalid"] = self.din("blkvalid", [128, c.NBLK])
        self.i = i
        o = {}
        o["yp"] = self.dout("yp", [c.NOUT * 128, D])
        o["ys"] = self.dout("ys", [128, D])
        o["kp"] = self.dout("kp", [c.NOUT * 128, SBW])
        o["vp"] = self.dout("vp", [c.NOUT * 128, SBW])
        o["ksm"] = self.dout("ksm", [128, SBW])
        o["vsm"] = self.dout("vsm", [128, SBW])
        o["sp_state"] = self.dout("sp_state", [DNH, DND, DND])
        o["ss_state"] = self.dout("ss_state", [NSEQ, DNH, DND, DND])
        o["dcp"] = self.dout("dcp", [3, 3 * DNW])
        o["dcs"] = self.dout("dcs", [NSEQ * 3, 3 * DNW])
        o["fcp"] = self.dout("fcp", [2, 2 * DFF])
        o["fcs"] = self.dout("fcs", [NSEQ * 2, 2 * DFF])
        for name, shape in self.dbg:
            o[name] = self.dout(name, shape)
        self.o = o
        self.modd = self.dscr("modd", [17, 6 * D])
        self.mixd = self.dscr("mixd", [(c.NOWN + 1) * 128, D], BF16)
        self.t_modd = Tile(None, "modd")
        self.t_mixd = [Tile(None, "mixd%d" % k) for k in range(c.NOWN + 1)]

    def stream_cast(self, stack_tiles, src_view, dst_tile, dst_ap, eng="dve"):
        stg = stack_tiles[self.sci % len(stack_tiles)]
        self.sci += 1
        shp = src_view.shape
        sap = stg[:, 0:shp[1] * shp[2]].rearrange("p (a b) -> p a b", a=shp[1])
        self.dma(sap, src_view, writes=[stg])
        if eng == "act":
            self.act(dst_ap, sap, AF.Copy, [stg], [dst_tile])
        else:
            self.E(eng, "tensor_copy", [stg], [dst_tile], out=dst_ap, in_=sap)

    def rsqrt_ops(self, ss, rs, n, scale, reads_extra=()):
        self.act(rs[:, 0:n], ss[:, 0:n], AF.Ln, [ss] + list(reads_extra), [rs], scale=scale, bias=self.epsb[:, 0:1])
        self.act(rs[:, 0:n], rs[:, 0:n], AF.Exp, [rs], [rs], scale=-0.5)

    def build(self):
        nc = self.nc
        c = self.cfg
        self.declare()
        i, o = self.i, self.o
        self.gi = 0
        self.ti = 0
        self.sci = 0
        with contextlib.ExitStack() as top:
            self.pg = [Tile(top.enter_context(nc.psum_tensor("pg%d" % k, [128, 512], F32)), "pg%d" % k, True) for k in range(2)]
            self.zb = [Tile(top.enter_context(nc.psum_tensor("zb%d" % k, [128, 512], F32)), "zb%d" % k, True) for k in range(2)]
            self.po = [Tile(top.enter_context(nc.psum_tensor("po%d" % k, [128, 512], F32)), "po%d" % k, True) for k in range(2)]
            self.ptb = [Tile(top.enter_context(nc.psum_tensor("ptb%d" % k, [128, 1024], BF16)), "ptb%d" % k, True) for k in range(2)]
            ctm = self.sb(top, "ctm", list(CT_MAIN.shape), F32)
            self.ctm = ctm
            self.dma(ctm[:], i["ct_main"][:, :], writes=[ctm])

            def cm(name):
                o_, w_ = CO_MAIN[name]
                return ctm[:, o_:o_ + w_]
            self.cm = cm
            cbf = self.sb(top, "cbf", [128, 4 * 128], BF16)
            self.cbf = cbf
            self.E("dve", "tensor_copy", [ctm], [cbf], out=cbf[:, 0:128], in_=cm("ident"))
            self.E("dve", "tensor_copy", [ctm], [cbf], out=cbf[:, 128:256], in_=cm("ntri"))
            self.E("dve", "tensor_copy", [ctm], [cbf], out=cbf[:, 256:384], in_=cm("causneg"))
            self.E("dve", "tensor_scalar", [ctm], [cbf], out=cbf[:, 384:512], in0=cm("ones"), scalar1=-1.0, scalar2=None, op0=ALU.mult)
            self.identb = cbf[:, 0:128]
            self.ntrib = cbf[:, 128:256]
            self.causnegb = cbf[:, 256:384]
            self.negonesb = cbf[:, 384:512]
            epsb = self.sb(top, "epsb", [128, 1], F32)
            self.epsb = epsb
            self.E("pool", "memset", [], [epsb], epsb[:], EPS)
            sv = self.sb(top, "sv", [128, 64 * 3 + 128 + 4 + 4 + 8], F32)
            self.sv = sv
            self.dma(sv[:, 0:64], i["g_q"].partition_broadcast(128), writes=[sv])
            self.dma(sv[:, 64:128], i["g_k"].partition_broadcast(128), writes=[sv])
            self.dma(sv[:, 128:192], i["g_sb_out"].partition_broadcast(128), writes=[sv])
            self.dma(sv[:, 192:320], i["g_dn_out"].partition_broadcast(128), writes=[sv])
            self.dma(sv[:, 320:324], i["a_log"].partition_broadcast(128), writes=[sv])
            self.dma(sv[:, 324:328], i["dt_bias"].partition_broadcast(128), writes=[sv])
            self.dma(sv[:, 328:336], i["sb_bias"].partition_broadcast(128), writes=[sv])
            self.E("dve", "tensor_scalar", [sv], [sv], out=sv[:, 0:64], in0=sv[:, 0:64], scalar1=HD ** -0.5, scalar2=None, op0=ALU.mult)
            self.act(sv[:, 320:324], sv[:, 320:324], AF.Exp, [sv], [sv])
            self.E("dve", "tensor_scalar", [sv], [sv], out=sv[:, 320:324], in0=sv[:, 320:324], scalar1=-1.0, scalar2=None, op0=ALU.mult)
            self.gq8, self.gk, self.gso, self.gdn = sv[:, 0:64], sv[:, 64:128], sv[:, 128:192], sv[:, 192:320]
            self.negA, self.dtb = sv[:, 320:324], sv[:, 324:328]
            kvf = self.sb(top, "kvf", [128, 256], F32)
            self.dma(kvf[:], i["kvlo"][:, :], writes=[kvf])
            kvd = self.sb(top, "kvd", [128, 128], BF16)
            self.kvd = kvd
            self.E("dve", "tensor_copy", [kvf], [kvd], out=kvd[:], in_=kvf[:, 0:128])
            bvd = self.sb(top, "bvd", [128, c.NBLK], F32)
            self.bvd = bvd
            self.dma(bvd[:], i["blkvalid"][:, :], writes=[bvd])
            kvone = self.sb(top, "kvone", [128, 128], BF16)
            self.kvone = kvone
            self.E("dve", "tensor_copy", [kvf], [kvone], out=kvone[:], in_=kvf[:, 128:256])
            wdc = self.sb(top, "wdc", [128, 4, 12], F32)
            self.wdc = wdc
            for t_ in range(4):
                self.dma(wdc[:, t_, :], i["w_dn_conv"][t_].rearrange("(c p) -> p c", p=128), writes=[wdc], allow_slow_non_contiguous=True)
            wfc = self.sb(top, "wfc", [128, 3, 44], F32)
            self.wfc = wfc
            for t_ in range(3):
                self.dma(wfc[:, t_, :], i["w_ffn_conv"][t_].rearrange("(c p) -> p c", p=128), writes=[wfc], allow_slow_non_contiguous=True)

            if self.on("setup"):
                self.setup_mod()
            self.fence()
            if self.on("p1"):
                self.phase1()
            self.fence()
            if self.on("p2"):
                self.phase2()
            self.s.emit()
        return nc

    def setup_mod(self):
        i = self.i
        with contextlib.ExitStack() as st:
            cv = self.sb(st, "cv", [17, D], F32)
            ex = self.sb(st, "ex", [17, D], F32)
            scb = self.sb(st, "scb", [17, D], BF16)
            scT = self.sb(st, "scT", [128, DC, 17], BF16)
            stg = [self.sb(st, "stg%d" % k, [128, DC * 512], F32) for k in range(2)]
            wab = [self.sb(st, "wab%d" % k, [128, DC, 512], BF16) for k in range(2)]
            bada = self.sb(st, "bada", [17, 512], F32)
            gv = self.sb(st, "gv", [17, 2 * D], F32)
            mt = [self.sb(st, "mt%d" % k, [17, 512], F32) for k in range(2)]
            self.dma(cv[:], i["cvec"][:, :], writes=[cv])
            self.dma(gv[:, 0:D], i["g_attn"].partition_broadcast(17), writes=[gv])
            self.dma(gv[:, D:2 * D], i["g_ffn"].partition_broadcast(17), writes=[gv])
            self.act(ex[:], cv[:], AF.Exp, [cv], [ex], scale=-1.0)
            self.E("dve", "tensor_scalar", [ex], [ex], out=ex[:], in0=ex[:], scalar1=1.0, scalar2=None, op0=ALU.add)
            self.E("dve", "reciprocal", [ex], [ex], out=ex[:], in_=ex[:])
            self.E("dve", "tensor_tensor", [ex, cv], [scb], out=scb[:], in0=cv[:], in1=ex[:], op=ALU.mult)
            tb = self.TB()
            for dc in range(DC):
                self.tr(tb[:, dc * 32:dc * 32 + 17], scb[0:17, dc * 128:(dc + 1) * 128], self.identb[0:17, 0:17], [scb, self.cbf], [tb])
            self.E("dve", "tensor_copy", [tb], [scT], out=scT[:], in_=tb[:, 0:DC * 32].rearrange("p (a b) -> p a b", a=DC)[:, :, 0:17])
            wv = i["w_ada"].rearrange("(c p) n -> p c n", p=128)
            for ct in range(12):
                wb = wab[ct % 2]
                self.stream_cast(stg, wv[:, :, ct * 512:(ct + 1) * 512], wb, wb[:], eng="act" if ct % 2 else "dve")
                self.dma(bada[:], i["b_ada"][ct * 512:(ct + 1) * 512].partition_broadcast(17), writes=[bada])
                g = self.G()
                for dc in range(DC):
                    self.mm(g[0:17, :], scT[:, dc, :], wb[:, dc, :], dc == 0, dc == DC - 1, [scT, wb], [g])
                m = mt[ct % 2]
                self.E("dve", "tensor_tensor", [g, bada], [m], out=m[:], in0=g[0:17, :], in1=bada[:], op=ALU.add)
                if ct in (2, 3, 8, 9):
                    go = (ct - 2) * 512 if ct < 4 else D + (ct - 8) * 512
                    self.E("dve", "scalar_tensor_tensor", [m, gv], [m], out=m[:], in0=m[:], scalar=1.0, in1=gv[:, go:go + 512],
                           op0=ALU.add, op1=ALU.mult)
                self.dma(self.modd[:, ct * 512:(ct + 1) * 512], m[:], reads=[m], writes=[self.t_modd])

    def load_mod(self, tiles, idxs, sample):
        for t, ix in zip(tiles, idxs):
            if not sample:
                self.dma(t[:], self.modd[0, ix * D:(ix + 1) * D].partition_broadcast(128), reads=[self.t_modd], writes=[t])
            else:
                for s_ in range(NSEQ):
                    self.dma(t[s_ * TS:(s_ + 1) * TS, :], self.modd[1 + s_, ix * D:(ix + 1) * D].partition_broadcast(TS),
                             reads=[self.t_modd], writes=[t])

    def norm_mod(self, w, xt, scale, shift, hT, tmp=None):
        self.E("pool", "memset", [], [w.ssq], w.ssq[:], 0.0)
        self.act(w.h[:], xt[:], AF.Square, [xt, w.ssq], [w.h, w.ssq], accum_out=w.ssq[:, 0:1])
        self.rsqrt_ops(w.ssq, w.rstd, 1, 1.0 / D)
        if tmp is None:
            tmp = xt
        self.E("dve", "scalar_tensor_tensor", [xt, w.rstd, scale], [tmp], out=tmp[:], in0=xt[:], scalar=w.rstd[:, 0:1],
               in1=scale[:], op0=ALU.mult, op1=ALU.mult)
        self.E("pool", "tensor_tensor", [tmp, shift], [w.h], out=w.h[:], in0=tmp[:], in1=shift[:], op=ALU.add)
        tb = self.TB()
        for dc in range(DC):
            self.tr(tb[:, dc * 128:(dc + 1) * 128], w.h[:, dc * 128:(dc + 1) * 128], self.identb, [w.h, self.cbf], [tb])
        self.act(hT[:], tb[:, :].rearrange("p (a b) -> p a b", a=DC), AF.Copy, [tb], [hT])

    def head_norm(self, w, ps, nh, hd, out_f32, reads_ps, gvec, out_tile, out_ap, scale):
        v3 = lambda ap: ap.rearrange("p (a b) -> p a b", a=nh)
        self.act(out_f32[:, 0:nh * hd], ps, AF.Square, reads_ps, [out_f32])
        self.E("dve", "tensor_reduce", [out_f32], [w.ss8], out=w.ss8[:, 0:nh], in_=v3(out_f32[:, 0:nh * hd]), axis=AX.X, op=ALU.add)
        self.rsqrt_ops(w.ss8, w.rs8, nh, scale)
        self.E("dve", "tensor_tensor", reads_ps + [w.rs8], [out_f32], out=v3(out_f32[:, 0:nh * hd]), in0=v3(ps),
               in1=w.rs8[:, 0:nh].unsqueeze(2).to_broadcast([128, nh, hd]), op=ALU.mult)
        self.E("pool", "tensor_tensor", [out_f32, self.sv], [out_tile], out=v3(out_ap), in0=v3(out_f32[:, 0:nh * hd]),
               in1=gvec.unsqueeze(1).to_broadcast([128, nh, hd]), op=ALU.mult)

    def front_end(self, w, b, sample, own, outrow):
        c = self.cfg
        i, o = self.i, self.o
        xt = w.xt[self.xi % len(w.xt)]
        self.xi += 1
        src = i["xs"][:, :] if sample else i["xp"][b * 128:(b + 1) * 128, :]
        self.dma(xt[:], src, writes=[xt])
        hT = w.hT
        self.norm_mod(w, xt, w.scale1, w.shift1, hT)
        winb = self.winb
        yield
        gk_ = self.zb[0]
        for dc in range(DC):
            self.mm(gk_[:, :], hT[:, dc, :], winb[:, dc, 512:1024], dc == 0, dc == DC - 1, [hT, winb], [gk_])
        self.head_norm(w, gk_[:, :], HS, HD, w.f512, [gk_], self.gk, w.kn, w.kn[:], 1.0 / HD)
        if outrow is not None:
            self.dma(outrow[0], w.kn[:], reads=[w.kn])
        self.E("dve", "tensor_copy", [w.kn], [w.knb], out=w.knb[:], in_=w.kn[:])
        yield
        gv_ = self.zb[1]
        for dc in range(DC):
            self.mm(gv_[:, :], hT[:, dc, :], winb[:, dc, 1024:1536], dc == 0, dc == DC - 1, [hT, winb], [gv_])
        Vt = w.Vs if sample else self.Vt[b]
        self.act(Vt[:], gv_[:, :], AF.Copy, [gv_], [Vt])
        if outrow is not None:
            self.E("dve", "tensor_copy", [gv_], [w.vf], out=w.vf[:], in_=gv_[:, :])
            self.dma(outrow[1], w.vf[:], reads=[w.vf])
        yield
        if own:
            gq_ = self.po[0]
            for dc in range(DC):
                self.mm(gq_[:, :], hT[:, dc, :], winb[:, dc, 0:512], dc == 0, dc == DC - 1, [hT, winb], [gq_])
            self.head_norm(w, gq_[:, :], HS, HD, w.f512, [gq_], self.gq8, w.qnb, w.qnb[:], 1.0 / HD)
        if self.on("dn"):
            g = self.G()
            for dc in range(DC):
                self.mm(g[:, 0:8], hT[:, dc, :], winb[:, dc, 3072:3080], dc == 0, dc == DC - 1, [hT, winb], [g])
            self.E("dve", "tensor_copy", [g], [w.ba], out=w.ba[:], in_=g[:, 0:8])
            if own:
                g = self.po[1]
                for dc in range(DC):
                    self.mm(g[:, :], hT[:, dc, :], winb[:, dc, 3080:3592], dc == 0, dc == DC - 1, [hT, winb], [g])
                self.act(w.zs[:], g[:, :], AF.Copy, [g], [w.zs])
        yield
        KTt = w.KTs if sample else self.KTt[b]
        tb = self.TB()
        for pr in range(4):
            self.tr(tb[:, pr * 128:(pr + 1) * 128], w.knb[:, pr * 128:(pr + 1) * 128], self.identb, [w.knb, self.cbf], [tb])
        self.act(KTt[:], tb[:, 0:512].rearrange("p (a b) -> p a b", a=4), AF.Copy, [tb], [KTt])
        if own:
            tb = self.TB()
            for pr in range(4):
                self.tr(tb[:, pr * 128:(pr + 1) * 128], w.qnb[:, pr * 128:(pr + 1) * 128], self.identb, [w.qnb, self.cbf], [tb])
            tq = tb[:, 0:512].rearrange("p (a b) -> p a b", a=4)
            self.act(w.QT[0:64, :, 0, :], tq[0:64, :, :], AF.Copy, [tb], [w.QT])
            self.E("dve", "tensor_copy", [tb], [w.QT], out=w.QT[64:128, :, 1, :], in_=tq[64:128, :, :])

    def front_dn(self, w, b, sample, fe):
        next(fe)
        if not self.on("dn"):
            for _ in fe:
                pass
            return
        if (not sample) and b == 0:
            self.E("pool", "memset", [], [w.Hst], w.Hst[:], 0.0)
        g0, g1, g2 = [self.dn_group(w, b, sample, j) for j in range(3)]
        next(g0); next(g0)
        next(fe)
        next(g1)
        next(g0)
        next(g1)
        next(fe)
        next(g2)
        next(g1)
        next(g2)
        next(fe)
        next(g2)
        for _ in fe:
            pass

    def dn_group(self, w, b, sample, j):
        T = TS if sample else 128
        ns = NSEQ if sample else 1
        hT, winb = w.hT, self.winb
        XE4 = w.XE4
        xe4 = XE4[:, :, :].rearrange("p c (s t) -> p c s t", s=ns)
        Hst = w.Hst
        hs4 = Hst[:, :, 0:ns * 3].rearrange("p c (s t) -> p c s t", s=ns)
        g = self.G()
        for ch in range(4):
            col = 1536 + (j * 4 + ch) * 128
            for dc in range(DC):
                self.mm(g[:, ch * 128:(ch + 1) * 128], winb[:, dc, col:col + 128], hT[:, dc, :], dc == 0, dc == DC - 1, [hT, winb], [g])
        yield
        src4 = g[:, :].rearrange("p (c s t) -> p c s t", c=4, s=ns)
        self.E("pool", "tensor_copy", [Hst], [XE4], out=xe4[:, :, :, 0:3], in_=hs4[:, j * 4:(j + 1) * 4, :, :])
        if sample:
            self.act(xe4[:, :, :, 3:3 + T], src4, AF.Copy, [g], [XE4])
        else:
            self.act(xe4[:, :, :, 3:3 + T], src4, AF.Copy, [g, self.bvd], [XE4], scale=self.bvd[:, b:b + 1])
        self.E("pool", "tensor_copy", [XE4], [Hst], out=hs4[:, j * 4:(j + 1) * 4, :, :], in_=xe4[:, :, :, T:T + 3])
        Y4 = w.Y4
        for ch in range(4):
            cc = j * 4 + ch
            yv = Y4[:, ch, :].rearrange("p (s t) -> p s t", s=ns)
            self.E("dve", "tensor_scalar", [XE4, self.wdc], [Y4], out=yv, in0=xe4[:, ch, :, 0:T], scalar1=self.wdc[:, 0, cc:cc + 1],
                   scalar2=None, op0=ALU.mult)
            for k in range(1, 4):
                self.E("dve", "scalar_tensor_tensor", [XE4, self.wdc, Y4], [Y4], out=yv, in0=xe4[:, ch, :, k:k + T],
                       scalar=self.wdc[:, k, cc:cc + 1], in1=yv, op0=ALU.mult, op1=ALU.add)
        E4 = w.E4
        self.act(E4[:], Y4[:], AF.Exp, [Y4], [E4], scale=-1.0)
        self.act(E4[:], E4[:], AF.Ln, [E4], [E4], bias=1.0)
        self.act(E4[:], E4[:], AF.Exp, [E4], [E4], scale=-1.0)
        self.E("dve", "tensor_tensor", [E4, Y4], [Y4], out=Y4[:], in0=Y4[:], in1=E4[:], op=ALU.mult)
        SQ = E4
        if j < 2:
            self.act(SQ[:], Y4[:], AF.Square, [Y4], [SQ])
        yield
        if j < 2:
            g = self.G()
            self.mm(g[:, :], self.cm("ones"), SQ[:].rearrange("p a b -> p (a b)"), True, True, [SQ, self.ctm], [g])
            self.act(SQ[:].rearrange("p a b -> p (a b)"), g[:, :], AF.Ln, [g], [SQ], bias=self.epsb[:, 0:1])
            self.act(SQ[:], SQ[:], AF.Exp, [SQ], [SQ], scale=-0.5)
            if j == 0:
                self.E("dve", "scalar_tensor_tensor", [Y4, SQ], [w.QnT], out=w.QnT[:], in0=Y4[:], scalar=DND ** -0.5, in1=SQ[:],
                       op0=ALU.mult, op1=ALU.mult)
            else:
                self.E("pool", "tensor_tensor", [Y4, SQ], [w.KnT], out=w.KnT[:], in0=Y4[:], in1=SQ[:], op=ALU.mult)
        else:
            self.act(w.Vcb[:], Y4[:], AF.Copy, [Y4], [w.Vcb])
        yield

    def dn_chunk(self, w, b, sample, own, tabs, last_prompt):
        c = self.cfg
        i, o = self.i, self.o
        T = TS if sample else 128
        ns = NSEQ if sample else 1
        sc = w.sc
        hT, winb = w.hT, self.winb
        ltincl, seqm, ms01 = tabs
        bc4 = lambda ap: ap.unsqueeze(2).to_broadcast([128, 4, 128])
        hb4 = lambda ap: ap.unsqueeze(1).to_broadcast([128, 4, 128])
        v4 = lambda ap: ap.rearrange("p (a b) -> p a b", a=4)
        Hst = w.Hst
        if sample or last_prompt:
            ncol = ns * 3
            dst = o["dcs"] if sample else o["dcp"]
            for j in range(3):
                g = self.G()
                for ch in range(4):
                    self.tr(g[0:ncol, ch * 128:(ch + 1) * 128], Hst[:, j * 4 + ch, 0:ncol], self.cm("ident"), [Hst, self.ctm], [g])
                self.E("dve", "tensor_copy", [g], [w.f512], out=w.f512[0:ncol, :], in_=g[0:ncol, :])
                self.dma(dst[:, j * 512:(j + 1) * 512], w.f512[0:ncol, :], reads=[w.f512])
        tb = self.TB()
        for h in range(4):
            self.tr(tb[:, h * 128:(h + 1) * 128], w.KnT[:, h, :], self.identb, [w.KnT, self.cbf], [tb])
        self.act(w.Ktok[:], v4(tb[:, 0:512]), AF.Copy, [tb], [w.Ktok])
        tb = self.TB()
        for h in range(4):
            self.tr(tb[:, h * 128:(h + 1) * 128], w.Vcb[:, h, :], self.identb, [w.Vcb, self.cbf], [tb])
        self.E("dve", "tensor_copy", [tb], [w.Vtok], out=w.Vtok[:], in_=v4(tb[:, 0:512]))
        yield
        ba = w.ba
        self.act(sc[:, 0:4], ba[:, 0:4], AF.Exp, [ba], [sc], scale=-1.0)
        self.act(sc[:, 0:4], sc[:, 0:4], AF.Ln, [sc], [sc], bias=1.0)
        self.act(sc[:, 0:4], sc[:, 0:4], AF.Exp, [sc], [sc], scale=-1.0)
        if not sample:
            self.E("dve", "tensor_scalar", [sc, self.bvd], [sc], out=sc[:, 0:4], in0=sc[:, 0:4], scalar1=self.bvd[:, b:b + 1], scalar2=None, op0=ALU.mult)
        self.E("dve", "tensor_tensor", [ba, self.sv], [sc], out=sc[:, 32:36], in0=ba[:, 4:8], in1=self.dtb, op=ALU.add)
        self.act(sc[:, 32:36], sc[:, 32:36], AF.Exp, [sc], [sc])
        self.act(sc[:, 32:36], sc[:, 32:36], AF.Ln, [sc], [sc], bias=1.0)
        self.E("dve", "tensor_tensor", [sc, self.sv], [sc], out=sc[:, 4:8], in0=sc[:, 32:36], in1=self.negA, op=ALU.mult)
        g = self.G()
        self.mm(g[:, 0:4], ltincl, sc[:, 4:8], True, True, [sc, w.tabt], [g])
        self.mm(g[:, 4:8], seqm, sc[:, 4:8], True, True, [sc, w.tabt], [g])
        self.E("dve", "tensor_copy", [g], [sc], out=sc[:, 8:16], in_=g[:, 0:8])
        self.act(sc[:, 16:20], sc[:, 8:12], AF.Exp, [sc], [sc])
        self.E("dve", "scalar_tensor_tensor", [sc], [sc], out=sc[:, 20:24], in0=sc[:, 0:4], scalar=-1.0, in1=sc[:, 16:20], op0=ALU.mult, op1=ALU.mult)
        self.E("dve", "tensor_tensor", [sc], [sc], out=sc[:, 24:28], in0=sc[:, 12:16], in1=sc[:, 8:12], op=ALU.subtract)
        self.act(sc[:, 24:28], sc[:, 24:28], AF.Exp, [sc], [sc])
        self.E("dve", "tensor_scalar", [sc], [sc], out=sc[:, 28:32], in0=sc[:, 0:4], scalar1=-1.0, scalar2=None, op0=ALU.mult)
        beta, gg, negbg, kds, negbeta = sc[:, 0:4], sc[:, 8:12], sc[:, 20:24], sc[:, 24:28], sc[:, 28:32]
        yield
        GR = self.G()
        for h in range(4):
            dg = w.dg[h % 2]
            self.E("dve", "tensor_scalar", [sc, self.ctm], [dg], out=dg[:], in0=self.cm("ident"), scalar1=sc[:, 8 + h:9 + h], scalar2=None, op0=ALU.mult)
            self.mm(GR[:, h * 128:(h + 1) * 128], self.cm("ones"), dg[:], True, True, [dg, self.ctm], [GR])
        fa, fb, fc, fd = w.fa, w.fb, w.fc, w.fd
        self.E("dve", "tensor_tensor", [GR, sc], [fa], out=v4(fa[:]), in0=v4(GR[:, :]), in1=bc4(gg), op=ALU.subtract)
        self.act(fd[:], GR[:, :], AF.Exp, [GR], [fd])
        self.E("dve", "tensor_scalar", [fa], [fb], out=fb[:], in0=fa[:], scalar1=0.0, scalar2=None, op0=ALU.max)
        self.E("dve", "tensor_scalar", [fa], [fc], out=fc[:], in0=fa[:], scalar1=0.0, scalar2=None, op0=ALU.min)
        self.act(fb[:], fb[:], AF.Exp, [fb], [fb], scale=-1.0)
        self.act(fc[:], fc[:], AF.Exp, [fc], [fc])
        yield
        g = self.G()
        for h in range(4):
            self.mm(g[:, h * 128:(h + 1) * 128], w.KnT[:, h, :], w.KnT[:, h, :], True, True, [w.KnT], [g])
        self.E("dve", "tensor_tensor", [g, fb], [fa], out=fa[:], in0=g[:, :], in1=fb[:], op=ALU.mult)
        self.E("pool", "tensor_tensor", [fa, sc], [fa], out=v4(fa[:]), in0=v4(fa[:]), in1=bc4(negbeta), op=ALU.mult)
        self.E("dve", "tensor_tensor", [fa, w.tabt], [fa], out=v4(fa[:]), in0=v4(fa[:]), in1=hb4(ms01), op=ALU.mult)
        g = self.G()
        for h in range(4):
            self.mm(g[:, h * 128:(h + 1) * 128], w.KnT[:, h, :], w.QnT[:, h, :], True, True, [w.KnT, w.QnT], [g])
        self.E("dve", "tensor_tensor", [g, fc], [fc], out=fc[:], in0=g[:, :], in1=fc[:], op=ALU.mult)
        self.E("pool", "tensor_tensor", [fc, w.tabt], [w.intraT], out=w.intraT[:], in0=v4(fc[:]), in1=hb4(ltincl), op=ALU.mult)
        yield
        MTb = w.MTb
        nlev = 2 if sample else 6
        h2 = lambda ap: ap.rearrange("p (a b) -> p a b", a=2)
        hb2 = lambda ap: ap.unsqueeze(1).to_broadcast([128, 2, 128])
        identf = self.cm("ident")
        P, PT, MT = w.Pf[0], w.PTf[0], w.MT
        P = fa_t = None
        P = w.Pf[0]
        self.E("pool", "tensor_copy", [fa], [P], out=P[:], in_=v4(fa[:]))
        g = self.G()
        for h in range(4):
            self.tr(g[:, h * 128:(h + 1) * 128], P[:, h, :], identf, [P, self.ctm], [g])
        self.act(PT[:], v4(g[:, :]), AF.Copy, [g], [PT])
        self.E("dve", "tensor_tensor", [PT, self.ctm], [MT], out=MT[:], in0=PT[:], in1=hb4(identf), op=ALU.add)
        for lev in range(1, nlev + 1):
            Pn, PTn = w.Pf[lev % 2], w.PTf[lev % 2]
            g1 = self.G()
            for h in range(4):
                self.mm(g1[:, h * 128:(h + 1) * 128], PT[:, h, :], P[:, h, :], True, True, [P, PT], [g1], r32=True)
            self.act(Pn[:], v4(g1[:, :]), AF.Copy, [g1], [Pn])
            if lev < nlev:
                g2 = self.G()
                for h in range(4):
                    self.mm(g2[:, h * 128:(h + 1) * 128], P[:, h, :], PT[:, h, :], True, True, [P, PT], [g2], r32=True)
                self.E("dve", "tensor_copy", [g2], [PTn], out=PTn[:], in_=v4(g2[:, :]))
            g3 = self.G()
            for h in range(4):
                self.mm(g3[:, h * 128:(h + 1) * 128], Pn[:, h, :], MT[:, h, :], True, True, [Pn, MT], [g3], r32=True)
            self.E("dve", "tensor_tensor", [g3, MT], [MT], out=MT[:], in0=v4(g3[:, :]), in1=MT[:], op=ALU.add)
            P, PT = Pn, PTn
            yield
        self.act(MTb[:], MT[:], AF.Copy, [MT], [MTb])
        yield
        self.E("pool", "tensor_tensor", [w.QnT, fd], [w.QdT], out=w.QdT[:], in0=w.QnT[:], in1=v4(fd[:]), op=ALU.mult)
        self.E("dve", "tensor_tensor", [w.Vtok, sc], [w.Vtok], out=w.Vtok[:], in0=w.Vtok[:], in1=bc4(beta), op=ALU.mult)
        self.E("pool", "tensor_tensor", [w.Ktok, sc], [w.kdec], out=w.kdec[:], in0=w.Ktok[:], in1=bc4(kds), op=ALU.mult)
        if not sample:
            Sf, Sb = self.Sf, self.Sb
            g = self.G()
            for h in range(4):
                self.mm(g[:, h * 128:(h + 1) * 128], w.KnT[:, h, :], Sb[:, h, :], True, True, [w.KnT, Sb], [g])
            self.E("dve", "tensor_tensor", [g, sc], [fa], out=v4(fa[:]), in0=v4(g[:, :]), in1=bc4(negbg), op=ALU.mult)
            self.E("pool", "tensor_tensor", [fa, w.Vtok], [w.W], out=w.W[:], in0=v4(fa[:]), in1=w.Vtok[:], op=ALU.add)
            g = self.G()
            for h in range(4):
                self.mm(g[:, h * 128:(h + 1) * 128], MTb[:, h, :], w.W[:, h, :], True, True, [MTb, w.W], [g])
            self.act(w.vnew[:], v4(g[:, :]), AF.Copy, [g], [w.vnew])
            yield
            if own:
                po = self.G()
                po_dn = po
                for h in range(4):
                    self.mm(po[:, h * 128:(h + 1) * 128], w.QdT[:, h, :], Sb[:, h, :], True, False, [w.QdT, Sb], [po])
                    self.mm(po[:, h * 128:(h + 1) * 128], w.intraT[:, h, :], w.vnew[:, h, :], False, True, [w.intraT, w.vnew], [po])
            g = self.G()
            for h in range(4):
                self.mm(g[:, h * 128:(h + 1) * 128], w.kdec[:, h, :], w.vnew[:, h, :], True, True, [w.kdec, w.vnew], [g])
            self.E("dve", "tensor_tensor", [Sf, fd], [Sf], out=Sf[:], in0=Sf[:], in1=v4(fd[:])[:, :, 127:128].to_broadcast([128, 4, 128]), op=ALU.mult)
            self.E("dve", "tensor_tensor", [g, Sf], [Sf], out=Sf[:], in0=v4(g[:, :]), in1=Sf[:], op=ALU.add)
            self.act(Sb[:], Sf[:], AF.Copy, [Sf], [Sb])
            if last_prompt:
                self.dma(o["sp_state"].rearrange("h k v -> k h v"), Sf[:], reads=[Sf])
        else:
            colmask, rowmask = w.colmask, w.rowmask
            cm3 = colmask.rearrange("p (s t) -> p s t", s=NSEQ)
            s0v = i["s0"]
            po = self.po[1]
            for h in range(4):
                Sfh, Sbh = w.Sfh[0], w.Sbh[0]
                self.dma(Sfh[:], s0v[:, h, :, :].rearrange("s k v -> k s v"), writes=[Sfh])
                self.E("pool", "tensor_copy", [Sfh], [Sbh], out=Sbh[:], in_=Sfh[:])
                Km, Qm, kdm = w.Km, w.Qm, w.kdm
                self.E("pool", "tensor_tensor", [w.KnT, w.tabt], [Km], out=Km[:], in0=w.KnT[:, h, :].unsqueeze(1).to_broadcast([128, NSEQ, 128]), in1=cm3, op=ALU.mult)
                g = self.G()
                for s_ in range(NSEQ):
                    self.mm(g[:, 0:128], Km[:, s_, :], Sbh[:, s_, :], s_ == 0, s_ == NSEQ - 1, [Km, Sbh], [g])
                self.E("dve", "tensor_scalar", [g, sc], [fa], out=fa[:, 0:128], in0=g[:, 0:128], scalar1=sc[:, 20 + h:21 + h], scalar2=None, op0=ALU.mult)
                self.E("pool", "tensor_tensor", [fa, w.Vtok], [w.W], out=w.W[:, h, :], in0=fa[:, 0:128], in1=w.Vtok[:, h, :], op=ALU.add)
                g = self.G()
                self.mm(g[:, 0:128], MTb[:, h, :], w.W[:, h, :], True, True, [MTb, w.W], [g])
                self.act(w.vnew[:, h, :], g[:, 0:128], AF.Copy, [g], [w.vnew])
                self.E("dve", "tensor_tensor", [w.QdT, w.tabt], [Qm], out=Qm[:], in0=w.QdT[:, h, :].unsqueeze(1).to_broadcast([128, NSEQ, 128]), in1=cm3, op=ALU.mult)
                for s_ in range(NSEQ):
                    self.mm(po[:, h * 128:(h + 1) * 128], Qm[:, s_, :], Sbh[:, s_, :], s_ == 0, False, [Qm, Sbh], [po])
                self.mm(po[:, h * 128:(h + 1) * 128], w.intraT[:, h, :], w.vnew[:, h, :], False, True, [w.intraT, w.vnew], [po])
                Sn = w.Sn[0]
                self.E("pool", "tensor_tensor", [w.kdec, w.tabt], [kdm], out=kdm[:], in0=w.kdec[:, h, :].unsqueeze(1).to_broadcast([128, NSEQ, 128]),
                       in1=rowmask[:, 0:NSEQ].unsqueeze(2).to_broadcast([128, NSEQ, 128]), op=ALU.mult)
                for q4 in range(4):
                    g = self.G()
                    for k in range(4):
                        s_ = q4 * 4 + k
                        self.mm(g[:, k * 128:(k + 1) * 128], kdm[:, s_, :], w.vnew[:, h, :], True, True, [kdm, w.vnew], [g])
                    for k in range(4):
                        s_ = q4 * 4 + k
                        self.E("dve" if k % 2 == 0 else "pool" if False else "dve", "scalar_tensor_tensor", [g, Sfh, fd], [Sn], out=Sn[:, s_, :], in0=Sfh[:, s_, :],
                               scalar=fd[:, h * 128 + s_ * TS + TS - 1: h * 128 + s_ * TS + TS], in1=g[:, k * 128:(k + 1) * 128], op0=ALU.mult, op1=ALU.add)
                self.dma(o["ss_state"][:, h, :, :].rearrange("s k v -> k s v"), Sn[:], reads=[Sn])
        yield
        if own:
            po = self.po[1] if sample else po_dn
            self.act(fa[:], po[:, :], AF.Square, [po], [fa])
            self.E("dve", "tensor_reduce", [fa], [w.ss8], out=w.ss8[:, 0:4], in_=v4(fa[:]), axis=AX.X, op=ALU.add)
            self.rsqrt_ops(w.ss8, w.rs8, 4, 1.0 / DND)
            self.E("dve", "tensor_tensor", [po, w.rs8], [fa], out=v4(fa[:]), in0=v4(po[:, :]), in1=bc4(w.rs8[:, 0:4]), op=ALU.mult)
            self.E("pool", "tensor_tensor", [fa, self.sv], [fa], out=v4(fa[:]), in0=v4(fa[:]), in1=hb4(self.gdn), op=ALU.mult)
            zs = w.zs
            self.act(fb[:], zs[:], AF.Exp, [zs], [fb], scale=-1.0)
            self.act(fb[:], fb[:], AF.Ln, [fb], [fb], bias=1.0)
            self.act(fb[:], fb[:], AF.Exp, [fb], [fb], scale=-1.0)
            self.E("dve", "tensor_tensor", [fb, zs], [fb], out=fb[:], in0=fb[:], in1=zs[:], op=ALU.mult)
            self.E("dve", "tensor_tensor", [fa, fb], [w.mixed], out=w.mixed[:, 512:1024], in0=fa[:], in1=fb[:], op=ALU.mult)

    def attn_step(self, w, S, nh, nq, qT, kT, vv, nk, kvl, kvl_t, brow_ap, maskneg, mask01, mask_t, O, o_cols, first, last,
                  qreads, kreads, vreads, att_out=None, att_lhs=None, first_o=None, last_o=None):
        W_ = nh * nq
        et, spt, att, Rb = S.et, S.spt, S.att, S.Rb
        Z = S.zbank
        for p_ in range(nh // 2):
            self.mm(Z[0:nk, p_ * 2 * nq:(p_ + 1) * 2 * nq], kT[p_], qT[p_], p_ == 0, False, qreads + kreads, [Z], skip=True)
        self.mm(Z[0:nk, 0:W_], kvl, brow_ap, False, True, [kvl_t, self.brow], [Z], skip=True)
        yield
        self.act(et[0:nk, 0:W_], Z[0:nk, 0:W_], AF.Exp, [Z], [et])
        self.act(spt[0:nk, 0:W_], et[0:nk, 0:W_], AF.Ln, [et], [spt], bias=1.0)
        if mask01 is not None:
            self.E("dve", "tensor_tensor", [spt, mask_t], [spt], out=spt[0:nk, 0:W_], in0=spt[0:nk, 0:W_], in1=mask01, op=ALU.mult)
        yield
        U = Z
        fin = first and maskneg is None
        self.mm(U[0:nk, 0:W_], self.ntrib[0:nk, 0:nk], spt[0:nk, 0:W_], False, fin, [spt, self.cbf], [U], skip=True)
        if not first:
            self.mm(U[0:nk, 0:W_], self.negonesb[:, 0:nk], Rb[:, 0:W_], False, maskneg is None, [Rb, self.cbf], [U], skip=True)
        if maskneg is not None:
            self.mm(U[0:nk, 0:W_], self.identb[0:nk, 0:nk], maskneg, False, True, [mask_t, self.cbf], [U], skip=True)
        yield
        if att_out is None:
            self.act(att[0:nk, 0:W_], U[0:nk, 0:W_], AF.Exp, [U], [att])
        else:
            self.act(att_out[0], U[0:nk, 0:W_].rearrange("p (h q) -> p h q", h=nh), AF.Exp, [U], [att_out[1]])
        if not last:
            if first:
                self.E("dve", "tensor_copy", [spt], [Rb], out=Rb[0:nk, 0:W_], in_=spt[0:nk, 0:W_])
            else:
                self.E("dve", "tensor_tensor", [spt, Rb], [Rb], out=Rb[0:nk, 0:W_], in0=Rb[0:nk, 0:W_], in1=spt[0:nk, 0:W_], op=ALU.add)
        yield
        fo = first if first_o is None else first_o
        lo = last if last_o is None else last_o
        for h in range(nh):
            if att_lhs is None:
                lhs = att[0:nk, h * nq:(h + 1) * nq]
                rd = [att]
            else:
                lhs = att_lhs[0][h]
                rd = [att_lhs[1]]
            self.mm(O[o_cols[h]], lhs, vv[h], fo and h == 0, lo, rd + vreads, [O], skip=True)
        yield

    @staticmethod
    def interleave(gens):
        gens = list(gens)
        while gens:
            for g in list(gens):
                try:
                    next(g)
                except StopIteration:
                    gens.remove(g)

    def attn_prompt(self, w, b, dn_gen=None):
        c = self.cfg

        def stream(hg):
            O = self.po[hg]
            S = w.streams[hg]
            for kb in range(b, -1, -1):
                qT = [w.QT[:, hg * 2 + p_, :, :] for p_ in range(2)]
                kT = [self.KTt[kb][:, hg * 2 + p_, :] for p_ in range(2)]
                vv = [self.Vt[kb][:, (hg * 4 + h) * 64:(hg * 4 + h + 1) * 64] for h in range(4)]
                diag = kb == b
                kvl = self.kvd if kb < c.OUT0 else self.kvone
                yield from self.attn_step(w, S, 4, 128, qT, kT, vv, 128, kvl[:, :], kvl,
                                          self.brow[:, hg * 512:(hg + 1) * 512],
                                          w.causrep[:, 0:512] if diag else None, w.caus01rep[:, 0:512] if diag else None, w.causrep_t,
                                          O, [(slice(None), slice(h * 64, (h + 1) * 64)) for h in range(4)],
                                          kb == b, kb == 0, [w.QT], [self.KTt[kb]], [self.Vt[kb]])
            self.head_norm(w, O[:, 0:256], 4, HD, w.f512, [O], self.gso, w.mixed, w.mixed[:, hg * 256:(hg + 1) * 256], 1.0 / HD)
        gens = [stream(0), stream(1)]
        n_rounds = 5 * (b + 1) + 1
        stride = max(1, n_rounds // 24)
        rnd = 0
        dn_live = dn_gen is not None
        while gens or dn_live:
            if dn_live and (rnd % stride == 0 or not gens):
                try:
                    next(dn_gen)
                except StopIteration:
                    dn_live = False
            for g_ in list(gens):
                try:
                    next(g_)
                except StopIteration:
                    gens.remove(g_)
            rnd += 1

    def attn_sample(self, w):
        c = self.cfg
        i = self.i
        npg = c.NPG
        O = self.po[0]
        ck = i["cache_k"]
        cv = i["cache_v"]

        NSTR = len(w.streams)

        def stream(si):
            S = w.streams[si]
            for s_ in range(si, NSEQ, NSTR):
                attpad = w.attpad[si]
                self.E("pool", "memset", [], [attpad], attpad[:], 0.0)
                qT = [w.QT[:, p_, :, s_ * TS:(s_ + 1) * TS] for p_ in range(4)]
                att_out = (attpad[:, :, s_ * TS:(s_ + 1) * TS], attpad)
                att_lhs = ([attpad[:, h, :] for h in range(8)], attpad)
                o_cols = [(slice(None), slice(h * 64, (h + 1) * 64)) for h in range(8)]
                for blk in range(npg, -1, -1):
                    if blk == npg:
                        kT = [w.KTs[:, p_, :] for p_ in range(4)]
                        vv = [w.Vs[:, h * 64:(h + 1) * 64] for h in range(8)]
                        kreads, vreads = [w.KTs], [w.Vs]
                        mneg, m01 = w.smneg[:, s_, :], w.sm01[:, s_, :]
                    else:
                        j = s_ * npg + blk
                        kk = si
                        kpf, vpf, kpb, vpb, ktp = w.kpf[kk], w.vpf[kk], w.kpb[kk], w.vpb[kk], w.ktp[kk]
                        self.s.add("pool", lambda e, kpf=kpf, j=j: e.indirect_dma_start(
                            out=kpf[:], out_offset=None, in_=ck, in_offset=bass.IndirectOffsetOnAxis(ap=w.idx[:, j:j + 1], axis=0)),
                            [w.idx], [kpf], is_dma=True)
                        self.s.add("pool", lambda e, vpf=vpf, j=j: e.indirect_dma_start(
                            out=vpf[:], out_offset=None, in_=cv, in_offset=bass.IndirectOffsetOnAxis(ap=w.idx[:, j:j + 1], axis=0)),
                            [w.idx], [vpf], is_dma=True)
                        self.E("dve", "tensor_copy", [kpf], [kpb], out=kpb[:], in_=kpf[:])
                        self.act(vpb[:], vpf[:], AF.Copy, [vpf], [vpb])
                        tb = self.TB()
                        for pr in range(4):
                            self.tr(tb[:, pr * 128:(pr + 1) * 128], kpb[:, pr * 128:(pr + 1) * 128], self.identb, [kpb, self.cbf], [tb])
                        self.E("dve", "tensor_copy", [tb], [ktp], out=ktp[:], in_=tb[:, 0:512].rearrange("p (a b) -> p a b", a=4))
                        kT = [ktp[:, p_, :] for p_ in range(4)]
                        vv = [vpb[:, h * 64:(h + 1) * 64] for h in range(8)]
                        kreads, vreads = [ktp], [vpb]
                        mneg, m01 = None, None
                    yield from self.attn_step(w, S, 8, TS, qT, kT, vv, 128, self.kvone[:, :], self.kvone, self.brow[:, 1024:1088],
                                              mneg, m01, w.smt, O, o_cols, blk == npg, blk == 0, [w.QT], kreads, vreads,
                                              att_out=att_out, att_lhs=att_lhs,
                                              first_o=(s_ == 0 and blk == npg), last_o=(s_ == NSEQ - 1 and blk == 0))
        self.interleave([stream(k) for k in range(NSTR)])
        self.head_norm(w, O[:, :], HS, HD, w.f512, [O], self.gso, w.mixed, w.mixed[:, 0:512], 1.0 / HD)

    def alloc_work(self, st, sample):
        class WS:
            pass
        w = WS()
        sb = lambda name, shape, dt: self.sb(st, name, shape, dt)
        w.xt = [sb("xt", [128, D], F32)]
        w.h = sb("h", [128, D], BF16)
        w.hT = sb("hT", [128, DC, 128], BF16)
        w.ssq = sb("ssq", [128, 1], F32)
        w.rstd = sb("rstd", [128, 1], F32)
        w.ss8 = sb("ss8", [128, 8], F32)
        w.rs8 = sb("rs8", [128, 8], F32)
        w.f512 = sb("f512", [128, 512], F32)
        w.kn = sb("kn", [128, 512], F32)
        w.knb = sb("knb", [128, 512], BF16)
        w.vf = w.f512
        w.qnb = sb("qnb", [128, 512], BF16)
        w.QT = sb("QT", [128, 4, 2, 128], BF16)
        self.E("pool", "memset", [], [w.QT], w.QT[:], 0.0)
        ns, T = (NSEQ, TS) if sample else (1, 128)
        w.XE4 = sb("XE4", [128, 4, ns * (3 + T)], F32)
        w.Hst = sb("Hst", [128, 12, ns * 3], F32)
        w.ba = sb("ba", [128, 8], F32)
        w.zs = sb("zs", [128, 512], BF16)
        w.Y4 = sb("Y4", [128, 4, 128], F32)
        w.E4 = sb("E4", [128, 4, 128], F32)
        w.QnT = sb("QnT", [128, 4, 128], BF16)
        w.KnT = sb("KnT", [128, 4, 128], BF16)
        w.Ktok = sb("Ktok", [128, 4, 128], BF16)
        w.Vtok = sb("Vtok", [128, 4, 128], F32)
        w.sc = sb("sc", [128, 40], F32)
        w.dg = [sb("dg", [128, 128], F32)] * 2
        w.fa = sb("fa", [128, 512], F32)
        w.fb = sb("fb", [128, 512], F32)
        w.fc = sb("fc", [128, 512], F32)
        w.fd = sb("fd", [128, 512], F32)
        w.Pf = [sb("Pf%d" % k, [128, 4, 128], F32) for k in range(2)]
        w.PTf = [sb("PTf%d" % k, [128, 4, 128], F32) for k in range(2)]
        w.MT = sb("MT", [128, 4, 128], F32)
        w.MTb = sb("MTb", [128, 4, 128], BF16)
        w.intraT = sb("intraT", [128, 4, 128], BF16)
        w.QdT = sb("QdT", [128, 4, 128], BF16)
        w.kdec = w.Ktok
        w.W = sb("W", [128, 4, 128], BF16)
        w.Vcb = w.W
        w.vnew = sb("vnew", [128, 4, 128], BF16)
        w.mixed = sb("mixed", [128, D], BF16)
        wd = 64 if sample else 512
        class ST:
            pass
        w.streams = []
        for k in range(2):
            S = ST()
            S.zbank = self.zb[k]
            S.et = sb("et%d" % k, [128, wd], BF16)
            S.spt = sb("spt%d" % k, [128, wd], BF16)
            S.att = sb("att%d" % k, [128, wd], BF16)
            S.Rb = sb("Rb%d" % k, [128, wd], BF16)
            w.streams.append(S)
        return w

    def phase1(self):
        c = self.cfg
        i, o = self.i, self.o
        self.xi = 0
        self.ai = 0
        self.pgi = 0
        with contextlib.ExitStack() as p1:
            winb = self.sb(p1, "winb", [128, DC, INC], BF16)
            self.winb = winb
            scale1 = self.sb(p1, "scale1", [128, D], F32)
            shift1 = self.sb(p1, "shift1", [128, D], F32)
            sv = self.sv
            WB = 512 * 2 + 64
            brow = self.sb(p1, "brow", [128, WB], BF16)
            nb = self.sb(p1, "nb", [128, 1], F32)
            with contextlib.ExitStack() as st:
                bexp = self.sb(st, "bexp", [128, WB], F32)
                for hg in range(2):
                    for h in range(4):
                        self.E("dve", "tensor_copy", [sv], [bexp], out=bexp[:, hg * 512 + h * 128: hg * 512 + (h + 1) * 128],
                               in_=sv[:, 328 + hg * 4 + h: 329 + hg * 4 + h].to_broadcast([128, 128]))
                for h in range(8):
                    self.E("dve", "tensor_copy", [sv], [bexp], out=bexp[:, 1024 + h * 8: 1024 + (h + 1) * 8],
                           in_=sv[:, 328 + h: 329 + h].to_broadcast([128, 8]))
                bhi = self.sb(st, "bhi", [128, WB], BF16)
                self.brow = brow
                idf = self.cm("ident")
                self.E("dve", "tensor_copy", [bexp], [bhi], out=bhi[:], in_=bexp[:])
                self.E("dve", "tensor_tensor", [bexp, bhi], [bexp], out=bexp[:], in0=bexp[:], in1=bhi[:], op=ALU.subtract)
                self.E("dve", "tensor_scalar", [bexp, self.ctm], [bexp], out=bexp[:], in0=bexp[:], scalar1=idf[:, 1:2], scalar2=None, op0=ALU.mult)
                self.E("dve", "scalar_tensor_tensor", [bhi, bexp, self.ctm], [bexp], out=bexp[:], in0=bhi[:], scalar=idf[:, 0:1], in1=bexp[:],
                       op0=ALU.mult, op1=ALU.add)
                self.E("dve", "tensor_scalar", [self.ctm], [nb], out=nb[:], in0=idf[:, 2:3], scalar1=-BIG, scalar2=None, op0=ALU.mult)
                self.E("dve", "tensor_scalar", [bexp, nb], [brow], out=brow[:], in0=bexp[:], scalar1=nb[:, 0:1], scalar2=None, op0=ALU.add)
                stg = [self.sb(st, "stg%d" % k, [128, DC * 512], F32) for k in range(2)]
                wv = i["w_in"].rearrange("(c p) n -> p c n", p=128)
                for ct in range(8):
                    n0, n1 = ct * 512, min(INC, (ct + 1) * 512)
                    self.stream_cast(stg, wv[:, :, n0:n1], winb, winb[:, :, n0:n1], eng="act" if ct % 2 else "dve")
            self.fence()
            with contextlib.ExitStack() as st:
                KT = self.sb(st, "KT", [128, 4, c.NBLK * 128], BF16)
                Vr = self.sb(st, "Vr", [128, c.NBLK, 512], BF16)
                self.KTt = [Tile(KT[:, :, b * 128:(b + 1) * 128], "KT%d" % b) for b in range(c.NBLK)]
                self.Vt = [Tile(Vr[:, b, :], "V%d" % b) for b in range(c.NBLK)]
                w = self.alloc_work(st, False)
                w.scale1, w.shift1 = scale1, shift1
                w.tabt = self.ctm
                w.causrep = self.sb(st, "causrep", [128, 512], BF16)
                w.caus01rep = self.sb(st, "caus01rep", [128, 512], BF16)
                w.causrep_t = self.sb(st, "causrep_t", [1, 1], F32)
                for h in range(4):
                    self.E("dve", "tensor_copy", [self.ctm], [w.causrep_t, w.causrep], out=w.causrep[:, h * 128:(h + 1) * 128], in_=self.cm("causneg"))
                    self.E("dve", "tensor_copy", [self.ctm], [w.causrep_t, w.caus01rep], out=w.caus01rep[:, h * 128:(h + 1) * 128], in_=self.cm("caus01"))
                self.Sf = self.sb(st, "Sf", [128, 4, 128], F32)
                self.Sb = self.sb(st, "Sb", [128, 4, 128], BF16)
                self.E("pool", "memset", [], [self.Sf], self.Sf[:], 0.0)
                self.E("pool", "memset", [], [self.Sb], self.Sb[:], 0.0)
                self.load_mod([shift1, scale1], [0, 1], False)
                tabs = (self.cm("ltincl_p"), self.cm("ones"), self.cm("ms01_p"))
                for b in range(c.NBLK):
                    own = b >= c.OWN0
                    outrow = None
                    if b >= c.OUT0:
                        r0 = (b - c.OUT0) * 128
                        outrow = (o["kp"][r0:r0 + 128, :], o["vp"][r0:r0 + 128, :])
                    fe = self.front_end(w, b, False, own, outrow)
                    self.front_dn(w, b, False, fe)
                    dn_gen = self.dn_chunk(w, b, False, own, tabs, b == c.NBLK - 1) if self.on("dn") else iter(())
                    if own and self.on("attn"):
                        self.attn_prompt(w, b, dn_gen)
                    else:
                        for _ in dn_gen:
                            pass
                    if own:
                        k = b - c.OWN0
                        if self.on("dn") and self.on("attn"):
                            self.dma(self.mixd[k * 128:(k + 1) * 128, :], w.mixed[:], reads=[w.mixed], writes=[self.t_mixd[k]])
                        if "mixed_p" in self.o and b >= c.OUT0:
                            r0 = (b - c.OUT0) * 128
                            self.E("dve", "tensor_copy", [w.mixed], [w.xt[0]], out=w.xt[0][:], in_=w.mixed[:])
                            self.dma(self.o["mixed_p"][r0:r0 + 128, :], w.xt[0][:], reads=[w.xt[0]])
            self.fence()
            if self.on("sample"):
                with contextlib.ExitStack() as st:
                    w = self.alloc_work(st, True)
                    w.scale1, w.shift1 = scale1, shift1
                    cts = self.sb(st, "cts", list(CT_SAMP.shape), F32)
                    self.dma(cts[:], i["ct_samp"][:, :], writes=[cts])
                    w.tabt = cts

                    def cs(name):
                        o_, w_ = CO_SAMP[name]
                        return cts[:, o_:o_ + w_]
                    w.colmask, w.rowmask = cs("colmask"), cs("rowmask")
                    w.KTs = self.sb(st, "KTs", [128, 4, 128], BF16)
                    w.Vs = self.sb(st, "Vs", [128, 512], BF16)
                    w.Sfh = [self.sb(st, "Sfh", [128, NSEQ, 128], F32)]
                    w.Sbh = [self.sb(st, "Sbh", [128, NSEQ, 128], BF16)]
                    w.Sn = w.Sfh
                    w.Km = self.sb(st, "Km", [128, NSEQ, 128], BF16)
                    w.Qm = w.Km
                    w.kdm = w.Km
                    self.load_mod([shift1, scale1], [0, 1], True)
                    hst_t = w.Sfh[0]
                    hst = hst_t[:, :, :].rearrange("p s v -> p (s v)")[0:NSEQ * 3, 0:3 * DNW]
                    self.dma(hst, i["dnc0"][:, :], writes=[hst_t])
                    for j in range(3):
                        g = self.G()
                        for ch in range(4):
                            self.tr(g[:, ch * 48:(ch + 1) * 48], hst[:, (j * 4 + ch) * 128:(j * 4 + ch + 1) * 128], self.cm("ident")[0:48, 0:48], [hst_t, self.ctm], [g])
                        self.act(w.Hst[:, j * 4:(j + 1) * 4, :], g[:, 0:192].rearrange("p (c t) -> p c t", c=4), AF.Copy, [g], [w.Hst])
                    fe = self.front_end(w, 0, True, True, (o["ksm"][:, :], o["vsm"][:, :]))
                    tabs = (cs("ltincl_s"), cs("seqm_s"), cs("ms01_s"))
                    self.front_dn(w, 0, True, fe)
                    dn_gen = self.dn_chunk(w, 0, True, True, tabs, False) if self.on("dn") else iter(())
                    for _ in dn_gen:
                        pass
                    if self.on("attn"):
                        pti = self.sb(st, "pti", [128, NSEQ * c.NPG], I32)
                        ptf_t = w.f512
                        ptf = ptf_t
                        io = self.sb(st, "io", [128, 1], I32)
                        iof = self.sb(st, "iof", [128, 1], F32)
                        w.idx = self.sb(st, "idx", [128, NSEQ * c.NPG], I32)
                        self.dma(pti[:], i["ptab"].partition_broadcast(128), writes=[pti])
                        self.E("pool", "iota", [], [io], io[:], pattern=[[0, 1]], base=0, channel_multiplier=1)
                        self.E("dve", "tensor_copy", [io], [iof], out=iof[:], in_=io[:])
                        npt = NSEQ * c.NPG
                        self.E("dve", "tensor_copy", [pti], [ptf], out=ptf[:, 0:npt], in_=pti[:])
                        self.E("dve", "tensor_scalar", [ptf, iof], [ptf], out=ptf[:, 0:npt], in0=ptf[:, 0:npt], scalar1=128.0, scalar2=iof[:, 0:1], op0=ALU.mult, op1=ALU.add)
                        self.E("dve", "tensor_copy", [ptf], [w.idx], out=w.idx[:], in_=ptf[:, 0:npt])
                        w.smt = self.sb(st, "smt", [1, 1], F32)
                        sm01 = self.sb(st, "sm01", [128, NSEQ, 64], BF16)
                        smneg = self.sb(st, "smneg", [128, NSEQ, 64], BF16)
                        o_, w_ = CO_SAMP["smask01"]
                        src = cts[:, o_:o_ + w_].rearrange("p (s q) -> p s q", s=NSEQ)
                        self.E("dve", "tensor_copy", [cts], [w.smt, sm01], out=sm01[:], in_=src)
                        self.E("dve", "tensor_scalar", [cts], [w.smt, smneg], out=smneg[:], in0=src, scalar1=-1.0, scalar2=BIG, op0=ALU.add, op1=ALU.mult)
                        w.sm01, w.smneg = sm01, smneg
                        w.attpad = [self.sb(st, "attpad%d" % k, [128, 8, 128], BF16) for k in range(2)]
                        w.kpf = [self.sb(st, "kpf%d" % k, [128, 512], F32) for k in range(4)]
                        w.vpf = [self.sb(st, "vpf%d" % k, [128, 512], F32) for k in range(4)]
                        w.kpb = [self.sb(st, "kpb%d" % k, [128, 512], BF16) for k in range(4)]
                        w.vpb = [self.sb(st, "vpb%d" % k, [128, 512], BF16) for k in range(4)]
                        w.ktp = [self.sb(st, "ktp%d" % k, [128, 4, 128], BF16) for k in range(4)]
                        self.attn_sample(w)
                    k = c.NOWN
                    if self.on("dn") and self.on("attn"):
                        self.dma(self.mixd[k * 128:(k + 1) * 128, :], w.mixed[:], reads=[w.mixed], writes=[self.t_mixd[k]])
                    if "mixed_s" in self.o:
                        self.E("dve", "tensor_copy", [w.mixed], [w.xt[0]], out=w.xt[0][:], in_=w.mixed[:])
                        self.dma(self.o["mixed_s"][:, :], w.xt[0][:], reads=[w.xt[0]])

    def phase2(self):
        c = self.cfg
        i, o = self.i, self.o
        with contextlib.ExitStack() as p2:
            sb = lambda name, shape, dt: self.sb(p2, name, shape, dt)
            woutb = sb("woutb", [128, DC, D], BF16)
            wupb = sb("wupb", [128, DC, 2 * DFF], BF16)
            wdnb = sb("wdnb", [128, FC, D], BF16)
            with contextlib.ExitStack() as st:
                stg = [self.sb(st, "stg%d" % k, [128, DC * 512], F32) for k in range(2)]
                n = 0
                wv = i["w_out"].rearrange("(c p) n -> p c n", p=128)
                for ct in range(2):
                    self.stream_cast(stg, wv[:, :, ct * 512:(ct + 1) * 512], woutb, woutb[:, :, ct * 512:(ct + 1) * 512], eng="act" if n % 2 else "dve")
                    n += 1
                wv = i["w_up"].rearrange("(c p) n -> p c n", p=128)
                for ct in range(11):
                    self.stream_cast(stg, wv[:, :, ct * 512:(ct + 1) * 512], wupb, wupb[:, :, ct * 512:(ct + 1) * 512], eng="act" if n % 2 else "dve")
                    n += 1
                wv = i["w_down"].rearrange("(c p) n -> p c n", p=128)
                for c0 in range(0, FC, 4):
                    c1 = min(FC, c0 + 4)
                    self.stream_cast(stg, wv[:, c0:c1, :], wdnb, wdnb[:, c0:c1, :], eng="act" if n % 2 else "dve")
                    n += 1
            self.fence()
            gt1, scale2, shift2, gt2 = [sb(nm, [128, D], F32) for nm in ("gt1", "scale2", "shift2", "gt2")]

            class WS:
                pass
            w = WS()
            w.xt = [sb("xt2", [128, D], F32)]
            w.h = sb("h2", [128, D], BF16)
            w.hT = sb("h2T", [128, DC, 128], BF16)
            w.ssq = sb("ssq2", [128, 1], F32)
            w.rstd = sb("rstd2", [128, 1], F32)
            mixb = sb("mixb", [128, D], BF16)
            mT = sb("mT", [128, DC, 128], BF16)
            yt = sb("yt", [128, D], F32)
            UE = sb("UE", [128, 4, NSEQ * (2 + TS)], F32)
            C4 = sb("C4", [128, 4, 128], F32)
            E2 = sb("E2", [128, 2, 128], F32)
            actT = sb("actT", [128, FC, 128], BF16)
            FH = sb("FH", [128, 44, NSEQ * 2], F32)
            fso = sb("fso", [NSEQ * 2, 512], F32)
            self.E("pool", "memset", [], [FH], FH[:], 0.0)
            blocks = [(b, False) for b in range(c.OWN0, c.NBLK)] + ([(0, True)] if self.on("sample") else [])
            cur_mod = None
            for (b, sample) in blocks:
                if cur_mod != sample:
                    self.load_mod([gt1, scale2, shift2, gt2], [2, 4, 3, 5], sample)
                    cur_mod = sample
                ns, T = (NSEQ, TS) if sample else (1, 128)
                k = c.NOWN if sample else b - c.OWN0
                halo = (not sample) and b == c.OWN0
                xt = w.xt[0]
                self.dma(xt[:], i["xs"][:, :] if sample else i["xp"][b * 128:(b + 1) * 128, :], writes=[xt])
                self.dma(mixb[:], self.mixd[k * 128:(k + 1) * 128, :], reads=[self.t_mixd[k]], writes=[mixb])
                tb = self.TB()
                for dc in range(DC):
                    self.tr(tb[:, dc * 128:(dc + 1) * 128], mixb[:, dc * 128:(dc + 1) * 128], self.identb, [mixb, self.cbf], [tb])
                self.act(mT[:], tb[:, :].rearrange("p (a b) -> p a b", a=DC), AF.Copy, [tb], [mT])
                for n in range(2):
                    g = self.G()
                    for dc in range(DC):
                        self.mm(g[:, :], mT[:, dc, :], woutb[:, dc, n * 512:(n + 1) * 512], dc == 0, dc == DC - 1, [mT, woutb], [g])
                    self.E("dve", "tensor_tensor", [g, gt1], [yt], out=yt[:, n * 512:(n + 1) * 512], in0=g[:, :], in1=gt1[:, n * 512:(n + 1) * 512], op=ALU.mult)
                self.E("pool", "tensor_tensor", [yt, xt], [xt], out=xt[:], in0=yt[:], in1=xt[:], op=ALU.add)
                x1 = xt
                if "x1_p" in o and (not sample) and b >= c.OUT0:
                    r0 = (b - c.OUT0) * 128
                    self.dma(o["x1_p"][r0:r0 + 128, :], x1[:], reads=[x1])
                self.norm_mod(w, x1, scale2, shift2, w.hT, tmp=yt)
                if sample:
                    for j in range(11):
                        hst = fso
                        self.dma(hst[:, :], i["ffc0"][:, j * 512:(j + 1) * 512], writes=[hst])
                        g = self.G()
                        for ch in range(4):
                            self.tr(g[:, ch * 32:(ch + 1) * 32], hst[:, ch * 128:(ch + 1) * 128], self.cm("ident")[0:32, 0:32], [hst, self.ctm], [g])
                        self.act(FH[:, j * 4:(j + 1) * 4, :], g[:, 0:128].rearrange("p (c t) -> p c t", c=4), AF.Copy, [g], [FH])
                ue4 = UE[:, :, 0:ns * (2 + T)].rearrange("p c (s t) -> p c s t", s=ns)
                fh4 = FH[:, :, 0:ns * 2].rearrange("p c (s t) -> p c s t", s=ns)
                for gi_ in range(11):
                    chs = [2 * gi_, 2 * gi_ + 1, FC + 2 * gi_, FC + 2 * gi_ + 1]
                    g = self.G()
                    for q_, ch in enumerate(chs):
                        for dc in range(DC):
                            self.mm(g[:, q_ * 128:(q_ + 1) * 128], wupb[:, dc, ch * 128:(ch + 1) * 128], w.hT[:, dc, :], dc == 0, dc == DC - 1, [w.hT, wupb], [g])
                    for half in range(2):
                        self.E("pool", "tensor_copy", [FH], [UE], out=ue4[:, half * 2:half * 2 + 2, :, 0:2], in_=fh4[:, chs[half * 2]:chs[half * 2] + 2, :, :])
                    src4 = g[:, :].rearrange("p (c s t) -> p c s t", c=4, s=ns)
                    if halo:
                        self.act(ue4[:, :, :, 2:2 + T], src4, AF.Copy, [g, self.bvd], [UE], scale=self.bvd[:, b:b + 1])
                    else:
                        self.act(ue4[:, :, :, 2:2 + T], src4, AF.Copy, [g], [UE])
                    for half in range(2):
                        self.E("pool", "tensor_copy", [UE], [FH], out=fh4[:, chs[half * 2]:chs[half * 2] + 2, :, :], in_=ue4[:, half * 2:half * 2 + 2, :, T:T + 2])
                    if halo:
                        continue
                    for q_, ch in enumerate(chs):
                        eng = "dve"
                        yv = C4[:, q_, :].rearrange("p (s t) -> p s t", s=ns)
                        self.E(eng, "tensor_scalar", [UE, self.wfc], [C4], out=yv, in0=ue4[:, q_, :, 0:T], scalar1=self.wfc[:, 0, ch:ch + 1], scalar2=None, op0=ALU.mult)
                        for kk in range(1, 3):
                            self.E(eng, "scalar_tensor_tensor", [UE, self.wfc, C4], [C4], out=yv, in0=ue4[:, q_, :, kk:kk + T],
                                   scalar=self.wfc[:, kk, ch:ch + 1], in1=yv, op0=ALU.mult, op1=ALU.add)
                    self.act(E2[:], C4[:, 2:4, :], AF.Exp, [C4], [E2], scale=-1.0)
                    self.act(E2[:], E2[:], AF.Ln, [E2], [E2], bias=1.0)
                    self.act(E2[:], E2[:], AF.Exp, [E2], [E2], scale=-1.0)
                    self.E("pool", "tensor_tensor", [E2, C4], [E2], out=E2[:], in0=E2[:], in1=C4[:, 2:4, :], op=ALU.mult)
                    self.E("dve", "tensor_tensor", [E2, C4], [actT], out=actT[:, 2 * gi_:2 * gi_ + 2, :], in0=E2[:], in1=C4[:, 0:2, :], op=ALU.mult)
                last_p = (not sample) and b == c.NBLK - 1
                if sample or last_p:
                    ncol = ns * 2
                    dst = o["fcs"] if sample else o["fcp"]
                    for j in range(11):
                        g = self.G()
                        for ch in range(4):
                            self.tr(g[0:ncol, ch * 128:(ch + 1) * 128], FH[:, j * 4 + ch, 0:ncol], self.cm("ident"), [FH, self.ctm], [g])
                        self.E("dve", "tensor_copy", [g], [fso], out=fso[0:ncol, :], in_=g[0:ncol, :])
                        self.dma(dst[:, j * 512:(j + 1) * 512], fso[0:ncol, :], reads=[fso])
                if halo:
                    continue
                for n in range(2):
                    g = self.G()
                    for fc_ in range(FC):
                        self.mm(g[:, :], actT[:, fc_, :], wdnb[:, fc_, n * 512:(n + 1) * 512], fc_ == 0, fc_ == FC - 1, [actT, wdnb], [g])
                    self.E("dve", "tensor_tensor", [g, gt2], [yt], out=yt[:, n * 512:(n + 1) * 512], in0=g[:, :], in1=gt2[:, n * 512:(n + 1) * 512], op=ALU.mult)
                self.E("pool", "tensor_tensor", [yt, x1], [yt], out=yt[:], in0=yt[:], in1=x1[:], op=ALU.add)
                if sample:
                    self.dma(o["ys"][:, :], yt[:], reads=[yt])
                elif b >= c.OUT0:
                    r0 = (b - c.OUT0) * 128
                    self.dma(o["yp"][r0:r0 + 128, :], yt[:], reads=[yt])


def core_inputs(cfg, core, inp):
    b, half = core // 2, core % 2
    S = cfg.NBLK * 128
    xp_full = np.asarray(inp["x_prompt"][b], np.float32)
    if half == 1:
        xp = xp_full
    else:
        xp = np.concatenate([np.zeros((S // 2, D), np.float32), xp_full[:S // 2]], axis=0)
    s0, s1 = core * NSEQ, (core + 1) * NSEQ
    m = {}
    m["xp"] = np.ascontiguousarray(xp)
    m["xs"] = np.ascontiguousarray(np.asarray(inp["x_sample"][s0:s1], np.float32).reshape(NSEQ * TS, D))
    m["cvec"] = np.ascontiguousarray(np.concatenate([np.asarray(inp["c_prompt"][b:b + 1], np.float32),
                                                     np.asarray(inp["c_sample"][s0:s1], np.float32)], axis=0))
    m["cache_k"] = np.asarray(inp["cache_k"], np.float32).reshape(cfg.NPHYS * 128, SBW)
    m["cache_v"] = np.asarray(inp["cache_v"], np.float32).reshape(cfg.NPHYS * 128, SBW)
    m["ptab"] = np.ascontiguousarray(np.asarray(inp["page_table"][s0:s1], np.int32).reshape(-1))
    m["s0"] = np.ascontiguousarray(np.asarray(inp["state_delta"][0, s0:s1], np.float32))
    m["dnc0"] = np.ascontiguousarray(np.asarray(inp["state_dn_conv"][0, s0:s1], np.float32).reshape(NSEQ * 3, 3 * DNW))
    m["ffc0"] = np.ascontiguousarray(np.asarray(inp["state_ffn_conv"][0, s0:s1], np.float32).reshape(NSEQ * 2, 2 * DFF))
    for k, nm in (("w_ada", "w_ada"), ("b_ada", "b_ada"), ("g_attn", "g_attn_norm"), ("w_in", "w_in"), ("g_q", "g_q"),
                  ("g_k", "g_k"), ("sb_bias", "sb_bias"), ("g_sb_out", "g_sb_out"), ("w_dn_conv", "w_dn_conv"),
                  ("a_log", "a_log"), ("dt_bias", "dt_bias"), ("g_dn_out", "g_dn_out"), ("w_out", "w_out"),
                  ("g_ffn", "g_ffn_norm"), ("w_up", "w_up"), ("w_ffn_conv", "w_ffn_conv"), ("w_down", "w_down")):
        m[k] = np.ascontiguousarray(np.asarray(inp[nm], np.float32)[0])
    m["ct_main"] = CT_MAIN
    m["ct_samp"] = CT_SAMP
    kv = np.zeros((128, 256), np.float32)
    if half == 1:
        kv[0:2, 0:128] = 1.0
    else:
        kv[2, 0:128] = 1.0
    kv[0:2, 128:256] = 1.0
    m["kvlo"] = kv
    bv = np.ones((128, cfg.NBLK), np.float32)
    if half == 0:
        bv[:, :cfg.NBLK // 2] = 0.0
    m["blkvalid"] = bv
    return m


def assemble(cfg, res, nb, nsamp):
    S = cfg.NBLK * 128
    H = S // 2
    yp = np.zeros((nb, S, D), np.float32)
    ys = np.zeros((nsamp, TS, D), np.float32)
    kp = np.zeros((1, nb, S, HS, HD), np.float32)
    vp = np.zeros((1, nb, S, HS, HD), np.float32)
    ks = np.zeros((1, nsamp, TS, HS, HD), np.float32)
    vs = np.zeros((1, nsamp, TS, HS, HD), np.float32)
    sp = np.zeros((1, nb, DNH, DND, DND), np.float32)
    ss = np.zeros((1, nsamp, DNH, DND, DND), np.float32)
    dcp = np.zeros((1, nb, 3, 3 * DNW), np.float32)
    dcs = np.zeros((1, nsamp, 3, 3 * DNW), np.float32)
    fcp = np.zeros((1, nb, 2, 2 * DFF), np.float32)
    fcs = np.zeros((1, nsamp, 2, 2 * DFF), np.float32)
    for core, r in res.items():
        b, half = core // 2, core % 2
        s0, s1 = core * NSEQ, (core + 1) * NSEQ
        yp[b, half * H:(half + 1) * H] = r["yp"]
        kp[0, b, half * H:(half + 1) * H] = r["kp"].reshape(H, HS, HD)
        vp[0, b, half * H:(half + 1) * H] = r["vp"].reshape(H, HS, HD)
        ys[s0:s1] = r["ys"].reshape(NSEQ, TS, D)
        ks[0, s0:s1] = r["ksm"].reshape(NSEQ, TS, HS, HD)
        vs[0, s0:s1] = r["vsm"].reshape(NSEQ, TS, HS, HD)
        ss[0, s0:s1] = r["ss_state"]
        dcs[0, s0:s1] = r["dcs"].reshape(NSEQ, 3, 3 * DNW)
        fcs[0, s0:s1] = r["fcs"].reshape(NSEQ, 2, 2 * DFF)
        if half == 1:
            sp[0, b] = r["sp_state"]
            dcp[0, b] = r["dcp"]
            fcp[0, b] = r["fcp"]
    return (yp, ys, kp, vp, ks, vs, sp, ss, dcp, dcs, fcp, fcs)


_NC_CACHE = {}


def kernel(**inputs):
    cfg = Cfg(nblk=inputs["x_prompt"].shape[1] // 128, npg=inputs["page_table"].shape[1], nphys=inputs["cache_k"].shape[1])
    key = (cfg.NBLK, cfg.NPG, cfg.NPHYS)
    if key not in _NC_CACHE:
        _NC_CACHE[key] = Builder(cfg).build()
    nc = _NC_CACHE[key]
    ncores = 8
    in_maps = [core_inputs(cfg, c, inputs) for c in range(ncores)]
    res = run_bass_kernel_spmd(nc, in_maps, core_ids=list(range(ncores)))
    out = assemble(cfg, {c: res.results[c] for c in range(ncores)}, inputs["x_prompt"].shape[0], inputs["x_sample"].shape[0])
    return out
```

```python
import contextlib
import numpy as np
import concourse.bass as bass
import concourse.mybir as mybir
from concourse.bass_utils import run_bass_kernel_spmd

F32 = mybir.dt.float32
BF16 = mybir.dt.bfloat16
I32 = mybir.dt.int32
AF = mybir.ActivationFunctionType
ALU = mybir.AluOpType
AX = mybir.AxisListType

D = 1024
DC = 8
HS = 8
HD = 64
SBW = 512
DNH = 4
DND = 128
DNW = 512
DFF = 2816
FC = 22
INC = 3592
EPS = 1e-6
BIG = 30000.0
NSEQ = 16
TS = 8


class Cfg:
    def __init__(self, nblk=32, npg=16, nphys=2560):
        self.NBLK = nblk
        self.OWN0 = nblk // 2 - 1
        self.OUT0 = nblk // 2
        self.NPG = npg
        self.NPHYS = nphys
        self.NOUT = nblk - self.OUT0
        self.NOWN = nblk - self.OWN0


class Tile:
    __slots__ = ("ap", "name", "last_w", "readers", "excl")

    def __init__(self, ap, name="", excl=False):
        self.ap = ap
        self.name = name
        self.last_w = None
        self.readers = []
        self.excl = excl

    def __getitem__(self, k):
        return self.ap[k]


class Op:
    __slots__ = ("eng", "fn", "deps", "need_inc", "count", "sem", "is_dma", "idx")

    def __init__(self, eng, fn, is_dma=False):
        self.eng = eng
        self.fn = fn
        self.deps = set()
        self.need_inc = is_dma
        self.count = 0
        self.sem = None
        self.is_dma = is_dma


COMPUTE = ("pe", "act", "dve", "pool")


class Sched:
    def __init__(self, nc, n_dma_sems=16):
        self.nc = nc
        self.ops = {e: [] for e in COMPUTE + ("sp",)}
        self.n_dma_sems = n_dma_sems
        self.nops = 0
        self.junk_fn = None
        self.junk_n = 0
        self.n_pe_waits = 0

    def _track(self, op, reads, writes):
        ex = [t for t in reads if t.excl]
        if ex:
            reads = [t for t in reads if not t.excl]
            writes = list(writes) + [t for t in ex if t not in writes]
        for t in reads:
            if t.last_w is not None:
                op.deps.add(t.last_w)
        for t in writes:
            if t.last_w is not None:
                op.deps.add(t.last_w)
            for r in t.readers:
                op.deps.add(r)
        for t in reads:
            t.readers.append(op)
        for t in writes:
            t.last_w = op
            t.readers = []
        op.deps.discard(op)

    def add(self, eng, fn, reads=(), writes=(), is_dma=False):
        op = Op(eng, fn, is_dma)
        self._track(op, reads, writes)
        self.ops[eng].append(op)
        self.nops += 1
        return op

    def dma(self, out_ap, in_ap, reads=(), writes=(), queue="sp", **kw):
        def fn(e, out_ap=out_ap, in_ap=in_ap, kw=kw):
            return e.dma_start(out=out_ap, in_=in_ap, **kw)
        return self.add(queue, fn, reads, writes, is_dma=True)

    def emit(self):
        nc = self.nc

        def skip(d, op):
            return d.eng == "pe" and op.eng == "pe" and not d.is_dma and not op.is_dma
        for e in self.ops:
            for op in self.ops[e]:
                for d in op.deps:
                    if not skip(d, op):
                        d.need_inc = True
        with contextlib.ExitStack() as st:
            sems = {e: st.enter_context(nc.semaphore("s_" + e)) for e in COMPUTE}
            dma_sems = {}
            for q in self.ops:
                if any(o.is_dma for o in self.ops[q]):
                    dma_sems[q] = [st.enter_context(nc.semaphore("d_%s_%d" % (q, i)))
                                   for i in range(self.n_dma_sems)]
            for e in self.ops:
                c = 0
                j = 0
                for op in self.ops[e]:
                    if op.is_dma:
                        ring = dma_sems[e]
                        op.sem = ring[j % len(ring)]
                        op.count = 16 * (j // len(ring) + 1)
                        j += 1
                    elif op.need_inc:
                        c += 1
                        op.count = c
                        op.sem = sems[e]
            block = st.enter_context(nc.Block())
            handles = {"pe": block.tensor, "act": block.scalar, "dve": block.vector,
                       "pool": block.gpsimd, "sp": block.sync}

            def make(e):
                oplist = self.ops[e]

                def body(eng):
                    known = {}
                    nwait = [0]

                    def wait(sem, val):
                        if known.get(id(sem), 0) >= val:
                            return
                        eng.wait_ge(sem, val)
                        known[id(sem)] = val
                    for op in oplist:
                        need = {}
                        for d in op.deps:
                            if skip(d, op):
                                continue
                            k = id(d.sem)
                            if k not in need or need[k][1] < d.count:
                                need[k] = (d.sem, d.count)
                        if op.is_dma and op.count > 16:
                            k = id(op.sem)
                            v = op.count - 16
                            if k not in need or need[k][1] < v:
                                need[k] = (op.sem, v)
                        pend = [(sem, val) for sem, val in need.values() if known.get(id(sem), 0) < val]
                        if pend and e == "pe" and self.junk_fn is not None and op.fn is not None:
                            nwait[0] += 1
                            for _ in range(self.junk_n):
                                self.junk_fn(eng)
                        for sem, val in pend:
                            wait(sem, val)
                        if op.fn is None:
                            continue
                        ins = op.fn(eng)
                        if op.is_dma:
                            ins.then_inc(op.sem, 16)
                        elif op.need_inc:
                            ins.then_inc(op.sem, 1)
                    last = {}
                    for op in oplist:
                        if op.is_dma:
                            last[id(op.sem)] = (op.sem, op.count)
                    for sem, val in last.values():
                        wait(sem, val)
                    if e == "pe":
                        self.n_pe_waits = nwait[0]
                return body
            for e in self.ops:
                if self.ops[e]:
                    handles[e](make(e))


def host_consts():
    i = np.arange(128)
    c = {}
    c["ident"] = np.eye(128, dtype=np.float32)
    c["ones"] = np.ones((128, 128), np.float32)
    for nm, nseq in (("p", 1), ("s", NSEQ)):
        t = 128 // nseq
        seq = i // t
        same = seq[:, None] == seq[None, :]
        incl = same & (i[None, :] <= i[:, None])
        strict = same & (i[None, :] < i[:, None])
        c["ltincl_" + nm] = incl.T.astype(np.float32)
        c["seqm_" + nm] = same.astype(np.float32)
        c["nmincl_" + nm] = np.where(incl, 0.0, BIG).astype(np.float32)
        c["nminclT_" + nm] = np.where(incl.T, 0.0, -BIG).astype(np.float32)
        c["ms01_" + nm] = strict.astype(np.float32)
    caus = i[:, None] < i[None, :]
    c["caus01"] = caus.astype(np.float32)
    c["causneg"] = np.where(caus, 0.0, -BIG).astype(np.float32)
    kt = i[:, None, None]
    ss_ = np.arange(NSEQ)[None, :, None]
    qq = (np.arange(64) % TS)[None, None, :]
    c["smask01"] = ((kt // TS == ss_) & (kt % TS < qq)).astype(np.float32).reshape(128, NSEQ * 64)
    c["ntri"] = np.where(i[:, None] >= i[None, :], -1.0, 0.0).astype(np.float32)
    cm = (np.arange(NSEQ)[:, None] == (i // TS)[None, :]).astype(np.float32)
    c["colmask"] = np.broadcast_to(cm.reshape(1, NSEQ * 128), (128, NSEQ * 128)).copy()
    c["rowmask"] = np.zeros((128, 128), np.float32)
    c["rowmask"][:, :NSEQ] = cm.T
    main = ["ident", "ones", "ltincl_p", "ms01_p", "caus01", "causneg", "ntri"]
    samp = ["ltincl_s", "seqm_s", "ms01_s", "rowmask", "colmask", "smask01"]

    def pack(names):
        off = {}
        o = 0
        for k in names:
            off[k] = (o, c[k].shape[1])
            o += c[k].shape[1]
        return np.concatenate([c[k] for k in names], axis=1).astype(np.float32), off
    return pack(main) + pack(samp)


CT_MAIN, CO_MAIN, CT_SAMP, CO_SAMP = host_consts()


class Builder:
    def __init__(self, cfg, stages=("all",), dbg=()):
        self.cfg = cfg
        self.stages = stages
        self.dbg = dbg
        self.nc = bass.Bass("TRN2", target_bir_lowering=False)
        self.s = Sched(self.nc)
        self.fence_id = 0
        self._uid = 0
        import os
        self.use_r32 = os.environ.get("USE_R32", "0") == "1"

    def on(self, st):
        return "all" in self.stages or st in self.stages

    def sb(self, stack, name, shape, dt):
        self._uid += 1
        h = stack.enter_context(self.nc.sbuf_tensor("%s_%d" % (name, self._uid), list(shape), dt))
        return Tile(h, name)

    def view(self, ap, name=""):
        return Tile(ap, name)

    def din(self, name, shape, dt=F32):
        return self.nc.dram_tensor(name, list(shape), dt, kind="ExternalInput").ap()

    def dout(self, name, shape, dt=F32):
        return self.nc.dram_tensor(name, list(shape), dt, kind="ExternalOutput").ap()

    def dscr(self, name, shape, dt=F32):
        return self.nc.dram_tensor(name, list(shape), dt, kind="Internal").ap()

    def E(self, eng, meth, reads, writes, *a, **kw):
        return self.s.add(eng, lambda e: getattr(e, meth)(*a, **kw), reads, writes)

    def mm(self, out, lhsT, rhs, start, stop, reads, writes, skip=False, r32=False):
        if r32 and self.use_r32:
            lhsT = lhsT.bitcast(mybir.dt.float32r)
            rhs = rhs.bitcast(mybir.dt.float32r)
        return self.s.add("pe", lambda e: e.matmul(out, lhsT=lhsT, rhs=rhs, start=start, stop=stop, skip_group_check=skip),
                          reads, writes)

    def tr(self, out, in_, ident, reads, writes):
        return self.s.add("pe", lambda e: e.transpose(out=out, in_=in_, identity=ident), reads, writes)

    def act(self, out, in_, func, reads, writes, **kw):
        return self.s.add("act", lambda e: e.activation(out=out, in_=in_, func=func, **kw), reads, writes)

    def dma(self, out, in_, reads=(), writes=(), **kw):
        return self.s.dma(out, in_, reads, writes, **kw)

    def fence(self):
        s = self.s
        f = set()
        for e in COMPUTE:
            real = [o for o in s.ops[e] if o.fn is not None and not o.is_dma]
            if real:
                f.add(real[-1])
        for q in s.ops:
            d = [o for o in s.ops[q] if o.is_dma]
            for o in d[-s.n_dma_sems:]:
                f.add(o)
        for e in COMPUTE + ("sp",):
            op = Op(e, None)
            op.deps = set(f)
            s.ops[e].append(op)

    def G(self):
        t = self.pg[self.gi % len(self.pg)]
        self.gi += 1
        return t

    def TB(self):
        t = self.ptb[self.ti % len(self.ptb)]
        self.ti += 1
        return t

    def declare(self):
        c = self.cfg
        NT = c.NBLK * 128
        i = {}
        i["xp"] = self.din("xp", [NT, D])
        i["xs"] = self.din("xs", [128, D])
        i["cvec"] = self.din("cvec", [17, D])
        i["cache_k"] = self.din("cache_k", [c.NPHYS * 128, SBW])
        i["cache_v"] = self.din("cache_v", [c.NPHYS * 128, SBW])
        i["ptab"] = self.din("ptab", [NSEQ * c.NPG], I32)
        i["s0"] = self.din("s0", [NSEQ, DNH, DND, DND])
        i["dnc0"] = self.din("dnc0", [NSEQ * 3, 3 * DNW])
        i["ffc0"] = self.din("ffc0", [NSEQ * 2, 2 * DFF])
        i["w_ada"] = self.din("w_ada", [D, 6 * D])
        i["b_ada"] = self.din("b_ada", [6 * D])
        i["g_attn"] = self.din("g_attn", [D])
        i["w_in"] = self.din("w_in", [D, INC])
        i["g_q"] = self.din("g_q", [HD])
        i["g_k"] = self.din("g_k", [HD])
        i["sb_bias"] = self.din("sb_bias", [HS])
        i["g_sb_out"] = self.din("g_sb_out", [HD])
        i["w_dn_conv"] = self.din("w_dn_conv", [4, 3 * DNW])
        i["a_log"] = self.din("a_log", [DNH])
        i["dt_bias"] = self.din("dt_bias", [DNH])
        i["g_dn_out"] = self.din("g_dn_out", [DND])
        i["w_out"] = self.din("w_out", [D, D])
        i["g_ffn"] = self.din("g_ffn", [D])
        i["w_up"] = self.din("w_up", [D, 2 * DFF])
        i["w_ffn_conv"] = self.din("w_ffn_conv", [3, 2 * DFF])
        i["w_down"] = self.din("w_down", [DFF, D])
        i["ct_main"] = self.din("ct_main", list(CT_MAIN.shape))
        i["ct_samp"] = self.din("ct_samp", list(CT_SAMP.shape))
        i["kvlo"] = self.din("kvlo", [128, 256])
        i["blkvalid"] = self.din("blkvalid", [128, c.NBLK])
        self.i = i
        o = {}
        o["yp"] = self.dout("yp", [c.NOUT * 128, D])
        o["ys"] = self.dout("ys", [128, D])
        o["kp"] = self.dout("kp", [c.NOUT * 128, SBW])
        o["vp"] = self.dout("vp", [c.NOUT * 128, SBW])
        o["ksm"] = self.dout("ksm", [128, SBW])
        o["vsm"] = self.dout("vsm", [128, SBW])
        o["sp_state"] = self.dout("sp_state", [DNH, DND, DND])
        o["ss_state"] = self.dout("ss_state", [NSEQ, DNH, DND, DND])
        o["dcp"] = self.dout("dcp", [3, 3 * DNW])
        o["dcs"] = self.dout("dcs", [NSEQ * 3, 3 * DNW])
        o["fcp"] = self.dout("fcp", [2, 2 * DFF])
        o["fcs"] = self.dout("fcs", [NSEQ * 2, 2 * DFF])
        for name, shape in self.dbg:
            o[name] = self.dout(name, shape)
        self.o = o
        self.modd = self.dscr("modd", [17, 6 * D])
        self.mixd = self.dscr("mixd", [(c.NOWN + 1) * 128, D], BF16)
        self.t_modd = Tile(None, "modd")
        self.t_mixd = [Tile(None, "mixd%d" % k) for k in range(c.NOWN + 1)]

    def stream_cast(self, stack_tiles, src_view, dst_tile, dst_ap, eng="dve"):
        stg = stack_tiles[self.sci % len(stack_tiles)]
        self.sci += 1
        shp = src_view.shape
        sap = stg[:, 0:shp[1] * shp[2]].rearrange("p (a b) -> p a b", a=shp[1])
        self.dma(sap, src_view, writes=[stg])
        if eng == "act":
            self.act(dst_ap, sap, AF.Copy, [stg], [dst_tile])
        else:
            self.E(eng, "tensor_copy", [stg], [dst_tile], out=dst_ap, in_=sap)

    def rsqrt_ops(self, ss, rs, n, scale, reads_extra=()):
        self.act(rs[:, 0:n], ss[:, 0:n], AF.Ln, [ss] + list(reads_extra), [rs], scale=scale, bias=self.epsb[:, 0:1])
        self.act(rs[:, 0:n], rs[:, 0:n], AF.Exp, [rs], [rs], scale=-0.5)

    def build(self):
        nc = self.nc
        c = self.cfg
        self.declare()
        i, o = self.i, self.o
        self.gi = 0
        self.ti = 0
        self.sci = 0
        with contextlib.ExitStack() as top:
            self.pg = [Tile(top.enter_context(nc.psum_tensor("pg%d" % k, [128, 512], F32)), "pg%d" % k, True) for k in range(2)]
            self.zb = [Tile(top.enter_context(nc.psum_tensor("zb%d" % k, [128, 512], F32)), "zb%d" % k, True) for k in range(2)]
            self.po = [Tile(top.enter_context(nc.psum_tensor("po%d" % k, [128, 512], F32)), "po%d" % k, True) for k in range(2)]
            self.ptb = [Tile(top.enter_context(nc.psum_tensor("ptb%d" % k, [128, 1024], BF16)), "ptb%d" % k, True) for k in range(2)]
            ctm = self.sb(top, "ctm", list(CT_MAIN.shape), F32)
            self.ctm = ctm
            self.dma(ctm[:], i["ct_main"][:, :], writes=[ctm])

            def cm(name):
                o_, w_ = CO_MAIN[name]
                return ctm[:, o_:o_ + w_]
            self.cm = cm
            cbf = self.sb(top, "cbf", [128, 4 * 128], BF16)
            self.cbf = cbf
            self.E("dve", "tensor_copy", [ctm], [cbf], out=cbf[:, 0:128], in_=cm("ident"))
            self.E("dve", "tensor_copy", [ctm], [cbf], out=cbf[:, 128:256], in_=cm("ntri"))
            self.E("dve", "tensor_copy", [ctm], [cbf], out=cbf[:, 256:384], in_=cm("causneg"))
            self.E("dve", "tensor_scalar", [ctm], [cbf], out=cbf[:, 384:512], in0=cm("ones"), scalar1=-1.0, scalar2=None, op0=ALU.mult)
            self.identb = cbf[:, 0:128]
            self.ntrib = cbf[:, 128:256]
            self.causnegb = cbf[:, 256:384]
            self.negonesb = cbf[:, 384:512]
            epsb = self.sb(top, "epsb", [128, 1], F32)
            self.epsb = epsb
            self.E("pool", "memset", [], [epsb], epsb[:], EPS)
            sv = self.sb(top, "sv", [128, 64 * 3 + 128 + 4 + 4 + 8], F32)
            self.sv = sv
            self.dma(sv[:, 0:64], i["g_q"].partition_broadcast(128), writes=[sv])
            self.dma(sv[:, 64:128], i["g_k"].partition_broadcast(128), writes=[sv])
            self.dma(sv[:, 128:192], i["g_sb_out"].partition_broadcast(128), writes=[sv])
            self.dma(sv[:, 192:320], i["g_dn_out"].partition_broadcast(128), writes=[sv])
            self.dma(sv[:, 320:324], i["a_log"].partition_broadcast(128), writes=[sv])
            self.dma(sv[:, 324:328], i["dt_bias"].partition_broadcast(128), writes=[sv])
            self.dma(sv[:, 328:336], i["sb_bias"].partition_broadcast(128), writes=[sv])
            self.E("dve", "tensor_scalar", [sv], [sv], out=sv[:, 0:64], in0=sv[:, 0:64], scalar1=HD ** -0.5, scalar2=None, op0=ALU.mult)
            self.act(sv[:, 320:324], sv[:, 320:324], AF.Exp, [sv], [sv])
            self.E("dve", "tensor_scalar", [sv], [sv], out=sv[:, 320:324], in0=sv[:, 320:324], scalar1=-1.0, scalar2=None, op0=ALU.mult)
            self.gq8, self.gk, self.gso, self.gdn = sv[:, 0:64], sv[:, 64:128], sv[:, 128:192], sv[:, 192:320]
            self.negA, self.dtb = sv[:, 320:324], sv[:, 324:328]
            kvf = self.sb(top, "kvf", [128, 256], F32)
            self.dma(kvf[:], i["kvlo"][:, :], writes=[kvf])
            kvd = self.sb(top, "kvd", [128, 128], BF16)
            self.kvd = kvd
            self.E("dve", "tensor_copy", [kvf], [kvd], out=kvd[:], in_=kvf[:, 0:128])
            bvd = self.sb(top, "bvd", [128, c.NBLK], F32)
            self.bvd = bvd
            self.dma(bvd[:], i["blkvalid"][:, :], writes=[bvd])
            kvone = self.sb(top, "kvone", [128, 128], BF16)
            self.kvone = kvone
            self.E("dve", "tensor_copy", [kvf], [kvone], out=kvone[:], in_=kvf[:, 128:256])
            wdc = self.sb(top, "wdc", [128, 4, 12], F32)
            self.wdc = wdc
            for t_ in range(4):
                self.dma(wdc[:, t_, :], i["w_dn_conv"][t_].rearrange("(c p) -> p c", p=128), writes=[wdc], allow_slow_non_contiguous=True)
            wfc = self.sb(top, "wfc", [128, 3, 44], F32)
            self.wfc = wfc
            for t_ in range(3):
                self.dma(wfc[:, t_, :], i["w_ffn_conv"][t_].rearrange("(c p) -> p c", p=128), writes=[wfc], allow_slow_non_contiguous=True)

            if self.on("setup"):
                self.setup_mod()
            self.fence()
            if self.on("p1"):
                self.phase1()
            self.fence()
            if self.on("p2"):
                self.phase2()
            self.s.emit()
        return nc

    def setup_mod(self):
        i = self.i
        with contextlib.ExitStack() as st:
            cv = self.sb(st, "cv", [17, D], F32)
            ex = self.sb(st, "ex", [17, D], F32)
            scb = self.sb(st, "scb", [17, D], BF16)
            scT = self.sb(st, "scT", [128, DC, 17], BF16)
            stg = [self.sb(st, "stg%d" % k, [128, DC * 512], F32) for k in range(2)]
            wab = [self.sb(st, "wab%d" % k, [128, DC, 512], BF16) for k in range(2)]
            bada = self.sb(st, "bada", [17, 512], F32)
            gv = self.sb(st, "gv", [17, 2 * D], F32)
            mt = [self.sb(st, "mt%d" % k, [17, 512], F32) for k in range(2)]
            self.dma(cv[:], i["cvec"][:, :], writes=[cv])
            self.dma(gv[:, 0:D], i["g_attn"].partition_broadcast(17), writes=[gv])
            self.dma(gv[:, D:2 * D], i["g_ffn"].partition_broadcast(17), writes=[gv])
            self.act(ex[:], cv[:], AF.Exp, [cv], [ex], scale=-1.0)
            self.E("dve", "tensor_scalar", [ex], [ex], out=ex[:], in0=ex[:], scalar1=1.0, scalar2=None, op0=ALU.add)
            self.E("dve", "reciprocal", [ex], [ex], out=ex[:], in_=ex[:])
            self.E("dve", "tensor_tensor", [ex, cv], [scb], out=scb[:], in0=cv[:], in1=ex[:], op=ALU.mult)
            tb = self.TB()
            for dc in range(DC):
                self.tr(tb[:, dc * 32:dc * 32 + 17], scb[0:17, dc * 128:(dc + 1) * 128], self.identb[0:17, 0:17], [scb, self.cbf], [tb])
            self.E("dve", "tensor_copy", [tb], [scT], out=scT[:], in_=tb[:, 0:DC * 32].rearrange("p (a b) -> p a b", a=DC)[:, :, 0:17])
            wv = i["w_ada"].rearrange("(c p) n -> p c n", p=128)
            for ct in range(12):
                wb = wab[ct % 2]
                self.stream_cast(stg, wv[:, :, ct * 512:(ct + 1) * 512], wb, wb[:], eng="act" if ct % 2 else "dve")
                self.dma(bada[:], i["b_ada"][ct * 512:(ct + 1) * 512].partition_broadcast(17), writes=[bada])
                g = self.G()
                for dc in range(DC):
                    self.mm(g[0:17, :], scT[:, dc, :], wb[:, dc, :], dc == 0, dc == DC - 1, [scT, wb], [g])
                m = mt[ct % 2]
                self.E("dve", "tensor_tensor", [g, bada], [m], out=m[:], in0=g[0:17, :], in1=bada[:], op=ALU.add)
                if ct in (2, 3, 8, 9):
                    go = (ct - 2) * 512 if ct < 4 else D + (ct - 8) * 512
                    self.E("dve", "scalar_tensor_tensor", [m, gv], [m], out=m[:], in0=m[:], scalar=1.0, in1=gv[:, go:go + 512],
                           op0=ALU.add, op1=ALU.mult)
                self.dma(self.modd[:, ct * 512:(ct + 1) * 512], m[:], reads=[m], writes=[self.t_modd])

    def load_mod(self, tiles, idxs, sample):
        for t, ix in zip(tiles, idxs):
            if not sample:
                self.dma(t[:], self.modd[0, ix * D:(ix + 1) * D].partition_broadcast(128), reads=[self.t_modd], writes=[t])
            else:
                for s_ in range(NSEQ):
                    self.dma(t[s_ * TS:(s_ + 1) * TS, :], self.modd[1 + s_, ix * D:(ix + 1) * D].partition_broadcast(TS),
                             reads=[self.t_modd], writes=[t])

    def norm_mod(self, w, xt, scale, shift, hT, tmp=None):
        self.E("pool", "memset", [], [w.ssq], w.ssq[:], 0.0)
        self.act(w.h[:], xt[:], AF.Square, [xt, w.ssq], [w.h, w.ssq], accum_out=w.ssq[:, 0:1])
        self.rsqrt_ops(w.ssq, w.rstd, 1, 1.0 / D)
        if tmp is None:
            tmp = xt
        self.E("dve", "scalar_tensor_tensor", [xt, w.rstd, scale], [tmp], out=tmp[:], in0=xt[:], scalar=w.rstd[:, 0:1],
               in1=scale[:], op0=ALU.mult, op1=ALU.mult)
        self.E("pool", "tensor_tensor", [tmp, shift], [w.h], out=w.h[:], in0=tmp[:], in1=shift[:], op=ALU.add)
        tb = self.TB()
        for dc in range(DC):
            self.tr(tb[:, dc * 128:(dc + 1) * 128], w.h[:, dc * 128:(dc + 1) * 128], self.identb, [w.h, self.cbf], [tb])
        self.act(hT[:], tb[:, :].rearrange("p (a b) -> p a b", a=DC), AF.Copy, [tb], [hT])

    def head_norm(self, w, ps, nh, hd, out_f32, reads_ps, gvec, out_tile, out_ap, scale):
        v3 = lambda ap: ap.rearrange("p (a b) -> p a b", a=nh)
        self.act(out_f32[:, 0:nh * hd], ps, AF.Square, reads_ps, [out_f32])
        self.E("dve", "tensor_reduce", [out_f32], [w.ss8], out=w.ss8[:, 0:nh], in_=v3(out_f32[:, 0:nh * hd]), axis=AX.X, op=ALU.add)
        self.rsqrt_ops(w.ss8, w.rs8, nh, scale)
        self.E("dve", "tensor_tensor", reads_ps + [w.rs8], [out_f32], out=v3(out_f32[:, 0:nh * hd]), in0=v3(ps),
               in1=w.rs8[:, 0:nh].unsqueeze(2).to_broadcast([128, nh, hd]), op=ALU.mult)
        self.E("pool", "tensor_tensor", [out_f32, self.sv], [out_tile], out=v3(out_ap), in0=v3(out_f32[:, 0:nh * hd]),
               in1=gvec.unsqueeze(1).to_broadcast([128, nh, hd]), op=ALU.mult)

    def front_end(self, w, b, sample, own, outrow):
        c = self.cfg
        i, o = self.i, self.o
        xt = w.xt[self.xi % len(w.xt)]
        self.xi += 1
        src = i["xs"][:, :] if sample else i["xp"][b * 128:(b + 1) * 128, :]
        self.dma(xt[:], src, writes=[xt])
        hT = w.hT
        self.norm_mod(w, xt, w.scale1, w.shift1, hT)
        winb = self.winb
        yield
        gk_ = self.zb[0]
        for dc in range(DC):
            self.mm(gk_[:, :], hT[:, dc, :], winb[:, dc, 512:1024], dc == 0, dc == DC - 1, [hT, winb], [gk_])
        self.head_norm(w, gk_[:, :], HS, HD, w.f512, [gk_], self.gk, w.kn, w.kn[:], 1.0 / HD)
        if outrow is not None:
            self.dma(outrow[0], w.kn[:], reads=[w.kn])
        self.E("dve", "tensor_copy", [w.kn], [w.knb], out=w.knb[:], in_=w.kn[:])
        yield
        gv_ = self.zb[1]
        for dc in range(DC):
            self.mm(gv_[:, :], hT[:, dc, :], winb[:, dc, 1024:1536], dc == 0, dc == DC - 1, [hT, winb], [gv_])
        Vt = w.Vs if sample else self.Vt[b]
        self.act(Vt[:], gv_[:, :], AF.Copy, [gv_], [Vt])
        if outrow is not None:
            self.E("dve", "tensor_copy", [gv_], [w.vf], out=w.vf[:], in_=gv_[:, :])
            self.dma(outrow[1], w.vf[:], reads=[w.vf])
        yield
        if own:
            gq_ = self.po[0]
            for dc in range(DC):
                self.mm(gq_[:, :], hT[:, dc, :], winb[:, dc, 0:512], dc == 0, dc == DC - 1, [hT, winb], [gq_])
            self.head_norm(w, gq_[:, :], HS, HD, w.f512, [gq_], self.gq8, w.qnb, w.qnb[:], 1.0 / HD)
        if self.on("dn"):
            g = self.G()
            for dc in range(DC):
                self.mm(g[:, 0:8], hT[:, dc, :], winb[:, dc, 3072:3080], dc == 0, dc == DC - 1, [hT, winb], [g])
            self.E("dve", "tensor_copy", [g], [w.ba], out=w.ba[:], in_=g[:, 0:8])
            if own:
                g = self.po[1]
                for dc in range(DC):
                    self.mm(g[:, :], hT[:, dc, :], winb[:, dc, 3080:3592], dc == 0, dc == DC - 1, [hT, winb], [g])
                self.act(w.zs[:], g[:, :], AF.Copy, [g], [w.zs])
        yield
        KTt = w.KTs if sample else self.KTt[b]
        tb = self.TB()
        for pr in range(4):
            self.tr(tb[:, pr * 128:(pr + 1) * 128], w.knb[:, pr * 128:(pr + 1) * 128], self.identb, [w.knb, self.cbf], [tb])
        self.act(KTt[:], tb[:, 0:512].rearrange("p (a b) -> p a b", a=4), AF.Copy, [tb], [KTt])
        if own:
            tb = self.TB()
            for pr in range(4):
                self.tr(tb[:, pr * 128:(pr + 1) * 128], w.qnb[:, pr * 128:(pr + 1) * 128], self.identb, [w.qnb, self.cbf], [tb])
            tq = tb[:, 0:512].rearrange("p (a b) -> p a b", a=4)
            self.act(w.QT[0:64, :, 0, :], tq[0:64, :, :], AF.Copy, [tb], [w.QT])
            self.E("dve", "tensor_copy", [tb], [w.QT], out=w.QT[64:128, :, 1, :], in_=tq[64:128, :, :])

    def front_dn(self, w, b, sample, fe):
        next(fe)
        if not self.on("dn"):
            for _ in fe:
                pass
            return
        if (not sample) and b == 0:
            self.E("pool", "memset", [], [w.Hst], w.Hst[:], 0.0)
        g0, g1, g2 = [self.dn_group(w, b, sample, j) for j in range(3)]
        next(g0); next(g0)
        next(fe)
        next(g1)
        next(g0)
        next(g1)
        next(fe)
        next(g2)
        next(g1)
        next(g2)
        next(fe)
        next(g2)
        for _ in fe:
            pass

    def dn_group(self, w, b, sample, j):
        T = TS if sample else 128
        ns = NSEQ if sample else 1
        hT, winb = w.hT, self.winb
        XE4 = w.XE4
        xe4 = XE4[:, :, :].rearrange("p c (s t) -> p c s t", s=ns)
        Hst = w.Hst
        hs4 = Hst[:, :, 0:ns * 3].rearrange("p c (s t) -> p c s t", s=ns)
        g = self.G()
        for ch in range(4):
            col = 1536 + (j * 4 + ch) * 128
            for dc in range(DC):
                self.mm(g[:, ch * 128:(ch + 1) * 128], winb[:, dc, col:col + 128], hT[:, dc, :], dc == 0, dc == DC - 1, [hT, winb], [g])
        yield
        src4 = g[:, :].rearrange("p (c s t) -> p c s t", c=4, s=ns)
        self.E("pool", "tensor_copy", [Hst], [XE4], out=xe4[:, :, :, 0:3], in_=hs4[:, j * 4:(j + 1) * 4, :, :])
        if sample:
            self.act(xe4[:, :, :, 3:3 + T], src4, AF.Copy, [g], [XE4])
        else:
            self.act(xe4[:, :, :, 3:3 + T], src4, AF.Copy, [g, self.bvd], [XE4], scale=self.bvd[:, b:b + 1])
        self.E("pool", "tensor_copy", [XE4], [Hst], out=hs4[:, j * 4:(j + 1) * 4, :, :], in_=xe4[:, :, :, T:T + 3])
        Y4 = w.Y4
        for ch in range(4):
            cc = j * 4 + ch
            yv = Y4[:, ch, :].rearrange("p (s t) -> p s t", s=ns)
            self.E("dve", "tensor_scalar", [XE4, self.wdc], [Y4], out=yv, in0=xe4[:, ch, :, 0:T], scalar1=self.wdc[:, 0, cc:cc + 1],
                   scalar2=None, op0=ALU.mult)
            for k in range(1, 4):
                self.E("dve", "scalar_tensor_tensor", [XE4, self.wdc, Y4], [Y4], out=yv, in0=xe4[:, ch, :, k:k + T],
                       scalar=self.wdc[:, k, cc:cc + 1], in1=yv, op0=ALU.mult, op1=ALU.add)
        E4 = w.E4
        self.act(E4[:], Y4[:], AF.Exp, [Y4], [E4], scale=-1.0)
        self.act(E4[:], E4[:], AF.Ln, [E4], [E4], bias=1.0)
        self.act(E4[:], E4[:], AF.Exp, [E4], [E4], scale=-1.0)
        self.E("dve", "tensor_tensor", [E4, Y4], [Y4], out=Y4[:], in0=Y4[:], in1=E4[:], op=ALU.mult)
        SQ = E4
        if j < 2:
            self.act(SQ[:], Y4[:], AF.Square, [Y4], [SQ])
        yield
        if j < 2:
            g = self.G()
            self.mm(g[:, :], self.cm("ones"), SQ[:].rearrange("p a b -> p (a b)"), True, True, [SQ, self.ctm], [g])
            self.act(SQ[:].rearrange("p a b -> p (a b)"), g[:, :], AF.Ln, [g], [SQ], bias=self.epsb[:, 0:1])
            self.act(SQ[:], SQ[:], AF.Exp, [SQ], [SQ], scale=-0.5)
            if j == 0:
                self.E("dve", "scalar_tensor_tensor", [Y4, SQ], [w.QnT], out=w.QnT[:], in0=Y4[:], scalar=DND ** -0.5, in1=SQ[:],
                       op0=ALU.mult, op1=ALU.mult)
            else:
                self.E("pool", "tensor_tensor", [Y4, SQ], [w.KnT], out=w.KnT[:], in0=Y4[:], in1=SQ[:], op=ALU.mult)
        else:
            self.act(w.Vcb[:], Y4[:], AF.Copy, [Y4], [w.Vcb])
        yield

    def dn_chunk(self, w, b, sample, own, tabs, last_prompt):
        c = self.cfg
        i, o = self.i, self.o
        T = TS if sample else 128
        ns = NSEQ if sample else 1
        sc = w.sc
        hT, winb = w.hT, self.winb
        ltincl, seqm, ms01 = tabs
        bc4 = lambda ap: ap.unsqueeze(2).to_broadcast([128, 4, 128])
        hb4 = lambda ap: ap.unsqueeze(1).to_broadcast([128, 4, 128])
        v4 = lambda ap: ap.rearrange("p (a b) -> p a b", a=4)
        Hst = w.Hst
        if sample or last_prompt:
            ncol = ns * 3
            dst = o["dcs"] if sample else o["dcp"]
            for j in range(3):
                g = self.G()
                for ch in range(4):
                    self.tr(g[0:ncol, ch * 128:(ch + 1) * 128], Hst[:, j * 4 + ch, 0:ncol], self.cm("ident"), [Hst, self.ctm], [g])
                self.E("dve", "tensor_copy", [g], [w.f512], out=w.f512[0:ncol, :], in_=g[0:ncol, :])
                self.dma(dst[:, j * 512:(j + 1) * 512], w.f512[0:ncol, :], reads=[w.f512])
        tb = self.TB()
        for h in range(4):
            self.tr(tb[:, h * 128:(h + 1) * 128], w.KnT[:, h, :], self.identb, [w.KnT, self.cbf], [tb])
        self.act(w.Ktok[:], v4(tb[:, 0:512]), AF.Copy, [tb], [w.Ktok])
        tb = self.TB()
        for h in range(4):
            self.tr(tb[:, h * 128:(h + 1) * 128], w.Vcb[:, h, :], self.identb, [w.Vcb, self.cbf], [tb])
        self.E("dve", "tensor_copy", [tb], [w.Vtok], out=w.Vtok[:], in_=v4(tb[:, 0:512]))
        yield
        ba = w.ba
        self.act(sc[:, 0:4], ba[:, 0:4], AF.Exp, [ba], [sc], scale=-1.0)
        self.act(sc[:, 0:4], sc[:, 0:4], AF.Ln, [sc], [sc], bias=1.0)
        self.act(sc[:, 0:4], sc[:, 0:4], AF.Exp, [sc], [sc], scale=-1.0)
        if not sample:
            self.E("dve", "tensor_scalar", [sc, self.bvd], [sc], out=sc[:, 0:4], in0=sc[:, 0:4], scalar1=self.bvd[:, b:b + 1], scalar2=None, op0=ALU.mult)
        self.E("dve", "tensor_tensor", [ba, self.sv], [sc], out=sc[:, 32:36], in0=ba[:, 4:8], in1=self.dtb, op=ALU.add)
        self.act(sc[:, 32:36], sc[:, 32:36], AF.Exp, [sc], [sc])
        self.act(sc[:, 32:36], sc[:, 32:36], AF.Ln, [sc], [sc], bias=1.0)
        self.E("dve", "tensor_tensor", [sc, self.sv], [sc], out=sc[:, 4:8], in0=sc[:, 32:36], in1=self.negA, op=ALU.mult)
        g = self.G()
        self.mm(g[:, 0:4], ltincl, sc[:, 4:8], True, True, [sc, w.tabt], [g])
        self.mm(g[:, 4:8], seqm, sc[:, 4:8], True, True, [sc, w.tabt], [g])
        self.E("dve", "tensor_copy", [g], [sc], out=sc[:, 8:16], in_=g[:, 0:8])
        self.act(sc[:, 16:20], sc[:, 8:12], AF.Exp, [sc], [sc])
        self.E("dve", "scalar_tensor_tensor", [sc], [sc], out=sc[:, 20:24], in0=sc[:, 0:4], scalar=-1.0, in1=sc[:, 16:20], op0=ALU.mult, op1=ALU.mult)
        self.E("dve", "tensor_tensor", [sc], [sc], out=sc[:, 24:28], in0=sc[:, 12:16], in1=sc[:, 8:12], op=ALU.subtract)
        self.act(sc[:, 24:28], sc[:, 24:28], AF.Exp, [sc], [sc])
        self.E("dve", "tensor_scalar", [sc], [sc], out=sc[:, 28:32], in0=sc[:, 0:4], scalar1=-1.0, scalar2=None, op0=ALU.mult)
        beta, gg, negbg, kds, negbeta = sc[:, 0:4], sc[:, 8:12], sc[:, 20:24], sc[:, 24:28], sc[:, 28:32]
        yield
        GR = self.G()
        for h in range(4):
            dg = w.dg[h % 2]
            self.E("dve", "tensor_scalar", [sc, self.ctm], [dg], out=dg[:], in0=self.cm("ident"), scalar1=sc[:, 8 + h:9 + h], scalar2=None, op0=ALU.mult)
            self.mm(GR[:, h * 128:(h + 1) * 128], self.cm("ones"), dg[:], True, True, [dg, self.ctm], [GR])
        fa, fb, fc, fd = w.fa, w.fb, w.fc, w.fd
        self.E("dve", "tensor_tensor", [GR, sc], [fa], out=v4(fa[:]), in0=v4(GR[:, :]), in1=bc4(gg), op=ALU.subtract)
        self.act(fd[:], GR[:, :], AF.Exp, [GR], [fd])
        self.E("dve", "tensor_scalar", [fa], [fb], out=fb[:], in0=fa[:], scalar1=0.0, scalar2=None, op0=ALU.max)
        self.E("dve", "tensor_scalar", [fa], [fc], out=fc[:], in0=fa[:], scalar1=0.0, scalar2=None, op0=ALU.min)
        self.act(fb[:], fb[:], AF.Exp, [fb], [fb], scale=-1.0)
        self.act(fc[:], fc[:], AF.Exp, [fc], [fc])
        yield
        g = self.G()
        for h in range(4):
            self.mm(g[:, h * 128:(h + 1) * 128], w.KnT[:, h, :], w.KnT[:, h, :], True, True, [w.KnT], [g])
        self.E("dve", "tensor_tensor", [g, fb], [fa], out=fa[:], in0=g[:, :], in1=fb[:], op=ALU.mult)
        self.E("pool", "tensor_tensor", [fa, sc], [fa], out=v4(fa[:]), in0=v4(fa[:]), in1=bc4(negbeta), op=ALU.mult)
        self.E("dve", "tensor_tensor", [fa, w.tabt], [fa], out=v4(fa[:]), in0=v4(fa[:]), in1=hb4(ms01), op=ALU.mult)
        g = self.G()
        for h in range(4):
            self.mm(g[:, h * 128:(h + 1) * 128], w.KnT[:, h, :], w.QnT[:, h, :], True, True, [w.KnT, w.QnT], [g])
        self.E("dve", "tensor_tensor", [g, fc], [fc], out=fc[:], in0=g[:, :], in1=fc[:], op=ALU.mult)
        self.E("pool", "tensor_tensor", [fc, w.tabt], [w.intraT], out=w.intraT[:], in0=v4(fc[:]), in1=hb4(ltincl), op=ALU.mult)
        yield
        MTb = w.MTb
        nlev = 2 if sample else 6
        h2 = lambda ap: ap.rearrange("p (a b) -> p a b", a=2)
        hb2 = lambda ap: ap.unsqueeze(1).to_broadcast([128, 2, 128])
        identf = self.cm("ident")
        P, PT, MT = w.Pf[0], w.PTf[0], w.MT
        P = fa_t = None
        P = w.Pf[0]
        self.E("pool", "tensor_copy", [fa], [P], out=P[:], in_=v4(fa[:]))
        g = self.G()
        for h in range(4):
            self.tr(g[:, h * 128:(h + 1) * 128], P[:, h, :], identf, [P, self.ctm], [g])
        self.act(PT[:], v4(g[:, :]), AF.Copy, [g], [PT])
        self.E("dve", "tensor_tensor", [PT, self.ctm], [MT], out=MT[:], in0=PT[:], in1=hb4(identf), op=ALU.add)
        for lev in range(1, nlev + 1):
            Pn, PTn = w.Pf[lev % 2], w.PTf[lev % 2]
            g1 = self.G()
            for h in range(4):
                self.mm(g1[:, h * 128:(h + 1) * 128], PT[:, h, :], P[:, h, :], True, True, [P, PT], [g1], r32=True)
            self.act(Pn[:], v4(g1[:, :]), AF.Copy, [g1], [Pn])
            if lev < nlev:
                g2 = self.G()
                for h in range(4):
                    self.mm(g2[:, h * 128:(h + 1) * 128], P[:, h, :], PT[:, h, :], True, True, [P, PT], [g2], r32=True)
                self.E("dve", "tensor_copy", [g2], [PTn], out=PTn[:], in_=v4(g2[:, :]))
            g3 = self.G()
            for h in range(4):
                self.mm(g3[:, h * 128:(h + 1) * 128], Pn[:, h, :], MT[:, h, :], True, True, [Pn, MT], [g3], r32=True)
            self.E("dve", "tensor_tensor", [g3, MT], [MT], out=MT[:], in0=v4(g3[:, :]), in1=MT[:], op=ALU.add)
            P, PT = Pn, PTn
            yield
        self.act(MTb[:], MT[:], AF.Copy, [MT], [MTb])
        yield
        self.E("pool", "tensor_tensor", [w.QnT, fd], [w.QdT], out=w.QdT[:], in0=w.QnT[:], in1=v4(fd[:]), op=ALU.mult)
        self.E("dve", "tensor_tensor", [w.Vtok, sc], [w.Vtok], out=w.Vtok[:], in0=w.Vtok[:], in1=bc4(beta), op=ALU.mult)
        self.E("pool", "tensor_tensor", [w.Ktok, sc], [w.kdec], out=w.kdec[:], in0=w.Ktok[:], in1=bc4(kds), op=ALU.mult)
        if not sample:
            Sf, Sb = self.Sf, self.Sb
            g = self.G()
            for h in range(4):
                self.mm(g[:, h * 128:(h + 1) * 128], w.KnT[:, h, :], Sb[:, h, :], True, True, [w.KnT, Sb], [g])
            self.E("dve", "tensor_tensor", [g, sc], [fa], out=v4(fa[:]), in0=v4(g[:, :]), in1=bc4(negbg), op=ALU.mult)
            self.E("pool", "tensor_tensor", [fa, w.Vtok], [w.W], out=w.W[:], in0=v4(fa[:]), in1=w.Vtok[:], op=ALU.add)
            g = self.G()
            for h in range(4):
                self.mm(g[:, h * 128:(h + 1) * 128], MTb[:, h, :], w.W[:, h, :], True, True, [MTb, w.W], [g])
            self.act(w.vnew[:], v4(g[:, :]), AF.Copy, [g], [w.vnew])
            yield
            if own:
                po = self.G()
                po_dn = po
                for h in range(4):
                    self.mm(po[:, h * 128:(h + 1) * 128], w.QdT[:, h, :], Sb[:, h, :], True, False, [w.QdT, Sb], [po])
                    self.mm(po[:, h * 128:(h + 1) * 128], w.intraT[:, h, :], w.vnew[:, h, :], False, True, [w.intraT, w.vnew], [po])
            g = self.G()
            for h in range(4):
                self.mm(g[:, h * 128:(h + 1) * 128], w.kdec[:, h, :], w.vnew[:, h, :], True, True, [w.kdec, w.vnew], [g])
            self.E("dve", "tensor_tensor", [Sf, fd], [Sf], out=Sf[:], in0=Sf[:], in1=v4(fd[:])[:, :, 127:128].to_broadcast([128, 4, 128]), op=ALU.mult)
            self.E("dve", "tensor_tensor", [g, Sf], [Sf], out=Sf[:], in0=v4(g[:, :]), in1=Sf[:], op=ALU.add)
            self.act(Sb[:], Sf[:], AF.Copy, [Sf], [Sb])
            if last_prompt:
                self.dma(o["sp_state"].rearrange("h k v -> k h v"), Sf[:], reads=[Sf])
        else:
            colmask, rowmask = w.colmask, w.rowmask
            cm3 = colmask.rearrange("p (s t) -> p s t", s=NSEQ)
            s0v = i["s0"]
            po = self.po[1]
            for h in range(4):
                Sfh, Sbh = w.Sfh[0], w.Sbh[0]
                self.dma(Sfh[:], s0v[:, h, :, :].rearrange("s k v -> k s v"), writes=[Sfh])
                self.E("pool", "tensor_copy", [Sfh], [Sbh], out=Sbh[:], in_=Sfh[:])
                Km, Qm, kdm = w.Km, w.Qm, w.kdm
                self.E("pool", "tensor_tensor", [w.KnT, w.tabt], [Km], out=Km[:], in0=w.KnT[:, h, :].unsqueeze(1).to_broadcast([128, NSEQ, 128]), in1=cm3, op=ALU.mult)
                g = self.G()
                for s_ in range(NSEQ):
                    self.mm(g[:, 0:128], Km[:, s_, :], Sbh[:, s_, :], s_ == 0, s_ == NSEQ - 1, [Km, Sbh], [g])
                self.E("dve", "tensor_scalar", [g, sc], [fa], out=fa[:, 0:128], in0=g[:, 0:128], scalar1=sc[:, 20 + h:21 + h], scalar2=None, op0=ALU.mult)
                self.E("pool", "tensor_tensor", [fa, w.Vtok], [w.W], out=w.W[:, h, :], in0=fa[:, 0:128], in1=w.Vtok[:, h, :], op=ALU.add)
                g = self.G()
                self.mm(g[:, 0:128], MTb[:, h, :], w.W[:, h, :], True, True, [MTb, w.W], [g])
                self.act(w.vnew[:, h, :], g[:, 0:128], AF.Copy, [g], [w.vnew])
                self.E("dve", "tensor_tensor", [w.QdT, w.tabt], [Qm], out=Qm[:], in0=w.QdT[:, h, :].unsqueeze(1).to_broadcast([128, NSEQ, 128]), in1=cm3, op=ALU.mult)
                for s_ in range(NSEQ):
                    self.mm(po[:, h * 128:(h + 1) * 128], Qm[:, s_, :], Sbh[:, s_, :], s_ == 0, False, [Qm, Sbh], [po])
                self.mm(po[:, h * 128:(h + 1) * 128], w.intraT[:, h, :], w.vnew[:, h, :], False, True, [w.intraT, w.vnew], [po])
                Sn = w.Sn[0]
                self.E("pool", "tensor_tensor", [w.kdec, w.tabt], [kdm], out=kdm[:], in0=w.kdec[:, h, :].unsqueeze(1).to_broadcast([128, NSEQ, 128]),
                       in1=rowmask[:, 0:NSEQ].unsqueeze(2).to_broadcast([128, NSEQ, 128]), op=ALU.mult)
                for q4 in range(4):
                    g = self.G()
                    for k in range(4):
                        s_ = q4 * 4 + k
                        self.mm(g[:, k * 128:(k + 1) * 128], kdm[:, s_, :], w.vnew[:, h, :], True, True, [kdm, w.vnew], [g])
                    for k in range(4):
                        s_ = q4 * 4 + k
                        self.E("dve" if k % 2 == 0 else "pool" if False else "dve", "scalar_tensor_tensor", [g, Sfh, fd], [Sn], out=Sn[:, s_, :], in0=Sfh[:, s_, :],
                               scalar=fd[:, h * 128 + s_ * TS + TS - 1: h * 128 + s_ * TS + TS], in1=g[:, k * 128:(k + 1) * 128], op0=ALU.mult, op1=ALU.add)
                self.dma(o["ss_state"][:, h, :, :].rearrange("s k v -> k s v"), Sn[:], reads=[Sn])
        yield
        if own:
            po = self.po[1] if sample else po_dn
            self.act(fa[:], po[:, :], AF.Square, [po], [fa])
            self.E("dve", "tensor_reduce", [fa], [w.ss8], out=w.ss8[:, 0:4], in_=v4(fa[:]), axis=AX.X, op=ALU.add)
            self.rsqrt_ops(w.ss8, w.rs8, 4, 1.0 / DND)
            self.E("dve", "tensor_tensor", [po, w.rs8], [fa], out=v4(fa[:]), in0=v4(po[:, :]), in1=bc4(w.rs8[:, 0:4]), op=ALU.mult)
            self.E("pool", "tensor_tensor", [fa, self.sv], [fa], out=v4(fa[:]), in0=v4(fa[:]), in1=hb4(self.gdn), op=ALU.mult)
            zs = w.zs
            self.act(fb[:], zs[:], AF.Exp, [zs], [fb], scale=-1.0)
            self.act(fb[:], fb[:], AF.Ln, [fb], [fb], bias=1.0)
            self.act(fb[:], fb[:], AF.Exp, [fb], [fb], scale=-1.0)
            self.E("dve", "tensor_tensor", [fb, zs], [fb], out=fb[:], in0=fb[:], in1=zs[:], op=ALU.mult)
            self.E("dve", "tensor_tensor", [fa, fb], [w.mixed], out=w.mixed[:, 512:1024], in0=fa[:], in1=fb[:], op=ALU.mult)

    def attn_step(self, w, S, nh, nq, qT, kT, vv, nk, kvl, kvl_t, brow_ap, maskneg, mask01, mask_t, O, o_cols, first, last,
                  qreads, kreads, vreads, att_out=None, att_lhs=None, first_o=None, last_o=None):
        W_ = nh * nq
        et, spt, att, Rb = S.et, S.spt, S.att, S.Rb
        Z = S.zbank
        for p_ in range(nh // 2):
            self.mm(Z[0:nk, p_ * 2 * nq:(p_ + 1) * 2 * nq], kT[p_], qT[p_], p_ == 0, False, qreads + kreads, [Z], skip=True)
        self.mm(Z[0:nk, 0:W_], kvl, brow_ap, False, True, [kvl_t, self.brow], [Z], skip=True)
        if getattr(S, "pending", None) is not None:
            S.pending()
            S.pending = None
        yield
        self.act(et[0:nk, 0:W_], Z[0:nk, 0:W_], AF.Exp, [Z], [et])
        self.act(spt[0:nk, 0:W_], et[0:nk, 0:W_], AF.Ln, [et], [spt], bias=1.0)
        if mask01 is not None:
            self.E("dve", "tensor_tensor", [spt, mask_t], [spt], out=spt[0:nk, 0:W_], in0=spt[0:nk, 0:W_], in1=mask01, op=ALU.mult)
        yield
        U = Z
        fin = first and maskneg is None
        self.mm(U[0:nk, 0:W_], self.ntrib[0:nk, 0:nk], spt[0:nk, 0:W_], False, fin, [spt, self.cbf], [U], skip=True)
        if not first:
            self.mm(U[0:nk, 0:W_], self.negonesb[:, 0:nk], Rb[:, 0:W_], False, maskneg is None, [Rb, self.cbf], [U], skip=True)
        if maskneg is not None:
            self.mm(U[0:nk, 0:W_], self.identb[0:nk, 0:nk], maskneg, False, True, [mask_t, self.cbf], [U], skip=True)
        yield
        if att_out is None:
            self.act(att[0:nk, 0:W_], U[0:nk, 0:W_], AF.Exp, [U], [att])
        else:
            self.act(att_out[0], U[0:nk, 0:W_].rearrange("p (h q) -> p h q", h=nh), AF.Exp, [U], [att_out[1]])
        if not last:
            if first:
                self.E("dve", "tensor_copy", [spt], [Rb], out=Rb[0:nk, 0:W_], in_=spt[0:nk, 0:W_])
            else:
                self.E("dve", "tensor_tensor", [spt, Rb], [Rb], out=Rb[0:nk, 0:W_], in0=Rb[0:nk, 0:W_], in1=spt[0:nk, 0:W_], op=ALU.add)
        fo = first if first_o is None else first_o
        lo = last if last_o is None else last_o

        def av():
            for h in range(nh):
                if att_lhs is None:
                    lhs = att[0:nk, h * nq:(h + 1) * nq]
                    rd = [att]
                else:
                    lhs = att_lhs[0][h]
                    rd = [att_lhs[1]]
                self.mm(O[o_cols[h]], lhs, vv[h], fo and h == 0, lo, rd + vreads, [O], skip=True)
        S.pending = av
        yield

    @staticmethod
    def interleave(gens):
        gens = list(gens)
        while gens:
            for g in list(gens):
                try:
                    next(g)
                except StopIteration:
                    gens.remove(g)

    def attn_prompt(self, w, b, dn_gen=None):
        c = self.cfg

        def stream(hg):
            O = self.po[hg]
            S = w.streams[hg]
            for kb in range(b, -1, -1):
                qT = [w.QT[:, hg * 2 + p_, :, :] for p_ in range(2)]
                kT = [self.KTt[kb][:, hg * 2 + p_, :] for p_ in range(2)]
                vv = [self.Vt[kb][:, (hg * 4 + h) * 64:(hg * 4 + h + 1) * 64] for h in range(4)]
                diag = kb == b
                kvl = self.kvd if kb < c.OUT0 else self.kvone
                yield from self.attn_step(w, S, 4, 128, qT, kT, vv, 128, kvl[:, :], kvl,
                                          self.brow[:, hg * 512:(hg + 1) * 512],
                                          w.causrep[:, 0:512] if diag else None, w.caus01rep[:, 0:512] if diag else None, w.causrep_t,
                                          O, [(slice(None), slice(h * 64, (h + 1) * 64)) for h in range(4)],
                                          kb == b, kb == 0, [w.QT], [self.KTt[kb]], [self.Vt[kb]])
            S.pending()
            S.pending = None
            self.head_norm(w, O[:, 0:256], 4, HD, w.f512, [O], self.gso, w.mixed, w.mixed[:, hg * 256:(hg + 1) * 256], 1.0 / HD)
        gens = [stream(0), stream(1)]
        n_rounds = 5 * (b + 1) + 1
        stride = max(1, n_rounds // 24)
        rnd = 0
        dn_live = dn_gen is not None
        while gens or dn_live:
            if dn_live and (rnd % stride == 0 or not gens):
                try:
                    next(dn_gen)
                except StopIteration:
                    dn_live = False
            for g_ in list(gens):
                try:
                    next(g_)
                except StopIteration:
                    gens.remove(g_)
            rnd += 1

    def attn_sample(self, w):
        c = self.cfg
        i = self.i
        npg = c.NPG
        O = self.po[0]
        ck = i["cache_k"]
        cv = i["cache_v"]

        NSTR = len(w.streams)

        def stream(si):
            S = w.streams[si]
            for s_ in range(si, NSEQ, NSTR):
                attpad = w.attpad[si]
                if getattr(S, "pending", None) is not None:
                    S.pending()
                    S.pending = None
                self.E("pool", "memset", [], [attpad], attpad[:], 0.0)
                qT = [w.QT[:, p_, :, s_ * TS:(s_ + 1) * TS] for p_ in range(4)]
                att_out = (attpad[:, :, s_ * TS:(s_ + 1) * TS], attpad)
                att_lhs = ([attpad[:, h, :] for h in range(8)], attpad)
                o_cols = [(slice(None), slice(h * 64, (h + 1) * 64)) for h in range(8)]
                for blk in range(npg, -1, -1):
                    if blk == npg:
                        kT = [w.KTs[:, p_, :] for p_ in range(4)]
                        vv = [w.Vs[:, h * 64:(h + 1) * 64] for h in range(8)]
                        kreads, vreads = [w.KTs], [w.Vs]
                        mneg, m01 = w.smneg[:, s_, :], w.sm01[:, s_, :]
                    else:
                        j = s_ * npg + blk
                        kk = si * 2 + (self.pgi[si] % 2)
                        self.pgi[si] += 1
                        kpf, vpf, kpb, vpb, ktp = w.kpf[kk], w.vpf[kk], w.kpb[kk], w.vpb[kk], w.ktp[kk]
                        self.s.add("pool", lambda e, kpf=kpf, j=j: e.indirect_dma_start(
                            out=kpf[:], out_offset=None, in_=ck, in_offset=bass.IndirectOffsetOnAxis(ap=w.idx[:, j:j + 1], axis=0)),
                            [w.idx], [kpf], is_dma=True)
                        self.s.add("pool", lambda e, vpf=vpf, j=j: e.indirect_dma_start(
                            out=vpf[:], out_offset=None, in_=cv, in_offset=bass.IndirectOffsetOnAxis(ap=w.idx[:, j:j + 1], axis=0)),
                            [w.idx], [vpf], is_dma=True)
                        self.E("dve", "tensor_copy", [kpf], [kpb], out=kpb[:], in_=kpf[:])
                        self.act(vpb[:], vpf[:], AF.Copy, [vpf], [vpb])
                        tb = self.TB()
                        for pr in range(4):
                            self.tr(tb[:, pr * 128:(pr + 1) * 128], kpb[:, pr * 128:(pr + 1) * 128], self.identb, [kpb, self.cbf], [tb])
                        self.E("dve", "tensor_copy", [tb], [ktp], out=ktp[:], in_=tb[:, 0:512].rearrange("p (a b) -> p a b", a=4))
                        kT = [ktp[:, p_, :] for p_ in range(4)]
                        vv = [vpb[:, h * 64:(h + 1) * 64] for h in range(8)]
                        kreads, vreads = [ktp], [vpb]
                        mneg, m01 = None, None
                    yield from self.attn_step(w, S, 8, TS, qT, kT, vv, 128, self.kvone[:, :], self.kvone, self.brow[:, 1024:1088],
                                              mneg, m01, w.smt, O, o_cols, blk == npg, blk == 0, [w.QT], kreads, vreads,
                                              att_out=att_out, att_lhs=att_lhs,
                                              first_o=(s_ == 0 and blk == npg), last_o=(s_ == NSEQ - 1 and blk == 0))
            S.pending()
            S.pending = None
        self.pgi = [0] * NSTR
        self.interleave([stream(k) for k in range(NSTR)])
        self.head_norm(w, O[:, :], HS, HD, w.f512, [O], self.gso, w.mixed, w.mixed[:, 0:512], 1.0 / HD)

    def alloc_work(self, st, sample):
        class WS:
            pass
        w = WS()
        sb = lambda name, shape, dt: self.sb(st, name, shape, dt)
        w.xt = [sb("xt", [128, D], F32)]
        w.h = sb("h", [128, D], BF16)
        w.hT = sb("hT", [128, DC, 128], BF16)
        w.ssq = sb("ssq", [128, 1], F32)
        w.rstd = sb("rstd", [128, 1], F32)
        w.ss8 = sb("ss8", [128, 8], F32)
        w.rs8 = sb("rs8", [128, 8], F32)
        w.f512 = sb("f512", [128, 512], F32)
        w.kn = sb("kn", [128, 512], F32)
        w.knb = sb("knb", [128, 512], BF16)
        w.vf = w.f512
        w.qnb = sb("qnb", [128, 512], BF16)
        w.QT = sb("QT", [128, 4, 2, 128], BF16)
        self.E("pool", "memset", [], [w.QT], w.QT[:], 0.0)
        ns, T = (NSEQ, TS) if sample else (1, 128)
        w.XE4 = sb("XE4", [128, 4, ns * (3 + T)], F32)
        w.Hst = sb("Hst", [128, 12, ns * 3], F32)
        w.ba = sb("ba", [128, 8], F32)
        w.zs = sb("zs", [128, 512], BF16)
        w.Y4 = sb("Y4", [128, 4, 128], F32)
        w.E4 = sb("E4", [128, 4, 128], F32)
        w.QnT = sb("QnT", [128, 4, 128], BF16)
        w.KnT = sb("KnT", [128, 4, 128], BF16)
        w.Ktok = sb("Ktok", [128, 4, 128], BF16)
        w.Vtok = sb("Vtok", [128, 4, 128], F32)
        w.sc = sb("sc", [128, 40], F32)
        w.dg = [sb("dg", [128, 128], F32)] * 2
        w.fa = sb("fa", [128, 512], F32)
        w.fb = sb("fb", [128, 512], F32)
        w.fc = sb("fc", [128, 512], F32)
        w.fd = sb("fd", [128, 512], F32)
        w.Pf = [sb("Pf%d" % k, [128, 4, 128], F32) for k in range(2)]
        w.PTf = [sb("PTf%d" % k, [128, 4, 128], F32) for k in range(2)]
        w.MT = sb("MT", [128, 4, 128], F32)
        w.MTb = sb("MTb", [128, 4, 128], BF16)
        w.intraT = sb("intraT", [128, 4, 128], BF16)
        w.QdT = sb("QdT", [128, 4, 128], BF16)
        w.kdec = w.Ktok
        w.W = sb("W", [128, 4, 128], BF16)
        w.Vcb = w.W
        w.vnew = sb("vnew", [128, 4, 128], BF16)
        w.mixed = sb("mixed", [128, D], BF16)
        wd = 64 if sample else 512
        class ST:
            pass
        w.streams = []
        for k in range(2):
            S = ST()
            S.zbank = self.zb[k]
            S.et = sb("et%d" % k, [128, wd], BF16)
            S.spt = sb("spt%d" % k, [128, wd], BF16)
            S.att = sb("att%d" % k, [128, wd], BF16)
            S.Rb = sb("Rb%d" % k, [128, wd], BF16)
            w.streams.append(S)
        return w

    def phase1(self):
        c = self.cfg
        i, o = self.i, self.o
        self.xi = 0
        self.ai = 0
        self.pgi = 0
        with contextlib.ExitStack() as p1:
            winb = self.sb(p1, "winb", [128, DC, INC], BF16)
            self.winb = winb
            scale1 = self.sb(p1, "scale1", [128, D], F32)
            shift1 = self.sb(p1, "shift1", [128, D], F32)
            sv = self.sv
            WB = 512 * 2 + 64
            brow = self.sb(p1, "brow", [128, WB], BF16)
            nb = self.sb(p1, "nb", [128, 1], F32)
            with contextlib.ExitStack() as st:
                bexp = self.sb(st, "bexp", [128, WB], F32)
                for hg in range(2):
                    for h in range(4):
                        self.E("dve", "tensor_copy", [sv], [bexp], out=bexp[:, hg * 512 + h * 128: hg * 512 + (h + 1) * 128],
                               in_=sv[:, 328 + hg * 4 + h: 329 + hg * 4 + h].to_broadcast([128, 128]))
                for h in range(8):
                    self.E("dve", "tensor_copy", [sv], [bexp], out=bexp[:, 1024 + h * 8: 1024 + (h + 1) * 8],
                           in_=sv[:, 328 + h: 329 + h].to_broadcast([128, 8]))
                bhi = self.sb(st, "bhi", [128, WB], BF16)
                self.brow = brow
                idf = self.cm("ident")
                self.E("dve", "tensor_copy", [bexp], [bhi], out=bhi[:], in_=bexp[:])
                self.E("dve", "tensor_tensor", [bexp, bhi], [bexp], out=bexp[:], in0=bexp[:], in1=bhi[:], op=ALU.subtract)
                self.E("dve", "tensor_scalar", [bexp, self.ctm], [bexp], out=bexp[:], in0=bexp[:], scalar1=idf[:, 1:2], scalar2=None, op0=ALU.mult)
                self.E("dve", "scalar_tensor_tensor", [bhi, bexp, self.ctm], [bexp], out=bexp[:], in0=bhi[:], scalar=idf[:, 0:1], in1=bexp[:],
                       op0=ALU.mult, op1=ALU.add)
                self.E("dve", "tensor_scalar", [self.ctm], [nb], out=nb[:], in0=idf[:, 2:3], scalar1=-BIG, scalar2=None, op0=ALU.mult)
                self.E("dve", "tensor_scalar", [bexp, nb], [brow], out=brow[:], in0=bexp[:], scalar1=nb[:, 0:1], scalar2=None, op0=ALU.add)
                stg = [self.sb(st, "stg%d" % k, [128, DC * 512], F32) for k in range(2)]
                wv = i["w_in"].rearrange("(c p) n -> p c n", p=128)
                for ct in range(8):
                    n0, n1 = ct * 512, min(INC, (ct + 1) * 512)
                    self.stream_cast(stg, wv[:, :, n0:n1], winb, winb[:, :, n0:n1], eng="act" if ct % 2 else "dve")
            self.fence()
            with contextlib.ExitStack() as st:
                KT = self.sb(st, "KT", [128, 4, c.NBLK * 128], BF16)
                Vr = self.sb(st, "Vr", [128, c.NBLK, 512], BF16)
                self.KTt = [Tile(KT[:, :, b * 128:(b + 1) * 128], "KT%d" % b) for b in range(c.NBLK)]
                self.Vt = [Tile(Vr[:, b, :], "V%d" % b) for b in range(c.NBLK)]
                w = self.alloc_work(st, False)
                w.scale1, w.shift1 = scale1, shift1
                w.tabt = self.ctm
                w.causrep = self.sb(st, "causrep", [128, 512], BF16)
                w.caus01rep = self.sb(st, "caus01rep", [128, 512], BF16)
                w.causrep_t = self.sb(st, "causrep_t", [1, 1], F32)
                for h in range(4):
                    self.E("dve", "tensor_copy", [self.ctm], [w.causrep_t, w.causrep], out=w.causrep[:, h * 128:(h + 1) * 128], in_=self.cm("causneg"))
                    self.E("dve", "tensor_copy", [self.ctm], [w.causrep_t, w.caus01rep], out=w.caus01rep[:, h * 128:(h + 1) * 128], in_=self.cm("caus01"))
                self.Sf = self.sb(st, "Sf", [128, 4, 128], F32)
                self.Sb = self.sb(st, "Sb", [128, 4, 128], BF16)
                self.E("pool", "memset", [], [self.Sf], self.Sf[:], 0.0)
                self.E("pool", "memset", [], [self.Sb], self.Sb[:], 0.0)
                self.load_mod([shift1, scale1], [0, 1], False)
                tabs = (self.cm("ltincl_p"), self.cm("ones"), self.cm("ms01_p"))
                for b in range(c.NBLK):
                    own = b >= c.OWN0
                    outrow = None
                    if b >= c.OUT0:
                        r0 = (b - c.OUT0) * 128
                        outrow = (o["kp"][r0:r0 + 128, :], o["vp"][r0:r0 + 128, :])
                    fe = self.front_end(w, b, False, own, outrow)
                    self.front_dn(w, b, False, fe)
                    dn_gen = self.dn_chunk(w, b, False, own, tabs, b == c.NBLK - 1) if self.on("dn") else iter(())
                    if own and self.on("attn"):
                        self.attn_prompt(w, b, dn_gen)
                    else:
                        for _ in dn_gen:
                            pass
                    if own:
                        k = b - c.OWN0
                        if self.on("dn") and self.on("attn"):
                            self.dma(self.mixd[k * 128:(k + 1) * 128, :], w.mixed[:], reads=[w.mixed], writes=[self.t_mixd[k]])
                        if "mixed_p" in self.o and b >= c.OUT0:
                            r0 = (b - c.OUT0) * 128
                            self.E("dve", "tensor_copy", [w.mixed], [w.xt[0]], out=w.xt[0][:], in_=w.mixed[:])
                            self.dma(self.o["mixed_p"][r0:r0 + 128, :], w.xt[0][:], reads=[w.xt[0]])
            self.fence()
            if self.on("sample"):
                with contextlib.ExitStack() as st:
                    w = self.alloc_work(st, True)
                    w.scale1, w.shift1 = scale1, shift1
                    cts = self.sb(st, "cts", list(CT_SAMP.shape), F32)
                    self.dma(cts[:], i["ct_samp"][:, :], writes=[cts])
                    w.tabt = cts

                    def cs(name):
                        o_, w_ = CO_SAMP[name]
                        return cts[:, o_:o_ + w_]
                    w.colmask, w.rowmask = cs("colmask"), cs("rowmask")
                    w.KTs = self.sb(st, "KTs", [128, 4, 128], BF16)
                    w.Vs = self.sb(st, "Vs", [128, 512], BF16)
                    w.Sfh = [self.sb(st, "Sfh", [128, NSEQ, 128], F32)]
                    w.Sbh = [self.sb(st, "Sbh", [128, NSEQ, 128], BF16)]
                    w.Sn = w.Sfh
                    w.Km = self.sb(st, "Km", [128, NSEQ, 128], BF16)
                    w.Qm = w.Km
                    w.kdm = w.Km
                    self.load_mod([shift1, scale1], [0, 1], True)
                    hst_t = w.Sfh[0]
                    hst = hst_t[:, :, :].rearrange("p s v -> p (s v)")[0:NSEQ * 3, 0:3 * DNW]
                    self.dma(hst, i["dnc0"][:, :], writes=[hst_t])
                    for j in range(3):
                        g = self.G()
                        for ch in range(4):
                            self.tr(g[:, ch * 48:(ch + 1) * 48], hst[:, (j * 4 + ch) * 128:(j * 4 + ch + 1) * 128], self.cm("ident")[0:48, 0:48], [hst_t, self.ctm], [g])
                        self.act(w.Hst[:, j * 4:(j + 1) * 4, :], g[:, 0:192].rearrange("p (c t) -> p c t", c=4), AF.Copy, [g], [w.Hst])
                    fe = self.front_end(w, 0, True, True, (o["ksm"][:, :], o["vsm"][:, :]))
                    tabs = (cs("ltincl_s"), cs("seqm_s"), cs("ms01_s"))
                    self.front_dn(w, 0, True, fe)
                    dn_gen = self.dn_chunk(w, 0, True, True, tabs, False) if self.on("dn") else iter(())
                    for _ in dn_gen:
                        pass
                    if self.on("attn"):
                        pti = self.sb(st, "pti", [128, NSEQ * c.NPG], I32)
                        ptf_t = w.f512
                        ptf = ptf_t
                        io = self.sb(st, "io", [128, 1], I32)
                        iof = self.sb(st, "iof", [128, 1], F32)
                        w.idx = self.sb(st, "idx", [128, NSEQ * c.NPG], I32)
                        self.dma(pti[:], i["ptab"].partition_broadcast(128), writes=[pti])
                        self.E("pool", "iota", [], [io], io[:], pattern=[[0, 1]], base=0, channel_multiplier=1)
                        self.E("dve", "tensor_copy", [io], [iof], out=iof[:], in_=io[:])
                        npt = NSEQ * c.NPG
                        self.E("dve", "tensor_copy", [pti], [ptf], out=ptf[:, 0:npt], in_=pti[:])
                        self.E("dve", "tensor_scalar", [ptf, iof], [ptf], out=ptf[:, 0:npt], in0=ptf[:, 0:npt], scalar1=128.0, scalar2=iof[:, 0:1], op0=ALU.mult, op1=ALU.add)
                        self.E("dve", "tensor_copy", [ptf], [w.idx], out=w.idx[:], in_=ptf[:, 0:npt])
                        w.smt = self.sb(st, "smt", [1, 1], F32)
                        sm01 = self.sb(st, "sm01", [128, NSEQ, 64], BF16)
                        smneg = self.sb(st, "smneg", [128, NSEQ, 64], BF16)
                        o_, w_ = CO_SAMP["smask01"]
                        src = cts[:, o_:o_ + w_].rearrange("p (s q) -> p s q", s=NSEQ)
                        self.E("dve", "tensor_copy", [cts], [w.smt, sm01], out=sm01[:], in_=src)
                        self.E("dve", "tensor_scalar", [cts], [w.smt, smneg], out=smneg[:], in0=src, scalar1=-1.0, scalar2=BIG, op0=ALU.add, op1=ALU.mult)
                        w.sm01, w.smneg = sm01, smneg
                        w.attpad = [self.sb(st, "attpad%d" % k, [128, 8, 128], BF16) for k in range(2)]
                        w.kpf = [self.sb(st, "kpf%d" % k, [128, 512], F32) for k in range(4)]
                        w.vpf = [self.sb(st, "vpf%d" % k, [128, 512], F32) for k in range(4)]
                        w.kpb = [self.sb(st, "kpb%d" % k, [128, 512], BF16) for k in range(4)]
                        w.vpb = [self.sb(st, "vpb%d" % k, [128, 512], BF16) for k in range(4)]
                        w.ktp = [self.sb(st, "ktp%d" % k, [128, 4, 128], BF16) for k in range(4)]
                        self.attn_sample(w)
                    k = c.NOWN
                    if self.on("dn") and self.on("attn"):
                        self.dma(self.mixd[k * 128:(k + 1) * 128, :], w.mixed[:], reads=[w.mixed], writes=[self.t_mixd[k]])
                    if "mixed_s" in self.o:
                        self.E("dve", "tensor_copy", [w.mixed], [w.xt[0]], out=w.xt[0][:], in_=w.mixed[:])
                        self.dma(self.o["mixed_s"][:, :], w.xt[0][:], reads=[w.xt[0]])

    def phase2(self):
        c = self.cfg
        i, o = self.i, self.o
        with contextlib.ExitStack() as p2:
            sb = lambda name, shape, dt: self.sb(p2, name, shape, dt)
            woutb = sb("woutb", [128, DC, D], BF16)
            wupb = sb("wupb", [128, DC, 2 * DFF], BF16)
            wdnb = sb("wdnb", [128, FC, D], BF16)
            with contextlib.ExitStack() as st:
                stg = [self.sb(st, "stg%d" % k, [128, DC * 512], F32) for k in range(2)]
                n = 0
                wv = i["w_out"].rearrange("(c p) n -> p c n", p=128)
                for ct in range(2):
                    self.stream_cast(stg, wv[:, :, ct * 512:(ct + 1) * 512], woutb, woutb[:, :, ct * 512:(ct + 1) * 512], eng="act" if n % 2 else "dve")
                    n += 1
                wv = i["w_up"].rearrange("(c p) n -> p c n", p=128)
                for ct in range(11):
                    self.stream_cast(stg, wv[:, :, ct * 512:(ct + 1) * 512], wupb, wupb[:, :, ct * 512:(ct + 1) * 512], eng="act" if n % 2 else "dve")
                    n += 1
                wv = i["w_down"].rearrange("(c p) n -> p c n", p=128)
                for c0 in range(0, FC, 4):
                    c1 = min(FC, c0 + 4)
                    self.stream_cast(stg, wv[:, c0:c1, :], wdnb, wdnb[:, c0:c1, :], eng="act" if n % 2 else "dve")
                    n += 1
            self.fence()
            gt1, scale2, shift2, gt2 = [sb(nm, [128, D], F32) for nm in ("gt1", "scale2", "shift2", "gt2")]

            class WS:
                pass
            w = WS()
            w.xt = [sb("xt2", [128, D], F32)]
            w.h = sb("h2", [128, D], BF16)
            w.hT = sb("h2T", [128, DC, 128], BF16)
            w.ssq = sb("ssq2", [128, 1], F32)
            w.rstd = sb("rstd2", [128, 1], F32)
            mixb = sb("mixb", [128, D], BF16)
            mT = sb("mT", [128, DC, 128], BF16)
            yt = sb("yt", [128, D], F32)
            UE = sb("UE", [128, 4, NSEQ * (2 + TS)], F32)
            C4 = sb("C4", [128, 4, 128], F32)
            E2 = sb("E2", [128, 2, 128], F32)
            actT = sb("actT", [128, FC, 128], BF16)
            FH = sb("FH", [128, 44, NSEQ * 2], F32)
            fso = sb("fso", [NSEQ * 2, 512], F32)
            self.E("pool", "memset", [], [FH], FH[:], 0.0)
            blocks = [(b, False) for b in range(c.OWN0, c.NBLK)] + ([(0, True)] if self.on("sample") else [])
            cur_mod = None
            for (b, sample) in blocks:
                if cur_mod != sample:
                    self.load_mod([gt1, scale2, shift2, gt2], [2, 4, 3, 5], sample)
                    cur_mod = sample
                ns, T = (NSEQ, TS) if sample else (1, 128)
                k = c.NOWN if sample else b - c.OWN0
                halo = (not sample) and b == c.OWN0
                xt = w.xt[0]
                self.dma(xt[:], i["xs"][:, :] if sample else i["xp"][b * 128:(b + 1) * 128, :], writes=[xt])
                self.dma(mixb[:], self.mixd[k * 128:(k + 1) * 128, :], reads=[self.t_mixd[k]], writes=[mixb])
                tb = self.TB()
                for dc in range(DC):
                    self.tr(tb[:, dc * 128:(dc + 1) * 128], mixb[:, dc * 128:(dc + 1) * 128], self.identb, [mixb, self.cbf], [tb])
                self.act(mT[:], tb[:, :].rearrange("p (a b) -> p a b", a=DC), AF.Copy, [tb], [mT])
                for n in range(2):
                    g = self.G()
                    for dc in range(DC):
                        self.mm(g[:, :], mT[:, dc, :], woutb[:, dc, n * 512:(n + 1) * 512], dc == 0, dc == DC - 1, [mT, woutb], [g])
                    self.E("dve", "tensor_tensor", [g, gt1], [yt], out=yt[:, n * 512:(n + 1) * 512], in0=g[:, :], in1=gt1[:, n * 512:(n + 1) * 512], op=ALU.mult)
                self.E("pool", "tensor_tensor", [yt, xt], [xt], out=xt[:], in0=yt[:], in1=xt[:], op=ALU.add)
                x1 = xt
                if "x1_p" in o and (not sample) and b >= c.OUT0:
                    r0 = (b - c.OUT0) * 128
                    self.dma(o["x1_p"][r0:r0 + 128, :], x1[:], reads=[x1])
                self.norm_mod(w, x1, scale2, shift2, w.hT, tmp=yt)
                if sample:
                    for j in range(11):
                        hst = fso
                        self.dma(hst[:, :], i["ffc0"][:, j * 512:(j + 1) * 512], writes=[hst])
                        g = self.G()
                        for ch in range(4):
                            self.tr(g[:, ch * 32:(ch + 1) * 32], hst[:, ch * 128:(ch + 1) * 128], self.cm("ident")[0:32, 0:32], [hst, self.ctm], [g])
                        self.act(FH[:, j * 4:(j + 1) * 4, :], g[:, 0:128].rearrange("p (c t) -> p c t", c=4), AF.Copy, [g], [FH])
                ue4 = UE[:, :, 0:ns * (2 + T)].rearrange("p c (s t) -> p c s t", s=ns)
                fh4 = FH[:, :, 0:ns * 2].rearrange("p c (s t) -> p c s t", s=ns)
                for gi_ in range(11):
                    chs = [2 * gi_, 2 * gi_ + 1, FC + 2 * gi_, FC + 2 * gi_ + 1]
                    g = self.G()
                    for q_, ch in enumerate(chs):
                        for dc in range(DC):
                            self.mm(g[:, q_ * 128:(q_ + 1) * 128], wupb[:, dc, ch * 128:(ch + 1) * 128], w.hT[:, dc, :], dc == 0, dc == DC - 1, [w.hT, wupb], [g])
                    for half in range(2):
                        self.E("pool", "tensor_copy", [FH], [UE], out=ue4[:, half * 2:half * 2 + 2, :, 0:2], in_=fh4[:, chs[half * 2]:chs[half * 2] + 2, :, :])
                    src4 = g[:, :].rearrange("p (c s t) -> p c s t", c=4, s=ns)
                    if halo:
                        self.act(ue4[:, :, :, 2:2 + T], src4, AF.Copy, [g, self.bvd], [UE], scale=self.bvd[:, b:b + 1])
                    else:
                        self.act(ue4[:, :, :, 2:2 + T], src4, AF.Copy, [g], [UE])
                    for half in range(2):
                        self.E("pool", "tensor_copy", [UE], [FH], out=fh4[:, chs[half * 2]:chs[half * 2] + 2, :, :], in_=ue4[:, half * 2:half * 2 + 2, :, T:T + 2])
                    if halo:
                        continue
                    for q_, ch in enumerate(chs):
                        eng = "dve"
                        yv = C4[:, q_, :].rearrange("p (s t) -> p s t", s=ns)
                        self.E(eng, "tensor_scalar", [UE, self.wfc], [C4], out=yv, in0=ue4[:, q_, :, 0:T], scalar1=self.wfc[:, 0, ch:ch + 1], scalar2=None, op0=ALU.mult)
                        for kk in range(1, 3):
                            self.E(eng, "scalar_tensor_tensor", [UE, self.wfc, C4], [C4], out=yv, in0=ue4[:, q_, :, kk:kk + T],
                                   scalar=self.wfc[:, kk, ch:ch + 1], in1=yv, op0=ALU.mult, op1=ALU.add)
                    self.act(E2[:], C4[:, 2:4, :], AF.Exp, [C4], [E2], scale=-1.0)
                    self.act(E2[:], E2[:], AF.Ln, [E2], [E2], bias=1.0)
                    self.act(E2[:], E2[:], AF.Exp, [E2], [E2], scale=-1.0)
                    self.E("pool", "tensor_tensor", [E2, C4], [E2], out=E2[:], in0=E2[:], in1=C4[:, 2:4, :], op=ALU.mult)
                    self.E("dve", "tensor_tensor", [E2, C4], [actT], out=actT[:, 2 * gi_:2 * gi_ + 2, :], in0=E2[:], in1=C4[:, 0:2, :], op=ALU.mult)
                last_p = (not sample) and b == c.NBLK - 1
                if sample or last_p:
                    ncol = ns * 2
                    dst = o["fcs"] if sample else o["fcp"]
                    for j in range(11):
                        g = self.G()
                        for ch in range(4):
                            self.tr(g[0:ncol, ch * 128:(ch + 1) * 128], FH[:, j * 4 + ch, 0:ncol], self.cm("ident"), [FH, self.ctm], [g])
                        self.E("dve", "tensor_copy", [g], [fso], out=fso[0:ncol, :], in_=g[0:ncol, :])
                        self.dma(dst[:, j * 512:(j + 1) * 512], fso[0:ncol, :], reads=[fso])
                if halo:
                    continue
                for n in range(2):
                    g = self.G()
                    for fc_ in range(FC):
                        self.mm(g[:, :], actT[:, fc_, :], wdnb[:, fc_, n * 512:(n + 1) * 512], fc_ == 0, fc_ == FC - 1, [actT, wdnb], [g])
                    self.E("dve", "tensor_tensor", [g, gt2], [yt], out=yt[:, n * 512:(n + 1) * 512], in0=g[:, :], in1=gt2[:, n * 512:(n + 1) * 512], op=ALU.mult)
                self.E("pool", "tensor_tensor", [yt, x1], [yt], out=yt[:], in0=yt[:], in1=x1[:], op=ALU.add)
                if sample:
                    self.dma(o["ys"][:, :], yt[:], reads=[yt])
                elif b >= c.OUT0:
                    r0 = (b - c.OUT0) * 128
                    self.dma(o["yp"][r0:r0 + 128, :], yt[:], reads=[yt])


def core_inputs(cfg, core, inp):
    b, half = core // 2, core % 2
    S = cfg.NBLK * 128
    xp_full = np.asarray(inp["x_prompt"][b], np.float32)
    if half == 1:
        xp = xp_full
    else:
        xp = np.concatenate([np.zeros((S // 2, D), np.float32), xp_full[:S // 2]], axis=0)
    s0, s1 = core * NSEQ, (core + 1) * NSEQ
    m = {}
    m["xp"] = np.ascontiguousarray(xp)
    m["xs"] = np.ascontiguousarray(np.asarray(inp["x_sample"][s0:s1], np.float32).reshape(NSEQ * TS, D))
    m["cvec"] = np.ascontiguousarray(np.concatenate([np.asarray(inp["c_prompt"][b:b + 1], np.float32),
                                                     np.asarray(inp["c_sample"][s0:s1], np.float32)], axis=0))
    m["cache_k"] = np.asarray(inp["cache_k"], np.float32).reshape(cfg.NPHYS * 128, SBW)
    m["cache_v"] = np.asarray(inp["cache_v"], np.float32).reshape(cfg.NPHYS * 128, SBW)
    m["ptab"] = np.ascontiguousarray(np.asarray(inp["page_table"][s0:s1], np.int32).reshape(-1))
    m["s0"] = np.ascontiguousarray(np.asarray(inp["state_delta"][0, s0:s1], np.float32))
    m["dnc0"] = np.ascontiguousarray(np.asarray(inp["state_dn_conv"][0, s0:s1], np.float32).reshape(NSEQ * 3, 3 * DNW))
    m["ffc0"] = np.ascontiguousarray(np.asarray(inp["state_ffn_conv"][0, s0:s1], np.float32).reshape(NSEQ * 2, 2 * DFF))
    for k, nm in (("w_ada", "w_ada"), ("b_ada", "b_ada"), ("g_attn", "g_attn_norm"), ("w_in", "w_in"), ("g_q", "g_q"),
                  ("g_k", "g_k"), ("sb_bias", "sb_bias"), ("g_sb_out", "g_sb_out"), ("w_dn_conv", "w_dn_conv"),
                  ("a_log", "a_log"), ("dt_bias", "dt_bias"), ("g_dn_out", "g_dn_out"), ("w_out", "w_out"),
                  ("g_ffn", "g_ffn_norm"), ("w_up", "w_up"), ("w_ffn_conv", "w_ffn_conv"), ("w_down", "w_down")):
        m[k] = np.ascontiguousarray(np.asarray(inp[nm], np.float32)[0])
    m["ct_main"] = CT_MAIN
    m["ct_samp"] = CT_SAMP
    kv = np.zeros((128, 256), np.float32)
    if half == 1:
        kv[0:2, 0:128] = 1.0
    else:
        kv[2, 0:128] = 1.0
    kv[0:2, 128:256] = 1.0
    m["kvlo"] = kv
    bv = np.ones((128, cfg.NBLK), np.float32)
    if half == 0:
        bv[:, :cfg.NBLK // 2] = 0.0
    m["blkvalid"] = bv
    return m


def assemble(cfg, res, nb, nsamp):
    S = cfg.NBLK * 128
    H = S // 2
    yp = np.zeros((nb, S, D), np.float32)
    ys = np.zeros((nsamp, TS, D), np.float32)
    kp = np.zeros((1, nb, S, HS, HD), np.float32)
    vp = np.zeros((1, nb, S, HS, HD), np.float32)
    ks = np.zeros((1, nsamp, TS, HS, HD), np.float32)
    vs = np.zeros((1, nsamp, TS, HS, HD), np.float32)
    sp = np.zeros((1, nb, DNH, DND, DND), np.float32)
    ss = np.zeros((1, nsamp, DNH, DND, DND), np.float32)
    dcp = np.zeros((1, nb, 3, 3 * DNW), np.float32)
    dcs = np.zeros((1, nsamp, 3, 3 * DNW), np.float32)
    fcp = np.zeros((1, nb, 2, 2 * DFF), np.float32)
    fcs = np.zeros((1, nsamp, 2, 2 * DFF), np.float32)
    for core, r in res.items():
        b, half = core // 2, core % 2
        s0, s1 = core * NSEQ, (core + 1) * NSEQ
        yp[b, half * H:(half + 1) * H] = r["yp"]
        kp[0, b, half * H:(half + 1) * H] = r["kp"].reshape(H, HS, HD)
        vp[0, b, half * H:(half + 1) * H] = r["vp"].reshape(H, HS, HD)
        ys[s0:s1] = r["ys"].reshape(NSEQ, TS, D)
        ks[0, s0:s1] = r["ksm"].reshape(NSEQ, TS, HS, HD)
        vs[0, s0:s1] = r["vsm"].reshape(NSEQ, TS, HS, HD)
        ss[0, s0:s1] = r["ss_state"]
        dcs[0, s0:s1] = r["dcs"].reshape(NSEQ, 3, 3 * DNW)
        fcs[0, s0:s1] = r["fcs"].reshape(NSEQ, 2, 2 * DFF)
        if half == 1:
            sp[0, b] = r["sp_state"]
            dcp[0, b] = r["dcp"]
            fcp[0, b] = r["fcp"]
    return (yp, ys, kp, vp, ks, vs, sp, ss, dcp, dcs, fcp, fcs)


_NC_CACHE = {}


def kernel(**inputs):
    cfg = Cfg(nblk=inputs["x_prompt"].shape[1] // 128, npg=inputs["page_table"].shape[1], nphys=inputs["cache_k"].shape[1])
    key = (cfg.NBLK, cfg.NPG, cfg.NPHYS)
    if key not in _NC_CACHE:
        _NC_CACHE[key] = Builder(cfg).build()
    nc = _NC_CACHE[key]
    ncores = 8
    in_maps = [core_inputs(cfg, c, inputs) for c in range(ncores)]
    res = run_bass_kernel_spmd(nc, in_maps, core_ids=list(range(ncores)))
    out = assemble(cfg, {c: res.results[c] for c in range(ncores)}, inputs["x_prompt"].shape[0], inputs["x_sample"].shape[0])
    return out
```

```python
import contextlib
import numpy as np
import concourse.bass as bass
import concourse.mybir as mybir
from concourse.bass_utils import run_bass_kernel_spmd

F32 = mybir.dt.float32
BF16 = mybir.dt.bfloat16
I32 = mybir.dt.int32
AF = mybir.ActivationFunctionType
ALU = mybir.AluOpType
AX = mybir.AxisListType

D = 1024
DC = 8
HS = 8
HD = 64
SBW = 512
DNH = 4
DND = 128
DNW = 512
DFF = 2816
FC = 22
INC = 3592
EPS = 1e-6
BIG = 30000.0
NSEQ = 16
TS = 8


class Cfg:
    def __init__(self, nblk=32, npg=16, nphys=2560):
        self.NBLK = nblk
        self.OWN0 = nblk // 2 - 1
        self.OUT0 = nblk // 2
        self.NPG = npg
        self.NPHYS = nphys
        self.NOUT = nblk - self.OUT0
        self.NOWN = nblk - self.OWN0


class Tile:
    __slots__ = ("ap", "name", "last_w", "readers", "excl")

    def __init__(self, ap, name="", excl=False):
        self.ap = ap
        self.name = name
        self.last_w = None
        self.readers = []
        self.excl = excl

    def __getitem__(self, k):
        return self.ap[k]


class Op:
    __slots__ = ("eng", "fn", "deps", "need_inc", "count", "sem", "is_dma", "idx")

    def __init__(self, eng, fn, is_dma=False):
        self.eng = eng
        self.fn = fn
        self.deps = set()
        self.need_inc = is_dma
        self.count = 0
        self.sem = None
        self.is_dma = is_dma


COMPUTE = ("pe", "act", "dve", "pool")


class Sched:
    def __init__(self, nc, n_dma_sems=16):
        self.nc = nc
        self.ops = {e: [] for e in COMPUTE + ("sp",)}
        self.n_dma_sems = n_dma_sems
        self.nops = 0
        self.junk_fn = None
        self.junk_n = 0
        self.n_pe_waits = 0

    def _track(self, op, reads, writes):
        ex = [t for t in reads if t.excl]
        if ex:
            reads = [t for t in reads if not t.excl]
            writes = list(writes) + [t for t in ex if t not in writes]
        for t in reads:
            if t.last_w is not None:
                op.deps.add(t.last_w)
        for t in writes:
            if t.last_w is not None:
                op.deps.add(t.last_w)
            for r in t.readers:
                op.deps.add(r)
        for t in reads:
            t.readers.append(op)
        for t in writes:
            t.last_w = op
            t.readers = []
        op.deps.discard(op)

    def add(self, eng, fn, reads=(), writes=(), is_dma=False):
        op = Op(eng, fn, is_dma)
        self._track(op, reads, writes)
        self.ops[eng].append(op)
        self.nops += 1
        return op

    def dma(self, out_ap, in_ap, reads=(), writes=(), queue="sp", **kw):
        def fn(e, out_ap=out_ap, in_ap=in_ap, kw=kw):
            return e.dma_start(out=out_ap, in_=in_ap, **kw)
        return self.add(queue, fn, reads, writes, is_dma=True)

    def emit(self):
        nc = self.nc

        def skip(d, op):
            return d.eng == "pe" and op.eng == "pe" and not d.is_dma and not op.is_dma
        for e in self.ops:
            for op in self.ops[e]:
                for d in op.deps:
                    if not skip(d, op):
                        d.need_inc = True
        with contextlib.ExitStack() as st:
            sems = {e: st.enter_context(nc.semaphore("s_" + e)) for e in COMPUTE}
            dma_sems = {}
            for q in self.ops:
                if any(o.is_dma for o in self.ops[q]):
                    dma_sems[q] = [st.enter_context(nc.semaphore("d_%s_%d" % (q, i)))
                                   for i in range(self.n_dma_sems)]
            for e in self.ops:
                c = 0
                j = 0
                for op in self.ops[e]:
                    if op.is_dma:
                        ring = dma_sems[e]
                        op.sem = ring[j % len(ring)]
                        op.count = 16 * (j // len(ring) + 1)
                        j += 1
                    elif op.need_inc:
                        c += 1
                        op.count = c
                        op.sem = sems[e]
            block = st.enter_context(nc.Block())
            handles = {"pe": block.tensor, "act": block.scalar, "dve": block.vector,
                       "pool": block.gpsimd, "sp": block.sync}

            def make(e):
                oplist = self.ops[e]

                def body(eng):
                    known = {}
                    nwait = [0]

                    def wait(sem, val):
                        if known.get(id(sem), 0) >= val:
                            return
                        eng.wait_ge(sem, val)
                        known[id(sem)] = val
                    for op in oplist:
                        need = {}
                        for d in op.deps:
                            if skip(d, op):
                                continue
                            k = id(d.sem)
                            if k not in need or need[k][1] < d.count:
                                need[k] = (d.sem, d.count)
                        if op.is_dma and op.count > 16:
                            k = id(op.sem)
                            v = op.count - 16
                            if k not in need or need[k][1] < v:
                                need[k] = (op.sem, v)
                        pend = [(sem, val) for sem, val in need.values() if known.get(id(sem), 0) < val]
                        if pend and e == "pe" and self.junk_fn is not None and op.fn is not None:
                            nwait[0] += 1
                            for _ in range(self.junk_n):
                                self.junk_fn(eng)
                        for sem, val in pend:
                            wait(sem, val)
                        if op.fn is None:
                            continue
                        ins = op.fn(eng)
                        if op.is_dma:
                            ins.then_inc(op.sem, 16)
                        elif op.need_inc:
                            ins.then_inc(op.sem, 1)
                    last = {}
                    for op in oplist:
                        if op.is_dma:
                            last[id(op.sem)] = (op.sem, op.count)
                    for sem, val in last.values():
                        wait(sem, val)
                    if e == "pe":
                        self.n_pe_waits = nwait[0]
                return body
            for e in self.ops:
                if self.ops[e]:
                    handles[e](make(e))


def host_consts():
    i = np.arange(128)
    c = {}
    c["ident"] = np.eye(128, dtype=np.float32)
    c["ones"] = np.ones((128, 128), np.float32)
    for nm, nseq in (("p", 1), ("s", NSEQ)):
        t = 128 // nseq
        seq = i // t
        same = seq[:, None] == seq[None, :]
        incl = same & (i[None, :] <= i[:, None])
        strict = same & (i[None, :] < i[:, None])
        c["ltincl_" + nm] = incl.T.astype(np.float32)
        c["seqm_" + nm] = same.astype(np.float32)
        c["nmincl_" + nm] = np.where(incl, 0.0, BIG).astype(np.float32)
        c["nminclT_" + nm] = np.where(incl.T, 0.0, -BIG).astype(np.float32)
        c["ms01_" + nm] = strict.astype(np.float32)
    caus = i[:, None] < i[None, :]
    c["caus01"] = caus.astype(np.float32)
    c["causneg"] = np.where(caus, 0.0, -BIG).astype(np.float32)
    kt = i[:, None, None]
    ss_ = np.arange(NSEQ)[None, :, None]
    qq = (np.arange(64) % TS)[None, None, :]
    c["smask01"] = ((kt // TS == ss_) & (kt % TS < qq)).astype(np.float32).reshape(128, NSEQ * 64)
    c["ntri"] = np.where(i[:, None] >= i[None, :], -1.0, 0.0).astype(np.float32)
    cm = (np.arange(NSEQ)[:, None] == (i // TS)[None, :]).astype(np.float32)
    c["colmask"] = np.broadcast_to(cm.reshape(1, NSEQ * 128), (128, NSEQ * 128)).copy()
    c["rowmask"] = np.zeros((128, 128), np.float32)
    c["rowmask"][:, :NSEQ] = cm.T
    main = ["ident", "ones", "ltincl_p", "ms01_p", "caus01", "causneg", "ntri"]
    samp = ["ltincl_s", "seqm_s", "ms01_s", "rowmask", "colmask", "smask01"]

    def pack(names):
        off = {}
        o = 0
        for k in names:
            off[k] = (o, c[k].shape[1])
            o += c[k].shape[1]
        return np.concatenate([c[k] for k in names], axis=1).astype(np.float32), off
    return pack(main) + pack(samp)


CT_MAIN, CO_MAIN, CT_SAMP, CO_SAMP = host_consts()


class Builder:
    def __init__(self, cfg, stages=("all",), dbg=()):
        self.cfg = cfg
        self.stages = stages
        self.dbg = dbg
        self.nc = bass.Bass("TRN2", target_bir_lowering=False)
        self.s = Sched(self.nc)
        self.fence_id = 0
        self._uid = 0
        import os
        self.use_r32 = os.environ.get("USE_R32", "0") == "1"

    def on(self, st):
        return "all" in self.stages or st in self.stages

    def sb(self, stack, name, shape, dt):
        self._uid += 1
        h = stack.enter_context(self.nc.sbuf_tensor("%s_%d" % (name, self._uid), list(shape), dt))
        return Tile(h, name)

    def view(self, ap, name=""):
        return Tile(ap, name)

    def din(self, name, shape, dt=F32):
        return self.nc.dram_tensor(name, list(shape), dt, kind="ExternalInput").ap()

    def dout(self, name, shape, dt=F32):
        return self.nc.dram_tensor(name, list(shape), dt, kind="ExternalOutput").ap()

    def dscr(self, name, shape, dt=F32):
        return self.nc.dram_tensor(name, list(shape), dt, kind="Internal").ap()

    def E(self, eng, meth, reads, writes, *a, **kw):
        return self.s.add(eng, lambda e: getattr(e, meth)(*a, **kw), reads, writes)

    def mm(self, out, lhsT, rhs, start, stop, reads, writes, skip=False, r32=False):
        if r32 and self.use_r32:
            lhsT = lhsT.bitcast(mybir.dt.float32r)
            rhs = rhs.bitcast(mybir.dt.float32r)
        return self.s.add("pe", lambda e: e.matmul(out, lhsT=lhsT, rhs=rhs, start=start, stop=stop, skip_group_check=skip),
                          reads, writes)

    def tr(self, out, in_, ident, reads, writes):
        return self.s.add("pe", lambda e: e.transpose(out=out, in_=in_, identity=ident), reads, writes)

    def act(self, out, in_, func, reads, writes, **kw):
        return self.s.add("act", lambda e: e.activation(out=out, in_=in_, func=func, **kw), reads, writes)

    def dma(self, out, in_, reads=(), writes=(), **kw):
        return self.s.dma(out, in_, reads, writes, **kw)

    def fence(self):
        s = self.s
        f = set()
        for e in COMPUTE:
            real = [o for o in s.ops[e] if o.fn is not None and not o.is_dma]
            if real:
                f.add(real[-1])
        for q in s.ops:
            d = [o for o in s.ops[q] if o.is_dma]
            for o in d[-s.n_dma_sems:]:
                f.add(o)
        for e in COMPUTE + ("sp",):
            op = Op(e, None)
            op.deps = set(f)
            s.ops[e].append(op)

    def G(self):
        t = self.pg[self.gi % len(self.pg)]
        self.gi += 1
        return t

    def TB(self):
        t = self.ptb[self.ti % len(self.ptb)]
        self.ti += 1
        return t

    def declare(self):
        c = self.cfg
        NT = c.NBLK * 128
        i = {}
        i["xp"] = self.din("xp", [NT, D])
        i["xs"] = self.din("xs", [128, D])
        i["cvec"] = self.din("cvec", [17, D])
        i["cache_k"] = self.din("cache_k", [c.NPHYS * 128, SBW])
        i["cache_v"] = self.din("cache_v", [c.NPHYS * 128, SBW])
        i["ptab"] = self.din("ptab", [NSEQ * c.NPG], I32)
        i["s0"] = self.din("s0", [NSEQ, DNH, DND, DND])
        i["dnc0"] = self.din("dnc0", [NSEQ * 3, 3 * DNW])
        i["ffc0"] = self.din("ffc0", [NSEQ * 2, 2 * DFF])
        i["w_ada"] = self.din("w_ada", [D, 6 * D])
        i["b_ada"] = self.din("b_ada", [6 * D])
        i["g_attn"] = self.din("g_attn", [D])
        i["w_in"] = self.din("w_in", [D, INC])
        i["g_q"] = self.din("g_q", [HD])
        i["g_k"] = self.din("g_k", [HD])
        i["sb_bias"] = self.din("sb_bias", [HS])
        i["g_sb_out"] = self.din("g_sb_out", [HD])
        i["w_dn_conv"] = self.din("w_dn_conv", [4, 3 * DNW])
        i["a_log"] = self.din("a_log", [DNH])
        i["dt_bias"] = self.din("dt_bias", [DNH])
        i["g_dn_out"] = self.din("g_dn_out", [DND])
        i["w_out"] = self.din("w_out", [D, D])
        i["g_ffn"] = self.din("g_ffn", [D])
        i["w_up"] = self.din("w_up", [D, 2 * DFF])
        i["w_ffn_conv"] = self.din("w_ffn_conv", [3, 2 * DFF])
        i["w_down"] = self.din("w_down", [DFF, D])
        i["ct_main"] = self.din("ct_main", list(CT_MAIN.shape))
        i["ct_samp"] = self.din("ct_samp", list(CT_SAMP.shape))
        i["kvlo"] = self.din("kvlo", [128, 256])
        i["blkvalid"] = self.din("blkvalid", [128, c.NBLK])
        self.i = i
        o = {}
        o["yp"] = self.dout("yp", [c.NOUT * 128, D])
        o["ys"] = self.dout("ys", [128, D])
        o["kp"] = self.dout("kp", [c.NOUT * 128, SBW])
        o["vp"] = self.dout("vp", [c.NOUT * 128, SBW])
        o["ksm"] = self.dout("ksm", [128, SBW])
        o["vsm"] = self.dout("vsm", [128, SBW])
        o["sp_state"] = self.dout("sp_state", [DNH, DND, DND])
        o["ss_state"] = self.dout("ss_state", [NSEQ, DNH, DND, DND])
        o["dcp"] = self.dout("dcp", [3, 3 * DNW])
        o["dcs"] = self.dout("dcs", [NSEQ * 3, 3 * DNW])
        o["fcp"] = self.dout("fcp", [2, 2 * DFF])
        o["fcs"] = self.dout("fcs", [NSEQ * 2, 2 * DFF])
        for name, shape in self.dbg:
            o[name] = self.dout(name, shape)
        self.o = o
        self.modd = self.dscr("modd", [17, 6 * D])
        self.mixd = self.dscr("mixd", [(c.NOWN + 1) * 128, D], BF16)
        self.t_modd = Tile(None, "modd")
        self.t_mixd = [Tile(None, "mixd%d" % k) for k in range(c.NOWN + 1)]

    def stream_cast(self, stack_tiles, src_view, dst_tile, dst_ap, eng="dve"):
        stg = stack_tiles[self.sci % len(stack_tiles)]
        self.sci += 1
        shp = src_view.shape
        sap = stg[:, 0:shp[1] * shp[2]].rearrange("p (a b) -> p a b", a=shp[1])
        self.dma(sap, src_view, writes=[stg])
        if eng == "act":
            self.act(dst_ap, sap, AF.Copy, [stg], [dst_tile])
        else:
            self.E(eng, "tensor_copy", [stg], [dst_tile], out=dst_ap, in_=sap)

    def rsqrt_ops(self, ss, rs, n, scale, reads_extra=()):
        self.act(rs[:, 0:n], ss[:, 0:n], AF.Ln, [ss] + list(reads_extra), [rs], scale=scale, bias=self.epsb[:, 0:1])
        self.act(rs[:, 0:n], rs[:, 0:n], AF.Exp, [rs], [rs], scale=-0.5)

    def build(self):
        nc = self.nc
        c = self.cfg
        self.declare()
        i, o = self.i, self.o
        self.gi = 0
        self.ti = 0
        self.sci = 0
        with contextlib.ExitStack() as top:
            self.pg = [Tile(top.enter_context(nc.psum_tensor("pg%d" % k, [128, 512], F32)), "pg%d" % k, True) for k in range(2)]
            self.zb = [Tile(top.enter_context(nc.psum_tensor("zb%d" % k, [128, 512], F32)), "zb%d" % k, True) for k in range(2)]
            self.po = [Tile(top.enter_context(nc.psum_tensor("po%d" % k, [128, 512], F32)), "po%d" % k, True) for k in range(2)]
            self.ptb = [Tile(top.enter_context(nc.psum_tensor("ptb%d" % k, [128, 1024], BF16)), "ptb%d" % k, True) for k in range(2)]
            ctm = self.sb(top, "ctm", list(CT_MAIN.shape), F32)
            self.ctm = ctm
            self.dma(ctm[:], i["ct_main"][:, :], writes=[ctm])

            def cm(name):
                o_, w_ = CO_MAIN[name]
                return ctm[:, o_:o_ + w_]
            self.cm = cm
            cbf = self.sb(top, "cbf", [128, 4 * 128], BF16)
            self.cbf = cbf
            self.E("dve", "tensor_copy", [ctm], [cbf], out=cbf[:, 0:128], in_=cm("ident"))
            self.E("dve", "tensor_copy", [ctm], [cbf], out=cbf[:, 128:256], in_=cm("ntri"))
            self.E("dve", "tensor_copy", [ctm], [cbf], out=cbf[:, 256:384], in_=cm("causneg"))
            self.E("dve", "tensor_scalar", [ctm], [cbf], out=cbf[:, 384:512], in0=cm("ones"), scalar1=-1.0, scalar2=None, op0=ALU.mult)
            self.identb = cbf[:, 0:128]
            self.ntrib = cbf[:, 128:256]
            self.causnegb = cbf[:, 256:384]
            self.negonesb = cbf[:, 384:512]
            epsb = self.sb(top, "epsb", [128, 1], F32)
            self.epsb = epsb
            self.E("pool", "memset", [], [epsb], epsb[:], EPS)
            sv = self.sb(top, "sv", [128, 64 * 3 + 128 + 4 + 4 + 8], F32)
            self.sv = sv
            self.dma(sv[:, 0:64], i["g_q"].partition_broadcast(128), writes=[sv])
            self.dma(sv[:, 64:128], i["g_k"].partition_broadcast(128), writes=[sv])
            self.dma(sv[:, 128:192], i["g_sb_out"].partition_broadcast(128), writes=[sv])
            self.dma(sv[:, 192:320], i["g_dn_out"].partition_broadcast(128), writes=[sv])
            self.dma(sv[:, 320:324], i["a_log"].partition_broadcast(128), writes=[sv])
            self.dma(sv[:, 324:328], i["dt_bias"].partition_broadcast(128), writes=[sv])
            self.dma(sv[:, 328:336], i["sb_bias"].partition_broadcast(128), writes=[sv])
            self.E("dve", "tensor_scalar", [sv], [sv], out=sv[:, 0:64], in0=sv[:, 0:64], scalar1=HD ** -0.5, scalar2=None, op0=ALU.mult)
            self.act(sv[:, 320:324], sv[:, 320:324], AF.Exp, [sv], [sv])
            self.E("dve", "tensor_scalar", [sv], [sv], out=sv[:, 320:324], in0=sv[:, 320:324], scalar1=-1.0, scalar2=None, op0=ALU.mult)
            self.gq8, self.gk, self.gso, self.gdn = sv[:, 0:64], sv[:, 64:128], sv[:, 128:192], sv[:, 192:320]
            self.negA, self.dtb = sv[:, 320:324], sv[:, 324:328]
            kvf = self.sb(top, "kvf", [128, 256], F32)
            self.dma(kvf[:], i["kvlo"][:, :], writes=[kvf])
            kvd = self.sb(top, "kvd", [128, 128], BF16)
            self.kvd = kvd
            self.E("dve", "tensor_copy", [kvf], [kvd], out=kvd[:], in_=kvf[:, 0:128])
            bvd = self.sb(top, "bvd", [128, c.NBLK], F32)
            self.bvd = bvd
            self.dma(bvd[:], i["blkvalid"][:, :], writes=[bvd])
            kvone = self.sb(top, "kvone", [128, 128], BF16)
            self.kvone = kvone
            self.E("dve", "tensor_copy", [kvf], [kvone], out=kvone[:], in_=kvf[:, 128:256])
            wdc = self.sb(top, "wdc", [128, 4, 12], F32)
            self.wdc = wdc
            for t_ in range(4):
                self.dma(wdc[:, t_, :], i["w_dn_conv"][t_].rearrange("(c p) -> p c", p=128), writes=[wdc], allow_slow_non_contiguous=True)
            wfc = self.sb(top, "wfc", [128, 3, 44], F32)
            self.wfc = wfc
            for t_ in range(3):
                self.dma(wfc[:, t_, :], i["w_ffn_conv"][t_].rearrange("(c p) -> p c", p=128), writes=[wfc], allow_slow_non_contiguous=True)

            if self.on("setup"):
                self.setup_mod()
            self.fence()
            if self.on("p1"):
                self.phase1()
            self.fence()
            if self.on("p2"):
                self.phase2()
            self.s.emit()
        return nc

    def setup_mod(self):
        i = self.i
        with contextlib.ExitStack() as st:
            cv = self.sb(st, "cv", [17, D], F32)
            ex = self.sb(st, "ex", [17, D], F32)
            scb = self.sb(st, "scb", [17, D], BF16)
            scT = self.sb(st, "scT", [128, DC, 17], BF16)
            stg = [self.sb(st, "stg%d" % k, [128, DC * 512], F32) for k in range(2)]
            wab = [self.sb(st, "wab%d" % k, [128, DC, 512], BF16) for k in range(2)]
            bada = self.sb(st, "bada", [17, 512], F32)
            gv = self.sb(st, "gv", [17, 2 * D], F32)
            mt = [self.sb(st, "mt%d" % k, [17, 512], F32) for k in range(2)]
            self.dma(cv[:], i["cvec"][:, :], writes=[cv])
            self.dma(gv[:, 0:D], i["g_attn"].partition_broadcast(17), writes=[gv])
            self.dma(gv[:, D:2 * D], i["g_ffn"].partition_broadcast(17), writes=[gv])
            self.act(ex[:], cv[:], AF.Exp, [cv], [ex], scale=-1.0)
            self.E("dve", "tensor_scalar", [ex], [ex], out=ex[:], in0=ex[:], scalar1=1.0, scalar2=None, op0=ALU.add)
            self.E("dve", "reciprocal", [ex], [ex], out=ex[:], in_=ex[:])
            self.E("dve", "tensor_tensor", [ex, cv], [scb], out=scb[:], in0=cv[:], in1=ex[:], op=ALU.mult)
            tb = self.TB()
            for dc in range(DC):
                self.tr(tb[:, dc * 32:dc * 32 + 17], scb[0:17, dc * 128:(dc + 1) * 128], self.identb[0:17, 0:17], [scb, self.cbf], [tb])
            self.E("dve", "tensor_copy", [tb], [scT], out=scT[:], in_=tb[:, 0:DC * 32].rearrange("p (a b) -> p a b", a=DC)[:, :, 0:17])
            wv = i["w_ada"].rearrange("(c p) n -> p c n", p=128)
            for ct in range(12):
                wb = wab[ct % 2]
                self.stream_cast(stg, wv[:, :, ct * 512:(ct + 1) * 512], wb, wb[:], eng="act" if ct % 2 else "dve")
                self.dma(bada[:], i["b_ada"][ct * 512:(ct + 1) * 512].partition_broadcast(17), writes=[bada])
                g = self.G()
                for dc in range(DC):
                    self.mm(g[0:17, :], scT[:, dc, :], wb[:, dc, :], dc == 0, dc == DC - 1, [scT, wb], [g])
                m = mt[ct % 2]
                self.E("dve", "tensor_tensor", [g, bada], [m], out=m[:], in0=g[0:17, :], in1=bada[:], op=ALU.add)
                if ct in (2, 3, 8, 9):
                    go = (ct - 2) * 512 if ct < 4 else D + (ct - 8) * 512
                    self.E("dve", "scalar_tensor_tensor", [m, gv], [m], out=m[:], in0=m[:], scalar=1.0, in1=gv[:, go:go + 512],
                           op0=ALU.add, op1=ALU.mult)
                self.dma(self.modd[:, ct * 512:(ct + 1) * 512], m[:], reads=[m], writes=[self.t_modd])

    def load_mod(self, tiles, idxs, sample):
        for t, ix in zip(tiles, idxs):
            if not sample:
                self.dma(t[:], self.modd[0, ix * D:(ix + 1) * D].partition_broadcast(128), reads=[self.t_modd], writes=[t])
            else:
                for s_ in range(NSEQ):
                    self.dma(t[s_ * TS:(s_ + 1) * TS, :], self.modd[1 + s_, ix * D:(ix + 1) * D].partition_broadcast(TS),
                             reads=[self.t_modd], writes=[t])

    def norm_mod(self, w, xt, scale, shift, hT, tmp=None):
        self.E("pool", "memset", [], [w.ssq], w.ssq[:], 0.0)
        self.act(w.h[:], xt[:], AF.Square, [xt, w.ssq], [w.h, w.ssq], accum_out=w.ssq[:, 0:1])
        self.rsqrt_ops(w.ssq, w.rstd, 1, 1.0 / D)
        if tmp is None:
            tmp = xt
        self.E("dve", "scalar_tensor_tensor", [xt, w.rstd, scale], [tmp], out=tmp[:], in0=xt[:], scalar=w.rstd[:, 0:1],
               in1=scale[:], op0=ALU.mult, op1=ALU.mult)
        self.E("pool", "tensor_tensor", [tmp, shift], [w.h], out=w.h[:], in0=tmp[:], in1=shift[:], op=ALU.add)
        tb = self.TB()
        for dc in range(DC):
            self.tr(tb[:, dc * 128:(dc + 1) * 128], w.h[:, dc * 128:(dc + 1) * 128], self.identb, [w.h, self.cbf], [tb])
        self.act(hT[:], tb[:, :].rearrange("p (a b) -> p a b", a=DC), AF.Copy, [tb], [hT])

    def head_norm(self, w, ps, nh, hd, out_f32, reads_ps, gvec, out_tile, out_ap, scale):
        v3 = lambda ap: ap.rearrange("p (a b) -> p a b", a=nh)
        self.act(out_f32[:, 0:nh * hd], ps, AF.Square, reads_ps, [out_f32])
        self.E("dve", "tensor_reduce", [out_f32], [w.ss8], out=w.ss8[:, 0:nh], in_=v3(out_f32[:, 0:nh * hd]), axis=AX.X, op=ALU.add)
        self.rsqrt_ops(w.ss8, w.rs8, nh, scale)
        self.E("dve", "tensor_tensor", reads_ps + [w.rs8], [out_f32], out=v3(out_f32[:, 0:nh * hd]), in0=v3(ps),
               in1=w.rs8[:, 0:nh].unsqueeze(2).to_broadcast([128, nh, hd]), op=ALU.mult)
        self.E("pool", "tensor_tensor", [out_f32, self.sv], [out_tile], out=v3(out_ap), in0=v3(out_f32[:, 0:nh * hd]),
               in1=gvec.unsqueeze(1).to_broadcast([128, nh, hd]), op=ALU.mult)

    def front_end(self, w, b, sample, own, outrow):
        c = self.cfg
        i, o = self.i, self.o
        xt = w.xt[self.xi % len(w.xt)]
        self.xi += 1
        src = i["xs"][:, :] if sample else i["xp"][b * 128:(b + 1) * 128, :]
        self.dma(xt[:], src, writes=[xt])
        hT = w.hT
        self.norm_mod(w, xt, w.scale1, w.shift1, hT)
        winb = self.winb
        yield
        gk_ = self.zb[0]
        for dc in range(DC):
            self.mm(gk_[:, :], hT[:, dc, :], winb[:, dc, 512:1024], dc == 0, dc == DC - 1, [hT, winb], [gk_])
        self.head_norm(w, gk_[:, :], HS, HD, w.f512, [gk_], self.gk, w.kn, w.kn[:], 1.0 / HD)
        if outrow is not None:
            self.dma(outrow[0], w.kn[:], reads=[w.kn])
        self.E("dve", "tensor_copy", [w.kn], [w.knb], out=w.knb[:], in_=w.kn[:])
        yield
        gv_ = self.zb[1]
        for dc in range(DC):
            self.mm(gv_[:, :], hT[:, dc, :], winb[:, dc, 1024:1536], dc == 0, dc == DC - 1, [hT, winb], [gv_])
        Vt = w.Vs if sample else self.Vt[b]
        self.act(Vt[:], gv_[:, :], AF.Copy, [gv_], [Vt])
        if outrow is not None:
            self.E("dve", "tensor_copy", [gv_], [w.vf], out=w.vf[:], in_=gv_[:, :])
            self.dma(outrow[1], w.vf[:], reads=[w.vf])
        yield
        if own:
            gq_ = self.po[0]
            for dc in range(DC):
                self.mm(gq_[:, :], hT[:, dc, :], winb[:, dc, 0:512], dc == 0, dc == DC - 1, [hT, winb], [gq_])
            self.head_norm(w, gq_[:, :], HS, HD, w.f512, [gq_], self.gq8, w.qnb, w.qnb[:], 1.0 / HD)
        if self.on("dn"):
            g = self.G()
            for dc in range(DC):
                self.mm(g[:, 0:8], hT[:, dc, :], winb[:, dc, 3072:3080], dc == 0, dc == DC - 1, [hT, winb], [g])
            self.E("dve", "tensor_copy", [g], [w.ba], out=w.ba[:], in_=g[:, 0:8])
            if own:
                g = self.po[1]
                for dc in range(DC):
                    self.mm(g[:, :], hT[:, dc, :], winb[:, dc, 3080:3592], dc == 0, dc == DC - 1, [hT, winb], [g])
                self.act(w.zs[:], g[:, :], AF.Copy, [g], [w.zs])
        yield
        KTt = w.KTs if sample else self.KTt[b]
        tb = self.TB()
        for pr in range(4):
            self.tr(tb[:, pr * 128:(pr + 1) * 128], w.knb[:, pr * 128:(pr + 1) * 128], self.identb, [w.knb, self.cbf], [tb])
        self.act(KTt[:], tb[:, 0:512].rearrange("p (a b) -> p a b", a=4), AF.Copy, [tb], [KTt])
        if own:
            tb = self.TB()
            for pr in range(4):
                self.tr(tb[:, pr * 128:(pr + 1) * 128], w.qnb[:, pr * 128:(pr + 1) * 128], self.identb, [w.qnb, self.cbf], [tb])
            tq = tb[:, 0:512].rearrange("p (a b) -> p a b", a=4)
            self.act(w.QT[0:64, :, 0, :], tq[0:64, :, :], AF.Copy, [tb], [w.QT])
            self.E("dve", "tensor_copy", [tb], [w.QT], out=w.QT[64:128, :, 1, :], in_=tq[64:128, :, :])

    def front_dn(self, w, b, sample, fe):
        next(fe)
        if not self.on("dn"):
            for _ in fe:
                pass
            return
        if (not sample) and b == 0:
            self.E("pool", "memset", [], [w.Hst], w.Hst[:], 0.0)
        g0, g1, g2 = [self.dn_group(w, b, sample, j) for j in range(3)]
        next(g0); next(g0)
        next(fe)
        next(g1)
        next(g0)
        next(g1)
        next(fe)
        next(g2)
        next(g1)
        next(g2)
        next(fe)
        next(g2)
        for _ in fe:
            pass

    def dn_group(self, w, b, sample, j):
        T = TS if sample else 128
        ns = NSEQ if sample else 1
        hT, winb = w.hT, self.winb
        XE4 = w.XE4
        xe4 = XE4[:, :, :].rearrange("p c (s t) -> p c s t", s=ns)
        Hst = w.Hst
        hs4 = Hst[:, :, 0:ns * 3].rearrange("p c (s t) -> p c s t", s=ns)
        g = self.G()
        for ch in range(4):
            col = 1536 + (j * 4 + ch) * 128
            for dc in range(DC):
                self.mm(g[:, ch * 128:(ch + 1) * 128], winb[:, dc, col:col + 128], hT[:, dc, :], dc == 0, dc == DC - 1, [hT, winb], [g])
        yield
        src4 = g[:, :].rearrange("p (c s t) -> p c s t", c=4, s=ns)
        self.E("pool", "tensor_copy", [Hst], [XE4], out=xe4[:, :, :, 0:3], in_=hs4[:, j * 4:(j + 1) * 4, :, :])
        if sample:
            self.act(xe4[:, :, :, 3:3 + T], src4, AF.Copy, [g], [XE4])
        else:
            self.act(xe4[:, :, :, 3:3 + T], src4, AF.Copy, [g, self.bvd], [XE4], scale=self.bvd[:, b:b + 1])
        self.E("pool", "tensor_copy", [XE4], [Hst], out=hs4[:, j * 4:(j + 1) * 4, :, :], in_=xe4[:, :, :, T:T + 3])
        Y4 = w.Y4
        if not sample:
            Pt = w.E4
            wv_ = lambda k: self.wdc[:, k, j * 4:(j + 1) * 4].unsqueeze(2).to_broadcast([128, 4, T])
            self.E("pool", "tensor_tensor", [XE4, self.wdc], [Pt], out=Pt[:], in0=XE4[:, :, 1:1 + T], in1=wv_(1), op=ALU.mult)
            self.E("dve", "tensor_tensor", [XE4, self.wdc], [Y4], out=Y4[:], in0=XE4[:, :, 0:T], in1=wv_(0), op=ALU.mult)
            for k in range(1, 4):
                self.E("dve", "tensor_tensor", [Y4, Pt], [Y4], out=Y4[:], in0=Y4[:], in1=Pt[:], op=ALU.add)
                if k < 3:
                    self.E("pool", "tensor_tensor", [XE4, self.wdc], [Pt], out=Pt[:], in0=XE4[:, :, k + 1:k + 1 + T], in1=wv_(k + 1), op=ALU.mult)
        for ch in range(4 if sample else 0):
            cc = j * 4 + ch
            yv = Y4[:, ch, :].rearrange("p (s t) -> p s t", s=ns)
            self.E("dve", "tensor_scalar", [XE4, self.wdc], [Y4], out=yv, in0=xe4[:, ch, :, 0:T], scalar1=self.wdc[:, 0, cc:cc + 1],
                   scalar2=None, op0=ALU.mult)
            for k in range(1, 4):
                self.E("dve", "scalar_tensor_tensor", [XE4, self.wdc, Y4], [Y4], out=yv, in0=xe4[:, ch, :, k:k + T],
                       scalar=self.wdc[:, k, cc:cc + 1], in1=yv, op0=ALU.mult, op1=ALU.add)
        E4 = w.E4
        self.act(E4[:], Y4[:], AF.Exp, [Y4], [E4], scale=-1.0)
        self.act(E4[:], E4[:], AF.Ln, [E4], [E4], bias=1.0)
        self.act(E4[:], E4[:], AF.Exp, [E4], [E4], scale=-1.0)
        self.E("dve", "tensor_tensor", [E4, Y4], [Y4], out=Y4[:], in0=Y4[:], in1=E4[:], op=ALU.mult)
        SQ = E4
        if j < 2:
            self.act(SQ[:], Y4[:], AF.Square, [Y4], [SQ])
        yield
        if j < 2:
            g = self.G()
            self.mm(g[:, :], self.cm("ones"), SQ[:].rearrange("p a b -> p (a b)"), True, True, [SQ, self.ctm], [g])
            self.act(SQ[:].rearrange("p a b -> p (a b)"), g[:, :], AF.Ln, [g], [SQ], bias=self.epsb[:, 0:1])
            self.act(SQ[:], SQ[:], AF.Exp, [SQ], [SQ], scale=-0.5)
            if j == 0:
                self.E("dve", "scalar_tensor_tensor", [Y4, SQ], [w.QnT], out=w.QnT[:], in0=Y4[:], scalar=DND ** -0.5, in1=SQ[:],
                       op0=ALU.mult, op1=ALU.mult)
            else:
                self.E("pool", "tensor_tensor", [Y4, SQ], [w.KnT], out=w.KnT[:], in0=Y4[:], in1=SQ[:], op=ALU.mult)
        else:
            self.act(w.Vcb[:], Y4[:], AF.Copy, [Y4], [w.Vcb])
        yield

    def dn_chunk(self, w, b, sample, own, tabs, last_prompt):
        c = self.cfg
        i, o = self.i, self.o
        T = TS if sample else 128
        ns = NSEQ if sample else 1
        sc = w.sc
        hT, winb = w.hT, self.winb
        ltincl, seqm, ms01 = tabs
        bc4 = lambda ap: ap.unsqueeze(2).to_broadcast([128, 4, 128])
        hb4 = lambda ap: ap.unsqueeze(1).to_broadcast([128, 4, 128])
        v4 = lambda ap: ap.rearrange("p (a b) -> p a b", a=4)
        Hst = w.Hst
        if sample or last_prompt:
            ncol = ns * 3
            dst = o["dcs"] if sample else o["dcp"]
            for j in range(3):
                g = self.G()
                for ch in range(4):
                    self.tr(g[0:ncol, ch * 128:(ch + 1) * 128], Hst[:, j * 4 + ch, 0:ncol], self.cm("ident"), [Hst, self.ctm], [g])
                self.E("dve", "tensor_copy", [g], [w.f512], out=w.f512[0:ncol, :], in_=g[0:ncol, :])
                self.dma(dst[:, j * 512:(j + 1) * 512], w.f512[0:ncol, :], reads=[w.f512])
        tb = self.TB()
        for h in range(4):
            self.tr(tb[:, h * 128:(h + 1) * 128], w.KnT[:, h, :], self.identb, [w.KnT, self.cbf], [tb])
        self.act(w.Ktok[:], v4(tb[:, 0:512]), AF.Copy, [tb], [w.Ktok])
        tb = self.TB()
        for h in range(4):
            self.tr(tb[:, h * 128:(h + 1) * 128], w.Vcb[:, h, :], self.identb, [w.Vcb, self.cbf], [tb])
        self.E("dve", "tensor_copy", [tb], [w.Vtok], out=w.Vtok[:], in_=v4(tb[:, 0:512]))
        yield
        ba = w.ba
        self.act(sc[:, 0:4], ba[:, 0:4], AF.Exp, [ba], [sc], scale=-1.0)
        self.act(sc[:, 0:4], sc[:, 0:4], AF.Ln, [sc], [sc], bias=1.0)
        self.act(sc[:, 0:4], sc[:, 0:4], AF.Exp, [sc], [sc], scale=-1.0)
        if not sample:
            self.E("dve", "tensor_scalar", [sc, self.bvd], [sc], out=sc[:, 0:4], in0=sc[:, 0:4], scalar1=self.bvd[:, b:b + 1], scalar2=None, op0=ALU.mult)
        self.E("dve", "tensor_tensor", [ba, self.sv], [sc], out=sc[:, 32:36], in0=ba[:, 4:8], in1=self.dtb, op=ALU.add)
        self.act(sc[:, 32:36], sc[:, 32:36], AF.Exp, [sc], [sc])
        self.act(sc[:, 32:36], sc[:, 32:36], AF.Ln, [sc], [sc], bias=1.0)
        self.E("dve", "tensor_tensor", [sc, self.sv], [sc], out=sc[:, 4:8], in0=sc[:, 32:36], in1=self.negA, op=ALU.mult)
        g = self.G()
        self.mm(g[:, 0:4], ltincl, sc[:, 4:8], True, True, [sc, w.tabt], [g])
        self.mm(g[:, 4:8], seqm, sc[:, 4:8], True, True, [sc, w.tabt], [g])
        self.E("dve", "tensor_copy", [g], [sc], out=sc[:, 8:16], in_=g[:, 0:8])
        self.act(sc[:, 16:20], sc[:, 8:12], AF.Exp, [sc], [sc])
        self.E("dve", "scalar_tensor_tensor", [sc], [sc], out=sc[:, 20:24], in0=sc[:, 0:4], scalar=-1.0, in1=sc[:, 16:20], op0=ALU.mult, op1=ALU.mult)
        self.E("dve", "tensor_tensor", [sc], [sc], out=sc[:, 24:28], in0=sc[:, 12:16], in1=sc[:, 8:12], op=ALU.subtract)
        self.act(sc[:, 24:28], sc[:, 24:28], AF.Exp, [sc], [sc])
        self.E("dve", "tensor_scalar", [sc], [sc], out=sc[:, 28:32], in0=sc[:, 0:4], scalar1=-1.0, scalar2=None, op0=ALU.mult)
        beta, gg, negbg, kds, negbeta = sc[:, 0:4], sc[:, 8:12], sc[:, 20:24], sc[:, 24:28], sc[:, 28:32]
        yield
        GR = self.G()
        for h in range(4):
            dg = w.dg[h % 2]
            self.E("dve", "tensor_scalar", [sc, self.ctm], [dg], out=dg[:], in0=self.cm("ident"), scalar1=sc[:, 8 + h:9 + h], scalar2=None, op0=ALU.mult)
            self.mm(GR[:, h * 128:(h + 1) * 128], self.cm("ones"), dg[:], True, True, [dg, self.ctm], [GR])
        fa, fb, fc, fd = w.fa, w.fb, w.fc, w.fd
        self.E("dve", "tensor_tensor", [GR, sc], [fa], out=v4(fa[:]), in0=v4(GR[:, :]), in1=bc4(gg), op=ALU.subtract)
        self.act(fd[:], GR[:, :], AF.Exp, [GR], [fd])
        self.E("dve", "tensor_scalar", [fa], [fb], out=fb[:], in0=fa[:], scalar1=0.0, scalar2=None, op0=ALU.max)
        self.E("dve", "tensor_scalar", [fa], [fc], out=fc[:], in0=fa[:], scalar1=0.0, scalar2=None, op0=ALU.min)
        self.act(fb[:], fb[:], AF.Exp, [fb], [fb], scale=-1.0)
        self.act(fc[:], fc[:], AF.Exp, [fc], [fc])
        yield
        g = self.G()
        for h in range(4):
            self.mm(g[:, h * 128:(h + 1) * 128], w.KnT[:, h, :], w.KnT[:, h, :], True, True, [w.KnT], [g])
        self.E("dve", "tensor_tensor", [g, fb], [fa], out=fa[:], in0=g[:, :], in1=fb[:], op=ALU.mult)
        self.E("pool", "tensor_tensor", [fa, sc], [fa], out=v4(fa[:]), in0=v4(fa[:]), in1=bc4(negbeta), op=ALU.mult)
        self.E("dve", "tensor_tensor", [fa, w.tabt], [fa], out=v4(fa[:]), in0=v4(fa[:]), in1=hb4(ms01), op=ALU.mult)
        g = self.G()
        for h in range(4):
            self.mm(g[:, h * 128:(h + 1) * 128], w.KnT[:, h, :], w.QnT[:, h, :], True, True, [w.KnT, w.QnT], [g])
        self.E("dve", "tensor_tensor", [g, fc], [fc], out=fc[:], in0=g[:, :], in1=fc[:], op=ALU.mult)
        self.E("pool", "tensor_tensor", [fc, w.tabt], [w.intraT], out=w.intraT[:], in0=v4(fc[:]), in1=hb4(ltincl), op=ALU.mult)
        yield
        MTb = w.MTb
        nlev = 2 if sample else 6
        h2 = lambda ap: ap.rearrange("p (a b) -> p a b", a=2)
        hb2 = lambda ap: ap.unsqueeze(1).to_broadcast([128, 2, 128])
        identf = self.cm("ident")
        P, PT, MT = w.Pf[0], w.PTf[0], w.MT
        P = fa_t = None
        P = w.Pf[0]
        self.E("pool", "tensor_copy", [fa], [P], out=P[:], in_=v4(fa[:]))
        g = self.G()
        for h in range(4):
            self.tr(g[:, h * 128:(h + 1) * 128], P[:, h, :], identf, [P, self.ctm], [g])
        self.act(PT[:], v4(g[:, :]), AF.Copy, [g], [PT])
        self.E("dve", "tensor_tensor", [PT, self.ctm], [MT], out=MT[:], in0=PT[:], in1=hb4(identf), op=ALU.add)
        for lev in range(1, nlev + 1):
            Pn, PTn = w.Pf[lev % 2], w.PTf[lev % 2]
            g1 = self.G()
            for h in range(4):
                self.mm(g1[:, h * 128:(h + 1) * 128], PT[:, h, :], P[:, h, :], True, True, [P, PT], [g1], r32=True)
            self.act(Pn[:], v4(g1[:, :]), AF.Copy, [g1], [Pn])
            if lev < nlev:
                g2 = self.G()
                for h in range(4):
                    self.mm(g2[:, h * 128:(h + 1) * 128], P[:, h, :], PT[:, h, :], True, True, [P, PT], [g2], r32=True)
                self.E("dve", "tensor_copy", [g2], [PTn], out=PTn[:], in_=v4(g2[:, :]))
            g3 = self.G()
            for h in range(4):
                self.mm(g3[:, h * 128:(h + 1) * 128], Pn[:, h, :], MT[:, h, :], True, True, [Pn, MT], [g3], r32=True)
            self.E("dve", "tensor_tensor", [g3, MT], [MT], out=MT[:], in0=v4(g3[:, :]), in1=MT[:], op=ALU.add)
            P, PT = Pn, PTn
            yield
        self.act(MTb[:], MT[:], AF.Copy, [MT], [MTb])
        yield
        self.E("pool", "tensor_tensor", [w.QnT, fd], [w.QdT], out=w.QdT[:], in0=w.QnT[:], in1=v4(fd[:]), op=ALU.mult)
        self.E("dve", "tensor_tensor", [w.Vtok, sc], [w.Vtok], out=w.Vtok[:], in0=w.Vtok[:], in1=bc4(beta), op=ALU.mult)
        self.E("pool", "tensor_tensor", [w.Ktok, sc], [w.kdec], out=w.kdec[:], in0=w.Ktok[:], in1=bc4(kds), op=ALU.mult)
        if not sample:
            Sf, Sb = self.Sf, self.Sb
            g = self.G()
            for h in range(4):
                self.mm(g[:, h * 128:(h + 1) * 128], w.KnT[:, h, :], Sb[:, h, :], True, True, [w.KnT, Sb], [g])
            self.E("dve", "tensor_tensor", [g, sc], [fa], out=v4(fa[:]), in0=v4(g[:, :]), in1=bc4(negbg), op=ALU.mult)
            self.E("pool", "tensor_tensor", [fa, w.Vtok], [w.W], out=w.W[:], in0=v4(fa[:]), in1=w.Vtok[:], op=ALU.add)
            g = self.G()
            for h in range(4):
                self.mm(g[:, h * 128:(h + 1) * 128], MTb[:, h, :], w.W[:, h, :], True, True, [MTb, w.W], [g])
            self.act(w.vnew[:], v4(g[:, :]), AF.Copy, [g], [w.vnew])
            yield
            if own:
                po = self.G()
                po_dn = po
                for h in range(4):
                    self.mm(po[:, h * 128:(h + 1) * 128], w.QdT[:, h, :], Sb[:, h, :], True, False, [w.QdT, Sb], [po])
                    self.mm(po[:, h * 128:(h + 1) * 128], w.intraT[:, h, :], w.vnew[:, h, :], False, True, [w.intraT, w.vnew], [po])
            g = self.G()
            for h in range(4):
                self.mm(g[:, h * 128:(h + 1) * 128], w.kdec[:, h, :], w.vnew[:, h, :], True, True, [w.kdec, w.vnew], [g])
            self.E("dve", "tensor_tensor", [Sf, fd], [Sf], out=Sf[:], in0=Sf[:], in1=v4(fd[:])[:, :, 127:128].to_broadcast([128, 4, 128]), op=ALU.mult)
            self.E("dve", "tensor_tensor", [g, Sf], [Sf], out=Sf[:], in0=v4(g[:, :]), in1=Sf[:], op=ALU.add)
            self.act(Sb[:], Sf[:], AF.Copy, [Sf], [Sb])
            if last_prompt:
                self.dma(o["sp_state"].rearrange("h k v -> k h v"), Sf[:], reads=[Sf])
        else:
            colmask, rowmask = w.colmask, w.rowmask
            cm3 = colmask.rearrange("p (s t) -> p s t", s=NSEQ)
            s0v = i["s0"]
            po = self.po[1]
            for h in range(4):
                Sfh, Sbh = w.Sfh[0], w.Sbh[0]
                self.dma(Sfh[:], s0v[:, h, :, :].rearrange("s k v -> k s v"), writes=[Sfh])
                self.E("pool", "tensor_copy", [Sfh], [Sbh], out=Sbh[:], in_=Sfh[:])
                Km, Qm, kdm = w.Km, w.Qm, w.kdm
                self.E("pool", "tensor_tensor", [w.KnT, w.tabt], [Km], out=Km[:], in0=w.KnT[:, h, :].unsqueeze(1).to_broadcast([128, NSEQ, 128]), in1=cm3, op=ALU.mult)
                g = self.G()
                for s_ in range(NSEQ):
                    self.mm(g[:, 0:128], Km[:, s_, :], Sbh[:, s_, :], s_ == 0, s_ == NSEQ - 1, [Km, Sbh], [g])
                self.E("dve", "tensor_scalar", [g, sc], [fa], out=fa[:, 0:128], in0=g[:, 0:128], scalar1=sc[:, 20 + h:21 + h], scalar2=None, op0=ALU.mult)
                self.E("pool", "tensor_tensor", [fa, w.Vtok], [w.W], out=w.W[:, h, :], in0=fa[:, 0:128], in1=w.Vtok[:, h, :], op=ALU.add)
                g = self.G()
                self.mm(g[:, 0:128], MTb[:, h, :], w.W[:, h, :], True, True, [MTb, w.W], [g])
                self.act(w.vnew[:, h, :], g[:, 0:128], AF.Copy, [g], [w.vnew])
                self.E("dve", "tensor_tensor", [w.QdT, w.tabt], [Qm], out=Qm[:], in0=w.QdT[:, h, :].unsqueeze(1).to_broadcast([128, NSEQ, 128]), in1=cm3, op=ALU.mult)
                for s_ in range(NSEQ):
                    self.mm(po[:, h * 128:(h + 1) * 128], Qm[:, s_, :], Sbh[:, s_, :], s_ == 0, False, [Qm, Sbh], [po])
                self.mm(po[:, h * 128:(h + 1) * 128], w.intraT[:, h, :], w.vnew[:, h, :], False, True, [w.intraT, w.vnew], [po])
                Sn = w.Sn[0]
                self.E("pool", "tensor_tensor", [w.kdec, w.tabt], [kdm], out=kdm[:], in0=w.kdec[:, h, :].unsqueeze(1).to_broadcast([128, NSEQ, 128]),
                       in1=rowmask[:, 0:NSEQ].unsqueeze(2).to_broadcast([128, NSEQ, 128]), op=ALU.mult)
                for q4 in range(4):
                    g = self.G()
                    for k in range(4):
                        s_ = q4 * 4 + k
                        self.mm(g[:, k * 128:(k + 1) * 128], kdm[:, s_, :], w.vnew[:, h, :], True, True, [kdm, w.vnew], [g])
                    for k in range(4):
                        s_ = q4 * 4 + k
                        self.E("dve" if k % 2 == 0 else "pool" if False else "dve", "scalar_tensor_tensor", [g, Sfh, fd], [Sn], out=Sn[:, s_, :], in0=Sfh[:, s_, :],
                               scalar=fd[:, h * 128 + s_ * TS + TS - 1: h * 128 + s_ * TS + TS], in1=g[:, k * 128:(k + 1) * 128], op0=ALU.mult, op1=ALU.add)
                self.dma(o["ss_state"][:, h, :, :].rearrange("s k v -> k s v"), Sn[:], reads=[Sn])
        yield
        if own:
            po = self.po[1] if sample else po_dn
            self.act(fa[:], po[:, :], AF.Square, [po], [fa])
            self.E("dve", "tensor_reduce", [fa], [w.ss8], out=w.ss8[:, 0:4], in_=v4(fa[:]), axis=AX.X, op=ALU.add)
            self.rsqrt_ops(w.ss8, w.rs8, 4, 1.0 / DND)
            self.E("dve", "tensor_tensor", [po, w.rs8], [fa], out=v4(fa[:]), in0=v4(po[:, :]), in1=bc4(w.rs8[:, 0:4]), op=ALU.mult)
            self.E("pool", "tensor_tensor", [fa, self.sv], [fa], out=v4(fa[:]), in0=v4(fa[:]), in1=hb4(self.gdn), op=ALU.mult)
            zs = w.zs
            self.act(fb[:], zs[:], AF.Exp, [zs], [fb], scale=-1.0)
            self.act(fb[:], fb[:], AF.Ln, [fb], [fb], bias=1.0)
            self.act(fb[:], fb[:], AF.Exp, [fb], [fb], scale=-1.0)
            self.E("dve", "tensor_tensor", [fb, zs], [fb], out=fb[:], in0=fb[:], in1=zs[:], op=ALU.mult)
            self.E("dve", "tensor_tensor", [fa, fb], [w.mixed], out=w.mixed[:, 512:1024], in0=fa[:], in1=fb[:], op=ALU.mult)

    def attn_step(self, w, S, nh, nq, qT, kT, vv, nk, kvl, kvl_t, brow_ap, maskneg, mask01, mask_t, O, o_cols, first, last,
                  qreads, kreads, vreads, att_out=None, att_lhs=None, first_o=None, last_o=None):
        W_ = nh * nq
        et, spt, att, Rb = S.et, S.spt, S.att, S.Rb
        Z = S.zbank
        for p_ in range(nh // 2):
            self.mm(Z[0:nk, p_ * 2 * nq:(p_ + 1) * 2 * nq], kT[p_], qT[p_], p_ == 0, False, qreads + kreads, [Z], skip=True)
        self.mm(Z[0:nk, 0:W_], kvl, brow_ap, False, True, [kvl_t, self.brow], [Z], skip=True)
        if getattr(S, "pending", None) is not None:
            S.pending()
            S.pending = None
        yield
        self.act(et[0:nk, 0:W_], Z[0:nk, 0:W_], AF.Exp, [Z], [et])
        self.act(spt[0:nk, 0:W_], et[0:nk, 0:W_], AF.Ln, [et], [spt], bias=1.0)
        if mask01 is not None:
            self.E("dve", "tensor_tensor", [spt, mask_t], [spt], out=spt[0:nk, 0:W_], in0=spt[0:nk, 0:W_], in1=mask01, op=ALU.mult)
        yield
        U = Z
        fin = first and maskneg is None
        self.mm(U[0:nk, 0:W_], self.ntrib[0:nk, 0:nk], spt[0:nk, 0:W_], False, fin, [spt, self.cbf], [U], skip=True)
        if not first:
            self.mm(U[0:nk, 0:W_], self.negonesb[:, 0:nk], Rb[:, 0:W_], False, maskneg is None, [Rb, self.cbf], [U], skip=True)
        if maskneg is not None:
            self.mm(U[0:nk, 0:W_], self.identb[0:nk, 0:nk], maskneg, False, True, [mask_t, self.cbf], [U], skip=True)
        yield
        if att_out is None:
            self.act(att[0:nk, 0:W_], U[0:nk, 0:W_], AF.Exp, [U], [att])
        else:
            self.act(att_out[0], U[0:nk, 0:W_].rearrange("p (h q) -> p h q", h=nh), AF.Exp, [U], [att_out[1]])
        if not last:
            if first:
                self.E("dve", "tensor_copy", [spt], [Rb], out=Rb[0:nk, 0:W_], in_=spt[0:nk, 0:W_])
            else:
                self.E("dve", "tensor_tensor", [spt, Rb], [Rb], out=Rb[0:nk, 0:W_], in0=Rb[0:nk, 0:W_], in1=spt[0:nk, 0:W_], op=ALU.add)
        fo = first if first_o is None else first_o
        lo = last if last_o is None else last_o

        def av():
            for h in range(nh):
                if att_lhs is None:
                    lhs = att[0:nk, h * nq:(h + 1) * nq]
                    rd = [att]
                else:
                    lhs = att_lhs[0][h]
                    rd = [att_lhs[1]]
                self.mm(O[o_cols[h]], lhs, vv[h], fo and h == 0, lo, rd + vreads, [O], skip=True)
        S.pending = av
        yield

    @staticmethod
    def interleave(gens):
        gens = list(gens)
        while gens:
            for g in list(gens):
                try:
                    next(g)
                except StopIteration:
                    gens.remove(g)

    def attn_prompt(self, w, b, dn_gen=None):
        c = self.cfg

        def stream(hg):
            O = self.po[hg]
            S = w.streams[hg]
            for kb in range(b, -1, -1):
                qT = [w.QT[:, hg * 2 + p_, :, :] for p_ in range(2)]
                kT = [self.KTt[kb][:, hg * 2 + p_, :] for p_ in range(2)]
                vv = [self.Vt[kb][:, (hg * 4 + h) * 64:(hg * 4 + h + 1) * 64] for h in range(4)]
                diag = kb == b
                kvl = self.kvd if kb < c.OUT0 else self.kvone
                yield from self.attn_step(w, S, 4, 128, qT, kT, vv, 128, kvl[:, :], kvl,
                                          self.brow[:, hg * 512:(hg + 1) * 512],
                                          w.causrep[:, 0:512] if diag else None, w.caus01rep[:, 0:512] if diag else None, w.causrep_t,
                                          O, [(slice(None), slice(h * 64, (h + 1) * 64)) for h in range(4)],
                                          kb == b, kb == 0, [w.QT], [self.KTt[kb]], [self.Vt[kb]])
            S.pending()
            S.pending = None
            self.head_norm(w, O[:, 0:256], 4, HD, w.f512, [O], self.gso, w.mixed, w.mixed[:, hg * 256:(hg + 1) * 256], 1.0 / HD)
        gens = [stream(0), stream(1)]
        n_rounds = 5 * (b + 1) + 1
        stride = max(1, n_rounds // 24)
        rnd = 0
        dn_live = dn_gen is not None
        while gens or dn_live:
            if dn_live and (rnd % stride == 0 or not gens):
                try:
                    next(dn_gen)
                except StopIteration:
                    dn_live = False
            for g_ in list(gens):
                try:
                    next(g_)
                except StopIteration:
                    gens.remove(g_)
            rnd += 1

    def attn_sample(self, w):
        c = self.cfg
        i = self.i
        npg = c.NPG
        O = self.po[0]
        ck = i["cache_k"]
        cv = i["cache_v"]

        NSTR = len(w.streams)

        def stream(si):
            S = w.streams[si]
            for s_ in range(si, NSEQ, NSTR):
                attpad = w.attpad[si]
                if getattr(S, "pending", None) is not None:
                    S.pending()
                    S.pending = None
                self.E("pool", "memset", [], [attpad], attpad[:], 0.0)
                qT = [w.QT[:, p_, :, s_ * TS:(s_ + 1) * TS] for p_ in range(4)]
                att_out = (attpad[:, :, s_ * TS:(s_ + 1) * TS], attpad)
                att_lhs = ([attpad[:, h, :] for h in range(8)], attpad)
                o_cols = [(slice(None), slice(h * 64, (h + 1) * 64)) for h in range(8)]
                for blk in range(npg, -1, -1):
                    if blk == npg:
                        kT = [w.KTs[:, p_, :] for p_ in range(4)]
                        vv = [w.Vs[:, h * 64:(h + 1) * 64] for h in range(8)]
                        kreads, vreads = [w.KTs], [w.Vs]
                        mneg, m01 = w.smneg[:, s_, :], w.sm01[:, s_, :]
                    else:
                        j = s_ * npg + blk
                        kk = si * 2 + (self.pgi[si] % 2)
                        self.pgi[si] += 1
                        kpf, vpf, kpb, vpb, ktp = w.kpf[kk], w.vpf[kk], w.kpb[kk], w.vpb[kk], w.ktp[kk]
                        self.s.add("pool", lambda e, kpf=kpf, j=j: e.indirect_dma_start(
                            out=kpf[:], out_offset=None, in_=ck, in_offset=bass.IndirectOffsetOnAxis(ap=w.idx[:, j:j + 1], axis=0)),
                            [w.idx], [kpf], is_dma=True)
                        self.s.add("pool", lambda e, vpf=vpf, j=j: e.indirect_dma_start(
                            out=vpf[:], out_offset=None, in_=cv, in_offset=bass.IndirectOffsetOnAxis(ap=w.idx[:, j:j + 1], axis=0)),
                            [w.idx], [vpf], is_dma=True)
                        self.E("dve", "tensor_copy", [kpf], [kpb], out=kpb[:], in_=kpf[:])
                        self.act(vpb[:], vpf[:], AF.Copy, [vpf], [vpb])
                        tb = self.TB()
                        for pr in range(4):
                            self.tr(tb[:, pr * 128:(pr + 1) * 128], kpb[:, pr * 128:(pr + 1) * 128], self.identb, [kpb, self.cbf], [tb])
                        self.E("dve", "tensor_copy", [tb], [ktp], out=ktp[:], in_=tb[:, 0:512].rearrange("p (a b) -> p a b", a=4))
                        kT = [ktp[:, p_, :] for p_ in range(4)]
                        vv = [vpb[:, h * 64:(h + 1) * 64] for h in range(8)]
                        kreads, vreads = [ktp], [vpb]
                        mneg, m01 = None, None
                    yield from self.attn_step(w, S, 8, TS, qT, kT, vv, 128, self.kvone[:, :], self.kvone, self.brow[:, 1024:1088],
                                              mneg, m01, w.smt, O, o_cols, blk == npg, blk == 0, [w.QT], kreads, vreads,
                                              att_out=att_out, att_lhs=att_lhs,
                                              first_o=(s_ == 0 and blk == npg), last_o=(s_ == NSEQ - 1 and blk == 0))
            S.pending()
            S.pending = None
        self.pgi = [0] * NSTR
        self.interleave([stream(k) for k in range(NSTR)])
        self.head_norm(w, O[:, :], HS, HD, w.f512, [O], self.gso, w.mixed, w.mixed[:, 0:512], 1.0 / HD)

    def alloc_work(self, st, sample):
        class WS:
            pass
        w = WS()
        sb = lambda name, shape, dt: self.sb(st, name, shape, dt)
        w.xt = [sb("xt", [128, D], F32)]
        w.h = sb("h", [128, D], BF16)
        w.hT = sb("hT", [128, DC, 128], BF16)
        w.ssq = sb("ssq", [128, 1], F32)
        w.rstd = sb("rstd", [128, 1], F32)
        w.ss8 = sb("ss8", [128, 8], F32)
        w.rs8 = sb("rs8", [128, 8], F32)
        w.f512 = sb("f512", [128, 512], F32)
        w.kn = sb("kn", [128, 512], F32)
        w.knb = sb("knb", [128, 512], BF16)
        w.vf = w.f512
        w.qnb = sb("qnb", [128, 512], BF16)
        w.QT = sb("QT", [128, 4, 2, 128], BF16)
        self.E("pool", "memset", [], [w.QT], w.QT[:], 0.0)
        ns, T = (NSEQ, TS) if sample else (1, 128)
        w.XE4 = sb("XE4", [128, 4, ns * (3 + T)], F32)
        w.Hst = sb("Hst", [128, 12, ns * 3], F32)
        w.ba = sb("ba", [128, 8], F32)
        w.zs = sb("zs", [128, 512], BF16)
        w.Y4 = sb("Y4", [128, 4, 128], F32)
        w.E4 = sb("E4", [128, 4, 128], F32)
        w.QnT = sb("QnT", [128, 4, 128], BF16)
        w.KnT = sb("KnT", [128, 4, 128], BF16)
        w.Ktok = sb("Ktok", [128, 4, 128], BF16)
        w.Vtok = sb("Vtok", [128, 4, 128], F32)
        w.sc = sb("sc", [128, 40], F32)
        w.dg = [sb("dg", [128, 128], F32)] * 2
        w.fa = sb("fa", [128, 512], F32)
        w.fb = sb("fb", [128, 512], F32)
        w.fc = sb("fc", [128, 512], F32)
        w.fd = sb("fd", [128, 512], F32)
        w.Pf = [sb("Pf%d" % k, [128, 4, 128], F32) for k in range(2)]
        w.PTf = [sb("PTf%d" % k, [128, 4, 128], F32) for k in range(2)]
        w.MT = sb("MT", [128, 4, 128], F32)
        w.MTb = sb("MTb", [128, 4, 128], BF16)
        w.intraT = sb("intraT", [128, 4, 128], BF16)
        w.QdT = sb("QdT", [128, 4, 128], BF16)
        w.kdec = w.Ktok
        w.W = sb("W", [128, 4, 128], BF16)
        w.Vcb = w.W
        w.vnew = sb("vnew", [128, 4, 128], BF16)
        w.mixed = sb("mixed", [128, D], BF16)
        wd = 64 if sample else 512
        class ST:
            pass
        w.streams = []
        for k in range(2):
            S = ST()
            S.zbank = self.zb[k]
            S.et = sb("et%d" % k, [128, wd], BF16)
            S.spt = sb("spt%d" % k, [128, wd], BF16)
            S.att = sb("att%d" % k, [128, wd], BF16)
            S.Rb = sb("Rb%d" % k, [128, wd], BF16)
            w.streams.append(S)
        return w

    def phase1(self):
        c = self.cfg
        i, o = self.i, self.o
        self.xi = 0
        self.ai = 0
        self.pgi = 0
        with contextlib.ExitStack() as p1:
            winb = self.sb(p1, "winb", [128, DC, INC], BF16)
            self.winb = winb
            scale1 = self.sb(p1, "scale1", [128, D], F32)
            shift1 = self.sb(p1, "shift1", [128, D], F32)
            sv = self.sv
            WB = 512 * 2 + 64
            brow = self.sb(p1, "brow", [128, WB], BF16)
            nb = self.sb(p1, "nb", [128, 1], F32)
            with contextlib.ExitStack() as st:
                bexp = self.sb(st, "bexp", [128, WB], F32)
                for hg in range(2):
                    for h in range(4):
                        self.E("dve", "tensor_copy", [sv], [bexp], out=bexp[:, hg * 512 + h * 128: hg * 512 + (h + 1) * 128],
                               in_=sv[:, 328 + hg * 4 + h: 329 + hg * 4 + h].to_broadcast([128, 128]))
                for h in range(8):
                    self.E("dve", "tensor_copy", [sv], [bexp], out=bexp[:, 1024 + h * 8: 1024 + (h + 1) * 8],
                           in_=sv[:, 328 + h: 329 + h].to_broadcast([128, 8]))
                bhi = self.sb(st, "bhi", [128, WB], BF16)
                self.brow = brow
                idf = self.cm("ident")
                self.E("dve", "tensor_copy", [bexp], [bhi], out=bhi[:], in_=bexp[:])
                self.E("dve", "tensor_tensor", [bexp, bhi], [bexp], out=bexp[:], in0=bexp[:], in1=bhi[:], op=ALU.subtract)
                self.E("dve", "tensor_scalar", [bexp, self.ctm], [bexp], out=bexp[:], in0=bexp[:], scalar1=idf[:, 1:2], scalar2=None, op0=ALU.mult)
                self.E("dve", "scalar_tensor_tensor", [bhi, bexp, self.ctm], [bexp], out=bexp[:], in0=bhi[:], scalar=idf[:, 0:1], in1=bexp[:],
                       op0=ALU.mult, op1=ALU.add)
                self.E("dve", "tensor_scalar", [self.ctm], [nb], out=nb[:], in0=idf[:, 2:3], scalar1=-BIG, scalar2=None, op0=ALU.mult)
                self.E("dve", "tensor_scalar", [bexp, nb], [brow], out=brow[:], in0=bexp[:], scalar1=nb[:, 0:1], scalar2=None, op0=ALU.add)
                stg = [self.sb(st, "stg%d" % k, [128, DC * 512], F32) for k in range(2)]
                wv = i["w_in"].rearrange("(c p) n -> p c n", p=128)
                for ct in range(8):
                    n0, n1 = ct * 512, min(INC, (ct + 1) * 512)
                    self.stream_cast(stg, wv[:, :, n0:n1], winb, winb[:, :, n0:n1], eng="act" if ct % 2 else "dve")
            self.fence()
            with contextlib.ExitStack() as st:
                KT = self.sb(st, "KT", [128, 4, c.NBLK * 128], BF16)
                Vr = self.sb(st, "Vr", [128, c.NBLK, 512], BF16)
                self.KTt = [Tile(KT[:, :, b * 128:(b + 1) * 128], "KT%d" % b) for b in range(c.NBLK)]
                self.Vt = [Tile(Vr[:, b, :], "V%d" % b) for b in range(c.NBLK)]
                w = self.alloc_work(st, False)
                w.scale1, w.shift1 = scale1, shift1
                w.tabt = self.ctm
                w.causrep = self.sb(st, "causrep", [128, 512], BF16)
                w.caus01rep = self.sb(st, "caus01rep", [128, 512], BF16)
                w.causrep_t = self.sb(st, "causrep_t", [1, 1], F32)
                for h in range(4):
                    self.E("dve", "tensor_copy", [self.ctm], [w.causrep_t, w.causrep], out=w.causrep[:, h * 128:(h + 1) * 128], in_=self.cm("causneg"))
                    self.E("dve", "tensor_copy", [self.ctm], [w.causrep_t, w.caus01rep], out=w.caus01rep[:, h * 128:(h + 1) * 128], in_=self.cm("caus01"))
                self.Sf = self.sb(st, "Sf", [128, 4, 128], F32)
                self.Sb = self.sb(st, "Sb", [128, 4, 128], BF16)
                self.E("pool", "memset", [], [self.Sf], self.Sf[:], 0.0)
                self.E("pool", "memset", [], [self.Sb], self.Sb[:], 0.0)
                self.load_mod([shift1, scale1], [0, 1], False)
                tabs = (self.cm("ltincl_p"), self.cm("ones"), self.cm("ms01_p"))
                for b in range(c.NBLK):
                    own = b >= c.OWN0
                    outrow = None
                    if b >= c.OUT0:
                        r0 = (b - c.OUT0) * 128
                        outrow = (o["kp"][r0:r0 + 128, :], o["vp"][r0:r0 + 128, :])
                    fe = self.front_end(w, b, False, own, outrow)
                    self.front_dn(w, b, False, fe)
                    dn_gen = self.dn_chunk(w, b, False, own, tabs, b == c.NBLK - 1) if self.on("dn") else iter(())
                    if own and self.on("attn"):
                        self.attn_prompt(w, b, dn_gen)
                    else:
                        for _ in dn_gen:
                            pass
                    if own:
                        k = b - c.OWN0
                        if self.on("dn") and self.on("attn"):
                            self.dma(self.mixd[k * 128:(k + 1) * 128, :], w.mixed[:], reads=[w.mixed], writes=[self.t_mixd[k]])
                        if "mixed_p" in self.o and b >= c.OUT0:
                            r0 = (b - c.OUT0) * 128
                            self.E("dve", "tensor_copy", [w.mixed], [w.xt[0]], out=w.xt[0][:], in_=w.mixed[:])
                            self.dma(self.o["mixed_p"][r0:r0 + 128, :], w.xt[0][:], reads=[w.xt[0]])
            self.fence()
            if self.on("sample"):
                with contextlib.ExitStack() as st:
                    w = self.alloc_work(st, True)
                    w.scale1, w.shift1 = scale1, shift1
                    cts = self.sb(st, "cts", list(CT_SAMP.shape), F32)
                    self.dma(cts[:], i["ct_samp"][:, :], writes=[cts])
                    w.tabt = cts

                    def cs(name):
                        o_, w_ = CO_SAMP[name]
                        return cts[:, o_:o_ + w_]
                    w.colmask, w.rowmask = cs("colmask"), cs("rowmask")
                    w.KTs = self.sb(st, "KTs", [128, 4, 128], BF16)
                    w.Vs = self.sb(st, "Vs", [128, 512], BF16)
                    w.Sfh = [self.sb(st, "Sfh", [128, NSEQ, 128], F32)]
                    w.Sbh = [self.sb(st, "Sbh", [128, NSEQ, 128], BF16)]
                    w.Sn = w.Sfh
                    w.Km = self.sb(st, "Km", [128, NSEQ, 128], BF16)
                    w.Qm = w.Km
                    w.kdm = w.Km
                    self.load_mod([shift1, scale1], [0, 1], True)
                    hst_t = w.Sfh[0]
                    hst = hst_t[:, :, :].rearrange("p s v -> p (s v)")[0:NSEQ * 3, 0:3 * DNW]
                    self.dma(hst, i["dnc0"][:, :], writes=[hst_t])
                    for j in range(3):
                        g = self.G()
                        for ch in range(4):
                            self.tr(g[:, ch * 48:(ch + 1) * 48], hst[:, (j * 4 + ch) * 128:(j * 4 + ch + 1) * 128], self.cm("ident")[0:48, 0:48], [hst_t, self.ctm], [g])
                        self.act(w.Hst[:, j * 4:(j + 1) * 4, :], g[:, 0:192].rearrange("p (c t) -> p c t", c=4), AF.Copy, [g], [w.Hst])
                    fe = self.front_end(w, 0, True, True, (o["ksm"][:, :], o["vsm"][:, :]))
                    tabs = (cs("ltincl_s"), cs("seqm_s"), cs("ms01_s"))
                    self.front_dn(w, 0, True, fe)
                    dn_gen = self.dn_chunk(w, 0, True, True, tabs, False) if self.on("dn") else iter(())
                    for _ in dn_gen:
                        pass
                    if self.on("attn"):
                        pti = self.sb(st, "pti", [128, NSEQ * c.NPG], I32)
                        ptf_t = w.f512
                        ptf = ptf_t
                        io = self.sb(st, "io", [128, 1], I32)
                        iof = self.sb(st, "iof", [128, 1], F32)
                        w.idx = self.sb(st, "idx", [128, NSEQ * c.NPG], I32)
                        self.dma(pti[:], i["ptab"].partition_broadcast(128), writes=[pti])
                        self.E("pool", "iota", [], [io], io[:], pattern=[[0, 1]], base=0, channel_multiplier=1)
                        self.E("dve", "tensor_copy", [io], [iof], out=iof[:], in_=io[:])
                        npt = NSEQ * c.NPG
                        self.E("dve", "tensor_copy", [pti], [ptf], out=ptf[:, 0:npt], in_=pti[:])
                        self.E("dve", "tensor_scalar", [ptf, iof], [ptf], out=ptf[:, 0:npt], in0=ptf[:, 0:npt], scalar1=128.0, scalar2=iof[:, 0:1], op0=ALU.mult, op1=ALU.add)
                        self.E("dve", "tensor_copy", [ptf], [w.idx], out=w.idx[:], in_=ptf[:, 0:npt])
                        w.smt = self.sb(st, "smt", [1, 1], F32)
                        sm01 = self.sb(st, "sm01", [128, NSEQ, 64], BF16)
                        smneg = self.sb(st, "smneg", [128, NSEQ, 64], BF16)
                        o_, w_ = CO_SAMP["smask01"]
                        src = cts[:, o_:o_ + w_].rearrange("p (s q) -> p s q", s=NSEQ)
                        self.E("dve", "tensor_copy", [cts], [w.smt, sm01], out=sm01[:], in_=src)
                        self.E("dve", "tensor_scalar", [cts], [w.smt, smneg], out=smneg[:], in0=src, scalar1=-1.0, scalar2=BIG, op0=ALU.add, op1=ALU.mult)
                        w.sm01, w.smneg = sm01, smneg
                        w.attpad = [self.sb(st, "attpad%d" % k, [128, 8, 128], BF16) for k in range(2)]
                        w.kpf = [self.sb(st, "kpf%d" % k, [128, 512], F32) for k in range(4)]
                        w.vpf = [self.sb(st, "vpf%d" % k, [128, 512], F32) for k in range(4)]
                        w.kpb = [self.sb(st, "kpb%d" % k, [128, 512], BF16) for k in range(4)]
                        w.vpb = [self.sb(st, "vpb%d" % k, [128, 512], BF16) for k in range(4)]
                        w.ktp = [self.sb(st, "ktp%d" % k, [128, 4, 128], BF16) for k in range(4)]
                        self.attn_sample(w)
                    k = c.NOWN
                    if self.on("dn") and self.on("attn"):
                        self.dma(self.mixd[k * 128:(k + 1) * 128, :], w.mixed[:], reads=[w.mixed], writes=[self.t_mixd[k]])
                    if "mixed_s" in self.o:
                        self.E("dve", "tensor_copy", [w.mixed], [w.xt[0]], out=w.xt[0][:], in_=w.mixed[:])
                        self.dma(self.o["mixed_s"][:, :], w.xt[0][:], reads=[w.xt[0]])

    def phase2(self):
        c = self.cfg
        i, o = self.i, self.o
        with contextlib.ExitStack() as p2:
            sb = lambda name, shape, dt: self.sb(p2, name, shape, dt)
            woutb = sb("woutb", [128, DC, D], BF16)
            wupb = sb("wupb", [128, DC, 2 * DFF], BF16)
            wdnb = sb("wdnb", [128, FC, D], BF16)
            with contextlib.ExitStack() as st:
                stg = [self.sb(st, "stg%d" % k, [128, DC * 512], F32) for k in range(2)]
                n = 0
                wv = i["w_out"].rearrange("(c p) n -> p c n", p=128)
                for ct in range(2):
                    self.stream_cast(stg, wv[:, :, ct * 512:(ct + 1) * 512], woutb, woutb[:, :, ct * 512:(ct + 1) * 512], eng="act" if n % 2 else "dve")
                    n += 1
                wv = i["w_up"].rearrange("(c p) n -> p c n", p=128)
                for ct in range(11):
                    self.stream_cast(stg, wv[:, :, ct * 512:(ct + 1) * 512], wupb, wupb[:, :, ct * 512:(ct + 1) * 512], eng="act" if n % 2 else "dve")
                    n += 1
                wv = i["w_down"].rearrange("(c p) n -> p c n", p=128)
                for c0 in range(0, FC, 4):
                    c1 = min(FC, c0 + 4)
                    self.stream_cast(stg, wv[:, c0:c1, :], wdnb, wdnb[:, c0:c1, :], eng="act" if n % 2 else "dve")
                    n += 1
            self.fence()
            gt1, scale2, shift2, gt2 = [sb(nm, [128, D], F32) for nm in ("gt1", "scale2", "shift2", "gt2")]

            class WS:
                pass
            w = WS()
            w.xt = [sb("xt2", [128, D], F32)]
            w.h = sb("h2", [128, D], BF16)
            w.hT = sb("h2T", [128, DC, 128], BF16)
            w.ssq = sb("ssq2", [128, 1], F32)
            w.rstd = sb("rstd2", [128, 1], F32)
            mixb = sb("mixb", [128, D], BF16)
            mT = sb("mT", [128, DC, 128], BF16)
            yt = sb("yt", [128, D], F32)
            UE = sb("UE", [128, 4, NSEQ * (2 + TS)], F32)
            C4 = sb("C4", [128, 4, 128], F32)
            E2 = sb("E2", [128, 2, 128], F32)
            actT = sb("actT", [128, FC, 128], BF16)
            Cp = sb("Cp", [128, 4, 128], F32)
            fso_ap = Cp[:].rearrange("p a b -> p (a b)")
            FH = sb("FH", [128, 44, NSEQ * 2], F32)
            self.E("pool", "memset", [], [FH], FH[:], 0.0)
            blocks = [(b, False) for b in range(c.OWN0, c.NBLK)] + ([(0, True)] if self.on("sample") else [])
            cur_mod = None
            for (b, sample) in blocks:
                if cur_mod != sample:
                    self.load_mod([gt1, scale2, shift2, gt2], [2, 4, 3, 5], sample)
                    cur_mod = sample
                ns, T = (NSEQ, TS) if sample else (1, 128)
                k = c.NOWN if sample else b - c.OWN0
                halo = (not sample) and b == c.OWN0
                xt = w.xt[0]
                self.dma(xt[:], i["xs"][:, :] if sample else i["xp"][b * 128:(b + 1) * 128, :], writes=[xt])
                self.dma(mixb[:], self.mixd[k * 128:(k + 1) * 128, :], reads=[self.t_mixd[k]], writes=[mixb])
                tb = self.TB()
                for dc in range(DC):
                    self.tr(tb[:, dc * 128:(dc + 1) * 128], mixb[:, dc * 128:(dc + 1) * 128], self.identb, [mixb, self.cbf], [tb])
                self.act(mT[:], tb[:, :].rearrange("p (a b) -> p a b", a=DC), AF.Copy, [tb], [mT])
                for n in range(2):
                    g = self.G()
                    for dc in range(DC):
                        self.mm(g[:, :], mT[:, dc, :], woutb[:, dc, n * 512:(n + 1) * 512], dc == 0, dc == DC - 1, [mT, woutb], [g])
                    self.E("dve", "tensor_tensor", [g, gt1], [yt], out=yt[:, n * 512:(n + 1) * 512], in0=g[:, :], in1=gt1[:, n * 512:(n + 1) * 512], op=ALU.mult)
                self.E("pool", "tensor_tensor", [yt, xt], [xt], out=xt[:], in0=yt[:], in1=xt[:], op=ALU.add)
                x1 = xt
                if "x1_p" in o and (not sample) and b >= c.OUT0:
                    r0 = (b - c.OUT0) * 128
                    self.dma(o["x1_p"][r0:r0 + 128, :], x1[:], reads=[x1])
                self.norm_mod(w, x1, scale2, shift2, w.hT, tmp=yt)
                if sample:
                    for j in range(11):
                        hst = fso_ap[0:NSEQ * 2, :]
                        self.dma(hst, i["ffc0"][:, j * 512:(j + 1) * 512], writes=[Cp])
                        g = self.G()
                        for ch in range(4):
                            self.tr(g[:, ch * 32:(ch + 1) * 32], hst[:, ch * 128:(ch + 1) * 128], self.cm("ident")[0:32, 0:32], [Cp, self.ctm], [g])
                        self.act(FH[:, j * 4:(j + 1) * 4, :], g[:, 0:128].rearrange("p (c t) -> p c t", c=4), AF.Copy, [g], [FH])
                ue4 = UE[:, :, 0:ns * (2 + T)].rearrange("p c (s t) -> p c s t", s=ns)
                fh4 = FH[:, :, 0:ns * 2].rearrange("p c (s t) -> p c s t", s=ns)
                for gi_ in range(11):
                    chs = [2 * gi_, 2 * gi_ + 1, FC + 2 * gi_, FC + 2 * gi_ + 1]
                    g = self.G()
                    for q_, ch in enumerate(chs):
                        for dc in range(DC):
                            self.mm(g[:, q_ * 128:(q_ + 1) * 128], wupb[:, dc, ch * 128:(ch + 1) * 128], w.hT[:, dc, :], dc == 0, dc == DC - 1, [w.hT, wupb], [g])
                    for half in range(2):
                        self.E("pool", "tensor_copy", [FH], [UE], out=ue4[:, half * 2:half * 2 + 2, :, 0:2], in_=fh4[:, chs[half * 2]:chs[half * 2] + 2, :, :])
                    src4 = g[:, :].rearrange("p (c s t) -> p c s t", c=4, s=ns)
                    if halo:
                        self.act(ue4[:, :, :, 2:2 + T], src4, AF.Copy, [g, self.bvd], [UE], scale=self.bvd[:, b:b + 1])
                    else:
                        self.act(ue4[:, :, :, 2:2 + T], src4, AF.Copy, [g], [UE])
                    for half in range(2):
                        self.E("pool", "tensor_copy", [UE], [FH], out=fh4[:, chs[half * 2]:chs[half * 2] + 2, :, :], in_=ue4[:, half * 2:half * 2 + 2, :, T:T + 2])
                    if halo:
                        continue
                    if not sample:
                        wv_ = lambda kk: self.wfc[:, kk, :].rearrange("p (a c) -> p a c", a=2)[:, :, 2 * gi_:2 * gi_ + 2].unsqueeze(3).to_broadcast([128, 2, 2, T])
                        xv_ = lambda kk: UE[:, :, kk:kk + T].rearrange("p (a c) t -> p a c t", a=2)
                        c4v = C4[:].rearrange("p (a c) t -> p a c t", a=2)
                        cpv = Cp[:].rearrange("p (a c) t -> p a c t", a=2)
                        self.E("pool", "tensor_tensor", [UE, self.wfc], [Cp], out=cpv, in0=xv_(1), in1=wv_(1), op=ALU.mult)
                        self.E("dve", "tensor_tensor", [UE, self.wfc], [C4], out=c4v, in0=xv_(0), in1=wv_(0), op=ALU.mult)
                        self.E("dve", "tensor_tensor", [C4, Cp], [C4], out=C4[:], in0=C4[:], in1=Cp[:], op=ALU.add)
                        self.E("pool", "tensor_tensor", [UE, self.wfc], [Cp], out=cpv, in0=xv_(2), in1=wv_(2), op=ALU.mult)
                        self.E("dve", "tensor_tensor", [C4, Cp], [C4], out=C4[:], in0=C4[:], in1=Cp[:], op=ALU.add)
                    for q_, ch in enumerate(chs if sample else []):
                        eng = "dve"
                        yv = C4[:, q_, :].rearrange("p (s t) -> p s t", s=ns)
                        self.E(eng, "tensor_scalar", [UE, self.wfc], [C4], out=yv, in0=ue4[:, q_, :, 0:T], scalar1=self.wfc[:, 0, ch:ch + 1], scalar2=None, op0=ALU.mult)
                        for kk in range(1, 3):
                            self.E(eng, "scalar_tensor_tensor", [UE, self.wfc, C4], [C4], out=yv, in0=ue4[:, q_, :, kk:kk + T],
                                   scalar=self.wfc[:, kk, ch:ch + 1], in1=yv, op0=ALU.mult, op1=ALU.add)
                    self.act(E2[:], C4[:, 2:4, :], AF.Exp, [C4], [E2], scale=-1.0)
                    self.act(E2[:], E2[:], AF.Ln, [E2], [E2], bias=1.0)
                    self.act(E2[:], E2[:], AF.Exp, [E2], [E2], scale=-1.0)
                    self.E("pool", "tensor_tensor", [E2, C4], [E2], out=E2[:], in0=E2[:], in1=C4[:, 2:4, :], op=ALU.mult)
                    self.E("dve", "tensor_tensor", [E2, C4], [actT], out=actT[:, 2 * gi_:2 * gi_ + 2, :], in0=E2[:], in1=C4[:, 0:2, :], op=ALU.mult)
                last_p = (not sample) and b == c.NBLK - 1
                if sample or last_p:
                    ncol = ns * 2
                    dst = o["fcs"] if sample else o["fcp"]
                    for j in range(11):
                        g = self.G()
                        for ch in range(4):
                            self.tr(g[0:ncol, ch * 128:(ch + 1) * 128], FH[:, j * 4 + ch, 0:ncol], self.cm("ident"), [FH, self.ctm], [g])
                        self.E("dve", "tensor_copy", [g], [Cp], out=fso_ap[0:ncol, :], in_=g[0:ncol, :])
                        self.dma(dst[:, j * 512:(j + 1) * 512], fso_ap[0:ncol, :], reads=[Cp])
                if halo:
                    continue
                for n in range(2):
                    g = self.G()
                    for fc_ in range(FC):
                        self.mm(g[:, :], actT[:, fc_, :], wdnb[:, fc_, n * 512:(n + 1) * 512], fc_ == 0, fc_ == FC - 1, [actT, wdnb], [g])
                    self.E("dve", "tensor_tensor", [g, gt2], [yt], out=yt[:, n * 512:(n + 1) * 512], in0=g[:, :], in1=gt2[:, n * 512:(n + 1) * 512], op=ALU.mult)
                self.E("pool", "tensor_tensor", [yt, x1], [yt], out=yt[:], in0=yt[:], in1=x1[:], op=ALU.add)
                if sample:
                    self.dma(o["ys"][:, :], yt[:], reads=[yt])
                elif b >= c.OUT0:
                    r0 = (b - c.OUT0) * 128
                    self.dma(o["yp"][r0:r0 + 128, :], yt[:], reads=[yt])


def core_inputs(cfg, core, inp):
    b, half = core // 2, core % 2
    S = cfg.NBLK * 128
    xp_full = np.asarray(inp["x_prompt"][b], np.float32)
    if half == 1:
        xp = xp_full
    else:
        xp = np.concatenate([np.zeros((S // 2, D), np.float32), xp_full[:S // 2]], axis=0)
    s0, s1 = core * NSEQ, (core + 1) * NSEQ
    m = {}
    m["xp"] = np.ascontiguousarray(xp)
    m["xs"] = np.ascontiguousarray(np.asarray(inp["x_sample"][s0:s1], np.float32).reshape(NSEQ * TS, D))
    m["cvec"] = np.ascontiguousarray(np.concatenate([np.asarray(inp["c_prompt"][b:b + 1], np.float32),
                                                     np.asarray(inp["c_sample"][s0:s1], np.float32)], axis=0))
    m["cache_k"] = np.asarray(inp["cache_k"], np.float32).reshape(cfg.NPHYS * 128, SBW)
    m["cache_v"] = np.asarray(inp["cache_v"], np.float32).reshape(cfg.NPHYS * 128, SBW)
    m["ptab"] = np.ascontiguousarray(np.asarray(inp["page_table"][s0:s1], np.int32).reshape(-1))
    m["s0"] = np.ascontiguousarray(np.asarray(inp["state_delta"][0, s0:s1], np.float32))
    m["dnc0"] = np.ascontiguousarray(np.asarray(inp["state_dn_conv"][0, s0:s1], np.float32).reshape(NSEQ * 3, 3 * DNW))
    m["ffc0"] = np.ascontiguousarray(np.asarray(inp["state_ffn_conv"][0, s0:s1], np.float32).reshape(NSEQ * 2, 2 * DFF))
    for k, nm in (("w_ada", "w_ada"), ("b_ada", "b_ada"), ("g_attn", "g_attn_norm"), ("w_in", "w_in"), ("g_q", "g_q"),
                  ("g_k", "g_k"), ("sb_bias", "sb_bias"), ("g_sb_out", "g_sb_out"), ("w_dn_conv", "w_dn_conv"),
                  ("a_log", "a_log"), ("dt_bias", "dt_bias"), ("g_dn_out", "g_dn_out"), ("w_out", "w_out"),
                  ("g_ffn", "g_ffn_norm"), ("w_up", "w_up"), ("w_ffn_conv", "w_ffn_conv"), ("w_down", "w_down")):
        m[k] = np.ascontiguousarray(np.asarray(inp[nm], np.float32)[0])
    m["ct_main"] = CT_MAIN
    m["ct_samp"] = CT_SAMP
    kv = np.zeros((128, 256), np.float32)
    if half == 1:
        kv[0:2, 0:128] = 1.0
    else:
        kv[2, 0:128] = 1.0
    kv[0:2, 128:256] = 1.0
    m["kvlo"] = kv
    bv = np.ones((128, cfg.NBLK), np.float32)
    if half == 0:
        bv[:, :cfg.NBLK // 2] = 0.0
    m["blkvalid"] = bv
    return m


def assemble(cfg, res, nb, nsamp):
    S = cfg.NBLK * 128
    H = S // 2
    yp = np.zeros((nb, S, D), np.float32)
    ys = np.zeros((nsamp, TS, D), np.float32)
    kp = np.zeros((1, nb, S, HS, HD), np.float32)
    vp = np.zeros((1, nb, S, HS, HD), np.float32)
    ks = np.zeros((1, nsamp, TS, HS, HD), np.float32)
    vs = np.zeros((1, nsamp, TS, HS, HD), np.float32)
    sp = np.zeros((1, nb, DNH, DND, DND), np.float32)
    ss = np.zeros((1, nsamp, DNH, DND, DND), np.float32)
    dcp = np.zeros((1, nb, 3, 3 * DNW), np.float32)
    dcs = np.zeros((1, nsamp, 3, 3 * DNW), np.float32)
    fcp = np.zeros((1, nb, 2, 2 * DFF), np.float32)
    fcs = np.zeros((1, nsamp, 2, 2 * DFF), np.float32)
    for core, r in res.items():
        b, half = core // 2, core % 2
        s0, s1 = core * NSEQ, (core + 1) * NSEQ
        yp[b, half * H:(half + 1) * H] = r["yp"]
        kp[0, b, half * H:(half + 1) * H] = r["kp"].reshape(H, HS, HD)
        vp[0, b, half * H:(half + 1) * H] = r["vp"].reshape(H, HS, HD)
        ys[s0:s1] = r["ys"].reshape(NSEQ, TS, D)
        ks[0, s0:s1] = r["ksm"].reshape(NSEQ, TS, HS, HD)
        vs[0, s0:s1] = r["vsm"].reshape(NSEQ, TS, HS, HD)
        ss[0, s0:s1] = r["ss_state"]
        dcs[0, s0:s1] = r["dcs"].reshape(NSEQ, 3, 3 * DNW)
        fcs[0, s0:s1] = r["fcs"].reshape(NSEQ, 2, 2 * DFF)
        if half == 1:
            sp[0, b] = r["sp_state"]
            dcp[0, b] = r["dcp"]
            fcp[0, b] = r["fcp"]
    return (yp, ys, kp, vp, ks, vs, sp, ss, dcp, dcs, fcp, fcs)


_NC_CACHE = {}


def kernel(**inputs):
    cfg = Cfg(nblk=inputs["x_prompt"].shape[1] // 128, npg=inputs["page_table"].shape[1], nphys=inputs["cache_k"].shape[1])
    key = (cfg.NBLK, cfg.NPG, cfg.NPHYS)
    if key not in _NC_CACHE:
        _NC_CACHE[key] = Builder(cfg).build()
    nc = _NC_CACHE[key]
    ncores = 8
    in_maps = [core_inputs(cfg, c, inputs) for c in range(ncores)]
    res = run_bass_kernel_spmd(nc, in_maps, core_ids=list(range(ncores)))
    out = assemble(cfg, {c: res.results[c] for c in range(ncores)}, inputs["x_prompt"].shape[0], inputs["x_sample"].shape[0])
    return out
```

```python
import contextlib
import numpy as np
import concourse.bass as bass
import concourse.mybir as mybir
from concourse.bass_utils import run_bass_kernel_spmd

F32 = mybir.dt.float32
BF16 = mybir.dt.bfloat16
I32 = mybir.dt.int32
AF = mybir.ActivationFunctionType
ALU = mybir.AluOpType
AX = mybir.AxisListType

D = 1024
DC = 8
HS = 8
HD = 64
SBW = 512
DNH = 4
DND = 128
DNW = 512
DFF = 2816
FC = 22
INC = 3592
EPS = 1e-6
BIG = 30000.0
NSEQ = 16
TS = 8


class Cfg:
    def __init__(self, nblk=32, npg=16, nphys=2560):
        self.NBLK = nblk
        self.OWN0 = nblk // 2 - 1
        self.OUT0 = nblk // 2
        self.NPG = npg
        self.NPHYS = nphys
        self.NOUT = nblk - self.OUT0
        self.NOWN = nblk - self.OWN0


class Tile:
    __slots__ = ("ap", "name", "last_w", "readers", "excl")

    def __init__(self, ap, name="", excl=False):
        self.ap = ap
        self.name = name
        self.last_w = None
        self.readers = []
        self.excl = excl

    def __getitem__(self, k):
        return self.ap[k]


class Op:
    __slots__ = ("eng", "fn", "deps", "need_inc", "count", "sem", "is_dma", "idx")

    def __init__(self, eng, fn, is_dma=False):
        self.eng = eng
        self.fn = fn
        self.deps = set()
        self.need_inc = is_dma
        self.count = 0
        self.sem = None
        self.is_dma = is_dma


COMPUTE = ("pe", "act", "dve", "pool")


class Sched:
    def __init__(self, nc, n_dma_sems=16):
        self.nc = nc
        self.ops = {e: [] for e in COMPUTE + ("sp",)}
        self.n_dma_sems = n_dma_sems
        self.nops = 0
        self.junk_fn = None
        self.junk_n = 0
        self.n_pe_waits = 0

    def _track(self, op, reads, writes):
        ex = [t for t in reads if t.excl]
        if ex:
            reads = [t for t in reads if not t.excl]
            writes = list(writes) + [t for t in ex if t not in writes]
        for t in reads:
            if t.last_w is not None:
                op.deps.add(t.last_w)
        for t in writes:
            if t.last_w is not None:
                op.deps.add(t.last_w)
            for r in t.readers:
                op.deps.add(r)
        for t in reads:
            t.readers.append(op)
        for t in writes:
            t.last_w = op
            t.readers = []
        op.deps.discard(op)

    def add(self, eng, fn, reads=(), writes=(), is_dma=False):
        op = Op(eng, fn, is_dma)
        self._track(op, reads, writes)
        self.ops[eng].append(op)
        self.nops += 1
        return op

    def dma(self, out_ap, in_ap, reads=(), writes=(), queue="sp", **kw):
        def fn(e, out_ap=out_ap, in_ap=in_ap, kw=kw):
            return e.dma_start(out=out_ap, in_=in_ap, **kw)
        return self.add(queue, fn, reads, writes, is_dma=True)

    def emit(self):
        nc = self.nc

        def skip(d, op):
            return d.eng == "pe" and op.eng == "pe" and not d.is_dma and not op.is_dma
        for e in self.ops:
            for op in self.ops[e]:
                for d in op.deps:
                    if not skip(d, op):
                        d.need_inc = True
        with contextlib.ExitStack() as st:
            sems = {e: st.enter_context(nc.semaphore("s_" + e)) for e in COMPUTE}
            dma_sems = {}
            for q in self.ops:
                if any(o.is_dma for o in self.ops[q]):
                    dma_sems[q] = [st.enter_context(nc.semaphore("d_%s_%d" % (q, i)))
                                   for i in range(self.n_dma_sems)]
            for e in self.ops:
                c = 0
                j = 0
                for op in self.ops[e]:
                    if op.is_dma:
                        ring = dma_sems[e]
                        op.sem = ring[j % len(ring)]
                        op.count = 16 * (j // len(ring) + 1)
                        j += 1
                    elif op.need_inc:
                        c += 1
                        op.count = c
                        op.sem = sems[e]
            block = st.enter_context(nc.Block())
            handles = {"pe": block.tensor, "act": block.scalar, "dve": block.vector,
                       "pool": block.gpsimd, "sp": block.sync}

            def make(e):
                oplist = self.ops[e]

                def body(eng):
                    known = {}
                    nwait = [0]

                    def wait(sem, val):
                        if known.get(id(sem), 0) >= val:
                            return
                        eng.wait_ge(sem, val)
                        known[id(sem)] = val
                    for op in oplist:
                        need = {}
                        for d in op.deps:
                            if skip(d, op):
                                continue
                            k = id(d.sem)
                            if k not in need or need[k][1] < d.count:
                                need[k] = (d.sem, d.count)
                        if op.is_dma and op.count > 16:
                            k = id(op.sem)
                            v = op.count - 16
                            if k not in need or need[k][1] < v:
                                need[k] = (op.sem, v)
                        pend = [(sem, val) for sem, val in need.values() if known.get(id(sem), 0) < val]
                        if pend and e == "pe" and self.junk_fn is not None and op.fn is not None:
                            nwait[0] += 1
                            for _ in range(self.junk_n):
                                self.junk_fn(eng)
                        for sem, val in pend:
                            wait(sem, val)
                        if op.fn is None:
                            continue
                        ins = op.fn(eng)
                        if op.is_dma:
                            ins.then_inc(op.sem, 16)
                        elif op.need_inc:
                            ins.then_inc(op.sem, 1)
                    last = {}
                    for op in oplist:
                        if op.is_dma:
                            last[id(op.sem)] = (op.sem, op.count)
                    for sem, val in last.values():
                        wait(sem, val)
                    if e == "pe":
                        self.n_pe_waits = nwait[0]
                return body
            for e in self.ops:
                if self.ops[e]:
                    handles[e](make(e))


def host_consts():
    i = np.arange(128)
    c = {}
    c["ident"] = np.eye(128, dtype=np.float32)
    c["ones"] = np.ones((128, 128), np.float32)
    for nm, nseq in (("p", 1), ("s", NSEQ)):
        t = 128 // nseq
        seq = i // t
        same = seq[:, None] == seq[None, :]
        incl = same & (i[None, :] <= i[:, None])
        strict = same & (i[None, :] < i[:, None])
        c["ltincl_" + nm] = incl.T.astype(np.float32)
        c["seqm_" + nm] = same.astype(np.float32)
        c["nmincl_" + nm] = np.where(incl, 0.0, BIG).astype(np.float32)
        c["nminclT_" + nm] = np.where(incl.T, 0.0, -BIG).astype(np.float32)
        c["ms01_" + nm] = strict.astype(np.float32)
    caus = i[:, None] < i[None, :]
    c["caus01"] = caus.astype(np.float32)
    c["causneg"] = np.where(caus, 0.0, -BIG).astype(np.float32)
    kt = i[:, None, None]
    ss_ = np.arange(NSEQ)[None, :, None]
    qq = (np.arange(64) % TS)[None, None, :]
    c["smask01"] = ((kt // TS == ss_) & (kt % TS < qq)).astype(np.float32).reshape(128, NSEQ * 64)
    c["ntri"] = np.where(i[:, None] >= i[None, :], -1.0, 0.0).astype(np.float32)
    cm = (np.arange(NSEQ)[:, None] == (i // TS)[None, :]).astype(np.float32)
    c["colmask"] = np.broadcast_to(cm.reshape(1, NSEQ * 128), (128, NSEQ * 128)).copy()
    c["rowmask"] = np.zeros((128, 128), np.float32)
    c["rowmask"][:, :NSEQ] = cm.T
    main = ["ident", "ones", "ltincl_p", "ms01_p", "caus01", "causneg", "ntri"]
    samp = ["ltincl_s", "seqm_s", "ms01_s", "rowmask", "colmask", "smask01"]

    def pack(names):
        off = {}
        o = 0
        for k in names:
            off[k] = (o, c[k].shape[1])
            o += c[k].shape[1]
        return np.concatenate([c[k] for k in names], axis=1).astype(np.float32), off
    return pack(main) + pack(samp)


CT_MAIN, CO_MAIN, CT_SAMP, CO_SAMP = host_consts()


class Builder:
    def __init__(self, cfg, stages=("all",), dbg=()):
        self.cfg = cfg
        self.stages = stages
        self.dbg = dbg
        self.nc = bass.Bass("TRN2", target_bir_lowering=False)
        self.s = Sched(self.nc)
        self.fence_id = 0
        self._uid = 0
        import os
        self.use_r32 = os.environ.get("USE_R32", "0") == "1"

    def on(self, st):
        return "all" in self.stages or st in self.stages

    def sb(self, stack, name, shape, dt):
        self._uid += 1
        h = stack.enter_context(self.nc.sbuf_tensor("%s_%d" % (name, self._uid), list(shape), dt))
        return Tile(h, name)

    def view(self, ap, name=""):
        return Tile(ap, name)

    def din(self, name, shape, dt=F32):
        return self.nc.dram_tensor(name, list(shape), dt, kind="ExternalInput").ap()

    def dout(self, name, shape, dt=F32):
        return self.nc.dram_tensor(name, list(shape), dt, kind="ExternalOutput").ap()

    def dscr(self, name, shape, dt=F32):
        return self.nc.dram_tensor(name, list(shape), dt, kind="Internal").ap()

    def E(self, eng, meth, reads, writes, *a, **kw):
        return self.s.add(eng, lambda e: getattr(e, meth)(*a, **kw), reads, writes)

    def mm(self, out, lhsT, rhs, start, stop, reads, writes, skip=False, r32=False):
        if r32 and self.use_r32:
            lhsT = lhsT.bitcast(mybir.dt.float32r)
            rhs = rhs.bitcast(mybir.dt.float32r)
        return self.s.add("pe", lambda e: e.matmul(out, lhsT=lhsT, rhs=rhs, start=start, stop=stop, skip_group_check=skip),
                          reads, writes)

    def tr(self, out, in_, ident, reads, writes):
        return self.s.add("pe", lambda e: e.transpose(out=out, in_=in_, identity=ident), reads, writes)

    def act(self, out, in_, func, reads, writes, **kw):
        return self.s.add("act", lambda e: e.activation(out=out, in_=in_, func=func, **kw), reads, writes)

    def dma(self, out, in_, reads=(), writes=(), **kw):
        return self.s.dma(out, in_, reads, writes, **kw)

    def fence(self):
        s = self.s
        f = set()
        for e in COMPUTE:
            real = [o for o in s.ops[e] if o.fn is not None and not o.is_dma]
            if real:
                f.add(real[-1])
        for q in s.ops:
            d = [o for o in s.ops[q] if o.is_dma]
            for o in d[-s.n_dma_sems:]:
                f.add(o)
        for e in COMPUTE + ("sp",):
            op = Op(e, None)
            op.deps = set(f)
            s.ops[e].append(op)

    def G(self):
        t = self.pg[self.gi % len(self.pg)]
        self.gi += 1
        return t

    def TB(self):
        t = self.ptb[self.ti % len(self.ptb)]
        self.ti += 1
        return t

    def declare(self):
        c = self.cfg
        NT = c.NBLK * 128
        i = {}
        i["xp"] = self.din("xp", [NT, D])
        i["xs"] = self.din("xs", [128, D])
        i["cvec"] = self.din("cvec", [17, D])
        i["cache_k"] = self.din("cache_k", [c.NPHYS * 128, SBW])
        i["cache_v"] = self.din("cache_v", [c.NPHYS * 128, SBW])
        i["ptab"] = self.din("ptab", [NSEQ * c.NPG], I32)
        i["s0"] = self.din("s0", [NSEQ, DNH, DND, DND])
        i["dnc0"] = self.din("dnc0", [NSEQ * 3, 3 * DNW])
        i["ffc0"] = self.din("ffc0", [NSEQ * 2, 2 * DFF])
        i["w_ada"] = self.din("w_ada", [D, 6 * D])
        i["b_ada"] = self.din("b_ada", [6 * D])
        i["g_attn"] = self.din("g_attn", [D])
        i["w_in"] = self.din("w_in", [D, INC])
        i["g_q"] = self.din("g_q", [HD])
        i["g_k"] = self.din("g_k", [HD])
        i["sb_bias"] = self.din("sb_bias", [HS])
        i["g_sb_out"] = self.din("g_sb_out", [HD])
        i["w_dn_conv"] = self.din("w_dn_conv", [4, 3 * DNW])
        i["a_log"] = self.din("a_log", [DNH])
        i["dt_bias"] = self.din("dt_bias", [DNH])
        i["g_dn_out"] = self.din("g_dn_out", [DND])
        i["w_out"] = self.din("w_out", [D, D])
        i["g_ffn"] = self.din("g_ffn", [D])
        i["w_up"] = self.din("w_up", [D, 2 * DFF])
        i["w_ffn_conv"] = self.din("w_ffn_conv", [3, 2 * DFF])
        i["w_down"] = self.din("w_down", [DFF, D])
        i["ct_main"] = self.din("ct_main", list(CT_MAIN.shape))
        i["ct_samp"] = self.din("ct_samp", list(CT_SAMP.shape))
        i["kvlo"] = self.din("kvlo", [128, 256])
        i["blkvalid"] = self.din("blkvalid", [128, c.NBLK])
        self.i = i
        o = {}
        o["yp"] = self.dout("yp", [c.NOUT * 128, D])
        o["ys"] = self.dout("ys", [128, D])
        o["kp"] = self.dout("kp", [c.NOUT * 128, SBW])
        o["vp"] = self.dout("vp", [c.NOUT * 128, SBW])
        o["ksm"] = self.dout("ksm", [128, SBW])
        o["vsm"] = self.dout("vsm", [128, SBW])
        o["sp_state"] = self.dout("sp_state", [DNH, DND, DND])
        o["ss_state"] = self.dout("ss_state", [NSEQ, DNH, DND, DND])
        o["dcp"] = self.dout("dcp", [3, 3 * DNW])
        o["dcs"] = self.dout("dcs", [NSEQ * 3, 3 * DNW])
        o["fcp"] = self.dout("fcp", [2, 2 * DFF])
        o["fcs"] = self.dout("fcs", [NSEQ * 2, 2 * DFF])
        for name, shape in self.dbg:
            o[name] = self.dout(name, shape)
        self.o = o
        self.modd = self.dscr("modd", [17, 6 * D])
        self.mixd = self.dscr("mixd", [(c.NOWN + 1) * 128, D], BF16)
        self.t_modd = Tile(None, "modd")
        self.t_mixd = [Tile(None, "mixd%d" % k) for k in range(c.NOWN + 1)]

    def stream_cast(self, stack_tiles, src_view, dst_tile, dst_ap, eng="dve"):
        stg = stack_tiles[self.sci % len(stack_tiles)]
        self.sci += 1
        shp = src_view.shape
        sap = stg[:, 0:shp[1] * shp[2]].rearrange("p (a b) -> p a b", a=shp[1])
        self.dma(sap, src_view, writes=[stg])
        if eng == "act":
            self.act(dst_ap, sap, AF.Copy, [stg], [dst_tile])
        else:
            self.E(eng, "tensor_copy", [stg], [dst_tile], out=dst_ap, in_=sap)

    def rsqrt_ops(self, ss, rs, n, scale, reads_extra=()):
        self.act(rs[:, 0:n], ss[:, 0:n], AF.Ln, [ss] + list(reads_extra), [rs], scale=scale, bias=self.epsb[:, 0:1])
        self.act(rs[:, 0:n], rs[:, 0:n], AF.Exp, [rs], [rs], scale=-0.5)

    def build(self):
        nc = self.nc
        c = self.cfg
        self.declare()
        i, o = self.i, self.o
        self.gi = 0
        self.ti = 0
        self.sci = 0
        with contextlib.ExitStack() as top:
            self.pg = [Tile(top.enter_context(nc.psum_tensor("pg%d" % k, [128, 512], F32)), "pg%d" % k, True) for k in range(2)]
            self.zb = [Tile(top.enter_context(nc.psum_tensor("zb%d" % k, [128, 512], F32)), "zb%d" % k, True) for k in range(2)]
            self.po = [Tile(top.enter_context(nc.psum_tensor("po%d" % k, [128, 512], F32)), "po%d" % k, True) for k in range(2)]
            self.ptb = [Tile(top.enter_context(nc.psum_tensor("ptb%d" % k, [128, 1024], BF16)), "ptb%d" % k, True) for k in range(2)]
            ctm = self.sb(top, "ctm", list(CT_MAIN.shape), F32)
            self.ctm = ctm
            self.dma(ctm[:], i["ct_main"][:, :], writes=[ctm])

            def cm(name):
                o_, w_ = CO_MAIN[name]
                return ctm[:, o_:o_ + w_]
            self.cm = cm
            cbf = self.sb(top, "cbf", [128, 4 * 128], BF16)
            self.cbf = cbf
            self.E("dve", "tensor_copy", [ctm], [cbf], out=cbf[:, 0:128], in_=cm("ident"))
            self.E("dve", "tensor_copy", [ctm], [cbf], out=cbf[:, 128:256], in_=cm("ntri"))
            self.E("dve", "tensor_copy", [ctm], [cbf], out=cbf[:, 256:384], in_=cm("causneg"))
            self.E("dve", "tensor_scalar", [ctm], [cbf], out=cbf[:, 384:512], in0=cm("ones"), scalar1=-1.0, scalar2=None, op0=ALU.mult)
            self.identb = cbf[:, 0:128]
            self.ntrib = cbf[:, 128:256]
            self.causnegb = cbf[:, 256:384]
            self.negonesb = cbf[:, 384:512]
            epsb = self.sb(top, "epsb", [128, 1], F32)
            self.epsb = epsb
            self.E("pool", "memset", [], [epsb], epsb[:], EPS)
            sv = self.sb(top, "sv", [128, 64 * 3 + 128 + 4 + 4 + 8], F32)
            self.sv = sv
            self.dma(sv[:, 0:64], i["g_q"].partition_broadcast(128), writes=[sv])
            self.dma(sv[:, 64:128], i["g_k"].partition_broadcast(128), writes=[sv])
            self.dma(sv[:, 128:192], i["g_sb_out"].partition_broadcast(128), writes=[sv])
            self.dma(sv[:, 192:320], i["g_dn_out"].partition_broadcast(128), writes=[sv])
            self.dma(sv[:, 320:324], i["a_log"].partition_broadcast(128), writes=[sv])
            self.dma(sv[:, 324:328], i["dt_bias"].partition_broadcast(128), writes=[sv])
            self.dma(sv[:, 328:336], i["sb_bias"].partition_broadcast(128), writes=[sv])
            self.E("dve", "tensor_scalar", [sv], [sv], out=sv[:, 0:64], in0=sv[:, 0:64], scalar1=HD ** -0.5, scalar2=None, op0=ALU.mult)
            self.act(sv[:, 320:324], sv[:, 320:324], AF.Exp, [sv], [sv])
            self.E("dve", "tensor_scalar", [sv], [sv], out=sv[:, 320:324], in0=sv[:, 320:324], scalar1=-1.0, scalar2=None, op0=ALU.mult)
            self.gq8, self.gk, self.gso, self.gdn = sv[:, 0:64], sv[:, 64:128], sv[:, 128:192], sv[:, 192:320]
            self.negA, self.dtb = sv[:, 320:324], sv[:, 324:328]
            kvf = self.sb(top, "kvf", [128, 256], F32)
            self.dma(kvf[:], i["kvlo"][:, :], writes=[kvf])
            kvd = self.sb(top, "kvd", [128, 128], BF16)
            self.kvd = kvd
            self.E("dve", "tensor_copy", [kvf], [kvd], out=kvd[:], in_=kvf[:, 0:128])
            bvd = self.sb(top, "bvd", [128, c.NBLK], F32)
            self.bvd = bvd
            self.dma(bvd[:], i["blkvalid"][:, :], writes=[bvd])
            kvone = self.sb(top, "kvone", [128, 128], BF16)
            self.kvone = kvone
            self.E("dve", "tensor_copy", [kvf], [kvone], out=kvone[:], in_=kvf[:, 128:256])
            wdc = self.sb(top, "wdc", [128, 4, 12], F32)
            self.wdc = wdc
            for t_ in range(4):
                self.dma(wdc[:, t_, :], i["w_dn_conv"][t_].rearrange("(c p) -> p c", p=128), writes=[wdc], allow_slow_non_contiguous=True)
            wfc = self.sb(top, "wfc", [128, 3, 44], F32)
            self.wfc = wfc
            for t_ in range(3):
                self.dma(wfc[:, t_, :], i["w_ffn_conv"][t_].rearrange("(c p) -> p c", p=128), writes=[wfc], allow_slow_non_contiguous=True)

            if self.on("setup"):
                self.setup_mod()
            self.fence()
            if self.on("p1"):
                self.phase1()
            self.fence()
            if self.on("p2"):
                self.phase2()
            self.s.emit()
        return nc

    def setup_mod(self):
        i = self.i
        with contextlib.ExitStack() as st:
            cv = self.sb(st, "cv", [17, D], F32)
            ex = self.sb(st, "ex", [17, D], F32)
            scb = self.sb(st, "scb", [17, D], BF16)
            scT = self.sb(st, "scT", [128, DC, 17], BF16)
            stg = [self.sb(st, "stg%d" % k, [128, DC * 512], F32) for k in range(2)]
            wab = [self.sb(st, "wab%d" % k, [128, DC, 512], BF16) for k in range(2)]
            bada = self.sb(st, "bada", [17, 512], F32)
            gv = self.sb(st, "gv", [17, 2 * D], F32)
            mt = [self.sb(st, "mt%d" % k, [17, 512], F32) for k in range(2)]
            self.dma(cv[:], i["cvec"][:, :], writes=[cv])
            self.dma(gv[:, 0:D], i["g_attn"].partition_broadcast(17), writes=[gv])
            self.dma(gv[:, D:2 * D], i["g_ffn"].partition_broadcast(17), writes=[gv])
            self.act(ex[:], cv[:], AF.Exp, [cv], [ex], scale=-1.0)
            self.E("dve", "tensor_scalar", [ex], [ex], out=ex[:], in0=ex[:], scalar1=1.0, scalar2=None, op0=ALU.add)
            self.E("dve", "reciprocal", [ex], [ex], out=ex[:], in_=ex[:])
            self.E("dve", "tensor_tensor", [ex, cv], [scb], out=scb[:], in0=cv[:], in1=ex[:], op=ALU.mult)
            tb = self.TB()
            for dc in range(DC):
                self.tr(tb[:, dc * 32:dc * 32 + 17], scb[0:17, dc * 128:(dc + 1) * 128], self.identb[0:17, 0:17], [scb, self.cbf], [tb])
            self.E("dve", "tensor_copy", [tb], [scT], out=scT[:], in_=tb[:, 0:DC * 32].rearrange("p (a b) -> p a b", a=DC)[:, :, 0:17])
            wv = i["w_ada"].rearrange("(c p) n -> p c n", p=128)
            for ct in range(12):
                wb = wab[ct % 2]
                self.stream_cast(stg, wv[:, :, ct * 512:(ct + 1) * 512], wb, wb[:], eng="act" if ct % 2 else "dve")
                self.dma(bada[:], i["b_ada"][ct * 512:(ct + 1) * 512].partition_broadcast(17), writes=[bada])
                g = self.G()
                for dc in range(DC):
                    self.mm(g[0:17, :], scT[:, dc, :], wb[:, dc, :], dc == 0, dc == DC - 1, [scT, wb], [g])
                m = mt[ct % 2]
                self.E("dve", "tensor_tensor", [g, bada], [m], out=m[:], in0=g[0:17, :], in1=bada[:], op=ALU.add)
                if ct in (2, 3, 8, 9):
                    go = (ct - 2) * 512 if ct < 4 else D + (ct - 8) * 512
                    self.E("dve", "scalar_tensor_tensor", [m, gv], [m], out=m[:], in0=m[:], scalar=1.0, in1=gv[:, go:go + 512],
                           op0=ALU.add, op1=ALU.mult)
                self.dma(self.modd[:, ct * 512:(ct + 1) * 512], m[:], reads=[m], writes=[self.t_modd])

    def load_mod(self, tiles, idxs, sample):
        for t, ix in zip(tiles, idxs):
            if not sample:
                self.dma(t[:], self.modd[0, ix * D:(ix + 1) * D].partition_broadcast(128), reads=[self.t_modd], writes=[t])
            else:
                for s_ in range(NSEQ):
                    self.dma(t[s_ * TS:(s_ + 1) * TS, :], self.modd[1 + s_, ix * D:(ix + 1) * D].partition_broadcast(TS),
                             reads=[self.t_modd], writes=[t])

    def norm_mod(self, w, xt, scale, shift, hT, tmp=None):
        self.E("pool", "memset", [], [w.ssq], w.ssq[:], 0.0)
        self.act(w.h[:], xt[:], AF.Square, [xt, w.ssq], [w.h, w.ssq], accum_out=w.ssq[:, 0:1])
        self.rsqrt_ops(w.ssq, w.rstd, 1, 1.0 / D)
        if tmp is None:
            tmp = xt
        self.E("dve", "scalar_tensor_tensor", [xt, w.rstd, scale], [tmp], out=tmp[:], in0=xt[:], scalar=w.rstd[:, 0:1],
               in1=scale[:], op0=ALU.mult, op1=ALU.mult)
        self.E("pool", "tensor_tensor", [tmp, shift], [w.h], out=w.h[:], in0=tmp[:], in1=shift[:], op=ALU.add)
        tb = self.TB()
        for dc in range(DC):
            self.tr(tb[:, dc * 128:(dc + 1) * 128], w.h[:, dc * 128:(dc + 1) * 128], self.identb, [w.h, self.cbf], [tb])
        self.act(hT[:], tb[:, :].rearrange("p (a b) -> p a b", a=DC), AF.Copy, [tb], [hT])

    def head_norm(self, w, ps, nh, hd, out_f32, reads_ps, gvec, out_tile, out_ap, scale):
        v3 = lambda ap: ap.rearrange("p (a b) -> p a b", a=nh)
        self.act(out_f32[:, 0:nh * hd], ps, AF.Square, reads_ps, [out_f32])
        self.E("dve", "tensor_reduce", [out_f32], [w.ss8], out=w.ss8[:, 0:nh], in_=v3(out_f32[:, 0:nh * hd]), axis=AX.X, op=ALU.add)
        self.rsqrt_ops(w.ss8, w.rs8, nh, scale)
        self.E("dve", "tensor_tensor", reads_ps + [w.rs8], [out_f32], out=v3(out_f32[:, 0:nh * hd]), in0=v3(ps),
               in1=w.rs8[:, 0:nh].unsqueeze(2).to_broadcast([128, nh, hd]), op=ALU.mult)
        self.E("pool", "tensor_tensor", [out_f32, self.sv], [out_tile], out=v3(out_ap), in0=v3(out_f32[:, 0:nh * hd]),
               in1=gvec.unsqueeze(1).to_broadcast([128, nh, hd]), op=ALU.mult)

    def front_end(self, w, b, sample, own, outrow):
        c = self.cfg
        i, o = self.i, self.o
        xt = w.xt[self.xi % len(w.xt)]
        self.xi += 1
        src = i["xs"][:, :] if sample else i["xp"][b * 128:(b + 1) * 128, :]
        self.dma(xt[:], src, writes=[xt])
        hT = w.hT
        self.norm_mod(w, xt, w.scale1, w.shift1, hT)
        winb = self.winb
        yield
        gk_ = self.zb[0]
        for dc in range(DC):
            self.mm(gk_[:, :], hT[:, dc, :], winb[:, dc, 512:1024], dc == 0, dc == DC - 1, [hT, winb], [gk_])
        self.head_norm(w, gk_[:, :], HS, HD, w.f512, [gk_], self.gk, w.kn, w.kn[:], 1.0 / HD)
        if outrow is not None:
            self.dma(outrow[0], w.kn[:], reads=[w.kn], queue="pool")
        self.E("dve", "tensor_copy", [w.kn], [w.knb], out=w.knb[:], in_=w.kn[:])
        yield
        gv_ = self.zb[1]
        for dc in range(DC):
            self.mm(gv_[:, :], hT[:, dc, :], winb[:, dc, 1024:1536], dc == 0, dc == DC - 1, [hT, winb], [gv_])
        Vt = w.Vs if sample else self.Vt[b]
        self.act(Vt[:], gv_[:, :], AF.Copy, [gv_], [Vt])
        if outrow is not None:
            self.E("dve", "tensor_copy", [gv_], [w.vf], out=w.vf[:], in_=gv_[:, :])
            self.dma(outrow[1], w.vf[:], reads=[w.vf], queue="pool")
        yield
        if own:
            gq_ = self.po[0]
            for dc in range(DC):
                self.mm(gq_[:, :], hT[:, dc, :], winb[:, dc, 0:512], dc == 0, dc == DC - 1, [hT, winb], [gq_])
            self.head_norm(w, gq_[:, :], HS, HD, w.f512, [gq_], self.gq8, w.qnb, w.qnb[:], 1.0 / HD)
        if self.on("dn"):
            g = self.G()
            for dc in range(DC):
                self.mm(g[:, 0:8], hT[:, dc, :], winb[:, dc, 3072:3080], dc == 0, dc == DC - 1, [hT, winb], [g])
            self.E("dve", "tensor_copy", [g], [w.ba], out=w.ba[:], in_=g[:, 0:8])
            if own:
                g = self.po[1]
                for dc in range(DC):
                    self.mm(g[:, :], hT[:, dc, :], winb[:, dc, 3080:3592], dc == 0, dc == DC - 1, [hT, winb], [g])
                self.act(w.zs[:], g[:, :], AF.Copy, [g], [w.zs])
        yield
        KTt = w.KTs if sample else self.KTt[b]
        tb = self.TB()
        for pr in range(4):
            self.tr(tb[:, pr * 128:(pr + 1) * 128], w.knb[:, pr * 128:(pr + 1) * 128], self.identb, [w.knb, self.cbf], [tb])
        self.act(KTt[:], tb[:, 0:512].rearrange("p (a b) -> p a b", a=4), AF.Copy, [tb], [KTt])
        if own:
            tb = self.TB()
            for pr in range(4):
                self.tr(tb[:, pr * 128:(pr + 1) * 128], w.qnb[:, pr * 128:(pr + 1) * 128], self.identb, [w.qnb, self.cbf], [tb])
            tq = tb[:, 0:512].rearrange("p (a b) -> p a b", a=4)
            self.act(w.QT[0:64, :, 0, :], tq[0:64, :, :], AF.Copy, [tb], [w.QT])
            self.E("dve", "tensor_copy", [tb], [w.QT], out=w.QT[64:128, :, 1, :], in_=tq[64:128, :, :])

    def front_dn(self, w, b, sample, fe):
        next(fe)
        if not self.on("dn"):
            for _ in fe:
                pass
            return
        if (not sample) and b == 0:
            self.E("pool", "memset", [], [w.Hst], w.Hst[:], 0.0)
        g0, g1, g2 = [self.dn_group(w, b, sample, j) for j in range(3)]
        next(g0); next(g0)
        next(fe)
        next(g1)
        next(g0)
        next(g1)
        next(fe)
        next(g2)
        next(g1)
        next(g2)
        next(fe)
        next(g2)
        for _ in fe:
            pass

    def dn_group(self, w, b, sample, j):
        T = TS if sample else 128
        ns = NSEQ if sample else 1
        hT, winb = w.hT, self.winb
        XE4 = w.XE4
        xe4 = XE4[:, :, :].rearrange("p c (s t) -> p c s t", s=ns)
        Hst = w.Hst
        hs4 = Hst[:, :, 0:ns * 3].rearrange("p c (s t) -> p c s t", s=ns)
        g = self.G()
        for ch in range(4):
            col = 1536 + (j * 4 + ch) * 128
            for dc in range(DC):
                self.mm(g[:, ch * 128:(ch + 1) * 128], winb[:, dc, col:col + 128], hT[:, dc, :], dc == 0, dc == DC - 1, [hT, winb], [g])
        yield
        src4 = g[:, :].rearrange("p (c s t) -> p c s t", c=4, s=ns)
        self.E("pool", "tensor_copy", [Hst], [XE4], out=xe4[:, :, :, 0:3], in_=hs4[:, j * 4:(j + 1) * 4, :, :])
        if sample:
            self.act(xe4[:, :, :, 3:3 + T], src4, AF.Copy, [g], [XE4])
        else:
            self.act(xe4[:, :, :, 3:3 + T], src4, AF.Copy, [g, self.bvd], [XE4], scale=self.bvd[:, b:b + 1])
        self.E("pool", "tensor_copy", [XE4], [Hst], out=hs4[:, j * 4:(j + 1) * 4, :, :], in_=xe4[:, :, :, T:T + 3])
        Y4 = w.Y4
        if not sample:
            Pt = w.E4
            wv_ = lambda k: self.wdc[:, k, j * 4:(j + 1) * 4].unsqueeze(2).to_broadcast([128, 4, T])
            self.E("pool", "tensor_tensor", [XE4, self.wdc], [Pt], out=Pt[:], in0=XE4[:, :, 1:1 + T], in1=wv_(1), op=ALU.mult)
            self.E("dve", "tensor_tensor", [XE4, self.wdc], [Y4], out=Y4[:], in0=XE4[:, :, 0:T], in1=wv_(0), op=ALU.mult)
            for k in range(1, 4):
                self.E("dve", "tensor_tensor", [Y4, Pt], [Y4], out=Y4[:], in0=Y4[:], in1=Pt[:], op=ALU.add)
                if k < 3:
                    self.E("pool", "tensor_tensor", [XE4, self.wdc], [Pt], out=Pt[:], in0=XE4[:, :, k + 1:k + 1 + T], in1=wv_(k + 1), op=ALU.mult)
        for ch in range(4 if sample else 0):
            cc = j * 4 + ch
            yv = Y4[:, ch, :].rearrange("p (s t) -> p s t", s=ns)
            self.E("dve", "tensor_scalar", [XE4, self.wdc], [Y4], out=yv, in0=xe4[:, ch, :, 0:T], scalar1=self.wdc[:, 0, cc:cc + 1],
                   scalar2=None, op0=ALU.mult)
            for k in range(1, 4):
                self.E("dve", "scalar_tensor_tensor", [XE4, self.wdc, Y4], [Y4], out=yv, in0=xe4[:, ch, :, k:k + T],
                       scalar=self.wdc[:, k, cc:cc + 1], in1=yv, op0=ALU.mult, op1=ALU.add)
        E4 = w.E4
        self.act(E4[:], Y4[:], AF.Exp, [Y4], [E4], scale=-1.0)
        self.act(E4[:], E4[:], AF.Ln, [E4], [E4], bias=1.0)
        self.act(E4[:], E4[:], AF.Exp, [E4], [E4], scale=-1.0)
        self.E("dve", "tensor_tensor", [E4, Y4], [Y4], out=Y4[:], in0=Y4[:], in1=E4[:], op=ALU.mult)
        SQ = E4
        if j < 2:
            self.act(SQ[:], Y4[:], AF.Square, [Y4], [SQ])
        yield
        if j < 2:
            g = self.G()
            self.mm(g[:, :], self.cm("ones"), SQ[:].rearrange("p a b -> p (a b)"), True, True, [SQ, self.ctm], [g])
            self.act(SQ[:].rearrange("p a b -> p (a b)"), g[:, :], AF.Ln, [g], [SQ], bias=self.epsb[:, 0:1])
            self.act(SQ[:], SQ[:], AF.Exp, [SQ], [SQ], scale=-0.5)
            if j == 0:
                self.E("dve", "scalar_tensor_tensor", [Y4, SQ], [w.QnT], out=w.QnT[:], in0=Y4[:], scalar=DND ** -0.5, in1=SQ[:],
                       op0=ALU.mult, op1=ALU.mult)
            else:
                self.E("pool", "tensor_tensor", [Y4, SQ], [w.KnT], out=w.KnT[:], in0=Y4[:], in1=SQ[:], op=ALU.mult)
        else:
            self.act(w.Vcb[:], Y4[:], AF.Copy, [Y4], [w.Vcb])
        yield

    def dn_chunk(self, w, b, sample, own, tabs, last_prompt):
        c = self.cfg
        i, o = self.i, self.o
        T = TS if sample else 128
        ns = NSEQ if sample else 1
        sc = w.sc
        hT, winb = w.hT, self.winb
        ltincl, seqm, ms01 = tabs
        bc4 = lambda ap: ap.unsqueeze(2).to_broadcast([128, 4, 128])
        hb4 = lambda ap: ap.unsqueeze(1).to_broadcast([128, 4, 128])
        v4 = lambda ap: ap.rearrange("p (a b) -> p a b", a=4)
        Hst = w.Hst
        if sample or last_prompt:
            ncol = ns * 3
            dst = o["dcs"] if sample else o["dcp"]
            for j in range(3):
                g = self.G()
                for ch in range(4):
                    self.tr(g[0:ncol, ch * 128:(ch + 1) * 128], Hst[:, j * 4 + ch, 0:ncol], self.cm("ident"), [Hst, self.ctm], [g])
                self.E("dve", "tensor_copy", [g], [w.f512], out=w.f512[0:ncol, :], in_=g[0:ncol, :])
                self.dma(dst[:, j * 512:(j + 1) * 512], w.f512[0:ncol, :], reads=[w.f512])
        tb = self.TB()
        for h in range(4):
            self.tr(tb[:, h * 128:(h + 1) * 128], w.KnT[:, h, :], self.identb, [w.KnT, self.cbf], [tb])
        self.act(w.Ktok[:], v4(tb[:, 0:512]), AF.Copy, [tb], [w.Ktok])
        tb = self.TB()
        for h in range(4):
            self.tr(tb[:, h * 128:(h + 1) * 128], w.Vcb[:, h, :], self.identb, [w.Vcb, self.cbf], [tb])
        self.E("dve", "tensor_copy", [tb], [w.Vtok], out=w.Vtok[:], in_=v4(tb[:, 0:512]))
        yield
        ba = w.ba
        self.act(sc[:, 0:4], ba[:, 0:4], AF.Exp, [ba], [sc], scale=-1.0)
        self.act(sc[:, 0:4], sc[:, 0:4], AF.Ln, [sc], [sc], bias=1.0)
        self.act(sc[:, 0:4], sc[:, 0:4], AF.Exp, [sc], [sc], scale=-1.0)
        if not sample:
            self.E("dve", "tensor_scalar", [sc, self.bvd], [sc], out=sc[:, 0:4], in0=sc[:, 0:4], scalar1=self.bvd[:, b:b + 1], scalar2=None, op0=ALU.mult)
        self.E("dve", "tensor_tensor", [ba, self.sv], [sc], out=sc[:, 32:36], in0=ba[:, 4:8], in1=self.dtb, op=ALU.add)
        self.act(sc[:, 32:36], sc[:, 32:36], AF.Exp, [sc], [sc])
        self.act(sc[:, 32:36], sc[:, 32:36], AF.Ln, [sc], [sc], bias=1.0)
        self.E("dve", "tensor_tensor", [sc, self.sv], [sc], out=sc[:, 4:8], in0=sc[:, 32:36], in1=self.negA, op=ALU.mult)
        g = self.G()
        self.mm(g[:, 0:4], ltincl, sc[:, 4:8], True, True, [sc, w.tabt], [g])
        self.mm(g[:, 4:8], seqm, sc[:, 4:8], True, True, [sc, w.tabt], [g])
        self.E("dve", "tensor_copy", [g], [sc], out=sc[:, 8:16], in_=g[:, 0:8])
        self.act(sc[:, 16:20], sc[:, 8:12], AF.Exp, [sc], [sc])
        self.E("dve", "scalar_tensor_tensor", [sc], [sc], out=sc[:, 20:24], in0=sc[:, 0:4], scalar=-1.0, in1=sc[:, 16:20], op0=ALU.mult, op1=ALU.mult)
        self.E("dve", "tensor_tensor", [sc], [sc], out=sc[:, 24:28], in0=sc[:, 12:16], in1=sc[:, 8:12], op=ALU.subtract)
        self.act(sc[:, 24:28], sc[:, 24:28], AF.Exp, [sc], [sc])
        self.E("dve", "tensor_scalar", [sc], [sc], out=sc[:, 28:32], in0=sc[:, 0:4], scalar1=-1.0, scalar2=None, op0=ALU.mult)
        beta, gg, negbg, kds, negbeta = sc[:, 0:4], sc[:, 8:12], sc[:, 20:24], sc[:, 24:28], sc[:, 28:32]
        yield
        GR = self.G()
        for h in range(4):
            dg = w.dg[h % 2]
            self.E("dve", "tensor_scalar", [sc, self.ctm], [dg], out=dg[:], in0=self.cm("ident"), scalar1=sc[:, 8 + h:9 + h], scalar2=None, op0=ALU.mult)
            self.mm(GR[:, h * 128:(h + 1) * 128], self.cm("ones"), dg[:], True, True, [dg, self.ctm], [GR])
        fa, fb, fc, fd = w.fa, w.fb, w.fc, w.fd
        self.E("dve", "tensor_tensor", [GR, sc], [fa], out=v4(fa[:]), in0=v4(GR[:, :]), in1=bc4(gg), op=ALU.subtract)
        self.act(fd[:], GR[:, :], AF.Exp, [GR], [fd])
        self.E("dve", "tensor_scalar", [fa], [fb], out=fb[:], in0=fa[:], scalar1=0.0, scalar2=None, op0=ALU.max)
        self.E("dve", "tensor_scalar", [fa], [fc], out=fc[:], in0=fa[:], scalar1=0.0, scalar2=None, op0=ALU.min)
        self.act(fb[:], fb[:], AF.Exp, [fb], [fb], scale=-1.0)
        self.act(fc[:], fc[:], AF.Exp, [fc], [fc])
        yield
        g = self.G()
        for h in range(4):
            self.mm(g[:, h * 128:(h + 1) * 128], w.KnT[:, h, :], w.KnT[:, h, :], True, True, [w.KnT], [g])
        self.E("dve", "tensor_tensor", [g, fb], [fa], out=fa[:], in0=g[:, :], in1=fb[:], op=ALU.mult)
        self.E("pool", "tensor_tensor", [fa, sc], [fa], out=v4(fa[:]), in0=v4(fa[:]), in1=bc4(negbeta), op=ALU.mult)
        self.E("dve", "tensor_tensor", [fa, w.tabt], [fa], out=v4(fa[:]), in0=v4(fa[:]), in1=hb4(ms01), op=ALU.mult)
        g = self.G()
        for h in range(4):
            self.mm(g[:, h * 128:(h + 1) * 128], w.KnT[:, h, :], w.QnT[:, h, :], True, True, [w.KnT, w.QnT], [g])
        self.E("dve", "tensor_tensor", [g, fc], [fc], out=fc[:], in0=g[:, :], in1=fc[:], op=ALU.mult)
        self.E("pool", "tensor_tensor", [fc, w.tabt], [w.intraT], out=w.intraT[:], in0=v4(fc[:]), in1=hb4(ltincl), op=ALU.mult)
        yield
        MTb = w.MTb
        nlev = 2 if sample else 6
        h2 = lambda ap: ap.rearrange("p (a b) -> p a b", a=2)
        hb2 = lambda ap: ap.unsqueeze(1).to_broadcast([128, 2, 128])
        identf = self.cm("ident")
        P, PT, MT = w.Pf[0], w.PTf[0], w.MT
        P = fa_t = None
        P = w.Pf[0]
        self.E("pool", "tensor_copy", [fa], [P], out=P[:], in_=v4(fa[:]))
        g = self.G()
        for h in range(4):
            self.tr(g[:, h * 128:(h + 1) * 128], P[:, h, :], identf, [P, self.ctm], [g])
        self.act(PT[:], v4(g[:, :]), AF.Copy, [g], [PT])
        self.E("dve", "tensor_tensor", [PT, self.ctm], [MT], out=MT[:], in0=PT[:], in1=hb4(identf), op=ALU.add)
        for lev in range(1, nlev + 1):
            Pn, PTn = w.Pf[lev % 2], w.PTf[lev % 2]
            g1 = self.G()
            for h in range(4):
                self.mm(g1[:, h * 128:(h + 1) * 128], PT[:, h, :], P[:, h, :], True, True, [P, PT], [g1], r32=True)
            self.act(Pn[:], v4(g1[:, :]), AF.Copy, [g1], [Pn])
            if lev < nlev:
                g2 = self.G()
                for h in range(4):
                    self.mm(g2[:, h * 128:(h + 1) * 128], P[:, h, :], PT[:, h, :], True, True, [P, PT], [g2], r32=True)
                self.E("dve", "tensor_copy", [g2], [PTn], out=PTn[:], in_=v4(g2[:, :]))
            g3 = self.G()
            for h in range(4):
                self.mm(g3[:, h * 128:(h + 1) * 128], Pn[:, h, :], MT[:, h, :], True, True, [Pn, MT], [g3], r32=True)
            self.E("dve", "tensor_tensor", [g3, MT], [MT], out=MT[:], in0=v4(g3[:, :]), in1=MT[:], op=ALU.add)
            P, PT = Pn, PTn
            yield
        self.act(MTb[:], MT[:], AF.Copy, [MT], [MTb])
        yield
        self.E("pool", "tensor_tensor", [w.QnT, fd], [w.QdT], out=w.QdT[:], in0=w.QnT[:], in1=v4(fd[:]), op=ALU.mult)
        self.E("dve", "tensor_tensor", [w.Vtok, sc], [w.Vtok], out=w.Vtok[:], in0=w.Vtok[:], in1=bc4(beta), op=ALU.mult)
        self.E("pool", "tensor_tensor", [w.Ktok, sc], [w.kdec], out=w.kdec[:], in0=w.Ktok[:], in1=bc4(kds), op=ALU.mult)
        if not sample:
            Sf, Sb = self.Sf, self.Sb
            g = self.G()
            for h in range(4):
                self.mm(g[:, h * 128:(h + 1) * 128], w.KnT[:, h, :], Sb[:, h, :], True, True, [w.KnT, Sb], [g])
            self.E("dve", "tensor_tensor", [g, sc], [fa], out=v4(fa[:]), in0=v4(g[:, :]), in1=bc4(negbg), op=ALU.mult)
            self.E("pool", "tensor_tensor", [fa, w.Vtok], [w.W], out=w.W[:], in0=v4(fa[:]), in1=w.Vtok[:], op=ALU.add)
            g = self.G()
            for h in range(4):
                self.mm(g[:, h * 128:(h + 1) * 128], MTb[:, h, :], w.W[:, h, :], True, True, [MTb, w.W], [g])
            self.act(w.vnew[:], v4(g[:, :]), AF.Copy, [g], [w.vnew])
            yield
            if own:
                po = self.G()
                po_dn = po
                for h in range(4):
                    self.mm(po[:, h * 128:(h + 1) * 128], w.QdT[:, h, :], Sb[:, h, :], True, False, [w.QdT, Sb], [po])
                    self.mm(po[:, h * 128:(h + 1) * 128], w.intraT[:, h, :], w.vnew[:, h, :], False, True, [w.intraT, w.vnew], [po])
            g = self.G()
            for h in range(4):
                self.mm(g[:, h * 128:(h + 1) * 128], w.kdec[:, h, :], w.vnew[:, h, :], True, True, [w.kdec, w.vnew], [g])
            self.E("dve", "tensor_tensor", [Sf, fd], [Sf], out=Sf[:], in0=Sf[:], in1=v4(fd[:])[:, :, 127:128].to_broadcast([128, 4, 128]), op=ALU.mult)
            self.E("dve", "tensor_tensor", [g, Sf], [Sf], out=Sf[:], in0=v4(g[:, :]), in1=Sf[:], op=ALU.add)
            self.act(Sb[:], Sf[:], AF.Copy, [Sf], [Sb])
            if last_prompt:
                self.dma(o["sp_state"].rearrange("h k v -> k h v"), Sf[:], reads=[Sf])
        else:
            colmask, rowmask = w.colmask, w.rowmask
            cm3 = colmask.rearrange("p (s t) -> p s t", s=NSEQ)
            s0v = i["s0"]
            po = self.po[1]
            for h in range(4):
                Sfh, Sbh = w.Sfh[0], w.Sbh[0]
                self.dma(Sfh[:], s0v[:, h, :, :].rearrange("s k v -> k s v"), writes=[Sfh])
                self.E("pool", "tensor_copy", [Sfh], [Sbh], out=Sbh[:], in_=Sfh[:])
                Km, Qm, kdm = w.Km, w.Qm, w.kdm
                self.E("pool", "tensor_tensor", [w.KnT, w.tabt], [Km], out=Km[:], in0=w.KnT[:, h, :].unsqueeze(1).to_broadcast([128, NSEQ, 128]), in1=cm3, op=ALU.mult)
                g = self.G()
                for s_ in range(NSEQ):
                    self.mm(g[:, 0:128], Km[:, s_, :], Sbh[:, s_, :], s_ == 0, s_ == NSEQ - 1, [Km, Sbh], [g])
                self.E("dve", "tensor_scalar", [g, sc], [fa], out=fa[:, 0:128], in0=g[:, 0:128], scalar1=sc[:, 20 + h:21 + h], scalar2=None, op0=ALU.mult)
                self.E("pool", "tensor_tensor", [fa, w.Vtok], [w.W], out=w.W[:, h, :], in0=fa[:, 0:128], in1=w.Vtok[:, h, :], op=ALU.add)
                g = self.G()
                self.mm(g[:, 0:128], MTb[:, h, :], w.W[:, h, :], True, True, [MTb, w.W], [g])
                self.act(w.vnew[:, h, :], g[:, 0:128], AF.Copy, [g], [w.vnew])
                self.E("dve", "tensor_tensor", [w.QdT, w.tabt], [Qm], out=Qm[:], in0=w.QdT[:, h, :].unsqueeze(1).to_broadcast([128, NSEQ, 128]), in1=cm3, op=ALU.mult)
                for s_ in range(NSEQ):
                    self.mm(po[:, h * 128:(h + 1) * 128], Qm[:, s_, :], Sbh[:, s_, :], s_ == 0, False, [Qm, Sbh], [po])
                self.mm(po[:, h * 128:(h + 1) * 128], w.intraT[:, h, :], w.vnew[:, h, :], False, True, [w.intraT, w.vnew], [po])
                Sn = w.Sn[0]
                self.E("pool", "tensor_tensor", [w.kdec, w.tabt], [kdm], out=kdm[:], in0=w.kdec[:, h, :].unsqueeze(1).to_broadcast([128, NSEQ, 128]),
                       in1=rowmask[:, 0:NSEQ].unsqueeze(2).to_broadcast([128, NSEQ, 128]), op=ALU.mult)
                for q4 in range(4):
                    g = self.G()
                    for k in range(4):
                        s_ = q4 * 4 + k
                        self.mm(g[:, k * 128:(k + 1) * 128], kdm[:, s_, :], w.vnew[:, h, :], True, True, [kdm, w.vnew], [g])
                    for k in range(4):
                        s_ = q4 * 4 + k
                        self.E("dve" if k % 2 == 0 else "pool" if False else "dve", "scalar_tensor_tensor", [g, Sfh, fd], [Sn], out=Sn[:, s_, :], in0=Sfh[:, s_, :],
                               scalar=fd[:, h * 128 + s_ * TS + TS - 1: h * 128 + s_ * TS + TS], in1=g[:, k * 128:(k + 1) * 128], op0=ALU.mult, op1=ALU.add)
                self.dma(o["ss_state"][:, h, :, :].rearrange("s k v -> k s v"), Sn[:], reads=[Sn])
        yield
        if own:
            po = self.po[1] if sample else po_dn
            self.act(fa[:], po[:, :], AF.Square, [po], [fa])
            self.E("dve", "tensor_reduce", [fa], [w.ss8], out=w.ss8[:, 0:4], in_=v4(fa[:]), axis=AX.X, op=ALU.add)
            self.rsqrt_ops(w.ss8, w.rs8, 4, 1.0 / DND)
            self.E("dve", "tensor_tensor", [po, w.rs8], [fa], out=v4(fa[:]), in0=v4(po[:, :]), in1=bc4(w.rs8[:, 0:4]), op=ALU.mult)
            self.E("pool", "tensor_tensor", [fa, self.sv], [fa], out=v4(fa[:]), in0=v4(fa[:]), in1=hb4(self.gdn), op=ALU.mult)
            zs = w.zs
            self.act(fb[:], zs[:], AF.Exp, [zs], [fb], scale=-1.0)
            self.act(fb[:], fb[:], AF.Ln, [fb], [fb], bias=1.0)
            self.act(fb[:], fb[:], AF.Exp, [fb], [fb], scale=-1.0)
            self.E("dve", "tensor_tensor", [fb, zs], [fb], out=fb[:], in0=fb[:], in1=zs[:], op=ALU.mult)
            self.E("dve", "tensor_tensor", [fa, fb], [w.mixed], out=w.mixed[:, 512:1024], in0=fa[:], in1=fb[:], op=ALU.mult)

    def attn_step(self, w, S, nh, nq, qT, kT, vv, nk, kvl, kvl_t, brow_ap, maskneg, mask01, mask_t, O, o_cols, first, last,
                  qreads, kreads, vreads, att_out=None, att_lhs=None, first_o=None, last_o=None):
        W_ = nh * nq
        et, spt, att, Rb = S.et, S.spt, S.att, S.Rb
        Z = S.zbank
        for p_ in range(nh // 2):
            self.mm(Z[0:nk, p_ * 2 * nq:(p_ + 1) * 2 * nq], kT[p_], qT[p_], p_ == 0, False, qreads + kreads, [Z], skip=True)
        self.mm(Z[0:nk, 0:W_], kvl, brow_ap, False, True, [kvl_t, self.brow], [Z], skip=True)
        if getattr(S, "pending", None) is not None:
            S.pending()
            S.pending = None
        yield
        self.act(et[0:nk, 0:W_], Z[0:nk, 0:W_], AF.Exp, [Z], [et])
        self.act(spt[0:nk, 0:W_], et[0:nk, 0:W_], AF.Ln, [et], [spt], bias=1.0)
        if mask01 is not None:
            self.E("dve", "tensor_tensor", [spt, mask_t], [spt], out=spt[0:nk, 0:W_], in0=spt[0:nk, 0:W_], in1=mask01, op=ALU.mult)
        yield
        U = Z
        fin = first and maskneg is None
        self.mm(U[0:nk, 0:W_], self.ntrib[0:nk, 0:nk], spt[0:nk, 0:W_], False, fin, [spt, self.cbf], [U], skip=True)
        if not first:
            self.mm(U[0:nk, 0:W_], self.negonesb[:, 0:nk], Rb[:, 0:W_], False, maskneg is None, [Rb, self.cbf], [U], skip=True)
        if maskneg is not None:
            self.mm(U[0:nk, 0:W_], self.identb[0:nk, 0:nk], maskneg, False, True, [mask_t, self.cbf], [U], skip=True)
        yield
        if att_out is None:
            self.act(att[0:nk, 0:W_], U[0:nk, 0:W_], AF.Exp, [U], [att])
        else:
            self.act(att_out[0], U[0:nk, 0:W_].rearrange("p (h q) -> p h q", h=nh), AF.Exp, [U], [att_out[1]])
        if not last:
            if first:
                self.E("dve", "tensor_copy", [spt], [Rb], out=Rb[0:nk, 0:W_], in_=spt[0:nk, 0:W_])
            else:
                self.E("dve", "tensor_tensor", [spt, Rb], [Rb], out=Rb[0:nk, 0:W_], in0=Rb[0:nk, 0:W_], in1=spt[0:nk, 0:W_], op=ALU.add)
        fo = first if first_o is None else first_o
        lo = last if last_o is None else last_o

        def av():
            for h in range(nh):
                if att_lhs is None:
                    lhs = att[0:nk, h * nq:(h + 1) * nq]
                    rd = [att]
                else:
                    lhs = att_lhs[0][h]
                    rd = [att_lhs[1]]
                self.mm(O[o_cols[h]], lhs, vv[h], fo and h == 0, lo, rd + vreads, [O], skip=True)
        S.pending = av
        yield

    @staticmethod
    def interleave(gens):
        gens = list(gens)
        while gens:
            for g in list(gens):
                try:
                    next(g)
                except StopIteration:
                    gens.remove(g)

    def attn_prompt(self, w, b, dn_gen=None):
        c = self.cfg

        def stream(hg):
            O = self.po[hg]
            S = w.streams[hg]
            for kb in range(b, -1, -1):
                qT = [w.QT[:, hg * 2 + p_, :, :] for p_ in range(2)]
                kT = [self.KTt[kb][:, hg * 2 + p_, :] for p_ in range(2)]
                vv = [self.Vt[kb][:, (hg * 4 + h) * 64:(hg * 4 + h + 1) * 64] for h in range(4)]
                diag = kb == b
                kvl = self.kvd if kb < c.OUT0 else self.kvone
                yield from self.attn_step(w, S, 4, 128, qT, kT, vv, 128, kvl[:, :], kvl,
                                          self.brow[:, hg * 512:(hg + 1) * 512],
                                          w.causrep[:, 0:512] if diag else None, w.caus01rep[:, 0:512] if diag else None, w.causrep_t,
                                          O, [(slice(None), slice(h * 64, (h + 1) * 64)) for h in range(4)],
                                          kb == b, kb == 0, [w.QT], [self.KTt[kb]], [self.Vt[kb]])
            S.pending()
            S.pending = None
            self.head_norm(w, O[:, 0:256], 4, HD, w.f512, [O], self.gso, w.mixed, w.mixed[:, hg * 256:(hg + 1) * 256], 1.0 / HD)
        gens = [stream(0), stream(1)]
        n_rounds = 5 * (b + 1) + 1
        stride = max(1, n_rounds // 24)
        rnd = 0
        dn_live = dn_gen is not None
        while gens or dn_live:
            if dn_live and (rnd % stride == 0 or not gens):
                try:
                    next(dn_gen)
                except StopIteration:
                    dn_live = False
            for g_ in list(gens):
                try:
                    next(g_)
                except StopIteration:
                    gens.remove(g_)
            rnd += 1

    def attn_sample(self, w):
        c = self.cfg
        i = self.i
        npg = c.NPG
        O = self.po[0]
        ck = i["cache_k"]
        cv = i["cache_v"]

        NSTR = len(w.streams)

        def stream(si):
            S = w.streams[si]
            for s_ in range(si, NSEQ, NSTR):
                attpad = w.attpad[si]
                if getattr(S, "pending", None) is not None:
                    S.pending()
                    S.pending = None
                self.E("pool", "memset", [], [attpad], attpad[:], 0.0)
                qT = [w.QT[:, p_, :, s_ * TS:(s_ + 1) * TS] for p_ in range(4)]
                att_out = (attpad[:, :, s_ * TS:(s_ + 1) * TS], attpad)
                att_lhs = ([attpad[:, h, :] for h in range(8)], attpad)
                o_cols = [(slice(None), slice(h * 64, (h + 1) * 64)) for h in range(8)]
                for blk in range(npg, -1, -1):
                    if blk == npg:
                        kT = [w.KTs[:, p_, :] for p_ in range(4)]
                        vv = [w.Vs[:, h * 64:(h + 1) * 64] for h in range(8)]
                        kreads, vreads = [w.KTs], [w.Vs]
                        mneg, m01 = w.smneg[:, s_, :], w.sm01[:, s_, :]
                    else:
                        j = s_ * npg + blk
                        kk = si * 2 + (self.pgi[si] % 2)
                        self.pgi[si] += 1
                        kpf, vpf, kpb, vpb, ktp = w.kpf[kk], w.vpf[kk], w.kpb[kk], w.vpb[kk], w.ktp[kk]
                        self.s.add("pool", lambda e, kpf=kpf, j=j: e.indirect_dma_start(
                            out=kpf[:], out_offset=None, in_=ck, in_offset=bass.IndirectOffsetOnAxis(ap=w.idx[:, j:j + 1], axis=0)),
                            [w.idx], [kpf], is_dma=True)
                        self.s.add("pool", lambda e, vpf=vpf, j=j: e.indirect_dma_start(
                            out=vpf[:], out_offset=None, in_=cv, in_offset=bass.IndirectOffsetOnAxis(ap=w.idx[:, j:j + 1], axis=0)),
                            [w.idx], [vpf], is_dma=True)
                        self.E("dve", "tensor_copy", [kpf], [kpb], out=kpb[:], in_=kpf[:])
                        self.act(vpb[:], vpf[:], AF.Copy, [vpf], [vpb])
                        tb = self.TB()
                        for pr in range(4):
                            self.tr(tb[:, pr * 128:(pr + 1) * 128], kpb[:, pr * 128:(pr + 1) * 128], self.identb, [kpb, self.cbf], [tb])
                        self.E("dve", "tensor_copy", [tb], [ktp], out=ktp[:], in_=tb[:, 0:512].rearrange("p (a b) -> p a b", a=4))
                        kT = [ktp[:, p_, :] for p_ in range(4)]
                        vv = [vpb[:, h * 64:(h + 1) * 64] for h in range(8)]
                        kreads, vreads = [ktp], [vpb]
                        mneg, m01 = None, None
                    yield from self.attn_step(w, S, 8, TS, qT, kT, vv, 128, self.kvone[:, :], self.kvone, self.brow[:, 1024:1088],
                                              mneg, m01, w.smt, O, o_cols, blk == npg, blk == 0, [w.QT], kreads, vreads,
                                              att_out=att_out, att_lhs=att_lhs,
                                              first_o=(s_ == 0 and blk == npg), last_o=(s_ == NSEQ - 1 and blk == 0))
            S.pending()
            S.pending = None
        self.pgi = [0] * NSTR
        self.interleave([stream(k) for k in range(NSTR)])
        self.head_norm(w, O[:, :], HS, HD, w.f512, [O], self.gso, w.mixed, w.mixed[:, 0:512], 1.0 / HD)

    def alloc_work(self, st, sample):
        class WS:
            pass
        w = WS()
        sb = lambda name, shape, dt: self.sb(st, name, shape, dt)
        w.xt = [sb("xt", [128, D], F32)]
        w.h = sb("h", [128, D], BF16)
        w.hT = sb("hT", [128, DC, 128], BF16)
        w.ssq = sb("ssq", [128, 1], F32)
        w.rstd = sb("rstd", [128, 1], F32)
        w.ss8 = sb("ss8", [128, 8], F32)
        w.rs8 = sb("rs8", [128, 8], F32)
        w.f512 = sb("f512", [128, 512], F32)
        w.kn = sb("kn", [128, 512], F32)
        w.knb = sb("knb", [128, 512], BF16)
        w.vf = w.f512
        w.qnb = sb("qnb", [128, 512], BF16)
        w.QT = sb("QT", [128, 4, 2, 128], BF16)
        self.E("pool", "memset", [], [w.QT], w.QT[:], 0.0)
        ns, T = (NSEQ, TS) if sample else (1, 128)
        w.XE4 = sb("XE4", [128, 4, ns * (3 + T)], F32)
        w.Hst = sb("Hst", [128, 12, ns * 3], F32)
        w.ba = sb("ba", [128, 8], F32)
        w.zs = sb("zs", [128, 512], BF16)
        w.Y4 = sb("Y4", [128, 4, 128], F32)
        w.E4 = sb("E4", [128, 4, 128], F32)
        w.QnT = sb("QnT", [128, 4, 128], BF16)
        w.KnT = sb("KnT", [128, 4, 128], BF16)
        w.Ktok = sb("Ktok", [128, 4, 128], BF16)
        w.Vtok = sb("Vtok", [128, 4, 128], F32)
        w.sc = sb("sc", [128, 40], F32)
        w.dg = [sb("dg", [128, 128], F32)] * 2
        w.fa = sb("fa", [128, 512], F32)
        w.fb = sb("fb", [128, 512], F32)
        w.fc = sb("fc", [128, 512], F32)
        w.fd = sb("fd", [128, 512], F32)
        w.Pf = [sb("Pf%d" % k, [128, 4, 128], F32) for k in range(2)]
        w.PTf = [sb("PTf%d" % k, [128, 4, 128], F32) for k in range(2)]
        w.MT = sb("MT", [128, 4, 128], F32)
        w.MTb = sb("MTb", [128, 4, 128], BF16)
        w.intraT = sb("intraT", [128, 4, 128], BF16)
        w.QdT = sb("QdT", [128, 4, 128], BF16)
        w.kdec = w.Ktok
        w.W = sb("W", [128, 4, 128], BF16)
        w.Vcb = w.W
        w.vnew = sb("vnew", [128, 4, 128], BF16)
        w.mixed = sb("mixed", [128, D], BF16)
        wd = 64 if sample else 512
        class ST:
            pass
        w.streams = []
        for k in range(2):
            S = ST()
            S.zbank = self.zb[k]
            S.et = sb("et%d" % k, [128, wd], BF16)
            S.spt = sb("spt%d" % k, [128, wd], BF16)
            S.att = sb("att%d" % k, [128, wd], BF16)
            S.Rb = sb("Rb%d" % k, [128, wd], BF16)
            w.streams.append(S)
        return w

    def phase1(self):
        c = self.cfg
        i, o = self.i, self.o
        self.xi = 0
        self.ai = 0
        self.pgi = 0
        with contextlib.ExitStack() as p1:
            winb = self.sb(p1, "winb", [128, DC, INC], BF16)
            self.winb = winb
            scale1 = self.sb(p1, "scale1", [128, D], F32)
            shift1 = self.sb(p1, "shift1", [128, D], F32)
            sv = self.sv
            WB = 512 * 2 + 64
            brow = self.sb(p1, "brow", [128, WB], BF16)
            nb = self.sb(p1, "nb", [128, 1], F32)
            with contextlib.ExitStack() as st:
                bexp = self.sb(st, "bexp", [128, WB], F32)
                for hg in range(2):
                    for h in range(4):
                        self.E("dve", "tensor_copy", [sv], [bexp], out=bexp[:, hg * 512 + h * 128: hg * 512 + (h + 1) * 128],
                               in_=sv[:, 328 + hg * 4 + h: 329 + hg * 4 + h].to_broadcast([128, 128]))
                for h in range(8):
                    self.E("dve", "tensor_copy", [sv], [bexp], out=bexp[:, 1024 + h * 8: 1024 + (h + 1) * 8],
                           in_=sv[:, 328 + h: 329 + h].to_broadcast([128, 8]))
                bhi = self.sb(st, "bhi", [128, WB], BF16)
                self.brow = brow
                idf = self.cm("ident")
                self.E("dve", "tensor_copy", [bexp], [bhi], out=bhi[:], in_=bexp[:])
                self.E("dve", "tensor_tensor", [bexp, bhi], [bexp], out=bexp[:], in0=bexp[:], in1=bhi[:], op=ALU.subtract)
                self.E("dve", "tensor_scalar", [bexp, self.ctm], [bexp], out=bexp[:], in0=bexp[:], scalar1=idf[:, 1:2], scalar2=None, op0=ALU.mult)
                self.E("dve", "scalar_tensor_tensor", [bhi, bexp, self.ctm], [bexp], out=bexp[:], in0=bhi[:], scalar=idf[:, 0:1], in1=bexp[:],
                       op0=ALU.mult, op1=ALU.add)
                self.E("dve", "tensor_scalar", [self.ctm], [nb], out=nb[:], in0=idf[:, 2:3], scalar1=-BIG, scalar2=None, op0=ALU.mult)
                self.E("dve", "tensor_scalar", [bexp, nb], [brow], out=brow[:], in0=bexp[:], scalar1=nb[:, 0:1], scalar2=None, op0=ALU.add)
                stg = [self.sb(st, "stg%d" % k, [128, DC * 512], F32) for k in range(2)]
                wv = i["w_in"].rearrange("(c p) n -> p c n", p=128)
                for ct in range(8):
                    n0, n1 = ct * 512, min(INC, (ct + 1) * 512)
                    self.stream_cast(stg, wv[:, :, n0:n1], winb, winb[:, :, n0:n1], eng="act" if ct % 2 else "dve")
            self.fence()
            with contextlib.ExitStack() as st:
                KT = self.sb(st, "KT", [128, 4, c.NBLK * 128], BF16)
                Vr = self.sb(st, "Vr", [128, c.NBLK, 512], BF16)
                self.KTt = [Tile(KT[:, :, b * 128:(b + 1) * 128], "KT%d" % b) for b in range(c.NBLK)]
                self.Vt = [Tile(Vr[:, b, :], "V%d" % b) for b in range(c.NBLK)]
                w = self.alloc_work(st, False)
                w.scale1, w.shift1 = scale1, shift1
                w.tabt = self.ctm
                w.causrep = self.sb(st, "causrep", [128, 512], BF16)
                w.caus01rep = self.sb(st, "caus01rep", [128, 512], BF16)
                w.causrep_t = self.sb(st, "causrep_t", [1, 1], F32)
                for h in range(4):
                    self.E("dve", "tensor_copy", [self.ctm], [w.causrep_t, w.causrep], out=w.causrep[:, h * 128:(h + 1) * 128], in_=self.cm("causneg"))
                    self.E("dve", "tensor_copy", [self.ctm], [w.causrep_t, w.caus01rep], out=w.caus01rep[:, h * 128:(h + 1) * 128], in_=self.cm("caus01"))
                self.Sf = self.sb(st, "Sf", [128, 4, 128], F32)
                self.Sb = self.sb(st, "Sb", [128, 4, 128], BF16)
                self.E("pool", "memset", [], [self.Sf], self.Sf[:], 0.0)
                self.E("pool", "memset", [], [self.Sb], self.Sb[:], 0.0)
                self.load_mod([shift1, scale1], [0, 1], False)
                tabs = (self.cm("ltincl_p"), self.cm("ones"), self.cm("ms01_p"))
                for b in range(c.NBLK):
                    own = b >= c.OWN0
                    outrow = None
                    if b >= c.OUT0:
                        r0 = (b - c.OUT0) * 128
                        outrow = (o["kp"][r0:r0 + 128, :], o["vp"][r0:r0 + 128, :])
                    fe = self.front_end(w, b, False, own, outrow)
                    self.front_dn(w, b, False, fe)
                    dn_gen = self.dn_chunk(w, b, False, own, tabs, b == c.NBLK - 1) if self.on("dn") else iter(())
                    if own and self.on("attn"):
                        self.attn_prompt(w, b, dn_gen)
                    else:
                        for _ in dn_gen:
                            pass
                    if own:
                        k = b - c.OWN0
                        if self.on("dn") and self.on("attn"):
                            self.dma(self.mixd[k * 128:(k + 1) * 128, :], w.mixed[:], reads=[w.mixed], writes=[self.t_mixd[k]], queue="pool")
                        if "mixed_p" in self.o and b >= c.OUT0:
                            r0 = (b - c.OUT0) * 128
                            self.E("dve", "tensor_copy", [w.mixed], [w.xt[0]], out=w.xt[0][:], in_=w.mixed[:])
                            self.dma(self.o["mixed_p"][r0:r0 + 128, :], w.xt[0][:], reads=[w.xt[0]])
            self.fence()
            if self.on("sample"):
                with contextlib.ExitStack() as st:
                    w = self.alloc_work(st, True)
                    w.scale1, w.shift1 = scale1, shift1
                    cts = self.sb(st, "cts", list(CT_SAMP.shape), F32)
                    self.dma(cts[:], i["ct_samp"][:, :], writes=[cts])
                    w.tabt = cts

                    def cs(name):
                        o_, w_ = CO_SAMP[name]
                        return cts[:, o_:o_ + w_]
                    w.colmask, w.rowmask = cs("colmask"), cs("rowmask")
                    w.KTs = self.sb(st, "KTs", [128, 4, 128], BF16)
                    w.Vs = self.sb(st, "Vs", [128, 512], BF16)
                    w.Sfh = [self.sb(st, "Sfh", [128, NSEQ, 128], F32)]
                    w.Sbh = [self.sb(st, "Sbh", [128, NSEQ, 128], BF16)]
                    w.Sn = w.Sfh
                    w.Km = self.sb(st, "Km", [128, NSEQ, 128], BF16)
                    w.Qm = w.Km
                    w.kdm = w.Km
                    self.load_mod([shift1, scale1], [0, 1], True)
                    hst_t = w.Sfh[0]
                    hst = hst_t[:, :, :].rearrange("p s v -> p (s v)")[0:NSEQ * 3, 0:3 * DNW]
                    self.dma(hst, i["dnc0"][:, :], writes=[hst_t])
                    for j in range(3):
                        g = self.G()
                        for ch in range(4):
                            self.tr(g[:, ch * 48:(ch + 1) * 48], hst[:, (j * 4 + ch) * 128:(j * 4 + ch + 1) * 128], self.cm("ident")[0:48, 0:48], [hst_t, self.ctm], [g])
                        self.act(w.Hst[:, j * 4:(j + 1) * 4, :], g[:, 0:192].rearrange("p (c t) -> p c t", c=4), AF.Copy, [g], [w.Hst])
                    fe = self.front_end(w, 0, True, True, (o["ksm"][:, :], o["vsm"][:, :]))
                    tabs = (cs("ltincl_s"), cs("seqm_s"), cs("ms01_s"))
                    self.front_dn(w, 0, True, fe)
                    dn_gen = self.dn_chunk(w, 0, True, True, tabs, False) if self.on("dn") else iter(())
                    for _ in dn_gen:
                        pass
                    if self.on("attn"):
                        pti = self.sb(st, "pti", [128, NSEQ * c.NPG], I32)
                        ptf_t = w.f512
                        ptf = ptf_t
                        io = self.sb(st, "io", [128, 1], I32)
                        iof = self.sb(st, "iof", [128, 1], F32)
                        w.idx = self.sb(st, "idx", [128, NSEQ * c.NPG], I32)
                        self.dma(pti[:], i["ptab"].partition_broadcast(128), writes=[pti])
                        self.E("pool", "iota", [], [io], io[:], pattern=[[0, 1]], base=0, channel_multiplier=1)
                        self.E("dve", "tensor_copy", [io], [iof], out=iof[:], in_=io[:])
                        npt = NSEQ * c.NPG
                        self.E("dve", "tensor_copy", [pti], [ptf], out=ptf[:, 0:npt], in_=pti[:])
                        self.E("dve", "tensor_scalar", [ptf, iof], [ptf], out=ptf[:, 0:npt], in0=ptf[:, 0:npt], scalar1=128.0, scalar2=iof[:, 0:1], op0=ALU.mult, op1=ALU.add)
                        self.E("dve", "tensor_copy", [ptf], [w.idx], out=w.idx[:], in_=ptf[:, 0:npt])
                        w.smt = self.sb(st, "smt", [1, 1], F32)
                        sm01 = self.sb(st, "sm01", [128, NSEQ, 64], BF16)
                        smneg = self.sb(st, "smneg", [128, NSEQ, 64], BF16)
                        o_, w_ = CO_SAMP["smask01"]
                        src = cts[:, o_:o_ + w_].rearrange("p (s q) -> p s q", s=NSEQ)
                        self.E("dve", "tensor_copy", [cts], [w.smt, sm01], out=sm01[:], in_=src)
                        self.E("dve", "tensor_scalar", [cts], [w.smt, smneg], out=smneg[:], in0=src, scalar1=-1.0, scalar2=BIG, op0=ALU.add, op1=ALU.mult)
                        w.sm01, w.smneg = sm01, smneg
                        w.attpad = [self.sb(st, "attpad%d" % k, [128, 8, 128], BF16) for k in range(2)]
                        w.kpf = [self.sb(st, "kpf%d" % k, [128, 512], F32) for k in range(4)]
                        w.vpf = [self.sb(st, "vpf%d" % k, [128, 512], F32) for k in range(4)]
                        w.kpb = [self.sb(st, "kpb%d" % k, [128, 512], BF16) for k in range(4)]
                        w.vpb = [self.sb(st, "vpb%d" % k, [128, 512], BF16) for k in range(4)]
                        w.ktp = [self.sb(st, "ktp%d" % k, [128, 4, 128], BF16) for k in range(4)]
                        self.attn_sample(w)
                    k = c.NOWN
                    if self.on("dn") and self.on("attn"):
                        self.dma(self.mixd[k * 128:(k + 1) * 128, :], w.mixed[:], reads=[w.mixed], writes=[self.t_mixd[k]])
                    if "mixed_s" in self.o:
                        self.E("dve", "tensor_copy", [w.mixed], [w.xt[0]], out=w.xt[0][:], in_=w.mixed[:])
                        self.dma(self.o["mixed_s"][:, :], w.xt[0][:], reads=[w.xt[0]])

    def phase2(self):
        c = self.cfg
        i, o = self.i, self.o
        with contextlib.ExitStack() as p2:
            sb = lambda name, shape, dt: self.sb(p2, name, shape, dt)
            woutb = sb("woutb", [128, DC, D], BF16)
            wupb = sb("wupb", [128, DC, 2 * DFF], BF16)
            wdnb = sb("wdnb", [128, FC, D], BF16)
            with contextlib.ExitStack() as st:
                stg = [self.sb(st, "stg%d" % k, [128, DC * 512], F32) for k in range(2)]
                n = 0
                wv = i["w_out"].rearrange("(c p) n -> p c n", p=128)
                for ct in range(2):
                    self.stream_cast(stg, wv[:, :, ct * 512:(ct + 1) * 512], woutb, woutb[:, :, ct * 512:(ct + 1) * 512], eng="act" if n % 2 else "dve")
                    n += 1
                wv = i["w_up"].rearrange("(c p) n -> p c n", p=128)
                for ct in range(11):
                    self.stream_cast(stg, wv[:, :, ct * 512:(ct + 1) * 512], wupb, wupb[:, :, ct * 512:(ct + 1) * 512], eng="act" if n % 2 else "dve")
                    n += 1
                wv = i["w_down"].rearrange("(c p) n -> p c n", p=128)
                for c0 in range(0, FC, 4):
                    c1 = min(FC, c0 + 4)
                    self.stream_cast(stg, wv[:, c0:c1, :], wdnb, wdnb[:, c0:c1, :], eng="act" if n % 2 else "dve")
                    n += 1
            self.fence()
            gt1, scale2, shift2, gt2 = [sb(nm, [128, D], F32) for nm in ("gt1", "scale2", "shift2", "gt2")]

            class WS:
                pass
            w = WS()
            w.xt = [sb("xt2", [128, D], F32)]
            w.h = sb("h2", [128, D], BF16)
            w.hT = sb("h2T", [128, DC, 128], BF16)
            w.ssq = sb("ssq2", [128, 1], F32)
            w.rstd = sb("rstd2", [128, 1], F32)
            mixb = sb("mixb", [128, D], BF16)
            mT = sb("mT", [128, DC, 128], BF16)
            yt = sb("yt", [128, D], F32)
            UE = sb("UE", [128, 4, NSEQ * (2 + TS)], F32)
            C4 = sb("C4", [128, 4, 128], F32)
            E2 = sb("E2", [128, 2, 128], F32)
            actT = sb("actT", [128, FC, 128], BF16)
            Cp = sb("Cp", [128, 4, 128], F32)
            fso_ap = Cp[:].rearrange("p a b -> p (a b)")
            FH = sb("FH", [128, 44, NSEQ * 2], F32)
            self.E("pool", "memset", [], [FH], FH[:], 0.0)
            blocks = [(b, False) for b in range(c.OWN0, c.NBLK)] + ([(0, True)] if self.on("sample") else [])
            cur_mod = None
            for (b, sample) in blocks:
                if cur_mod != sample:
                    self.load_mod([gt1, scale2, shift2, gt2], [2, 4, 3, 5], sample)
                    cur_mod = sample
                ns, T = (NSEQ, TS) if sample else (1, 128)
                k = c.NOWN if sample else b - c.OWN0
                halo = (not sample) and b == c.OWN0
                xt = w.xt[0]
                self.dma(mixb[:], self.mixd[k * 128:(k + 1) * 128, :], reads=[self.t_mixd[k]], writes=[mixb])
                self.dma(xt[:], i["xs"][:, :] if sample else i["xp"][b * 128:(b + 1) * 128, :], writes=[xt])
                tb = self.TB()
                for dc in range(DC):
                    self.tr(tb[:, dc * 128:(dc + 1) * 128], mixb[:, dc * 128:(dc + 1) * 128], self.identb, [mixb, self.cbf], [tb])
                self.act(mT[:], tb[:, :].rearrange("p (a b) -> p a b", a=DC), AF.Copy, [tb], [mT])
                for n in range(2):
                    g = self.G()
                    for dc in range(DC):
                        self.mm(g[:, :], mT[:, dc, :], woutb[:, dc, n * 512:(n + 1) * 512], dc == 0, dc == DC - 1, [mT, woutb], [g])
                    self.E("dve", "tensor_tensor", [g, gt1], [yt], out=yt[:, n * 512:(n + 1) * 512], in0=g[:, :], in1=gt1[:, n * 512:(n + 1) * 512], op=ALU.mult)
                self.E("pool", "tensor_tensor", [yt, xt], [xt], out=xt[:], in0=yt[:], in1=xt[:], op=ALU.add)
                x1 = xt
                if "x1_p" in o and (not sample) and b >= c.OUT0:
                    r0 = (b - c.OUT0) * 128
                    self.dma(o["x1_p"][r0:r0 + 128, :], x1[:], reads=[x1])
                self.norm_mod(w, x1, scale2, shift2, w.hT, tmp=yt)
                if sample:
                    for j in range(11):
                        hst = fso_ap[0:NSEQ * 2, :]
                        self.dma(hst, i["ffc0"][:, j * 512:(j + 1) * 512], writes=[Cp])
                        g = self.G()
                        for ch in range(4):
                            self.tr(g[:, ch * 32:(ch + 1) * 32], hst[:, ch * 128:(ch + 1) * 128], self.cm("ident")[0:32, 0:32], [Cp, self.ctm], [g])
                        self.act(FH[:, j * 4:(j + 1) * 4, :], g[:, 0:128].rearrange("p (c t) -> p c t", c=4), AF.Copy, [g], [FH])
                ue4 = UE[:, :, 0:ns * (2 + T)].rearrange("p c (s t) -> p c s t", s=ns)
                fh4 = FH[:, :, 0:ns * 2].rearrange("p c (s t) -> p c s t", s=ns)
                for gi_ in range(11):
                    chs = [2 * gi_, 2 * gi_ + 1, FC + 2 * gi_, FC + 2 * gi_ + 1]
                    g = self.G()
                    for q_, ch in enumerate(chs):
                        for dc in range(DC):
                            self.mm(g[:, q_ * 128:(q_ + 1) * 128], wupb[:, dc, ch * 128:(ch + 1) * 128], w.hT[:, dc, :], dc == 0, dc == DC - 1, [w.hT, wupb], [g])
                    for half in range(2):
                        self.E("pool", "tensor_copy", [FH], [UE], out=ue4[:, half * 2:half * 2 + 2, :, 0:2], in_=fh4[:, chs[half * 2]:chs[half * 2] + 2, :, :])
                    src4 = g[:, :].rearrange("p (c s t) -> p c s t", c=4, s=ns)
                    if halo:
                        self.act(ue4[:, :, :, 2:2 + T], src4, AF.Copy, [g, self.bvd], [UE], scale=self.bvd[:, b:b + 1])
                    else:
                        self.act(ue4[:, :, :, 2:2 + T], src4, AF.Copy, [g], [UE])
                    for half in range(2):
                        self.E("pool", "tensor_copy", [UE], [FH], out=fh4[:, chs[half * 2]:chs[half * 2] + 2, :, :], in_=ue4[:, half * 2:half * 2 + 2, :, T:T + 2])
                    if halo:
                        continue
                    if not sample:
                        wv_ = lambda kk: self.wfc[:, kk, :].rearrange("p (a c) -> p a c", a=2)[:, :, 2 * gi_:2 * gi_ + 2].unsqueeze(3).to_broadcast([128, 2, 2, T])
                        xv_ = lambda kk: UE[:, :, kk:kk + T].rearrange("p (a c) t -> p a c t", a=2)
                        c4v = C4[:].rearrange("p (a c) t -> p a c t", a=2)
                        cpv = Cp[:].rearrange("p (a c) t -> p a c t", a=2)
                        self.E("pool", "tensor_tensor", [UE, self.wfc], [Cp], out=cpv, in0=xv_(1), in1=wv_(1), op=ALU.mult)
                        self.E("dve", "tensor_tensor", [UE, self.wfc], [C4], out=c4v, in0=xv_(0), in1=wv_(0), op=ALU.mult)
                        self.E("dve", "tensor_tensor", [C4, Cp], [C4], out=C4[:], in0=C4[:], in1=Cp[:], op=ALU.add)
                        self.E("pool", "tensor_tensor", [UE, self.wfc], [Cp], out=cpv, in0=xv_(2), in1=wv_(2), op=ALU.mult)
                        self.E("dve", "tensor_tensor", [C4, Cp], [C4], out=C4[:], in0=C4[:], in1=Cp[:], op=ALU.add)
                    for q_, ch in enumerate(chs if sample else []):
                        eng = "dve"
                        yv = C4[:, q_, :].rearrange("p (s t) -> p s t", s=ns)
                        self.E(eng, "tensor_scalar", [UE, self.wfc], [C4], out=yv, in0=ue4[:, q_, :, 0:T], scalar1=self.wfc[:, 0, ch:ch + 1], scalar2=None, op0=ALU.mult)
                        for kk in range(1, 3):
                            self.E(eng, "scalar_tensor_tensor", [UE, self.wfc, C4], [C4], out=yv, in0=ue4[:, q_, :, kk:kk + T],
                                   scalar=self.wfc[:, kk, ch:ch + 1], in1=yv, op0=ALU.mult, op1=ALU.add)
                    self.act(E2[:], C4[:, 2:4, :], AF.Exp, [C4], [E2], scale=-1.0)
                    self.act(E2[:], E2[:], AF.Ln, [E2], [E2], bias=1.0)
                    self.act(E2[:], E2[:], AF.Exp, [E2], [E2], scale=-1.0)
                    self.E("pool", "tensor_tensor", [E2, C4], [E2], out=E2[:], in0=E2[:], in1=C4[:, 2:4, :], op=ALU.mult)
                    self.E("dve", "tensor_tensor", [E2, C4], [actT], out=actT[:, 2 * gi_:2 * gi_ + 2, :], in0=E2[:], in1=C4[:, 0:2, :], op=ALU.mult)
                last_p = (not sample) and b == c.NBLK - 1
                if sample or last_p:
                    ncol = ns * 2
                    dst = o["fcs"] if sample else o["fcp"]
                    for j in range(11):
                        g = self.G()
                        for ch in range(4):
                            self.tr(g[0:ncol, ch * 128:(ch + 1) * 128], FH[:, j * 4 + ch, 0:ncol], self.cm("ident"), [FH, self.ctm], [g])
                        self.E("dve", "tensor_copy", [g], [Cp], out=fso_ap[0:ncol, :], in_=g[0:ncol, :])
                        self.dma(dst[:, j * 512:(j + 1) * 512], fso_ap[0:ncol, :], reads=[Cp])
                if halo:
                    continue
                for n in range(2):
                    g = self.G()
                    for fc_ in range(FC):
                        self.mm(g[:, :], actT[:, fc_, :], wdnb[:, fc_, n * 512:(n + 1) * 512], fc_ == 0, fc_ == FC - 1, [actT, wdnb], [g])
                    self.E("dve", "tensor_tensor", [g, gt2], [yt], out=yt[:, n * 512:(n + 1) * 512], in0=g[:, :], in1=gt2[:, n * 512:(n + 1) * 512], op=ALU.mult)
                self.E("pool", "tensor_tensor", [yt, x1], [yt], out=yt[:], in0=yt[:], in1=x1[:], op=ALU.add)
                if sample:
                    self.dma(o["ys"][:, :], yt[:], reads=[yt])
                elif b >= c.OUT0:
                    r0 = (b - c.OUT0) * 128
                    self.dma(o["yp"][r0:r0 + 128, :], yt[:], reads=[yt], queue="pool")


def core_inputs(cfg, core, inp):
    b, half = core // 2, core % 2
    S = cfg.NBLK * 128
    xp_full = np.asarray(inp["x_prompt"][b], np.float32)
    if half == 1:
        xp = xp_full
    else:
        xp = np.concatenate([np.zeros((S // 2, D), np.float32), xp_full[:S // 2]], axis=0)
    s0, s1 = core * NSEQ, (core + 1) * NSEQ
    m = {}
    m["xp"] = np.ascontiguousarray(xp)
    m["xs"] = np.ascontiguousarray(np.asarray(inp["x_sample"][s0:s1], np.float32).reshape(NSEQ * TS, D))
    m["cvec"] = np.ascontiguousarray(np.concatenate([np.asarray(inp["c_prompt"][b:b + 1], np.float32),
                                                     np.asarray(inp["c_sample"][s0:s1], np.float32)], axis=0))
    m["cache_k"] = np.asarray(inp["cache_k"], np.float32).reshape(cfg.NPHYS * 128, SBW)
    m["cache_v"] = np.asarray(inp["cache_v"], np.float32).reshape(cfg.NPHYS * 128, SBW)
    m["ptab"] = np.ascontiguousarray(np.asarray(inp["page_table"][s0:s1], np.int32).reshape(-1))
    m["s0"] = np.ascontiguousarray(np.asarray(inp["state_delta"][0, s0:s1], np.float32))
    m["dnc0"] = np.ascontiguousarray(np.asarray(inp["state_dn_conv"][0, s0:s1], np.float32).reshape(NSEQ * 3, 3 * DNW))
    m["ffc0"] = np.ascontiguousarray(np.asarray(inp["state_ffn_conv"][0, s0:s1], np.float32).reshape(NSEQ * 2, 2 * DFF))
    for k, nm in (("w_ada", "w_ada"), ("b_ada", "b_ada"), ("g_attn", "g_attn_norm"), ("w_in", "w_in"), ("g_q", "g_q"),
                  ("g_k", "g_k"), ("sb_bias", "sb_bias"), ("g_sb_out", "g_sb_out"), ("w_dn_conv", "w_dn_conv"),
                  ("a_log", "a_log"), ("dt_bias", "dt_bias"), ("g_dn_out", "g_dn_out"), ("w_out", "w_out"),
                  ("g_ffn", "g_ffn_norm"), ("w_up", "w_up"), ("w_ffn_conv", "w_ffn_conv"), ("w_down", "w_down")):
        m[k] = np.ascontiguousarray(np.asarray(inp[nm], np.float32)[0])
    m["ct_main"] = CT_MAIN
    m["ct_samp"] = CT_SAMP
    kv = np.zeros((128, 256), np.float32)
    if half == 1:
        kv[0:2, 0:128] = 1.0
    else:
        kv[2, 0:128] = 1.0
    kv[0:2, 128:256] = 1.0
    m["kvlo"] = kv
    bv = np.ones((128, cfg.NBLK), np.float32)
    if half == 0:
        bv[:, :cfg.NBLK // 2] = 0.0
    m["blkvalid"] = bv
    return m


def assemble(cfg, res, nb, nsamp):
    S = cfg.NBLK * 128
    H = S // 2
    yp = np.zeros((nb, S, D), np.float32)
    ys = np.zeros((nsamp, TS, D), np.float32)
    kp = np.zeros((1, nb, S, HS, HD), np.float32)
    vp = np.zeros((1, nb, S, HS, HD), np.float32)
    ks = np.zeros((1, nsamp, TS, HS, HD), np.float32)
    vs = np.zeros((1, nsamp, TS, HS, HD), np.float32)
    sp = np.zeros((1, nb, DNH, DND, DND), np.float32)
    ss = np.zeros((1, nsamp, DNH, DND, DND), np.float32)
    dcp = np.zeros((1, nb, 3, 3 * DNW), np.float32)
    dcs = np.zeros((1, nsamp, 3, 3 * DNW), np.float32)
    fcp = np.zeros((1, nb, 2, 2 * DFF), np.float32)
    fcs = np.zeros((1, nsamp, 2, 2 * DFF), np.float32)
    for core, r in res.items():
        b, half = core // 2, core % 2
        s0, s1 = core * NSEQ, (core + 1) * NSEQ
        yp[b, half * H:(half + 1) * H] = r["yp"]
        kp[0, b, half * H:(half + 1) * H] = r["kp"].reshape(H, HS, HD)
        vp[0, b, half * H:(half + 1) * H] = r["vp"].reshape(H, HS, HD)
        ys[s0:s1] = r["ys"].reshape(NSEQ, TS, D)
        ks[0, s0:s1] = r["ksm"].reshape(NSEQ, TS, HS, HD)
        vs[0, s0:s1] = r["vsm"].reshape(NSEQ, TS, HS, HD)
        ss[0, s0:s1] = r["ss_state"]
        dcs[0, s0:s1] = r["dcs"].reshape(NSEQ, 3, 3 * DNW)
        fcs[0, s0:s1] = r["fcs"].reshape(NSEQ, 2, 2 * DFF)
        if half == 1:
            sp[0, b] = r["sp_state"]
            dcp[0, b] = r["dcp"]
            fcp[0, b] = r["fcp"]
    return (yp, ys, kp, vp, ks, vs, sp, ss, dcp, dcs, fcp, fcs)


_NC_CACHE = {}


def kernel(**inputs):
    cfg = Cfg(nblk=inputs["x_prompt"].shape[1] // 128, npg=inputs["page_table"].shape[1], nphys=inputs["cache_k"].shape[1])
    key = (cfg.NBLK, cfg.NPG, cfg.NPHYS)
    if key not in _NC_CACHE:
        _NC_CACHE[key] = Builder(cfg).build()
    nc = _NC_CACHE[key]
    ncores = 8
    in_maps = [core_inputs(cfg, c, inputs) for c in range(ncores)]
    res = run_bass_kernel_spmd(nc, in_maps, core_ids=list(range(ncores)))
    out = assemble(cfg, {c: res.results[c] for c in range(ncores)}, inputs["x_prompt"].shape[0], inputs["x_sample"].shape[0])
    return out
```
